# Optimizing a Trainium2 kernel written in Bass

```python
import math
import jax, jax.numpy as jnp
from jax import lax
import numpy as np

D_MODEL = 1024
BATCH = 8
SEQ = 2048
DEPTH = 1
DEC_BATCH = 32
DEC_SEQ = 1
PAST_LEN = 16384
PAGE_SIZE = 128

PLE_DIM = 256
GDN_HEADS = 8
GDN_DK = 64
GDN_DV = 64
GDN_WIDTH = GDN_HEADS * GDN_DV
GDN_QK_DIM = GDN_HEADS * GDN_DK
CONV_WIDTH = 4
CONV_DIM = 2 * GDN_QK_DIM + GDN_WIDTH
GDN_CHUNK = 64
MLA_HEADS = 8
MLA_NOPE = 64
MLA_ROPE = 32
MLA_VDIM = 64
MLA_QK = MLA_NOPE + MLA_ROPE
MLA_WIDTH = MLA_HEADS * MLA_VDIM
Q_LORA = 384
KV_LORA = 256
ROPE_THETA = 10000.0
ATTN_SCALE = MLA_QK ** -0.5
Q_BLOCK = 128
MIX_WIDTH = GDN_WIDTH + MLA_WIDTH
D_FF = -(-8 * D_MODEL // (3 * 256)) * 256
IN_SPLITS = (CONV_DIM, CONV_DIM + GDN_HEADS, CONV_DIM + 2 * GDN_HEADS,
             CONV_DIM + 2 * GDN_HEADS + GDN_WIDTH,
             CONV_DIM + 2 * GDN_HEADS + GDN_WIDTH + Q_LORA)
IN_DIM = IN_SPLITS[-1] + KV_LORA + MLA_ROPE
EPS = 1e-6

kernel_name = 'hybrid_gdn_mla_parallel_heads_step'


def rmsnorm(x, g):
    xf = x.astype(jnp.float32)
    y = xf * lax.rsqrt(jnp.mean(xf * xf, axis=-1, keepdims=True) + EPS)
    return (y * g.astype(jnp.float32)).astype(x.dtype)


def l2norm(x):
    xf = x.astype(jnp.float32)
    return xf * lax.rsqrt(jnp.sum(xf * xf, axis=-1, keepdims=True) + EPS)


def rope_tables(pos):
    half = MLA_ROPE // 2
    inv_freq = ROPE_THETA ** (-jnp.arange(half, dtype=jnp.float32) / half)
    ang = pos.astype(jnp.float32)[:, None] * inv_freq[None, :]
    return jnp.cos(ang), jnp.sin(ang)


def apply_rope(x, cos, sin):
    extra = x.ndim - 3
    c = cos.reshape(cos.shape[:1] + (1,) * extra + cos.shape[1:])
    s = sin.reshape(sin.shape[:1] + (1,) * extra + sin.shape[1:])
    xf = x.astype(jnp.float32)
    half = MLA_ROPE // 2
    x1, x2 = xf[..., :half], xf[..., half:]
    return jnp.concatenate([x1 * c - x2 * s, x1 * s + x2 * c], axis=-1).astype(x.dtype)


def split_in(z):
    return jnp.split(z, list(IN_SPLITS), axis=-1)


def causal_conv(ext, w):
    s = ext.shape[1] - (CONV_WIDTH - 1)
    y = ext[:, 0:s] * w[0]
    for j in range(1, CONV_WIDTH):
        y = y + ext[:, j:j + s] * w[j]
    return jax.nn.silu(y)


def gdn_prep(conv_out, a, b, a_log, dt_bias):
    bsz, s, _ = conv_out.shape
    q = l2norm(conv_out[..., :GDN_QK_DIM].reshape(bsz, s, GDN_HEADS, GDN_DK)) * (GDN_DK ** -0.5)
    k = l2norm(conv_out[..., GDN_QK_DIM:2 * GDN_QK_DIM].reshape(bsz, s, GDN_HEADS, GDN_DK))
    v = conv_out[..., 2 * GDN_QK_DIM:].reshape(bsz, s, GDN_HEADS, GDN_DV).astype(jnp.float32)
    g = -jnp.exp(a_log.astype(jnp.float32)) * jax.nn.softplus(a.astype(jnp.float32) + dt_bias.astype(jnp.float32))
    beta = jax.nn.sigmoid(b.astype(jnp.float32))
    return q, k, v, g, beta


def gdn_chunked(q, k, v, g, beta):
    bsz, s, h, dk = k.shape
    dv = v.shape[-1]
    c = GDN_CHUNK
    n = s // c

    def to_chunks(t):
        return jnp.moveaxis(t.reshape((bsz, n, c, h) + t.shape[3:]), 3, 1)

    q, k, v, g, beta = [to_chunks(t) for t in (q, k, v, g, beta)]
    gc = jnp.cumsum(g, axis=-1)
    tril = jnp.tril(jnp.ones((c, c), bool))
    strict = jnp.tril(jnp.ones((c, c), bool), -1)
    decay = jnp.exp(jnp.where(tril, gc[..., :, None] - gc[..., None, :], -jnp.inf))
    kb = k * beta[..., None]
    vb = v * beta[..., None]
    a_mat = jnp.where(strict, jnp.einsum('bhncd,bhnsd->bhncs', kb, k) * decay, 0.0)
    eye = jnp.eye(c, dtype=jnp.float32)
    t_inv = lax.linalg.triangular_solve(a_mat + eye, jnp.broadcast_to(eye, a_mat.shape), left_side=True, lower=True)
    u = t_inv @ vb
    w = t_inv @ (kb * jnp.exp(gc)[..., None])
    qk = jnp.einsum('bhncd,bhnsd->bhncs', q, k) * decay
    xs = tuple(jnp.moveaxis(t, 2, 0) for t in (q, k, u, w, gc, qk))

    def step(state, inp):
        q_c, k_c, u_c, w_c, g_c, qk_c = inp
        v_new = u_c - jnp.einsum('bhcd,bhde->bhce', w_c, state)
        o = jnp.einsum('bhcd,bhde->bhce', q_c * jnp.exp(g_c)[..., None], state) + jnp.einsum('bhcs,bhse->bhce', qk_c, v_new)
        g_last = g_c[..., -1]
        state = state * jnp.exp(g_last)[..., None, None] + jnp.einsum(
            'bhcd,bhce->bhde', k_c * jnp.exp(g_last[..., None] - g_c)[..., None], v_new)
        return state, o

    s0 = jnp.zeros((bsz, h, dk, dv), jnp.float32)
    s_fin, o = lax.scan(step, s0, xs)
    o = jnp.moveaxis(o, 0, 2).reshape(bsz, h, s, dv).transpose(0, 2, 1, 3)
    return o, s_fin


def gdn_recurrent(q, k, v, g, beta, s0):
    xs = tuple(jnp.moveaxis(t, 1, 0) for t in (q, k, v, g, beta))

    def step(state, inp):
        q_t, k_t, v_t, g_t, b_t = inp
        state = state * jnp.exp(g_t)[..., None, None]
        delta = (v_t - jnp.einsum('bhd,bhde->bhe', k_t, state)) * b_t[..., None]
        state = state + jnp.einsum('bhd,bhe->bhde', k_t, delta)
        return state, jnp.einsum('bhd,bhde->bhe', q_t, state)

    s_fin, o = lax.scan(step, s0.astype(jnp.float32), xs)
    return jnp.moveaxis(o, 0, 1), s_fin


def gdn_output(o, z, g_out):
    bsz, s = o.shape[:2]
    gate = jax.nn.silu(z.astype(jnp.float32).reshape(bsz, s, GDN_HEADS, GDN_DV))
    return (rmsnorm(o, g_out) * gate).reshape(bsz, s, GDN_WIDTH)


def mla_project(qa, kva, cos, sin, g_q_a, w_q_b, g_q_nope, g_q_rope, g_kv_a, g_k_rope):
    bsz, s = qa.shape[:2]
    q = (rmsnorm(qa, g_q_a) @ w_q_b).reshape(bsz, s, MLA_HEADS, MLA_QK)
    q_nope = rmsnorm(q[..., :MLA_NOPE], g_q_nope)
    q_rope = apply_rope(rmsnorm(q[..., MLA_NOPE:], g_q_rope), cos, sin)
    c_kv = rmsnorm(kva[..., :KV_LORA], g_kv_a)
    k_rope = apply_rope(rmsnorm(kva[..., KV_LORA:], g_k_rope), cos, sin)
    return q_nope, q_rope, c_kv, k_rope


def mla_expand(c_kv, w_kv_b, g_k_nope):
    kv = (c_kv @ w_kv_b).reshape(c_kv.shape[:-1] + (MLA_HEADS, MLA_NOPE + MLA_VDIM))
    return rmsnorm(kv[..., :MLA_NOPE], g_k_nope), kv[..., MLA_NOPE:]


def mla_prompt_attention(q_nope, q_rope, k_nope, k_rope, v):
    bsz, s, h, _ = q_nope.shape
    nb = s // Q_BLOCK
    qn = jnp.moveaxis(q_nope.reshape(bsz, nb, Q_BLOCK, h, MLA_NOPE), 1, 0)
    qr = jnp.moveaxis(q_rope.reshape(bsz, nb, Q_BLOCK, h, MLA_ROPE), 1, 0)
    key_pos = jnp.arange(s)
    kn = k_nope.astype(jnp.float32)
    kr = k_rope.astype(jnp.float32)
    vf = v.astype(jnp.float32)

    def block(args):
        qn_b, qr_b, i = args
        sc = (jnp.einsum('bqhd,bkhd->bhqk', qn_b.astype(jnp.float32), kn)
              + jnp.einsum('bqhd,bkd->bhqk', qr_b.astype(jnp.float32), kr)) * ATTN_SCALE
        qpos = i * Q_BLOCK + jnp.arange(Q_BLOCK)
        sc = jnp.where(qpos[:, None] >= key_pos[None, :], sc, -jnp.inf)
        p = jax.nn.softmax(sc, axis=-1)
        return jnp.einsum('bhqk,bkhd->bqhd', p, vf)

    o = lax.map(block, (qn, qr, jnp.arange(nb)))
    return jnp.moveaxis(o, 0, 1).reshape(bsz, s, h * MLA_VDIM)


def mla_sample_attention(q_nope, q_rope, ckv_new, kr_new, ckv_pool, kr_pool, page_table, w_kv_b, g_k_nope):
    bsz, t, h, _ = q_nope.shape
    qn = q_nope.astype(jnp.float32)
    qr = q_rope.astype(jnp.float32)

    def scores(k_nope, k_rope):
        return (jnp.einsum('bqhd,bkhd->bhqk', qn, k_nope.astype(jnp.float32))
                + jnp.einsum('bqhd,bkd->bhqk', qr, k_rope.astype(jnp.float32))) * ATTN_SCALE

    kn, vv = mla_expand(ckv_new, w_kv_b, g_k_nope)
    causal = jnp.tril(jnp.ones((t, t), bool))
    sc = jnp.where(causal, scores(kn, kr_new), -jnp.inf)
    m = sc.max(axis=-1)
    p = jnp.exp(sc - m[..., None])
    l = p.sum(axis=-1)
    acc = jnp.einsum('bhqk,bkhd->bhqd', p, vv.astype(jnp.float32))

    def page_step(carry, pages):
        m, l, acc = carry
        kn_p, v_p = mla_expand(ckv_pool[pages], w_kv_b, g_k_nope)
        sc_p = scores(kn_p, kr_pool[pages])
        m_new = jnp.maximum(m, sc_p.max(axis=-1))
        alpha = jnp.exp(m - m_new)
        p_p = jnp.exp(sc_p - m_new[..., None])
        l = l * alpha + p_p.sum(axis=-1)
        acc = acc * alpha[..., None] + jnp.einsum('bhqk,bkhd->bhqd', p_p, v_p.astype(jnp.float32))
        return (m_new, l, acc), None

    (m, l, acc), _ = lax.scan(page_step, (m, l, acc), page_table.T)
    o = acc / l[..., None]
    return o.transpose(0, 2, 1, 3).reshape(bsz, t, h * MLA_VDIM)


def layer_tail(h, o_mix, p, w_o, g_ffn, w_ffn_gate, w_ffn_up, w_ffn_down, g_ple, w_ple_gate, w_ple_proj):
    h = h + o_mix.astype(h.dtype) @ w_o
    u = rmsnorm(h, g_ffn)
    h = h + (jax.nn.silu(u @ w_ffn_gate) * (u @ w_ffn_up)) @ w_ffn_down
    gate = jax.nn.sigmoid(rmsnorm(h, g_ple) @ w_ple_gate)
    return h + (p.astype(h.dtype) @ w_ple_proj) * gate


def setup_inputs(seed: int = 0) -> dict:
    key = jax.random.key(seed)
    k = jax.random.split(key, 32)

    def nrm(i, shape, scale):
        return jax.random.normal(k[i], shape, jnp.float32) * scale

    def gain(i, n):
        return 1.0 + 0.02 * jax.random.normal(k[i], (DEPTH, n), jnp.float32)

    n_pages = PAST_LEN // PAGE_SIZE
    n_used = DEC_BATCH * n_pages
    n_phys = n_used + (n_used + 3) // 4
    page_table = jax.random.permutation(k[6], n_phys)[:n_used].reshape(DEC_BATCH, n_pages).astype(jnp.int32)
    a_log = jnp.log(jax.random.uniform(k[12], (DEPTH, GDN_HEADS), jnp.float32, 1.0, 16.0))
    dt = jnp.exp(jax.random.uniform(k[13], (DEPTH, GDN_HEADS), jnp.float32, math.log(1e-3), math.log(1e-1)))
    dt_bias = dt + jnp.log(-jnp.expm1(-dt))
    return {
        'x_prompt': nrm(0, (BATCH, SEQ, D_MODEL), 1.0),
        'x_sample': nrm(1, (DEC_BATCH, DEC_SEQ, D_MODEL), 1.0),
        'cache_ckv': nrm(2, (DEPTH, n_phys, PAGE_SIZE, KV_LORA), 1.0),
        'cache_krope': nrm(3, (DEPTH, n_phys, PAGE_SIZE, MLA_ROPE), 1.0),
        'state_gdn': nrm(4, (DEPTH, DEC_BATCH, GDN_HEADS, GDN_DK, GDN_DV), 0.1),
        'state_conv': nrm(5, (DEPTH, DEC_BATCH, CONV_WIDTH - 1, CONV_DIM), 1.0),
        'page_table': page_table,
        'p_prompt': nrm(7, (DEPTH, BATCH, SEQ, PLE_DIM), 1.0),
        'p_sample': nrm(8, (DEPTH, DEC_BATCH, DEC_SEQ, PLE_DIM), 1.0),
        'g_attn': gain(9, D_MODEL),
        'w_in': nrm(10, (DEPTH, D_MODEL, IN_DIM), D_MODEL ** -0.5),
        'w_conv': nrm(11, (DEPTH, CONV_WIDTH, CONV_DIM), CONV_WIDTH ** -0.5),
        'gdn_a_log': a_log,
        'gdn_dt_bias': dt_bias,
        'g_gdn_out': gain(14, GDN_DV),
        'g_q_a': gain(15, Q_LORA),
        'w_q_b': nrm(16, (DEPTH, Q_LORA, MLA_HEADS * MLA_QK), Q_LORA ** -0.5),
        'g_q_nope': gain(17, MLA_NOPE),
        'g_q_rope': gain(18, MLA_ROPE),
        'g_kv_a': gain(19, KV_LORA),
        'g_k_rope': gain(20, MLA_ROPE),
        'w_kv_b': nrm(21, (DEPTH, KV_LORA, MLA_HEADS * (MLA_NOPE + MLA_VDIM)), KV_LORA ** -0.5),
        'g_k_nope': gain(22, MLA_NOPE),
        'w_o': nrm(23, (DEPTH, MIX_WIDTH, D_MODEL), MIX_WIDTH ** -0.5),
        'g_ffn': gain(24, D_MODEL),
        'w_ffn_gate': nrm(25, (DEPTH, D_MODEL, D_FF), D_MODEL ** -0.5),
        'w_ffn_up': nrm(26, (DEPTH, D_MODEL, D_FF), D_MODEL ** -0.5),
        'w_ffn_down': nrm(27, (DEPTH, D_FF, D_MODEL), D_FF ** -0.5),
        'g_ple': gain(28, D_MODEL),
        'w_ple_gate': nrm(29, (DEPTH, D_MODEL, D_MODEL), D_MODEL ** -0.5),
        'w_ple_proj': nrm(30, (DEPTH, PLE_DIM, D_MODEL), PLE_DIM ** -0.5),
    }


def reference(x_prompt, x_sample, cache_ckv, cache_krope, state_gdn, state_conv, page_table,
              p_prompt, p_sample, g_attn, w_in, w_conv, gdn_a_log, gdn_dt_bias, g_gdn_out,
              g_q_a, w_q_b, g_q_nope, g_q_rope, g_kv_a, g_k_rope, w_kv_b, g_k_nope, w_o,
              g_ffn, w_ffn_gate, w_ffn_up, w_ffn_down, g_ple, w_ple_gate, w_ple_proj):
    bp, seq, _ = x_prompt.shape
    bs, dec_seq, _ = x_sample.shape
    past = page_table.shape[1] * cache_ckv.shape[2]
    cos_p, sin_p = rope_tables(jnp.arange(seq))
    cos_s, sin_s = rope_tables(past + jnp.arange(dec_seq))
    hp, hs = x_prompt, x_sample
    ckv_p_l, kr_p_l, gdn_p_l, conv_p_l = [], [], [], []
    ckv_s_l, kr_s_l, gdn_s_l, conv_s_l = [], [], [], []
    for i in range(DEPTH):
        conv_in, a, b, z, qa, kva = split_in(rmsnorm(hp, g_attn[i]) @ w_in[i])
        ext = jnp.concatenate([jnp.zeros((bp, CONV_WIDTH - 1, CONV_DIM), conv_in.dtype), conv_in], axis=1)
        q, k, v, g, beta = gdn_prep(causal_conv(ext, w_conv[i]), a, b, gdn_a_log[i], gdn_dt_bias[i])
        o_g, s_fin = gdn_chunked(q, k, v, g, beta)
        o_gdn = gdn_output(o_g, z, g_gdn_out[i])
        q_nope, q_rope, ckv, kr = mla_project(qa, kva, cos_p, sin_p, g_q_a[i], w_q_b[i], g_q_nope[i],
                                              g_q_rope[i], g_kv_a[i], g_k_rope[i])
        k_nope, v_mla = mla_expand(ckv, w_kv_b[i], g_k_nope[i])
        o_mla = mla_prompt_attention(q_nope, q_rope, k_nope, kr, v_mla)
        hp = layer_tail(hp, jnp.concatenate([o_gdn, o_mla.astype(o_gdn.dtype)], axis=-1), p_prompt[i], w_o[i],
                        g_ffn[i], w_ffn_gate[i], w_ffn_up[i], w_ffn_down[i], g_ple[i], w_ple_gate[i], w_ple_proj[i])
        ckv_p_l.append(ckv)
        kr_p_l.append(kr)
        gdn_p_l.append(s_fin)
        conv_p_l.append(ext[:, -(CONV_WIDTH - 1):])
        conv_in, a, b, z, qa, kva = split_in(rmsnorm(hs, g_attn[i]) @ w_in[i])
        ext = jnp.concatenate([state_conv[i].astype(conv_in.dtype), conv_in], axis=1)
        q, k, v, g, beta = gdn_prep(causal_conv(ext, w_conv[i]), a, b, gdn_a_log[i], gdn_dt_bias[i])
        o_g, s_fin = gdn_recurrent(q, k, v, g, beta, state_gdn[i])
        o_gdn = gdn_output(o_g, z, g_gdn_out[i])
        q_nope, q_rope, ckv, kr = mla_project(qa, kva, cos_s, sin_s, g_q_a[i], w_q_b[i], g_q_nope[i],
                                              g_q_rope[i], g_kv_a[i], g_k_rope[i])
        o_mla = mla_sample_attention(q_nope, q_rope, ckv, kr, cache_ckv[i], cache_krope[i], page_table,
                                     w_kv_b[i], g_k_nope[i])
        hs = layer_tail(hs, jnp.concatenate([o_gdn, o_mla.astype(o_gdn.dtype)], axis=-1), p_sample[i], w_o[i],
                        g_ffn[i], w_ffn_gate[i], w_ffn_up[i], w_ffn_down[i], g_ple[i], w_ple_gate[i], w_ple_proj[i])
        ckv_s_l.append(ckv)
        kr_s_l.append(kr)
        gdn_s_l.append(s_fin)
        conv_s_l.append(ext[:, -(CONV_WIDTH - 1):])
    y_prompt = hp
    y_sample = hs
    ckv_prompt = jnp.stack(ckv_p_l)
    krope_prompt = jnp.stack(kr_p_l)
    gdn_state_prompt = jnp.stack(gdn_p_l)
    conv_state_prompt = jnp.stack(conv_p_l)
    ckv_sample = jnp.stack(ckv_s_l)
    krope_sample = jnp.stack(kr_s_l)
    gdn_state_sample = jnp.stack(gdn_s_l)
    conv_state_sample = jnp.stack(conv_s_l)
    return (y_prompt, y_sample, ckv_prompt, krope_prompt, gdn_state_prompt, conv_state_prompt,
            ckv_sample, krope_sample, gdn_state_sample, conv_state_sample)
```

```python
import contextlib
import numpy as np
import concourse.bass as bass
import concourse.mybir as mybir

F32 = mybir.dt.float32
F32R = mybir.dt.float32r
BF16 = mybir.dt.bfloat16
I32 = mybir.dt.int32
AF = mybir.ActivationFunctionType
ALU = mybir.AluOpType
AX = mybir.AxisListType
SKIP_SELF_WAIT = False


class V:
    __slots__ = ("t", "ap")

    def __init__(self, t, ap):
        self.t = t
        self.ap = ap

    def __getitem__(self, idx):
        return V(self.t, self.ap[idx])

    def re(self, pat, **kw):
        return V(self.t, self.ap.rearrange(pat, **kw))

    def bc(self, dtype):
        return V(self.t, self.ap.bitcast(dtype))

    def un(self, axis):
        return V(self.t, self.ap.unsqueeze(axis))

    def bto(self, shape):
        return V(self.t, self.ap.broadcast_to(shape))

    def pb(self, n):
        return V(self.t, self.ap.partition_broadcast(n))


class T:
    def __init__(self, name, h):
        self.name = name
        self.h = h
        self.w = None
        self.r = {}
        self.dsem = None
        self.dtot = 0
        self.psum = False

    def __getitem__(self, idx):
        return V(self, self.h[idx])

    @property
    def v(self):
        return V(self, self.h[:])


class KB:
    def __init__(self, nc, es, needed=None):
        self.nc = nc
        self.es = es
        self.es_sem = es
        self.eng = {"pe": nc.tensor, "act": nc.scalar, "dve": nc.vector, "pool": nc.gpsimd, "sp": nc.sync}
        self.sem = {k: es.enter_context(nc.semaphore("s_" + k)) for k in self.eng}
        self.cnt = {k: 0 for k in self.eng}
        self.waited = {k: {} for k in self.eng}
        self.dma_sems = {}
        self.out_events = []
        self.ninst = 0
        self.needed = needed
        self.need_rec = {k: set() for k in self.eng}
        self.rank = {k: 0 for k in self.eng}
        self.rankmap = {k: {} for k in self.eng}
        self.engsem = {id(v): k for k, v in self.sem.items()}

    def sb(self, name, shape, dt):
        self.uid = getattr(self, "uid", 0) + 1
        name = f"{name}_u{self.uid}"
        return T(name, self.es.enter_context(self.nc.sbuf_tensor(name, list(shape), dt)))

    def ps(self, name, shape, dt):
        t = T(name, self.es.enter_context(self.nc.psum_tensor(name, list(shape), dt)))
        t.psum = True
        return t

    def dram(self, name, shape, dt, kind):
        return T(name, self.nc.dram_tensor(name, list(shape), dt, kind=kind).ap())

    def _wait(self, e, ev):
        sem, val = ev
        sid = id(sem)
        if e == "pe" and sem is self.sem["pe"]:
            return
        if SKIP_SELF_WAIT and sem is self.sem.get(e):
            return
        if sid in self.dma_sems:
            val = max(val, self.dma_sems[sid][1])
        w = self.waited[e]
        if w.get(sid, 0) >= val:
            return
        w[sid] = val
        f = self.engsem.get(sid)
        if f is not None:
            self.need_rec[f].add(val)
            if self.needed is not None:
                val = self.rankmap[f][val]
        self.eng[e].wait_ge(sem, val)

    def _deps(self, e, reads, writes):
        for v in reads:
            t = v.t
            if t.w is not None:
                self._wait(e, t.w)
            if t.psum:
                for ev in t.r.values():
                    if ev[0] is not self.sem.get(e):
                        self._wait(e, ev)
        for v in writes:
            t = v.t
            if t.w is not None:
                self._wait(e, t.w)
            for ev in t.r.values():
                self._wait(e, ev)

    def _record(self, ev, reads, writes):
        sem, val = ev
        for v in reads:
            v.t.r[id(sem)] = ev
        for v in writes:
            v.t.w = ev
            v.t.r = {}

    def op(self, e, fn, reads, writes):
        reads = [v for v in reads if isinstance(v, V)]
        self._deps(e, reads, writes)
        inst = fn(self.eng[e])
        self.cnt[e] += 1
        if self.needed is None:
            inst.then_inc(self.sem[e], 1)
        elif self.cnt[e] in self.needed[e]:
            inst.then_inc(self.sem[e], 1)
            self.rank[e] += 1
            self.rankmap[e][self.cnt[e]] = self.rank[e]
        self._record((self.sem[e], self.cnt[e]), reads, writes)
        self.ninst += 1
        return inst

    def dma(self, q, out, in_, **kw):
        self._deps(q, [in_], [out])
        own = out.t
        if own.dsem is None:
            own.dsem = self.es_sem.enter_context(self.nc.semaphore("d_" + own.name))
            self.dma_sems[id(own.dsem)] = [own.dsem, 0]
        inst = self.eng[q].dma_start(out=out.ap, in_=in_.ap, **kw)
        own.dtot += 16
        self.dma_sems[id(own.dsem)][1] = own.dtot
        inst.then_inc(own.dsem, 16)
        ev = (own.dsem, own.dtot)
        self._record(ev, [in_], [out])
        self.ninst += 1
        return ev

    def gather(self, out, in_, idx, **kw):
        q = "pool"
        self._deps(q, [in_, idx], [out])
        own = out.t
        if own.dsem is None:
            own.dsem = self.es_sem.enter_context(self.nc.semaphore("d_" + own.name))
            self.dma_sems[id(own.dsem)] = [own.dsem, 0]
        inst = self.nc.gpsimd.indirect_dma_start(
            out=out.ap, out_offset=None, in_=in_.ap,
            in_offset=bass.IndirectOffsetOnAxis(ap=idx.ap, axis=0), **kw)
        own.dtot += 16
        self.dma_sems[id(own.dsem)][1] = own.dtot
        inst.then_inc(own.dsem, 16)
        ev = (own.dsem, own.dtot)
        self._record(ev, [in_, idx], [out])
        return ev

    def barrier(self):
        for e in self.eng:
            for sem, tot in self.dma_sems.values():
                if tot:
                    self._wait(e, (sem, tot))
            for f in self.eng:
                if f != e and self.cnt[f]:
                    self._wait(e, (self.sem[f], self.cnt[f]))

    def finish(self):
        for sem, tot in self.dma_sems.values():
            if tot:
                self._wait("sp", (sem, tot))
        for e in self.eng:
            if e != "sp" and self.cnt[e]:
                self._wait("sp", (self.sem[e], self.cnt[e]))

    def mm(self, out, lhsT, rhs, start=True, stop=True):
        return self.op("pe", lambda e: e.matmul(out.ap, lhsT=lhsT.ap, rhs=rhs.ap, start=start, stop=stop),
                       [lhsT, rhs] + ([] if start else [out]), [out])

    def tr(self, out, in_, ident):
        return self.op("pe", lambda e: e.transpose(out.ap, in_.ap, ident.ap), [in_, ident], [out])

    def act(self, out, in_, func, scale=1.0, bias=0.0, accum=None, eng="act"):
        kw = {}
        if accum is not None:
            kw["accum_out"] = accum.ap
        sc = scale.ap if isinstance(scale, V) else scale
        bi = bias.ap if isinstance(bias, V) else bias
        return self.op("act", lambda e: e.activation(out=out.ap, in_=in_.ap, func=func, scale=sc, bias=bi, **kw),
                       [in_, scale, bias], [out] + ([accum] if accum is not None else []))

    def tt(self, e, out, in0, in1, op):
        return self.op(e, lambda g: g.tensor_tensor(out=out.ap, in0=in0.ap, in1=in1.ap, op=op), [in0, in1], [out])

    def ts(self, e, out, in0, s1, op0, s2=None, op1=None, accum=None):
        a1 = s1.ap if isinstance(s1, V) else s1
        a2 = s2.ap if isinstance(s2, V) else s2
        kw = {}
        if op1 is not None:
            kw["op1"] = op1
        if accum is not None:
            kw["accum_out"] = accum.ap
        return self.op(e, lambda g: g.tensor_scalar(out=out.ap, in0=in0.ap, scalar1=a1, scalar2=a2, op0=op0, **kw),
                       [in0, s1, s2], [out] + ([accum] if accum is not None else []))

    def stt(self, out, in0, scalar, in1, op0, op1, e="dve"):
        sc = scalar.ap if isinstance(scalar, V) else scalar
        return self.op(e, lambda g: g.scalar_tensor_tensor(out=out.ap, in0=in0.ap, scalar=sc, in1=in1.ap, op0=op0, op1=op1),
                       [in0, scalar, in1], [out])

    def copy(self, e, out, in_):
        if e == "act":
            return self.act(out, in_, AF.Copy)
        return self.op(e, lambda g: g.tensor_copy(out=out.ap, in_=in_.ap), [in_], [out])

    def memset(self, e, out, val):
        return self.op(e, lambda g: g.memset(out.ap, val), [], [out])

    def reduce(self, out, in_, op=ALU.add, axis=AX.X, e="dve"):
        return self.op(e, lambda g: g.tensor_reduce(out=out.ap, in_=in_.ap, axis=axis, op=op), [in_], [out])

    def recip(self, out, in_):
        return self.op("dve", lambda g: g.reciprocal(out=out.ap, in_=in_.ap), [in_], [out])

from concourse.bass_utils import run_bass_kernel_spmd

D_MODEL = 1024; PLE_DIM = 256
NH = 8; DK = 64; CONV_DIM = 1536
Q_LORA = 384; KV_LORA = 256; ROPE = 32; NOPE = 64; QK = 96
IN_DIM = 2736; D_FF = 2816
EPS = 1e-6
ATTN_SCALE = QK ** -0.5
C_AB = 1536; C_Z = 1552; C_QA = 2064; C_KVA = 2448
BIG = 1.0e4


class Rot:
    def __init__(self, items):
        self.items = items
        self.i = 0

    def __call__(self):
        t = self.items[self.i % len(self.items)]
        self.i += 1
        return t


def host_consts(S, past):
    c = {}
    i = np.arange(128)
    same = (i[:, None] // 64) == (i[None, :] // 64)
    c["ident"] = np.eye(128, dtype=np.float32)
    c["tri"] = (same & (i[:, None] <= i[None, :])).astype(np.float32)
    c["lastsel"] = (i[:, None] == (i[None, :] // 64) * 64 + 63).astype(np.float32)
    vis = same & (i[None, :] < i[:, None])
    c["negs"] = np.where(vis, 0.0, BIG).astype(np.float32)
    c["negt"] = np.where(vis.T, 0.0, -BIG).astype(np.float32)
    c["headblk"] = same.astype(np.float32)
    sel8 = np.zeros((8, 8, 128), np.float32)
    for h in range(8):
        sel8[h, h, :] = 1.0
    c["sel8"] = sel8
    selp = np.zeros((8, 4, 128), np.float32)
    for h in range(8):
        selp[h, h // 2, (h % 2) * 64:(h % 2) * 64 + 64] = 1.0
    c["selpair"] = selp
    c["cmask"] = (i[None, :] >= i[:, None]).astype(np.float32)
    dm = np.zeros((8, 8, 64), np.float32)
    for h in range(8):
        dm[h, h, :] = 1.0
    c["diagmask"] = dm.reshape(8, 512)
    oh = np.zeros((4, 4, 128), np.float32)
    for b in range(4):
        oh[b, b, :] = 1.0
    c["onehot4"] = oh
    half = ROPE // 2
    inv = (10000.0 ** (-np.arange(half, dtype=np.float32) / half)).astype(np.float32)
    pos = np.arange(S, dtype=np.float32)
    ang = pos[:, None] * inv[None, :]
    c["cos_p"] = np.cos(ang).astype(np.float32)
    c["sin_p"] = np.sin(ang).astype(np.float32)
    angs = (np.float32(past) * inv)[None, :].astype(np.float32)
    c["cos_s"] = np.repeat(np.cos(angs), 4, 0).astype(np.float32)
    c["sin_s"] = np.repeat(np.sin(angs), 4, 0).astype(np.float32)
    return c


CONST_SHAPES = dict(ident=[128, 128], tri=[128, 128], lastsel=[128, 128], negs=[128, 128], negt=[128, 128],
                    headblk=[128, 128], sel8=[8, 8, 128], selpair=[8, 4, 128], cmask=[128, 128],
                    diagmask=[8, 512], onehot4=[4, 4, 128], cos_s=[4, 16], sin_s=[4, 16])

W_SHAPES = dict(g_attn=[1, 1024], w_in=[1024, IN_DIM], w_conv=[4, 1536], gdn_a_log=[1, 8], gdn_dt_bias=[1, 8],
                g_gdn_out=[1, 64], g_q_a=[1, 384], w_q_b=[384, 768], g_q_nope=[1, 64], g_q_rope=[1, 32],
                g_kv_a=[1, 256], g_k_rope=[1, 32], w_kv_b=[256, 1024], g_k_nope=[1, 64], w_o=[1024, 1024],
                g_ffn=[1, 1024], w_ffn_gate=[1024, D_FF], w_ffn_up=[1024, D_FF], w_ffn_down=[D_FF, 1024],
                g_ple=[1, 1024], w_ple_gate=[1024, 1024], w_ple_proj=[256, 1024])


def build(S, NPG, NPHYS, stop_after=None):
    _, k1 = build1(S, NPG, NPHYS, stop_after, None)
    return build1(S, NPG, NPHYS, stop_after, k1.need_rec)


INV_DT = BF16


def build1(S, NPG, NPHYS, stop_after, needed):
    import os
    DBG = int(os.environ.get('KDBG', '0'))
    NBLK = S // 128
    NT = S // 512
    nc = bass.Bass("TRN2", target_bir_lowering=False)
    es = contextlib.ExitStack()
    with es:
        nc_lp = es.enter_context(nc.allow_low_precision("bf16 matmul operands by design"))
        es.enter_context(nc.allow_non_contiguous_dma("small strided state loads/stores"))
        k = KB(nc, es, needed)
        din = {}

        def DI(name, shape, dt=F32):
            din[name] = k.dram(name, shape, dt, "ExternalInput")
            return din[name]

        def DO(name, shape, dt=F32):
            return k.dram(name, shape, dt, "ExternalOutput")

        x_d = DI("x", [S, 1024]); xs_d = DI("xs", [4, 1024])
        p_d = DI("p", [S, 256]); psm_d = DI("psm", [4, 256])
        ccat_d = DI("cache_cat", [NPHYS * 128, 288])
        sgdn_d = DI("state_gdn", [4, 8, 64, 64]); sconv_d = DI("state_conv", [4, 3 * 1536])
        pt_d = DI("pt", [4, NPG], I32)
        W = {n: DI(n, s) for n, s in W_SHAPES.items()}
        C = {n: DI(n, s) for n, s in CONST_SHAPES.items()}
        C["cos_p"] = DI("cos_p", [S, 16]); C["sin_p"] = DI("sin_p", [S, 16])
        y_d = DO("y", [S, 1024]); ys_d = DO("ys", [4, 1024])
        ockv_d = DO("o_ckv", [S, 256]); okr_d = DO("o_kr", [S, 32])
        ogdn_d = DO("o_gdn", [8, 64, 64]); oconv_d = DO("o_conv", [3, 1536])
        ockvs_d = DO("o_ckv_s", [4, 256]); okrs_d = DO("o_kr_s", [4, 32])
        ogdns_d = DO("o_gdn_s", [4, 8, 64, 64]); oconvs_d = DO("o_conv_s", [4, 3 * 1536])
        omix_d = k.dram("omix_scr", [128, 8, S], BF16, "ExternalOutput" if DBG == 99 else "Internal")
        omixs_d = k.dram("omixs_scr", [128, 8, 4], BF16, "Internal")

        banks = [k.ps(f"ps{i}", [128, 512], F32) for i in range(8)]
        psb = Rot(banks[:7])
        accbank = banks[7]
        evi = [0]

        def evac(out, in_, scale=None):
            evi[0] += 1
            if scale is not None:
                return k.act(out, in_, AF.Copy, scale=scale)
            if evi[0] % 3:
                return k.act(out, in_, AF.Copy)
            return k.copy("dve", out, in_)

        def cload(name, shape, dt=F32, src=None, q="sp"):
            t = k.sb("c_" + name, shape, dt)
            k.dma(q, t.v, (src if src is not None else C[name].v))
            return t

        identf = cload("ident", [128, 128])
        identb = k.sb("identb", [128, 128], BF16); k.copy("dve", identb.v, identf.v)
        identr = k.sb("identr", [128, 128], F32R); k.copy("dve", identr.v, identf.v)
        tri = cload("tri", [128, 128]); lastsel = cload("lastsel", [128, 128])
        negs = cload("negs", [128, 128]); negt = cload("negt", [128, 128])
        headblk = cload("headblk", [128, 128])
        headblk_b = k.sb("headblk_b", [128, 128], BF16); k.copy("dve", headblk_b.v, headblk.v)
        sel8 = cload("sel8", [8, 8, 128]); selpair = cload("selpair", [8, 4, 128])
        cmaskf = cload("cmask", [128, 128])
        cmask = k.sb("cmaskb", [128, 128], BF16); k.copy("dve", cmask.v, cmaskf.v)
        diagmask = cload("diagmask", [8, 512]); onehot4 = cload("onehot4", [4, 4, 128])
        onesb = k.sb("onesb", [128, 64], BF16); k.memset("dve", onesb.v, 1.0)
        onesf = k.sb("onesf", [128, 8], F32); k.memset("dve", onesf.v, 1.0)

        def bload(name, F, q="sp"):
            t = k.sb("b_" + name, [128, F], F32)
            k.dma(q, t.v, W[name].v.bto([128, F]))
            return t

        g_attn = bload("g_attn", 1024); g_q_a = bload("g_q_a", 384); g_kv_a = bload("g_kv_a", 256)
        g_q_nope = bload("g_q_nope", 64); g_q_rope = bload("g_q_rope", 32)
        g_k_rope = bload("g_k_rope", 32); g_k_nope = bload("g_k_nope", 64)
        g_gdn_out = bload("g_gdn_out", 64)
        a_log = bload("gdn_a_log", 8); dtb = bload("gdn_dt_bias", 8)
        eA = k.sb("eA", [128, 8], F32); k.act(eA.v, a_log.v, AF.Exp)
        k.ts("dve", g_q_nope.v, g_q_nope.v, ATTN_SCALE, ALU.mult)
        k.ts("dve", g_q_rope.v, g_q_rope.v, ATTN_SCALE, ALU.mult)
        wcv = k.sb("wcv", [128, 12, 4], F32)
        for j in range(4):
            k.dma("sp", wcv[:, :, j], W["w_conv"].v[j:j + 1, :].re("o (c p) -> p (o c)", p=128))

        def wload(name, K, N, q="pool"):
            kc = K // 128
            t = k.sb("w_" + name, [128, kc, N], BF16)
            src = W[name].v.re("(k p) n -> p k n", p=128)
            for i in range(kc):
                for n0 in range(0, N, 1024):
                    n1 = min(N, n0 + 1024)
                    k.dma(q, t[:, i, n0:n1], src[:, i, n0:n1])
            return t

        sm = Rot([k.sb(f"sm{i}", [128, 8], F32) for i in range(24)])

        def rstd_from_ss(ss, n, F, rows):
            a = sm()
            k.ts("dve", a[rows, 0:n], ss, 1.0 / F, ALU.mult, EPS, ALU.add)
            k.act(a[rows, 0:n], a[rows, 0:n], AF.Ln)
            r = sm()
            k.act(r[rows, 0:n], a[rows, 0:n], AF.Exp, scale=-0.5)
            return r[rows, 0:n]

        junk = k.sb("junk", [128, 1024], BF16)

        def rmsnorm_rows(out, x, g, F, rows):
            ss = sm()
            k.act(junk[rows, 0:F], x, AF.Square, accum=ss[rows, 0:1])
            r = rstd_from_ss(ss[rows, 0:1], 1, F, rows)
            k.stt(out, x, r, g, ALU.mult, ALU.mult)

        def transpose_bf(dst_fn, src, nt, nch, width=128):
            bank = psb()
            pb = bank.v.bc(BF16)
            for j in range(nch):
                k.tr(pb[0:width, j * 128:j * 128 + nt], src[:, j * width:(j + 1) * width], identb[0:nt, 0:nt])
            return pb

        from types import SimpleNamespace as NS
        GROUPS_ALL = [(0, 512), (512, 1024), (1024, 1536), (1536, 1552), (1552, 2064), (2064, 2448), (2448, 2736)]
        GROUPS_A = GROUPS_ALL[:5]
        GROUPS_B = GROUPS_ALL[5:]
        if stop_after is None:
            w_o = wload("w_o", 1024, 1024)
            w_pg = wload("w_ple_gate", 1024, 1024)
            w_pp = wload("w_ple_proj", 256, 1024)
        esP1 = contextlib.ExitStack()
        esP1.__enter__()
        k.es_outer = k.es
        k.es = esP1
        x_blk = k.sb("x_blk", [128, 1024], F32)
        xn_t = k.sb("xn", [128, 1024], BF16)
        xnT = k.sb("xnT", [128, 8, 128], BF16)
        z_tok = k.sb("z_tok", [128, IN_DIM], F32)
        sq_t = k.sb("sq_t", [128, 512], F32)

        def inproj(x_src_v, nt, w_in, c_off, groups):
            rows = slice(0, nt)
            k.dma("sp", x_blk[rows, :], x_src_v)
            rmsnorm_rows(xn_t[rows, :], x_blk[rows, :], g_attn[rows, :], 1024, rows)
            pb = transpose_bf(None, xn_t[rows, :], nt, 8)
            evac(xnT[:, :, 0:nt], pb[:, 0:1024].re("p (j t) -> p j t", j=8)[:, :, 0:nt])
            for (c0, c1) in groups:
                bank = psb()
                for kk in range(8):
                    k.mm(bank[rows, 0:c1 - c0], xnT[:, kk, 0:nt], w_in[:, kk, c0 - c_off:c1 - c_off], start=(kk == 0), stop=(kk == 7))
                evac(z_tok[rows, c0:c1], bank[rows, 0:c1 - c0])

        def wload_cols(name, K, c0, c1, q="pool"):
            kc = K // 128
            t = k.sb("w_" + name + f"_{c0}", [128, kc, c1 - c0], BF16)
            src = W[name].v.re("(k p) n -> p k n", p=128)
            for i in range(kc):
                for n0 in range(c0, c1, 1024):
                    n1 = min(c1, n0 + 1024)
                    k.dma(q, t[:, i, n0 - c0:n1 - c0], src[:, i, n0:n1])
            return t

        def head_rms(out, xin, g, nt, width):
            rows = slice(0, nt)
            sq = sq_t[rows, 0:8 * width].re("p (h d) -> p h d", h=8)
            k.act(sq, xin, AF.Square)
            ss = sm()
            k.reduce(ss[rows, 0:8], sq)
            r = rstd_from_ss(ss[rows, 0:8], 8, width, rows)
            k.tt("dve", out, xin, r.un(2).bto([nt, 8, width]), ALU.mult)
            k.tt("pool", out, out, g.un(1).bto([nt, 8, width]), ALU.mult)

        def alloc_mla(w_qb, w_kvb):
            M = NS()
            M.w_qb = w_qb; M.w_kvb = w_kvb
            M.qa_n = k.sb("qa_n", [128, 384], BF16)
            M.qanT = k.sb("qanT", [128, 3, 128], BF16)
            M.qkv_tok = k.sb("qkv_tok", [128, 1024], F32)
            M.hh = k.sb("hh", [128, 8, 96], F32)
            M.hh_bf = k.sb("hh_bf", [128, 8, 96], BF16)
            M.rtmp = k.sb("rtmp", [128, 8, 32], F32)
            M.rtmp2 = k.sb("rtmp2", [128, 8, 16], F32)
            M.cos_t = k.sb("cos_t", [128, 16], F32); M.sin_t = k.sb("sin_t", [128, 16], F32)
            M.ckv_f = k.sb("ckv_f", [128, 256], F32)
            M.ckv_bf = k.sb("ckv_bf", [128, 260], BF16)
            k.memset("dve", M.ckv_bf[:, 256:257], 1.0)
            M.ckvT = k.sb("ckvT", [128, 2, 128], BF16)
            M.kr_f = k.sb("kr_f", [128, 32], F32)
            M.kr_n = k.sb("kr_n", [128, 32], F32)
            return M

        def rope(M, out, xin, nt, nh):
            rows = slice(0, nt)
            cb = M.cos_t[rows, :].un(1).bto([nt, nh, 16]); sb_ = M.sin_t[rows, :].un(1).bto([nt, nh, 16])
            x1 = xin[:, :, 0:16]; x2 = xin[:, :, 16:32]
            t2 = M.rtmp2[rows, 0:nh, :]
            k.tt("dve", out[:, :, 0:16], x1, cb, ALU.mult)
            k.tt("dve", t2, x2, sb_, ALU.mult)
            k.tt("dve", out[:, :, 0:16], out[:, :, 0:16], t2, ALU.subtract)
            k.tt("dve", out[:, :, 16:32], x1, sb_, ALU.mult)
            k.tt("dve", t2, x2, cb, ALU.mult)
            k.tt("dve", out[:, :, 16:32], out[:, :, 16:32], t2, ALU.add)

        def mla_q(M, nt, cos_src, sin_src):
            rows = slice(0, nt)
            k.dma("sp", M.cos_t[rows, :], cos_src); k.dma("sp", M.sin_t[rows, :], sin_src)
            rmsnorm_rows(M.qa_n[rows, :], z_tok[rows, C_QA:C_QA + 384], g_q_a[rows, :], 384, rows)
            pb = transpose_bf(None, M.qa_n[rows, :], nt, 3)
            evac(M.qanT[:, :, 0:nt], pb[:, 0:384].re("p (j t) -> p j t", j=3)[:, :, 0:nt])
            for (c0, c1) in ((0, 512), (512, 768)):
                bank = psb()
                for kk in range(3):
                    k.mm(bank[rows, 0:c1 - c0], M.qanT[:, kk, 0:nt], M.w_qb[:, kk, c0:c1], start=(kk == 0), stop=(kk == 2))
                evac(M.qkv_tok[rows, c0:c1], bank[rows, 0:c1 - c0])
            q3 = M.qkv_tok[rows, 0:768].re("p (h d) -> p h d", h=8)
            head_rms(M.hh[rows, :, 0:64], q3[:, :, 0:64], g_q_nope[rows, :], nt, 64)
            head_rms(M.rtmp[rows, :, :], q3[:, :, 64:96], g_q_rope[rows, :], nt, 32)
            rope(M, M.hh[rows, :, 64:96], M.rtmp[rows, :, :], nt, 8)

        def mla_kv(M, nt, ockv_v, okr_v):
            rows = slice(0, nt)
            rmsnorm_rows(M.ckv_f[rows, :], z_tok[rows, C_KVA:C_KVA + 256], g_kv_a[rows, :], 256, rows)
            k.dma("sp", ockv_v, M.ckv_f[rows, :])
            k.copy("pool", M.ckv_bf[rows, 0:256], M.ckv_f[rows, :])
            rmsnorm_rows(M.kr_n[rows, :], z_tok[rows, C_KVA + 256:C_KVA + 288], g_k_rope[rows, :], 32, rows)
            rope(M, M.kr_f[rows, :].un(1), M.kr_n[rows, :].un(1), nt, 1)
            k.dma("sp", okr_v, M.kr_f[rows, :])
            pb = transpose_bf(None, M.ckv_bf[rows, 0:256], nt, 2)
            evac(M.ckvT[:, :, 0:nt], pb[:, 0:256].re("p (j t) -> p j t", j=2)[:, :, 0:nt])

        def alloc_gdn(small=False):
            G = NS()
            G.cv = k.sb("cv", [128, 12, 128], F32)
            G.sqt = k.sb("sqt", [128, 4, 128], BF16)
            G.rst = k.sb("rst", [128, 4, 128], F32)
            G.qTg = k.sb("qTg", [128, 4, 128], F32)
            G.kTg = k.sb("kTg", [128, 4, 128], F32)
            G.tmp4 = k.sb("tmp4", [128, 4, 128], F32)
            G.sz_t = k.sb("sz_t", [128, 512], F32)
            if not small:
                G.o_tok = k.sb("o_tok", [128, 512], F32)
                G.on_t = k.sb("on_t", [128, 512], F32)
                G.og_bf = k.sb("og_bf", [128, 512], BF16)
            return G

        def gdn_scalars(nt):
            rows = slice(0, nt)
            ta = sm(); k.tt("dve", ta[rows, :], z_tok[rows, C_AB:C_AB + 8], dtb[rows, :], ALU.add)
            e = sm(); k.act(e[rows, :], ta[rows, :], AF.Exp)
            sp_ = sm(); k.act(sp_[rows, :], e[rows, :], AF.Ln, bias=1.0)
            g_tok = sm(); k.stt(g_tok[rows, :], sp_[rows, :], -1.0, eA[rows, :], ALU.mult, ALU.mult)
            beta = sm(); k.act(beta[rows, :], z_tok[rows, C_AB + 8:C_AB + 16], AF.Sigmoid)
            return g_tok, beta

        def l2norm_fm(G, nt):
            for half in range(2):
                k.act(G.sqt[:, :, 0:nt], G.cv[:, half * 4:half * 4 + 4, 0:nt], AF.Square)
                bank = psb()
                for c in range(4):
                    k.mm(bank[:, c * 128:c * 128 + nt], headblk_b.v, G.sqt[:, c, 0:nt])
                k.ts("dve", G.tmp4[:, :, 0:nt], bank.v.re("p (c t) -> p c t", c=4)[:, :, 0:nt], EPS, ALU.add)
                k.act(G.tmp4[:, :, 0:nt], G.tmp4[:, :, 0:nt], AF.Ln)
                k.act(G.rst[:, :, 0:nt], G.tmp4[:, :, 0:nt], AF.Exp, scale=-0.5)
                if half == 0:
                    k.stt(G.qTg[:, :, 0:nt], G.cv[:, 0:4, 0:nt], DK ** -0.5, G.rst[:, :, 0:nt], ALU.mult, ALU.mult)
                else:
                    k.tt("dve", G.kTg[:, :, 0:nt], G.cv[:, 4:8, 0:nt], G.rst[:, :, 0:nt], ALU.mult)

        def gdn_out_tok(G, nt):
            rows = slice(0, nt)
            o3 = G.o_tok[rows, :].re("p (h d) -> p h d", h=8)
            on3 = G.on_t[rows, :].re("p (h d) -> p h d", h=8)
            head_rms(on3, o3, g_gdn_out[rows, :], nt, 64)
            k.act(G.sz_t[rows, :], z_tok[rows, C_Z:C_Z + 512], AF.Silu)
            k.tt("dve", G.og_bf[rows, :], G.on_t[rows, :], G.sz_t[rows, :], ALU.mult)

        def sample_scope():
            w_in = wload_cols("w_in", 1024, 0, IN_DIM)
            w_qb = wload_cols("w_q_b", 384, 0, 768)
            w_kvb = wload_cols("w_kv_b", 256, 0, 1024)
            M = alloc_mla(w_qb, w_kvb)
            G = alloc_gdn(small=True)
            nt = 4; rows = slice(0, 4)
            inproj(xs_d.v, 4, w_in, 0, GROUPS_ALL)
            if DBG == 1:
                return
            oms = k.sb("oms", [128, 8, 4], BF16)
            esg = contextlib.ExitStack()
            with esg:
                k.es = esg
                st_tok = k.sb("st_tok", [12, 1536], F32)
                k.dma("sp", st_tok.v, sconv_d.v.re("b (j c) -> (b j) c", j=3))
                k.dma("sp", oconvs_d.v.re("b (j c) -> b j c", j=3)[:, 0:2, :], sconv_d.v.re("b (j c) -> b j c", j=3)[:, 1:3, :])
                k.dma("sp", oconvs_d.v.re("b (j c) -> b j c", j=3)[:, 2, :], z_tok[rows, 0:1536])
                ext = k.sb("ext_fm", [128, 12, 4, 4], F32)
                bank = psb()
                for c in range(12):
                    k.tr(bank[:, c * 12:(c + 1) * 12], st_tok[:, c * 128:(c + 1) * 128], identf[0:12, 0:12])
                evac(ext[:, :, :, 0:3], bank[:, 0:144].re("p (c b j) -> p c b j", c=12, b=4))
                bank = psb()
                for c in range(12):
                    k.tr(bank[:, c * 4:(c + 1) * 4], z_tok[rows, c * 128:(c + 1) * 128], identf[0:4, 0:4])
                evac(ext[:, :, :, 3], bank[:, 0:48].re("p (c b) -> p c b", c=12))
                k.tt("dve", ext.v, ext.v, wcv.v.un(2).bto([128, 12, 4, 4]), ALU.mult)
                cpre = k.sb("cpre", [128, 12, 4], F32)
                k.reduce(cpre.v, ext.v)
                k.act(G.cv[:, :, 0:4], cpre.v, AF.Silu)
                l2norm_fm(G, 4)
                g_tok, beta = gdn_scalars(4)
                eg = sm(); k.act(eg[rows, :], g_tok[rows, :], AF.Exp)
                sm_b = Rot([k.sb(f"smb{i}", [128, 4, 4], F32) for i in range(3)])

                def bc_bh(src):
                    bank = psb()
                    for b in range(4):
                        k.mm(bank[:, b * 8:(b + 1) * 8], onehot4[:, b, :], src)
                    o = sm_b()
                    for hp in range(2):
                        RH = slice(hp * 64, hp * 64 + 64)
                        evac(o[RH, :, :], bank[RH, 0:32].re("p (b pr hp) -> p b pr hp", b=4, pr=4)[:, :, :, hp])
                    return o
                eg_b = bc_bh(eg[rows, :]); beta_b = bc_bh(beta[rows, :])
                st = k.sb("st_s", [128, 4, 4, 64], F32)
                for b in range(4):
                    for hp in range(2):
                        k.dma("sp", st[hp * 64:(hp + 1) * 64, b, :, :],
                              sgdn_d.v[b].re("(pr hp) k v -> hp k pr v", hp=2)[hp])
                B4 = lambda t: t.v.un(3).bto([128, 4, 4, 64])
                k.tt("dve", st.v, st.v, B4(eg_b), ALU.mult)
                tmp = k.sb("tmp_s", [128, 4, 4, 64], F32)
                kcol = G.kTg[:, :, 0:4].re("p pr b -> p b pr")
                k.tt("dve", tmp.v, st.v, kcol.un(3).bto([128, 4, 4, 64]), ALU.mult)
                kSB = k.sb("kSB_s", [128, 4, 4, 64], F32)
                for hf in range(2):
                    bank = psb()
                    k.mm(bank.v, headblk.v, tmp[:, hf * 2:hf * 2 + 2, :, :].re("p b pr v -> p (b pr v)"))
                    evac(kSB[:, hf * 2:hf * 2 + 2, :, :].re("p b pr v -> p (b pr v)"), bank.v)
                v_tk = k.sb("v_tk", [4, 512], F32)
                bank = psb()
                for c in range(4):
                    k.tr(bank[0:4, c * 128:(c + 1) * 128], G.cv[:, 8 + c, 0:4], identf.v)
                evac(v_tk.v, bank[0:4, :])
                vB = k.sb("vB_s", [128, 4, 4, 64], F32)
                for b in range(4):
                    bank = psb()
                    k.mm(bank.v, onehot4[:, b, :], v_tk.v)
                    for hp in range(2):
                        RH = slice(hp * 64, hp * 64 + 64)
                        evac(vB[RH, b, :, :], bank[RH, :].re("p (pr hp v) -> p pr hp v", pr=4, hp=2)[:, :, hp, :])
                k.tt("dve", vB.v, vB.v, kSB.v, ALU.subtract)
                k.tt("dve", vB.v, vB.v, B4(beta_b), ALU.mult)
                k.tt("dve", tmp.v, vB.v, kcol.un(3).bto([128, 4, 4, 64]), ALU.mult)
                k.tt("dve", st.v, st.v, tmp.v, ALU.add)
                for b in range(4):
                    for hp in range(2):
                        k.dma("sp", ogdns_d.v[b].re("(pr hp) k v -> hp k pr v", hp=2)[hp], st[hp * 64:(hp + 1) * 64, b, :, :])
                oT = k.sb("oT_s", [128, 16], F32)
                for hp in range(2):
                    bank = psb()
                    RH = slice(hp * 64, hp * 64 + 64)
                    for b in range(4):
                        for pr in range(4):
                            k.mm(bank[RH, pr * 4 + b:pr * 4 + b + 1], st[RH, b, pr, :], G.qTg[RH, pr, b:b + 1])
                    evac(oT[RH, :], bank[RH, 0:16])
                osq = k.sb("osq_s", [128, 16], F32)
                k.act(osq.v, oT.v, AF.Square)
                bank = psb()
                k.mm(bank[:, 0:16], headblk.v, osq.v)
                a_ = k.sb("a_s_", [128, 16], F32); r_ = k.sb("r_s_", [128, 16], F32)
                k.ts("dve", a_.v, bank[:, 0:16], 1.0 / 64, ALU.mult, EPS, ALU.add)
                k.act(a_.v, a_.v, AF.Ln)
                k.act(r_.v, a_.v, AF.Exp, scale=-0.5)
                k.tt("dve", oT.v, oT.v, r_.v, ALU.mult)
                ggo_col = k.sb("ggo_col", [128, 1], F32)
                for hp in range(2):
                    k.dma("sp", ggo_col[hp * 64:(hp + 1) * 64, :], W["g_gdn_out"].v.re("o d -> d o"))
                k.ts("dve", oT.v, oT.v, ggo_col[:, 0:1], ALU.mult)
                k.act(G.sz_t[rows, :], z_tok[rows, C_Z:C_Z + 512], AF.Silu)
                bank = psb()
                for c in range(4):
                    k.tr(bank[:, c * 4:c * 4 + 4], G.sz_t[rows, c * 128:(c + 1) * 128], identf[0:4, 0:4])
                k.tt("dve", oms[:, 0:4, :], oT.v.re("p (pr b) -> p pr b", pr=4), bank[:, 0:16].re("p (pr b) -> p pr b", pr=4), ALU.mult)
                k.barrier()
            k.es = es1_cur[0]
            mla_q(M, 4, C["cos_s"].v, C["sin_s"].v)
            mla_kv(M, 4, ockvs_d.v, okrs_d.v)
            krb = k.sb("krb_s", [4, 32], BF16)
            k.copy("dve", krb.v, M.kr_f[0:4, :])
            krT_new = k.sb("krT_new", [32, 4], BF16)
            bank = psb(); pb = bank.v.bc(BF16)
            k.tr(pb[0:32, 0:4], krb.v, identb[0:4, 0:4])
            evac(krT_new.v, pb[0:32, 0:4])
            if DBG == 7:
                return
            WkT = k.sb("WkT", [64, 8, 256], BF16)
            wk4 = w_kvb.v.re("p k (h d) -> p k h d", h=8)
            for kk in range(2):
                bank = psb(); pb = bank.v.bc(BF16)
                for h in range(8):
                    k.tr(pb[0:64, h * 128:(h + 1) * 128], wk4[:, kk, h, 0:64], identb.v)
                evac(WkT[:, :, kk * 128:(kk + 1) * 128], pb[0:64, 0:1024].re("p (h t) -> p h t", h=8))
            qg = k.sb("qg_s", [4, 8, 64], BF16)
            k.tt("dve", qg.v, M.hh[rows, :, 0:64], g_k_nope[rows, :].un(1).bto([4, 8, 64]), ALU.mult)
            qr = k.sb("qr_s", [4, 8, 32], BF16)
            k.copy("dve", qr.v, M.hh[rows, :, 64:96])
            bank = psb(); pb = bank.v.bc(BF16)
            for h in range(8):
                k.tr(pb[0:64, h * 4:h * 4 + 4], qg[:, h, :], identb[0:4, 0:4])
                k.tr(pb[0:32, 64 + h * 4:64 + h * 4 + 4], qr[:, h, :], identb[0:4, 0:4])
            qgT = k.sb("qgT_s", [64, 8, 4], BF16); qrT = k.sb("qrT_s", [32, 8, 4], BF16)
            evac(qgT.v, pb[0:64, 0:32].re("p (h b) -> p h b", h=8))
            evac(qrT.v, pb[0:32, 64:96].re("p (h b) -> p h b", h=8))
            bank = psb()
            for kk in range(2):
                for h in range(8):
                    k.mm(bank[:, kk * 32 + h * 4:kk * 32 + h * 4 + 4], WkT[:, h, kk * 128:(kk + 1) * 128], qgT[:, h, :])
            qpT = k.sb("qpT_s", [128, 2, 4, 8], BF16)
            evac(qpT.v, bank[:, 0:64].re("p (k h b) -> p k b h", k=2, h=8))
            if DBG == 8:
                return
            pti = k.sb("pti", [128, 4 * NPG], I32)
            k.dma("sp", pti.v, pt_d.v.re("(o b) j -> o (b j)", o=1).bto([128, 4 * NPG]))
            ptf = k.sb("ptf", [128, 4 * NPG], F32)
            k.copy("dve", ptf.v, pti.v)
            iot = k.sb("iot", [128, 1], F32)
            k.op("pool", lambda g: g.iota(iot.v.ap, pattern=[[0, 1]], base=0, channel_multiplier=1,
                                          allow_small_or_imprecise_dtypes=True), [], [iot.v])
            k.ts("dve", ptf.v, ptf.v, 128.0, ALU.mult, iot[:, 0:1], ALU.add)
            idx = pti
            k.copy("dve", idx.v, ptf.v)
            G_ = 4
            pg_r = Rot([k.sb(f"pg{i}", [128, 292], BF16) for i in range(2 * G_ + 2)])
            for t_ in pg_r.items:
                k.memset("dve", t_[:, 288:289], 1.0)
            cT_r = Rot([k.sb(f"cT{i}", [128, 3, 128], BF16) for i in range(3)])
            sq_r = Rot([k.sb(f"sqp{i}", [128, G_, 512], BF16) for i in range(2)])
            sq1 = k.sb("sq1", [128, 512], BF16)
            p_r = Rot([k.sb(f"pp{i}", [128, G_ * 8], BF16) for i in range(3)])
            sg_r = Rot([k.sb(f"sgp{i}", [128, G_ * 8], F32) for i in range(8)])
            Wkc = k.sb("Wkc", [128, 2, 512], BF16); Wvc = k.sb("Wvc", [128, 2, 512], BF16)
            for kk in range(2):
                k.copy("dve", Wkc[:, kk, :].re("p (h d) -> p h d", h=8), wk4[:, kk, :, 0:64])
                k.copy("dve", Wvc[:, kk, :].re("p (h d) -> p h d", h=8), wk4[:, kk, :, 64:128])
            wk_rhs = lambda kk: Wkc[:, kk, :]
            wv_rhs = lambda kk: Wvc[:, kk, :]
            acc_sb = k.sb("acc_sb", [8, 257], F32)
            accn = k.sb("accn", [8, 256], BF16)
            accT = k.sb("accT", [128, 2, 8], BF16)
            om_f = k.sb("om_f", [8, 512], F32)
            trb = Rot([banks[0]]); bankA = banks[1:5]; bBs = [banks[5], banks[6]]; bB = bBs[0]
            qr_f = k.sb("qr_f", [4, 256], F32)
            k.copy("dve", qr_f.v.re("p (h d) -> p h d", h=8), M.hh[rows, :, 64:96])
            qrB = k.sb("qrB", [128, 4, 256], BF16)
            for b in range(4):
                bank = trb()
                k.mm(bank[:, 0:256], onehot4[:, b, :], qr_f.v)
                evac(qrB[:, b, :], bank[:, 0:256])
            rp_r = Rot([k.sb(f"rp{i}", [128, G_, 256], BF16) for i in range(2)])

            def newtok(b):
                rws = slice(0, 4)
                bA = bankA[0]
                for kk in range(2):
                    k.mm(bA[rws, 0:512], M.ckvT[:, kk, 0:4], wk_rhs(kk), start=(kk == 0), stop=(kk == 1))
                for kk in range(2):
                    k.mm(bB[rws, 0:8], M.ckvT[:, kk, 0:4], qpT[:, kk, b, :], start=(kk == 0), stop=(kk == 1))
                k.mm(bB[rws, 8:16], krT_new[:, 0:4], qrT[:, :, b])
                k.act(sq1[rws, :], bA[rws, :], AF.Square)
                ss = sm()
                k.reduce(ss[rws, :], sq1[rws, :].re("p (h d) -> p h d", h=8))
                r = rstd_from_ss(ss[rws, :], 8, 64, rws)
                s1 = sm()
                k.tt("dve", s1[rws, :], bB[rws, 0:8], r, ALU.mult)
                k.tt("dve", s1[rws, :], s1[rws, :], bB[rws, 8:16], ALU.add)
                s2 = sm()
                k.act(s2[rws, :], s1[rws, :], AF.Exp)
                pp = p_r()
                k.ts("dve", pp[rws, 0:8], s2[rws, :], identf[0:4, b:b + 1], ALU.mult)
                return pp

            def front(b, gi, j0, g):
                bBg = bBs[gi % 2]
                pgs = []
                sq = sq_r()
                rp = rp_r()
                for i in range(g):
                    pg = pg_r(); pgs.append(pg)
                    col = b * NPG + j0 + i
                    k.gather(pg[:, 0:288], ccat_d.v, idx[:, col:col + 1])
                    tb = trb(); pb = tb.v.bc(BF16)
                    k.tr(pb[:, 0:128], pg[:, 32:160], identb.v)
                    k.tr(pb[:, 128:256], pg[:, 160:288], identb.v)
                    cT = cT_r()
                    k.act(cT[:, 0:2, :], pb[:, 0:256].re("p (j t) -> p j t", j=2), AF.Copy)
                    for kk in range(2):
                        k.mm(bankA[i][:, 0:512], cT[:, kk, :], wk_rhs(kk), start=(kk == 0), stop=(kk == 1))
                    for kk in range(2):
                        k.mm(bBg[:, i * 8:i * 8 + 8], cT[:, kk, :], qpT[:, kk, b, :], start=(kk == 0), stop=(kk == 1))
                    k.act(sq[:, i, :], bankA[i].v, AF.Square)
                    k.tt("dve", rp[:, i, :].re("p (h d) -> p h d", h=8), pg[:, 0:32].un(1).bto([128, 8, 32]),
                         qrB[:, b, :].re("p (h d) -> p h d", h=8), ALU.mult)
                return pgs, sq, bBg, rp

            def back(b, j0, g, pgs, sq, bBg, rp):
                n8 = g * 8
                sr = sg_r()
                k.reduce(sr[:, 0:n8], rp[:, 0:g, :].re("p g (h d) -> p (g h) d", h=8))
                ss = sg_r()
                k.reduce(ss[:, 0:n8], sq[:, 0:g, :].re("p g (h d) -> p (g h) d", h=8))
                a = sg_r()
                k.ts("dve", a[:, 0:n8], ss[:, 0:n8], 1.0 / 64, ALU.mult, EPS, ALU.add)
                k.act(a[:, 0:n8], a[:, 0:n8], AF.Ln)
                r = sg_r()
                k.act(r[:, 0:n8], a[:, 0:n8], AF.Exp, scale=-0.5)
                s1 = sg_r()
                k.tt("dve", s1[:, 0:n8], bBg[:, 0:n8], r[:, 0:n8], ALU.mult)
                k.tt("dve", s1[:, 0:n8], s1[:, 0:n8], sr[:, 0:n8], ALU.add)
                pp = p_r()
                k.act(pp[:, 0:n8], s1[:, 0:n8], AF.Exp)
                for i in range(g):
                    k.mm(accbank[0:8, 0:257], pp[:, i * 8:(i + 1) * 8], pgs[i][:, 32:289], start=False,
                         stop=(j0 + i == NPG - 1))

            for b in range(4):
                pp = newtok(b)
                k.mm(accbank[0:8, 0:257], pp[0:4, 0:8], M.ckv_bf[0:4, 0:257], start=True, stop=(NPG == 0))
                groups = [(j0, min(G_, NPG - j0)) for j0 in range(0, NPG, G_)]
                pend = front(b, 0, *groups[0]) if groups else None
                for gi, (j0, g) in enumerate(groups):
                    cur = pend
                    if gi + 1 < len(groups):
                        pend = front(b, gi + 1, *groups[gi + 1])
                    back(b, j0, g, *cur)
                evac(acc_sb.v, accbank[0:8, 0:257])
                rl = sm(); k.recip(rl[0:8, 0:1], acc_sb[:, 256:257])
                k.ts("dve", accn.v, acc_sb[:, 0:256], rl[0:8, 0:1], ALU.mult)
                bank = trb(); pb = bank.v.bc(BF16)
                for kk in range(2):
                    k.tr(pb[:, kk * 8:kk * 8 + 8], accn[:, kk * 128:(kk + 1) * 128], identb[0:8, 0:8])
                evac(accT.v, pb[:, 0:16].re("p (k h) -> p k h", k=2))
                bank = trb()
                for kk in range(2):
                    k.mm(bank[0:8, :], accT[:, kk, :], wv_rhs(kk), start=(kk == 0), stop=(kk == 1))
                k.tt("dve", om_f.v, bank[0:8, :], diagmask.v, ALU.mult)
                bank2 = trb()
                for pr in range(4):
                    k.mm(bank2[:, pr:pr + 1], om_f[:, pr * 128:(pr + 1) * 128], onesf[0:8, 0:1])
                evac(oms[:, 4:8, b], bank2[:, 0:4])
            k.dma("sp", omixs_d.v, oms.v)

        def pass_a():
            w_in = wload_cols("w_in", 1024, 0, 2064)
            G = alloc_gdn()
            S2 = k.sb("S2", [128, 4, 128], F32)
            k.memset("pool", S2.v, 0.0)
            zcT = k.sb("zcT", [128, 12, 131], F32)
            k.memset("pool", zcT.v, 0.0)
            omixA = k.sb("omixA", [128, 4, 512], BF16)
            acc_r = Rot([k.sb(f"acc_c{i}", [128, 128], F32) for i in range(2)])
            accp_r = Rot([k.sb(f"acc_p{i}", [128, 128], F32) for i in range(2)])
            tmp_p = k.sb("tmp_p", [128, 128], F32)
            kbT = k.sb("kbT", [128, 4, 128], BF16); nwT = k.sb("nwT", [128, 4, 128], BF16)
            qgT = k.sb("qgT", [128, 4, 128], BF16)
            kT_b = k.sb("kT_b", [128, 4, 128], BF16); qT_b = k.sb("qT_b", [128, 4, 128], BF16)
            S2b = k.sb("S2b", [128, 4, 128], BF16)
            k.memset("pool", S2b.v, 0.0)
            egc_fm = k.sb("egc_fm", [128, 4, 128], F32)
            k_tok = G.on_t
            v_tok = k.sb("v_tok", [128, 512], F32); u_tok = v_tok
            vb_tok = k.sb("vb_tok", [128, 512], BF16)
            kbg_tok = k.sb("kbg_tok", [128, 512], BF16)
            kdec_tok = k.sb("kdec_tok", [128, 512], BF16)
            vnew_tok = k.sb("vnew_tok", [128, 512], BF16)
            gcT8 = k.sb("gcT8", [8, 128], F32)
            betaT8 = k.sb("betaT8", [8, 128], F32)
            qkT_all = k.sb("qkT_all", [128, 8, 128], BF16)
            U_all = k.sb("U_all", [128, 8, 128], BF16)
            g4 = Rot([k.sb(f"g4_{i}", [128, 4, 128], F32) for i in range(3)])
            r4s = [Rot([k.sb(f"r4_{j}_{i}", [128, 4, 128], INV_DT) for i in range(6)]) for j in range(2)]
            t4 = k.sb("t4", [128, 4, 128], F32)
            v4 = lambda b_: b_.v.re("p (c t) -> p c t", c=4)

            def gdn_block(tl):
                cv = G.cv; kTg = G.kTg; qTg = G.qTg
                for g3 in range(3):
                    bank = psb()
                    for c in range(4):
                        k.tr(bank[:, c * 128:(c + 1) * 128], z_tok[:, (g3 * 4 + c) * 128:(g3 * 4 + c + 1) * 128], identf.v)
                    evac(zcT[:, g3 * 4:g3 * 4 + 4, 3:131], v4(bank))
                for c in range(12):
                    if c < 12:
                        ac = acc_r()
                        k.ts("dve", ac.v, zcT[:, c, 0:128], wcv[:, c, 0:1], ALU.mult)
                        for j in (1, 2, 3):
                            k.stt(ac.v, zcT[:, c, j:j + 128], wcv[:, c, j:j + 1], ac.v, ALU.mult, ALU.add)
                    else:
                        ac = accp_r()
                        k.ts("pool", ac.v, zcT[:, c, 0:128], wcv[:, c, 0:1], ALU.mult)
                        for j in (1, 2, 3):
                            k.ts("pool", tmp_p.v, zcT[:, c, j:j + 128], wcv[:, c, j:j + 1], ALU.mult)
                            k.tt("pool", ac.v, ac.v, tmp_p.v, ALU.add)
                    k.act(cv[:, c, :], ac.v, AF.Silu)
                k.copy("pool", zcT[:, :, 0:3], zcT[:, :, 128:131])
                l2norm_fm(G, 128)
                k.copy("pool", kT_b.v, kTg.v)
                k.copy("pool", qT_b.v, qTg.v)
                bank = psb()
                for c in range(4):
                    k.tr(bank[:, c * 128:(c + 1) * 128], kTg[:, c, :], identf.v)
                evac(k_tok.v, bank.v)
                bank = psb()
                for c in range(4):
                    k.tr(bank[:, c * 128:(c + 1) * 128], cv[:, 8 + c, :], identf.v)
                evac(v_tok.v, bank.v)
                g_tok, beta = gdn_scalars(128)
                bank = psb()
                k.mm(bank[:, 0:8], tri.v, g_tok.v)
                gc = sm(); evac(gc.v, bank[:, 0:8])
                bank = psb()
                k.mm(bank[:, 0:8], lastsel.v, gc.v)
                dd = sm(); k.tt("dve", dd.v, bank[:, 0:8], gc.v, ALU.subtract)
                edec = sm(); k.act(edec.v, dd.v, AF.Exp)
                egc = sm(); k.act(egc.v, gc.v, AF.Exp)
                bge = sm(); k.tt("dve", bge.v, beta.v, egc.v, ALU.mult)
                b3 = lambda t: t.v.un(2).bto([128, 8, 64])
                r3 = lambda t: t.v.re("p (h d) -> p h d", h=8)
                k.tt("dve", r3(vb_tok), r3(v_tok), b3(beta), ALU.mult)
                k.tt("pool", r3(kdec_tok), r3(k_tok), b3(edec), ALU.mult)
                k.tt("pool", r3(kbg_tok), r3(k_tok), b3(bge), ALU.mult)
                bank = psb()
                k.tr(bank[0:8, 0:128], gc.v, identf.v)
                k.tr(bank[0:8, 128:256], beta.v, identf.v)
                evac(gcT8.v, bank[0:8, 0:128]); evac(betaT8.v, bank[0:8, 128:256])
                bank = psb()
                for pr in range(4):
                    k.mm(bank[:, pr * 128:(pr + 1) * 128], selpair[:, pr, :], gcT8.v)
                k.act(egc_fm.v, v4(bank), AF.Exp)
                bank = psb()
                for pr in range(4):
                    k.mm(bank[:, pr * 128:(pr + 1) * 128], selpair[:, pr, :], betaT8.v)
                k.tt("dve", kbT.v, kTg.v, v4(bank), ALU.mult)
                k.tt("dve", qgT.v, qTg.v, egc_fm.v, ALU.mult)
                R = lambda h: slice((h % 2) * 64, (h % 2) * 64 + 64)
                st8 = []
                for hg in range(2):
                    hs = [hg * 4 + i for i in range(4)]
                    r4 = r4s[hg]
                    bcb = psb()
                    for i, h in enumerate(hs):
                        k.mm(bcb[:, i * 128:(i + 1) * 128], sel8[:, h, :], gcT8.v)
                    d1 = g4()
                    k.tt("dve", d1.v, v4(bcb), gc[:, hg * 4:hg * 4 + 4].un(2).bto([128, 4, 128]), ALU.subtract)
                    e1 = g4()
                    k.tt("dve", e1.v, d1.v, negs.v.un(1).bto([128, 4, 128]), ALU.max)
                    Dm = g4()
                    k.act(Dm.v, e1.v, AF.Exp, scale=-1.0)
                    k.tt("dve", d1.v, d1.v, negt.v.un(1).bto([128, 4, 128]), ALU.min)
                    DTm = e1
                    k.act(DTm.v, d1.v, AF.Exp)
                    Bt = r4(); Ct = r4(); St = r4()
                    k.tt("pool", d1.v, DTm.v, identf.v.un(1).bto([128, 4, 128]), ALU.add)
                    for hp_ in range(2):
                        bKB = psb(); bKBT = psb(); bQKT = psb()
                        for which in range(3):
                            for i, h in enumerate(hs):
                                if h % 2 != hp_:
                                    continue
                                pr = h // 2
                                cs_ = slice((i // 2) * 128, (i // 2 + 1) * 128)
                                if which == 0:
                                    k.mm(bKB[:, cs_], kbT[R(h), pr, :], kT_b[R(h), pr, :])
                                elif which == 1:
                                    k.mm(bKBT[:, cs_], kT_b[R(h), pr, :], kbT[R(h), pr, :])
                                else:
                                    k.mm(bQKT[:, cs_], kT_b[R(h), pr, :], qT_b[R(h), pr, :])
                        v2 = lambda b_: b_[:, 0:256].re("p (c t) -> p c t", c=2)
                        k.stt(Bt[:, hp_::2, :], v2(bKB), -1.0, Dm[:, hp_::2, :], ALU.mult, ALU.mult)
                        k.stt(Ct[:, hp_::2, :], v2(bKBT), -1.0, DTm[:, hp_::2, :], ALU.mult, ALU.mult)
                        k.tt("dve", qkT_all[:, hg * 4 + hp_:hg * 4 + 4:2, :], v2(bQKT), d1[:, hp_::2, :], ALU.mult)
                    k.tt("pool", St.v, Ct.v, identf.v.un(1).bto([128, 4, 128]), ALU.add)
                    st8.append([Bt, Ct, St])
                for lvl in range(1, 6):
                    nBs = []
                    for hg in range(2):
                        Bt, Ct, St = st8[hg]
                        r4 = r4s[hg]
                        bB = psb()
                        for i in range(4):
                            k.mm(bB[:, i * 128:(i + 1) * 128], Ct[:, i, :], Bt[:, i, :])
                        nB = r4()
                        k.act(nB.v, v4(bB), AF.Copy)
                        nC = None
                        if lvl < 5:
                            bC = psb()
                            for i in range(4):
                                k.mm(bC[:, i * 128:(i + 1) * 128], Bt[:, i, :], Ct[:, i, :])
                            nC = r4()
                            k.act(nC.v, v4(bC), AF.Copy)
                        nBs.append((nB, nC))
                    for hg in range(2):
                        Bt, Ct, St = st8[hg]
                        nB, nC = nBs[hg]
                        r4 = r4s[hg]
                        bS = psb()
                        for i in range(4):
                            k.mm(bS[:, i * 128:(i + 1) * 128], nB[:, i, :], St[:, i, :])
                        if lvl < 5:
                            nS = r4()
                            k.tt("dve", nS.v, v4(bS), St.v, ALU.add)
                            st8[hg] = [nB, nC, nS]
                        else:
                            k.tt("dve", U_all[:, hg * 4:hg * 4 + 4, :], v4(bS), St.v, ALU.add)
                ub = psb(); wb = psb()
                for h in range(8):
                    pr = h // 2; Rh = slice((h % 2) * 64, (h % 2) * 64 + 64)
                    k.mm(ub[:, h * 64:(h + 1) * 64], U_all[:, h, :], vb_tok[:, h * 64:(h + 1) * 64])
                    k.mm(wb[Rh, pr * 128:(pr + 1) * 128], kbg_tok[:, h * 64:(h + 1) * 64], U_all[:, h, :])
                evac(u_tok.v, ub.v)
                k.act(nwT.v, v4(wb), AF.Copy, scale=-1.0)
                for ci in range(2):
                    RR = slice(ci * 64, ci * 64 + 64)
                    vbk = psb()
                    for pr in range(4):
                        k.mm(vbk[RR, pr * 128:(pr + 1) * 128], nwT[:, pr, RR], S2b[:, pr, :])
                    k.tt("dve", vnew_tok[RR, :], vbk[RR, :], u_tok[RR, :], ALU.add)
                    obk = psb()
                    for pr in range(4):
                        k.mm(obk[RR, pr * 128:(pr + 1) * 128], qgT[:, pr, RR], S2b[:, pr, :], start=True, stop=False)
                        for hp in range(2):
                            h = 2 * pr + hp
                            k.mm(obk[RR, pr * 128 + hp * 64:pr * 128 + (hp + 1) * 64], qkT_all[RR, h, RR],
                                 vnew_tok[RR, h * 64:(h + 1) * 64], start=False, stop=(hp == 1))
                    k.act(G.o_tok[RR, :], obk[RR, :], AF.Copy)
                    sbk = psb()
                    for pr in range(4):
                        cs = slice(pr * 128, (pr + 1) * 128)
                        k.mm(sbk[:, cs], kdec_tok[RR, cs], vnew_tok[RR, cs])
                    k.tt("dve", t4.v, v4(sbk), headblk.v.un(1).bto([128, 4, 128]), ALU.mult)
                    k.tt("pool", S2.v, S2.v, egc_fm[:, :, ci * 64 + 63:ci * 64 + 64].bto([128, 4, 128]), ALU.mult)
                    k.tt("pool", S2.v, S2.v, t4.v, ALU.add)
                    k.copy("pool", S2b.v, S2.v)
                gdn_out_tok(G, 128)
                pb = transpose_bf(None, G.og_bf.v, 128, 4)
                evac(omixA[:, :, tl * 128:(tl + 1) * 128], pb[:, 0:512].re("p (j t) -> p j t", j=4))

            for bi in range(NBLK):
                tl = bi % 4
                t0 = bi * 128
                inproj(x_d.v[t0:t0 + 128, :], 128, w_in, 0, GROUPS_A)
                if bi == NBLK - 1:
                    k.dma("sp", oconv_d.v, z_tok[125:128, 0:1536])
                gdn_block(tl)
                if tl == 3:
                    t = bi // 4
                    k.dma("sp", omix_d.v[:, 0:4, t * 512:(t + 1) * 512], omixA.v)
            for pr in range(4):
                for hp in range(2):
                    RH = slice(hp * 64, hp * 64 + 64)
                    k.dma("sp", ogdn_d.v[2 * pr + hp], S2[RH, pr, hp * 64:hp * 64 + 64])

        def pass_b():
            psb.items = banks[:5]
            w_in = wload_cols("w_in", 1024, C_QA, IN_DIM)
            w_qb = wload_cols("w_q_b", 384, 0, 768)
            w_kvb = wload_cols("w_kv_b", 256, 0, 1024)
            M = alloc_mla(w_qb, w_kvb)
            kT = k.sb("kT", [128, 8, S], BF16)
            v_sb = k.sb("v_sb", [128, NBLK, 8, 64], BF16)
            qT_tile = k.sb("qT_tile", [128, 8, 512], BF16)
            omixB = k.sb("omixB", [128, 4, 512], BF16)
            pT_r = Rot([k.sb(f"pT{i}", [128, 512], BF16) for i in range(4)])
            rl_t = k.sb("rl_t", [128, 512], F32)

            def mla_block(bi, tl):
                t0 = bi * 128
                mla_q(M, 128, C["cos_p"].v[t0:t0 + 128, :], C["sin_p"].v[t0:t0 + 128, :])
                k.copy("pool", M.hh_bf.v, M.hh.v)
                bank = psb(); pb = bank.v.bc(BF16)
                for h in range(8):
                    k.tr(pb[0:96, h * 128:(h + 1) * 128], M.hh_bf[:, h, :], identb.v)
                evac(qT_tile[0:96, :, tl * 128:(tl + 1) * 128], pb[0:96, 0:1024].re("p (h t) -> p h t", h=8))
                mla_kv(M, 128, ockv_d.v[t0:t0 + 128, :], okr_d.v[t0:t0 + 128, :])
                for (c0, c1) in ((0, 512), (512, 1024)):
                    bank = psb()
                    for kk in range(2):
                        k.mm(bank[:, 0:512], M.ckvT[:, kk, :], w_kvb[:, kk, c0:c1], start=(kk == 0), stop=(kk == 1))
                    evac(M.qkv_tok[:, c0:c1], bank.v)
                kv3 = M.qkv_tok.v.re("p (h d) -> p h d", h=8)
                head_rms(M.hh[:, :, 0:64], kv3[:, :, 0:64], g_k_nope.v, 128, 64)
                k.copy("pool", M.hh[:, :, 64:96], M.kr_f.v.un(1).bto([128, 8, 32]))
                k.copy("pool", M.hh_bf.v, M.hh.v)
                k.copy("dve", v_sb[:, bi, :, :], kv3[:, :, 64:128])
                bank = psb(); pb = bank.v.bc(BF16)
                for h in range(8):
                    k.tr(pb[0:96, h * 128:(h + 1) * 128], M.hh_bf[:, h, :], identb.v)
                evac(kT[0:96, :, t0:t0 + 128], pb[0:96, 0:1024].re("p (h t) -> p h t", h=8))

            def attention_tile(t):
                for pr in range(4):
                    o_ps = banks[5]; l_ps = banks[6]
                    for hp in range(2):
                        h = 2 * pr + hp
                        RH = slice(hp * 64, hp * 64 + 64)
                        nkb = 4 * t + 4
                        def stepA(j):
                            qlo = max(0, j - 4 * t)
                            ncol = (4 - qlo) * 128
                            qc = slice(qlo * 128, 512)
                            sc = psb()
                            k.mm(sc[:, 0:ncol], kT[0:96, h, j * 128:(j + 1) * 128], qT_tile[0:96, h, qc])
                            pT = pT_r()
                            k.act(pT[:, 0:ncol], sc[:, 0:ncol], AF.Exp)
                            if j >= 4 * t:
                                k.tt("pool", pT[:, 0:128], pT[:, 0:128], cmask.v, ALU.mult)
                            return pT, ncol, qc

                        def stepB(j, pT, ncol, qc):
                            k.mm(o_ps[RH, qc], v_sb[:, j, h, :], pT[:, 0:ncol], start=(j == 0), stop=(j == nkb - 1))
                            k.mm(l_ps[RH, qc], onesb.v, pT[:, 0:ncol], start=(j == 0), stop=(j == nkb - 1))

                        pend = stepA(0)
                        for j in range(nkb):
                            cur = pend
                            if j + 1 < nkb:
                                pend = stepA(j + 1)
                            stepB(j, *cur)
                    k.recip(rl_t.v, l_ps.v)
                    k.tt("dve", omixB[:, pr, :], o_ps.v, rl_t.v, ALU.mult)

            for bi in range(NBLK):
                tl = bi % 4
                t0 = bi * 128
                inproj(x_d.v[t0:t0 + 128, :], 128, w_in, C_QA, GROUPS_B)
                mla_block(bi, tl)
                if tl == 3:
                    t = bi // 4
                    attention_tile(t)
                    k.dma("sp", omix_d.v[:, 4:8, t * 512:(t + 1) * 512], omixB.v)

        k.es_saved = esP1
        es1_cur = [None]
        for name_, fn_ in (("sample", sample_scope), ("a", pass_a), ("b", pass_b)):
            es1 = contextlib.ExitStack()
            with es1:
                k.es = es1
                es1_cur[0] = es1
                fn_()
                k.barrier()
            k.es = k.es_saved
            if stop_after == name_:
                break
        psb.items = banks[:7]
        esP1.__exit__(None, None, None)
        k.es = k.es_outer

        if stop_after is None:
            w_dn = wload("w_ffn_down", D_FF, 1024)
            wg_scr = k.dram("wg_scr", [128, 8, D_FF], BF16, "Internal")
            wu_scr = k.dram("wu_scr", [128, 8, D_FF], BF16, "Internal")
            first_stream = [True]
            g_ffn = bload("g_ffn", 1024); g_ple = bload("g_ple", 1024)
            wg_r = Rot([k.sb(f"wg{i}", [128, 8, 512], BF16) for i in range(2)])
            wu_r = Rot([k.sb(f"wu{i}", [128, 8, 512], BF16) for i in range(2)])
            om_t = k.sb("om_t", [128, 8, 512], BF16)
            h1 = k.sb("h1", [128, 4, 1024], F32)
            un = k.sb("un", [128, 1024], BF16)
            uT = k.sb("uT", [128, 8, 512], BF16)
            hT = k.sb("hT", [128, 22, 512], BF16)
            sg = k.sb("sg", [128, 512], F32)
            sg2 = k.sb("sg2", [128, 512], F32)
            sg_rot = Rot([sg, sg2])
            p_t = k.sb("p_t", [128, 256], F32)
            p_bf = k.sb("p_bf", [128, 256], BF16)
            pT2 = k.sb("pT2", [128, 2, 128], BF16)
            gate_t = sg
            wgs = W["w_ffn_gate"].v.re("(k p) n -> p k n", p=128)
            wus = W["w_ffn_up"].v.re("(k p) n -> p k n", p=128)

            def phase2_tile(nt, om_src, x_src, p_src, y_dst):
                nb = (nt + 127) // 128
                bt = min(nt, 128)
                rows = slice(0, bt)
                k.dma("sp", om_t[:, :, 0:nt], om_src)
                for b in range(nb):
                    k.dma("sp", h1[rows, b, :], x_src(b))
                for b in range(nb):
                    cb = slice(b * 128, b * 128 + bt)
                    for hf in range(2):
                        bank = psb()
                        for kk in range(8):
                            k.mm(bank[rows, :], om_t[:, kk, cb], w_o[:, kk, hf * 512:(hf + 1) * 512], start=(kk == 0), stop=(kk == 7))
                        k.tt("dve", h1[rows, b, hf * 512:(hf + 1) * 512], bank[rows, :], h1[rows, b, hf * 512:(hf + 1) * 512], ALU.add)
                    rmsnorm_rows(un[rows, :], h1[rows, b, :], g_ffn[rows, :], 1024, rows)
                    pb = transpose_bf(None, un[rows, :], bt, 8)
                    evac(uT[:, :, cb], pb[:, 0:1024].re("p (j t) -> p j t", j=8)[:, :, 0:bt])
                for c0 in range(0, D_FF, 512):
                    c1 = min(D_FF, c0 + 512)
                    wg = wg_r(); wu = wu_r()
                    if first_stream[0]:
                        k.dma("pool", wg[:, :, 0:c1 - c0], wgs[:, :, c0:c1])
                        k.dma("pool", wu[:, :, 0:c1 - c0], wus[:, :, c0:c1])
                        k.dma("sp", wg_scr.v[:, :, c0:c1], wg[:, :, 0:c1 - c0])
                        k.dma("sp", wu_scr.v[:, :, c0:c1], wu[:, :, 0:c1 - c0])
                    else:
                        k.dma("sp", wg[:, :, 0:c1 - c0], wg_scr.v[:, :, c0:c1])
                        k.dma("sp", wu[:, :, 0:c1 - c0], wu_scr.v[:, :, c0:c1])
                    for m in range(c0 // 128, c1 // 128):
                        ms_ = slice(m * 128 - c0, (m + 1) * 128 - c0)
                        bg = psb(); bu = psb()
                        for kk in range(8):
                            k.mm(bg[:, 0:nt], wg[:, kk, ms_], uT[:, kk, 0:nt], start=(kk == 0), stop=(kk == 7))
                        for kk in range(8):
                            k.mm(bu[:, 0:nt], wu[:, kk, ms_], uT[:, kk, 0:nt], start=(kk == 0), stop=(kk == 7))
                        sgt = sg_rot()
                        k.act(sgt[:, 0:nt], bg[:, 0:nt], AF.Silu)
                        k.tt("dve", hT[:, m, 0:nt], sgt[:, 0:nt], bu[:, 0:nt], ALU.mult)
                for b in range(nb):
                    cb = slice(b * 128, b * 128 + bt)
                    for hf in range(2):
                        bank = psb()
                        for m in range(22):
                            k.mm(bank[rows, :], hT[:, m, cb], w_dn[:, m, hf * 512:(hf + 1) * 512], start=(m == 0), stop=(m == 21))
                        k.tt("dve", h1[rows, b, hf * 512:(hf + 1) * 512], bank[rows, :], h1[rows, b, hf * 512:(hf + 1) * 512], ALU.add)
                    rmsnorm_rows(un[rows, :], h1[rows, b, :], g_ple[rows, :], 1024, rows)
                    pb = transpose_bf(None, un[rows, :], bt, 8)
                    evac(uT[:, :, cb], pb[:, 0:1024].re("p (j t) -> p j t", j=8)[:, :, 0:bt])
                    k.dma("sp", p_t[rows, :], p_src(b))
                    k.copy("pool", p_bf[rows, :], p_t[rows, :])
                    pb = transpose_bf(None, p_bf[rows, :], bt, 2)
                    evac(pT2[:, :, 0:bt], pb[:, 0:256].re("p (j t) -> p j t", j=2)[:, :, 0:bt])
                    for hf in range(2):
                        hs_ = slice(hf * 512, (hf + 1) * 512)
                        bank = psb()
                        for kk in range(8):
                            k.mm(bank[rows, :], uT[:, kk, cb], w_pg[:, kk, hs_], start=(kk == 0), stop=(kk == 7))
                        k.act(gate_t[rows, :], bank[rows, :], AF.Sigmoid)
                        bank2 = psb()
                        for kk in range(2):
                            k.mm(bank2[rows, :], pT2[:, kk, 0:bt], w_pp[:, kk, hs_], start=(kk == 0), stop=(kk == 1))
                        k.tt("dve", gate_t[rows, :], bank2[rows, :], gate_t[rows, :], ALU.mult)
                        k.tt("pool", h1[rows, b, hs_], gate_t[rows, :], h1[rows, b, hs_], ALU.add)
                    k.dma("sp", y_dst(b), h1[rows, b, :])

            phase2_tile(4, omixs_d.v, lambda b: xs_d.v, lambda b: psm_d.v, lambda b: ys_d.v)
            first_stream[0] = False
            for t in range(NT):
                q0 = t * 512
                phase2_tile(512, omix_d.v[:, :, q0:q0 + 512],
                            lambda b: x_d.v[q0 + b * 128:q0 + (b + 1) * 128, :],
                            lambda b: p_d.v[q0 + b * 128:q0 + (b + 1) * 128, :],
                            lambda b: y_d.v[q0 + b * 128:q0 + (b + 1) * 128, :])
        k.finish()
    return nc, k


_NC_CACHE = {}


def run_cores(inputs, S, NPG, NPHYS, stop_after=None, ncores=8):
    key = (S, NPG, NPHYS, stop_after)
    if key not in _NC_CACHE:
        _NC_CACHE[key] = build(S, NPG, NPHYS, stop_after)
    nc, kb = _NC_CACHE[key]
    f = lambda a: np.ascontiguousarray(np.asarray(a))
    consts = host_consts(S, NPG * 128)
    cache_cat = np.concatenate([np.asarray(inputs["cache_krope"][0]).reshape(NPHYS * 128, 32),
                                np.asarray(inputs["cache_ckv"][0]).reshape(NPHYS * 128, 256)], axis=1)
    cache_cat = np.ascontiguousarray(cache_cat, dtype=np.float32)
    wmap = {n: f(inputs[n][0]).reshape(s) for n, s in W_SHAPES.items()}
    in_maps = []
    for c in range(ncores):
        m = dict(wmap)
        m.update(consts)
        m["x"] = f(inputs["x_prompt"][c])
        m["xs"] = f(inputs["x_sample"][4 * c:4 * c + 4, 0])
        m["p"] = f(inputs["p_prompt"][0, c])
        m["psm"] = f(inputs["p_sample"][0, 4 * c:4 * c + 4, 0])
        m["cache_cat"] = cache_cat
        m["state_gdn"] = f(inputs["state_gdn"][0, 4 * c:4 * c + 4])
        m["state_conv"] = f(inputs["state_conv"][0, 4 * c:4 * c + 4]).reshape(4, 3 * 1536)
        m["pt"] = f(inputs["page_table"][4 * c:4 * c + 4]).astype(np.int32)
        in_maps.append(m)
    res = run_bass_kernel_spmd(nc, in_maps, core_ids=list(range(ncores))).results
    g = lambda n: np.stack([np.asarray(r[n]) for r in res])
    y = g("y"); ys = g("ys").reshape(4 * ncores, 1, 1024)
    return (y, ys, g("o_ckv")[None], g("o_kr")[None], g("o_gdn")[None], g("o_conv")[None],
            g("o_ckv_s").reshape(1, 4 * ncores, 1, 256), g("o_kr_s").reshape(1, 4 * ncores, 1, 32),
            g("o_gdn_s").reshape(1, 4 * ncores, 8, 64, 64), g("o_conv_s").reshape(1, 4 * ncores, 3, 1536))


def kernel(**inputs):
    S = inputs["x_prompt"].shape[1]
    NPG = inputs["page_table"].shape[1]
    NPHYS = inputs["cache_ckv"].shape[1]
    outs = run_cores(inputs, S, NPG, NPHYS)
    return tuple(np.ascontiguousarray(o.astype(np.float32)) for o in outs)
```

```python
import contextlib
import numpy as np
import concourse.bass as bass
import concourse.mybir as mybir

F32 = mybir.dt.float32
F32R = mybir.dt.float32r
BF16 = mybir.dt.bfloat16
I32 = mybir.dt.int32
AF = mybir.ActivationFunctionType
ALU = mybir.AluOpType
AX = mybir.AxisListType
SKIP_SELF_WAIT = False


class V:
    __slots__ = ("t", "ap")

    def __init__(self, t, ap):
        self.t = t
        self.ap = ap

    def __getitem__(self, idx):
        return V(self.t, self.ap[idx])

    def re(self, pat, **kw):
        return V(self.t, self.ap.rearrange(pat, **kw))

    def bc(self, dtype):
        return V(self.t, self.ap.bitcast(dtype))

    def un(self, axis):
        return V(self.t, self.ap.unsqueeze(axis))

    def bto(self, shape):
        return V(self.t, self.ap.broadcast_to(shape))

    def pb(self, n):
        return V(self.t, self.ap.partition_broadcast(n))


class T:
    def __init__(self, name, h):
        self.name = name
        self.h = h
        self.w = None
        self.r = {}
        self.dsem = None
        self.dtot = 0
        self.psum = False

    def __getitem__(self, idx):
        return V(self, self.h[idx])

    @property
    def v(self):
        return V(self, self.h[:])


class KB:
    def __init__(self, nc, es, needed=None):
        self.nc = nc
        self.es = es
        self.es_sem = es
        self.eng = {"pe": nc.tensor, "act": nc.scalar, "dve": nc.vector, "pool": nc.gpsimd, "sp": nc.sync}
        self.sem = {k: es.enter_context(nc.semaphore("s_" + k)) for k in self.eng}
        self.cnt = {k: 0 for k in self.eng}
        self.waited = {k: {} for k in self.eng}
        self.dma_sems = {}
        self.out_events = []
        self.ninst = 0
        self.needed = needed
        self.need_rec = {k: set() for k in self.eng}
        self.rank = {k: 0 for k in self.eng}
        self.rankmap = {k: {} for k in self.eng}
        self.engsem = {id(v): k for k, v in self.sem.items()}

    def sb(self, name, shape, dt):
        self.uid = getattr(self, "uid", 0) + 1
        name = f"{name}_u{self.uid}"
        return T(name, self.es.enter_context(self.nc.sbuf_tensor(name, list(shape), dt)))

    def ps(self, name, shape, dt):
        t = T(name, self.es.enter_context(self.nc.psum_tensor(name, list(shape), dt)))
        t.psum = True
        return t

    def dram(self, name, shape, dt, kind):
        return T(name, self.nc.dram_tensor(name, list(shape), dt, kind=kind).ap())

    def _wait(self, e, ev):
        sem, val = ev
        sid = id(sem)
        if e == "pe" and sem is self.sem["pe"]:
            return
        if SKIP_SELF_WAIT and sem is self.sem.get(e):
            return
        if sid in self.dma_sems:
            val = max(val, self.dma_sems[sid][1])
        w = self.waited[e]
        if w.get(sid, 0) >= val:
            return
        w[sid] = val
        f = self.engsem.get(sid)
        if f is not None:
            self.need_rec[f].add(val)
            if self.needed is not None:
                val = self.rankmap[f][val]
        self.eng[e].wait_ge(sem, val)

    def _deps(self, e, reads, writes):
        for v in reads:
            t = v.t
            if t.w is not None:
                self._wait(e, t.w)
            if t.psum:
                for ev in t.r.values():
                    if ev[0] is not self.sem.get(e):
                        self._wait(e, ev)
        for v in writes:
            t = v.t
            if t.w is not None:
                self._wait(e, t.w)
            for ev in t.r.values():
                self._wait(e, ev)

    def _record(self, ev, reads, writes):
        sem, val = ev
        for v in reads:
            v.t.r[id(sem)] = ev
        for v in writes:
            v.t.w = ev
            v.t.r = {}

    def op(self, e, fn, reads, writes):
        reads = [v for v in reads if isinstance(v, V)]
        self._deps(e, reads, writes)
        inst = fn(self.eng[e])
        self.cnt[e] += 1
        if self.needed is None:
            inst.then_inc(self.sem[e], 1)
        elif self.cnt[e] in self.needed[e]:
            inst.then_inc(self.sem[e], 1)
            self.rank[e] += 1
            self.rankmap[e][self.cnt[e]] = self.rank[e]
        self._record((self.sem[e], self.cnt[e]), reads, writes)
        self.ninst += 1
        return inst

    def dma(self, q, out, in_, **kw):
        self._deps(q, [in_], [out])
        own = out.t
        if own.dsem is None:
            own.dsem = self.es_sem.enter_context(self.nc.semaphore("d_" + own.name))
            self.dma_sems[id(own.dsem)] = [own.dsem, 0]
        inst = self.eng[q].dma_start(out=out.ap, in_=in_.ap, **kw)
        own.dtot += 16
        self.dma_sems[id(own.dsem)][1] = own.dtot
        inst.then_inc(own.dsem, 16)
        ev = (own.dsem, own.dtot)
        self._record(ev, [in_], [out])
        self.ninst += 1
        return ev

    def gather(self, out, in_, idx, **kw):
        q = "pool"
        self._deps(q, [in_, idx], [out])
        own = out.t
        if own.dsem is None:
            own.dsem = self.es_sem.enter_context(self.nc.semaphore("d_" + own.name))
            self.dma_sems[id(own.dsem)] = [own.dsem, 0]
        inst = self.nc.gpsimd.indirect_dma_start(
            out=out.ap, out_offset=None, in_=in_.ap,
            in_offset=bass.IndirectOffsetOnAxis(ap=idx.ap, axis=0), **kw)
        own.dtot += 16
        self.dma_sems[id(own.dsem)][1] = own.dtot
        inst.then_inc(own.dsem, 16)
        ev = (own.dsem, own.dtot)
        self._record(ev, [in_, idx], [out])
        return ev

    def barrier(self):
        for e in self.eng:
            for sem, tot in self.dma_sems.values():
                if tot:
                    self._wait(e, (sem, tot))
            for f in self.eng:
                if f != e and self.cnt[f]:
                    self._wait(e, (self.sem[f], self.cnt[f]))

    def finish(self):
        for sem, tot in self.dma_sems.values():
            if tot:
                self._wait("sp", (sem, tot))
        for e in self.eng:
            if e != "sp" and self.cnt[e]:
                self._wait("sp", (self.sem[e], self.cnt[e]))

    def mm(self, out, lhsT, rhs, start=True, stop=True):
        return self.op("pe", lambda e: e.matmul(out.ap, lhsT=lhsT.ap, rhs=rhs.ap, start=start, stop=stop),
                       [lhsT, rhs] + ([] if start else [out]), [out])

    def tr(self, out, in_, ident):
        return self.op("pe", lambda e: e.transpose(out.ap, in_.ap, ident.ap), [in_, ident], [out])

    def act(self, out, in_, func, scale=1.0, bias=0.0, accum=None, eng="act"):
        kw = {}
        if accum is not None:
            kw["accum_out"] = accum.ap
        sc = scale.ap if isinstance(scale, V) else scale
        bi = bias.ap if isinstance(bias, V) else bias
        return self.op("act", lambda e: e.activation(out=out.ap, in_=in_.ap, func=func, scale=sc, bias=bi, **kw),
                       [in_, scale, bias], [out] + ([accum] if accum is not None else []))

    def tt(self, e, out, in0, in1, op):
        return self.op(e, lambda g: g.tensor_tensor(out=out.ap, in0=in0.ap, in1=in1.ap, op=op), [in0, in1], [out])

    def ts(self, e, out, in0, s1, op0, s2=None, op1=None, accum=None):
        a1 = s1.ap if isinstance(s1, V) else s1
        a2 = s2.ap if isinstance(s2, V) else s2
        kw = {}
        if op1 is not None:
            kw["op1"] = op1
        if accum is not None:
            kw["accum_out"] = accum.ap
        return self.op(e, lambda g: g.tensor_scalar(out=out.ap, in0=in0.ap, scalar1=a1, scalar2=a2, op0=op0, **kw),
                       [in0, s1, s2], [out] + ([accum] if accum is not None else []))

    def stt(self, out, in0, scalar, in1, op0, op1, e="dve"):
        sc = scalar.ap if isinstance(scalar, V) else scalar
        return self.op(e, lambda g: g.scalar_tensor_tensor(out=out.ap, in0=in0.ap, scalar=sc, in1=in1.ap, op0=op0, op1=op1),
                       [in0, scalar, in1], [out])

    def copy(self, e, out, in_):
        if e == "act":
            return self.act(out, in_, AF.Copy)
        return self.op(e, lambda g: g.tensor_copy(out=out.ap, in_=in_.ap), [in_], [out])

    def memset(self, e, out, val):
        return self.op(e, lambda g: g.memset(out.ap, val), [], [out])

    def reduce(self, out, in_, op=ALU.add, axis=AX.X, e="dve"):
        return self.op(e, lambda g: g.tensor_reduce(out=out.ap, in_=in_.ap, axis=axis, op=op), [in_], [out])

    def recip(self, out, in_):
        return self.op("dve", lambda g: g.reciprocal(out=out.ap, in_=in_.ap), [in_], [out])

from concourse.bass_utils import run_bass_kernel_spmd

D_MODEL = 1024; PLE_DIM = 256
NH = 8; DK = 64; CONV_DIM = 1536
Q_LORA = 384; KV_LORA = 256; ROPE = 32; NOPE = 64; QK = 96
IN_DIM = 2736; D_FF = 2816
EPS = 1e-6
ATTN_SCALE = QK ** -0.5
C_AB = 1536; C_Z = 1552; C_QA = 2064; C_KVA = 2448
BIG = 1.0e4


class Rot:
    def __init__(self, items):
        self.items = items
        self.i = 0

    def __call__(self):
        t = self.items[self.i % len(self.items)]
        self.i += 1
        return t


def host_consts(S, past):
    c = {}
    i = np.arange(128)
    same = (i[:, None] // 64) == (i[None, :] // 64)
    c["ident"] = np.eye(128, dtype=np.float32)
    c["tri"] = (same & (i[:, None] <= i[None, :])).astype(np.float32)
    c["lastsel"] = (i[:, None] == (i[None, :] // 64) * 64 + 63).astype(np.float32)
    vis = same & (i[None, :] < i[:, None])
    c["negs"] = np.where(vis, 0.0, BIG).astype(np.float32)
    c["negt"] = np.where(vis.T, 0.0, -BIG).astype(np.float32)
    c["headblk"] = same.astype(np.float32)
    sel8 = np.zeros((8, 8, 128), np.float32)
    for h in range(8):
        sel8[h, h, :] = 1.0
    c["sel8"] = sel8
    selp = np.zeros((8, 4, 128), np.float32)
    for h in range(8):
        selp[h, h // 2, (h % 2) * 64:(h % 2) * 64 + 64] = 1.0
    c["selpair"] = selp
    c["cmask"] = (i[None, :] >= i[:, None]).astype(np.float32)
    dm = np.zeros((8, 8, 64), np.float32)
    for h in range(8):
        dm[h, h, :] = 1.0
    c["diagmask"] = dm.reshape(8, 512)
    oh = np.zeros((4, 4, 128), np.float32)
    for b in range(4):
        oh[b, b, :] = 1.0
    c["onehot4"] = oh
    half = ROPE // 2
    inv = (10000.0 ** (-np.arange(half, dtype=np.float32) / half)).astype(np.float32)
    pos = np.arange(S, dtype=np.float32)
    ang = pos[:, None] * inv[None, :]
    c["cos_p"] = np.cos(ang).astype(np.float32)
    c["sin_p"] = np.sin(ang).astype(np.float32)
    angs = (np.float32(past) * inv)[None, :].astype(np.float32)
    c["cos_s"] = np.repeat(np.cos(angs), 4, 0).astype(np.float32)
    c["sin_s"] = np.repeat(np.sin(angs), 4, 0).astype(np.float32)
    return c


CONST_SHAPES = dict(ident=[128, 128], tri=[128, 128], lastsel=[128, 128], negs=[128, 128], negt=[128, 128],
                    headblk=[128, 128], sel8=[8, 8, 128], selpair=[8, 4, 128], cmask=[128, 128],
                    diagmask=[8, 512], onehot4=[4, 4, 128], cos_s=[4, 16], sin_s=[4, 16])

W_SHAPES = dict(g_attn=[1, 1024], w_in=[1024, IN_DIM], w_conv=[4, 1536], gdn_a_log=[1, 8], gdn_dt_bias=[1, 8],
                g_gdn_out=[1, 64], g_q_a=[1, 384], w_q_b=[384, 768], g_q_nope=[1, 64], g_q_rope=[1, 32],
                g_kv_a=[1, 256], g_k_rope=[1, 32], w_kv_b=[256, 1024], g_k_nope=[1, 64], w_o=[1024, 1024],
                g_ffn=[1, 1024], w_ffn_gate=[1024, D_FF], w_ffn_up=[1024, D_FF], w_ffn_down=[D_FF, 1024],
                g_ple=[1, 1024], w_ple_gate=[1024, 1024], w_ple_proj=[256, 1024])


def build(S, NPG, NPHYS, stop_after=None):
    _, k1 = build1(S, NPG, NPHYS, stop_after, None)
    return build1(S, NPG, NPHYS, stop_after, k1.need_rec)


INV_DT = BF16


def build1(S, NPG, NPHYS, stop_after, needed):
    import os
    DBG = int(os.environ.get('KDBG', '0'))
    NBLK = S // 128
    NT = S // 512
    nc = bass.Bass("TRN2", target_bir_lowering=False)
    es = contextlib.ExitStack()
    with es:
        nc_lp = es.enter_context(nc.allow_low_precision("bf16 matmul operands by design"))
        es.enter_context(nc.allow_non_contiguous_dma("small strided state loads/stores"))
        k = KB(nc, es, needed)
        din = {}

        def DI(name, shape, dt=F32):
            din[name] = k.dram(name, shape, dt, "ExternalInput")
            return din[name]

        def DO(name, shape, dt=F32):
            return k.dram(name, shape, dt, "ExternalOutput")

        x_d = DI("x", [S, 1024]); xs_d = DI("xs", [4, 1024])
        p_d = DI("p", [S, 256]); psm_d = DI("psm", [4, 256])
        ccat_d = DI("cache_cat", [NPHYS * 128, 288])
        sgdn_d = DI("state_gdn", [4, 8, 64, 64]); sconv_d = DI("state_conv", [4, 3 * 1536])
        pt_d = DI("pt", [4, NPG], I32)
        W = {n: DI(n, s) for n, s in W_SHAPES.items()}
        C = {n: DI(n, s) for n, s in CONST_SHAPES.items()}
        C["cos_p"] = DI("cos_p", [S, 16]); C["sin_p"] = DI("sin_p", [S, 16])
        y_d = DO("y", [S, 1024]); ys_d = DO("ys", [4, 1024])
        ockv_d = DO("o_ckv", [S, 256]); okr_d = DO("o_kr", [S, 32])
        ogdn_d = DO("o_gdn", [8, 64, 64]); oconv_d = DO("o_conv", [3, 1536])
        ockvs_d = DO("o_ckv_s", [4, 256]); okrs_d = DO("o_kr_s", [4, 32])
        ogdns_d = DO("o_gdn_s", [4, 8, 64, 64]); oconvs_d = DO("o_conv_s", [4, 3 * 1536])
        omix_d = k.dram("omix_scr", [128, 8, S], BF16, "ExternalOutput" if DBG == 99 else "Internal")
        omixs_d = k.dram("omixs_scr", [128, 8, 4], BF16, "Internal")

        banks = [k.ps(f"ps{i}", [128, 512], F32) for i in range(8)]
        psb = Rot(banks[:7])
        accbank = banks[7]
        evi = [0]

        def evac(out, in_, scale=None):
            evi[0] += 1
            if scale is not None:
                return k.act(out, in_, AF.Copy, scale=scale)
            if evi[0] % 3:
                return k.act(out, in_, AF.Copy)
            return k.copy("dve", out, in_)

        def cload(name, shape, dt=F32, src=None, q="sp"):
            t = k.sb("c_" + name, shape, dt)
            k.dma(q, t.v, (src if src is not None else C[name].v))
            return t

        identf = cload("ident", [128, 128])
        identb = k.sb("identb", [128, 128], BF16); k.copy("dve", identb.v, identf.v)
        identr = k.sb("identr", [128, 128], F32R); k.copy("dve", identr.v, identf.v)
        tri = cload("tri", [128, 128]); lastsel = cload("lastsel", [128, 128])
        negs = cload("negs", [128, 128]); negt = cload("negt", [128, 128])
        headblk = cload("headblk", [128, 128])
        headblk_b = k.sb("headblk_b", [128, 128], BF16); k.copy("dve", headblk_b.v, headblk.v)
        sel8 = cload("sel8", [8, 8, 128]); selpair = cload("selpair", [8, 4, 128])
        cmaskf = cload("cmask", [128, 128])
        cmask = k.sb("cmaskb", [128, 128], BF16); k.copy("dve", cmask.v, cmaskf.v)
        diagmask = cload("diagmask", [8, 512]); onehot4 = cload("onehot4", [4, 4, 128])
        onesb = k.sb("onesb", [128, 64], BF16); k.memset("dve", onesb.v, 1.0)
        onesf = k.sb("onesf", [128, 8], F32); k.memset("dve", onesf.v, 1.0)

        def bload(name, F, q="sp"):
            t = k.sb("b_" + name, [128, F], F32)
            k.dma(q, t.v, W[name].v.bto([128, F]))
            return t

        g_attn = bload("g_attn", 1024); g_q_a = bload("g_q_a", 384); g_kv_a = bload("g_kv_a", 256)
        g_q_nope = bload("g_q_nope", 64); g_q_rope = bload("g_q_rope", 32)
        g_k_rope = bload("g_k_rope", 32); g_k_nope = bload("g_k_nope", 64)
        g_gdn_out = bload("g_gdn_out", 64)
        a_log = bload("gdn_a_log", 8); dtb = bload("gdn_dt_bias", 8)
        eA = k.sb("eA", [128, 8], F32); k.act(eA.v, a_log.v, AF.Exp)
        k.ts("dve", g_q_nope.v, g_q_nope.v, ATTN_SCALE, ALU.mult)
        k.ts("dve", g_q_rope.v, g_q_rope.v, ATTN_SCALE, ALU.mult)
        wcv = k.sb("wcv", [128, 12, 4], F32)
        for j in range(4):
            k.dma("sp", wcv[:, :, j], W["w_conv"].v[j:j + 1, :].re("o (c p) -> p (o c)", p=128))

        def wload(name, K, N, q="pool"):
            kc = K // 128
            t = k.sb("w_" + name, [128, kc, N], BF16)
            src = W[name].v.re("(k p) n -> p k n", p=128)
            for i in range(kc):
                for n0 in range(0, N, 1024):
                    n1 = min(N, n0 + 1024)
                    k.dma(q, t[:, i, n0:n1], src[:, i, n0:n1])
            return t

        sm = Rot([k.sb(f"sm{i}", [128, 8], F32) for i in range(24)])

        def rstd_from_ss(ss, n, F, rows):
            a = sm()
            k.ts("dve", a[rows, 0:n], ss, 1.0 / F, ALU.mult, EPS, ALU.add)
            k.act(a[rows, 0:n], a[rows, 0:n], AF.Ln)
            r = sm()
            k.act(r[rows, 0:n], a[rows, 0:n], AF.Exp, scale=-0.5)
            return r[rows, 0:n]

        junk = k.sb("junk", [128, 1024], BF16)

        def rmsnorm_rows(out, x, g, F, rows):
            ss = sm()
            k.act(junk[rows, 0:F], x, AF.Square, accum=ss[rows, 0:1])
            r = rstd_from_ss(ss[rows, 0:1], 1, F, rows)
            k.stt(out, x, r, g, ALU.mult, ALU.mult)

        def transpose_bf(dst_fn, src, nt, nch, width=128):
            bank = psb()
            pb = bank.v.bc(BF16)
            for j in range(nch):
                k.tr(pb[0:width, j * 128:j * 128 + nt], src[:, j * width:(j + 1) * width], identb[0:nt, 0:nt])
            return pb

        from types import SimpleNamespace as NS
        GROUPS_ALL = [(0, 512), (512, 1024), (1024, 1536), (1536, 1552), (1552, 2064), (2064, 2448), (2448, 2736)]
        GROUPS_A = GROUPS_ALL[:5]
        GROUPS_B = GROUPS_ALL[5:]
        if stop_after is None:
            w_o = wload("w_o", 1024, 1024)
            w_pg = wload("w_ple_gate", 1024, 1024)
            w_pp = wload("w_ple_proj", 256, 1024)
        esP1 = contextlib.ExitStack()
        esP1.__enter__()
        k.es_outer = k.es
        k.es = esP1
        x_blk = k.sb("x_blk", [128, 1024], F32)
        xn_t = k.sb("xn", [128, 1024], BF16)
        xnT = k.sb("xnT", [128, 8, 128], BF16)
        z_tok = k.sb("z_tok", [128, IN_DIM], F32)
        sq_t = k.sb("sq_t", [128, 512], F32)

        def inproj(x_src_v, nt, w_in, c_off, groups):
            rows = slice(0, nt)
            k.dma("sp", x_blk[rows, :], x_src_v)
            rmsnorm_rows(xn_t[rows, :], x_blk[rows, :], g_attn[rows, :], 1024, rows)
            pb = transpose_bf(None, xn_t[rows, :], nt, 8)
            evac(xnT[:, :, 0:nt], pb[:, 0:1024].re("p (j t) -> p j t", j=8)[:, :, 0:nt])
            for (c0, c1) in groups:
                bank = psb()
                for kk in range(8):
                    k.mm(bank[rows, 0:c1 - c0], xnT[:, kk, 0:nt], w_in[:, kk, c0 - c_off:c1 - c_off], start=(kk == 0), stop=(kk == 7))
                evac(z_tok[rows, c0:c1], bank[rows, 0:c1 - c0])

        def wload_cols(name, K, c0, c1, q="pool"):
            kc = K // 128
            t = k.sb("w_" + name + f"_{c0}", [128, kc, c1 - c0], BF16)
            src = W[name].v.re("(k p) n -> p k n", p=128)
            for i in range(kc):
                for n0 in range(c0, c1, 1024):
                    n1 = min(c1, n0 + 1024)
                    k.dma(q, t[:, i, n0 - c0:n1 - c0], src[:, i, n0:n1])
            return t

        def head_rms(out, xin, g, nt, width):
            rows = slice(0, nt)
            sq = sq_t[rows, 0:8 * width].re("p (h d) -> p h d", h=8)
            k.act(sq, xin, AF.Square)
            ss = sm()
            k.reduce(ss[rows, 0:8], sq)
            r = rstd_from_ss(ss[rows, 0:8], 8, width, rows)
            k.tt("dve", out, xin, r.un(2).bto([nt, 8, width]), ALU.mult)
            k.tt("pool", out, out, g.un(1).bto([nt, 8, width]), ALU.mult)

        def alloc_mla(w_qb, w_kvb):
            M = NS()
            M.w_qb = w_qb; M.w_kvb = w_kvb
            M.qa_n = k.sb("qa_n", [128, 384], BF16)
            M.qanT = k.sb("qanT", [128, 3, 128], BF16)
            M.qkv_tok = k.sb("qkv_tok", [128, 1024], F32)
            M.hh = k.sb("hh", [128, 8, 96], F32)
            M.hh_bf = k.sb("hh_bf", [128, 8, 96], BF16)
            M.rtmp = k.sb("rtmp", [128, 8, 32], F32)
            M.rtmp2 = k.sb("rtmp2", [128, 8, 16], F32)
            M.cos_t = k.sb("cos_t", [128, 16], F32); M.sin_t = k.sb("sin_t", [128, 16], F32)
            M.ckv_f = k.sb("ckv_f", [128, 256], F32)
            M.ckv_bf = k.sb("ckv_bf", [128, 260], BF16)
            k.memset("dve", M.ckv_bf[:, 256:257], 1.0)
            M.ckvT = k.sb("ckvT", [128, 2, 128], BF16)
            M.kr_f = k.sb("kr_f", [128, 32], F32)
            M.kr_n = k.sb("kr_n", [128, 32], F32)
            return M

        def rope(M, out, xin, nt, nh):
            rows = slice(0, nt)
            cb = M.cos_t[rows, :].un(1).bto([nt, nh, 16]); sb_ = M.sin_t[rows, :].un(1).bto([nt, nh, 16])
            x1 = xin[:, :, 0:16]; x2 = xin[:, :, 16:32]
            t2 = M.rtmp2[rows, 0:nh, :]
            k.tt("dve", out[:, :, 0:16], x1, cb, ALU.mult)
            k.tt("dve", t2, x2, sb_, ALU.mult)
            k.tt("dve", out[:, :, 0:16], out[:, :, 0:16], t2, ALU.subtract)
            k.tt("dve", out[:, :, 16:32], x1, sb_, ALU.mult)
            k.tt("dve", t2, x2, cb, ALU.mult)
            k.tt("dve", out[:, :, 16:32], out[:, :, 16:32], t2, ALU.add)

        def mla_q(M, nt, cos_src, sin_src):
            rows = slice(0, nt)
            k.dma("sp", M.cos_t[rows, :], cos_src); k.dma("sp", M.sin_t[rows, :], sin_src)
            rmsnorm_rows(M.qa_n[rows, :], z_tok[rows, C_QA:C_QA + 384], g_q_a[rows, :], 384, rows)
            pb = transpose_bf(None, M.qa_n[rows, :], nt, 3)
            evac(M.qanT[:, :, 0:nt], pb[:, 0:384].re("p (j t) -> p j t", j=3)[:, :, 0:nt])
            for (c0, c1) in ((0, 512), (512, 768)):
                bank = psb()
                for kk in range(3):
                    k.mm(bank[rows, 0:c1 - c0], M.qanT[:, kk, 0:nt], M.w_qb[:, kk, c0:c1], start=(kk == 0), stop=(kk == 2))
                evac(M.qkv_tok[rows, c0:c1], bank[rows, 0:c1 - c0])
            q3 = M.qkv_tok[rows, 0:768].re("p (h d) -> p h d", h=8)
            head_rms(M.hh[rows, :, 0:64], q3[:, :, 0:64], g_q_nope[rows, :], nt, 64)
            head_rms(M.rtmp[rows, :, :], q3[:, :, 64:96], g_q_rope[rows, :], nt, 32)
            rope(M, M.hh[rows, :, 64:96], M.rtmp[rows, :, :], nt, 8)

        def mla_kv(M, nt, ockv_v, okr_v):
            rows = slice(0, nt)
            rmsnorm_rows(M.ckv_f[rows, :], z_tok[rows, C_KVA:C_KVA + 256], g_kv_a[rows, :], 256, rows)
            k.dma("sp", ockv_v, M.ckv_f[rows, :])
            k.copy("pool", M.ckv_bf[rows, 0:256], M.ckv_f[rows, :])
            rmsnorm_rows(M.kr_n[rows, :], z_tok[rows, C_KVA + 256:C_KVA + 288], g_k_rope[rows, :], 32, rows)
            rope(M, M.kr_f[rows, :].un(1), M.kr_n[rows, :].un(1), nt, 1)
            k.dma("sp", okr_v, M.kr_f[rows, :])
            pb = transpose_bf(None, M.ckv_bf[rows, 0:256], nt, 2)
            evac(M.ckvT[:, :, 0:nt], pb[:, 0:256].re("p (j t) -> p j t", j=2)[:, :, 0:nt])

        def alloc_gdn(small=False):
            G = NS()
            G.cv = k.sb("cv", [128, 12, 128], F32)
            G.sqt = k.sb("sqt", [128, 4, 128], BF16)
            G.rst = k.sb("rst", [128, 4, 128], F32)
            G.qTg = k.sb("qTg", [128, 4, 128], F32)
            G.kTg = k.sb("kTg", [128, 4, 128], F32)
            G.tmp4 = k.sb("tmp4", [128, 4, 128], F32)
            G.sz_t = k.sb("sz_t", [128, 512], F32)
            if not small:
                G.o_tok = k.sb("o_tok", [128, 512], F32)
                G.on_t = k.sb("on_t", [128, 512], F32)
                G.og_bf = k.sb("og_bf", [128, 512], BF16)
            return G

        def gdn_scalars(nt):
            rows = slice(0, nt)
            ta = sm(); k.tt("dve", ta[rows, :], z_tok[rows, C_AB:C_AB + 8], dtb[rows, :], ALU.add)
            e = sm(); k.act(e[rows, :], ta[rows, :], AF.Exp)
            sp_ = sm(); k.act(sp_[rows, :], e[rows, :], AF.Ln, bias=1.0)
            g_tok = sm(); k.stt(g_tok[rows, :], sp_[rows, :], -1.0, eA[rows, :], ALU.mult, ALU.mult)
            beta = sm(); k.act(beta[rows, :], z_tok[rows, C_AB + 8:C_AB + 16], AF.Sigmoid)
            return g_tok, beta

        def l2norm_fm(G, nt):
            for half in range(2):
                k.act(G.sqt[:, :, 0:nt], G.cv[:, half * 4:half * 4 + 4, 0:nt], AF.Square)
                bank = psb()
                for c in range(4):
                    k.mm(bank[:, c * 128:c * 128 + nt], headblk_b.v, G.sqt[:, c, 0:nt])
                k.ts("dve", G.tmp4[:, :, 0:nt], bank.v.re("p (c t) -> p c t", c=4)[:, :, 0:nt], EPS, ALU.add)
                k.act(G.tmp4[:, :, 0:nt], G.tmp4[:, :, 0:nt], AF.Ln)
                k.act(G.rst[:, :, 0:nt], G.tmp4[:, :, 0:nt], AF.Exp, scale=-0.5)
                if half == 0:
                    k.stt(G.qTg[:, :, 0:nt], G.cv[:, 0:4, 0:nt], DK ** -0.5, G.rst[:, :, 0:nt], ALU.mult, ALU.mult)
                else:
                    k.tt("dve", G.kTg[:, :, 0:nt], G.cv[:, 4:8, 0:nt], G.rst[:, :, 0:nt], ALU.mult)

        def gdn_out_tok(G, nt):
            rows = slice(0, nt)
            o3 = G.o_tok[rows, :].re("p (h d) -> p h d", h=8)
            on3 = G.on_t[rows, :].re("p (h d) -> p h d", h=8)
            head_rms(on3, o3, g_gdn_out[rows, :], nt, 64)
            k.act(G.sz_t[rows, :], z_tok[rows, C_Z:C_Z + 512], AF.Silu)
            k.tt("dve", G.og_bf[rows, :], G.on_t[rows, :], G.sz_t[rows, :], ALU.mult)

        def sample_scope():
            w_in = wload_cols("w_in", 1024, 0, IN_DIM)
            w_qb = wload_cols("w_q_b", 384, 0, 768)
            w_kvb = wload_cols("w_kv_b", 256, 0, 1024)
            M = alloc_mla(w_qb, w_kvb)
            G = alloc_gdn(small=True)
            nt = 4; rows = slice(0, 4)
            inproj(xs_d.v, 4, w_in, 0, GROUPS_ALL)
            if DBG == 1:
                return
            oms = k.sb("oms", [128, 8, 4], BF16)
            esg = contextlib.ExitStack()
            with esg:
                k.es = esg
                st_tok = k.sb("st_tok", [12, 1536], F32)
                k.dma("sp", st_tok.v, sconv_d.v.re("b (j c) -> (b j) c", j=3))
                k.dma("sp", oconvs_d.v.re("b (j c) -> b j c", j=3)[:, 0:2, :], sconv_d.v.re("b (j c) -> b j c", j=3)[:, 1:3, :])
                k.dma("sp", oconvs_d.v.re("b (j c) -> b j c", j=3)[:, 2, :], z_tok[rows, 0:1536])
                ext = k.sb("ext_fm", [128, 12, 4, 4], F32)
                bank = psb()
                for c in range(12):
                    k.tr(bank[:, c * 12:(c + 1) * 12], st_tok[:, c * 128:(c + 1) * 128], identf[0:12, 0:12])
                evac(ext[:, :, :, 0:3], bank[:, 0:144].re("p (c b j) -> p c b j", c=12, b=4))
                bank = psb()
                for c in range(12):
                    k.tr(bank[:, c * 4:(c + 1) * 4], z_tok[rows, c * 128:(c + 1) * 128], identf[0:4, 0:4])
                evac(ext[:, :, :, 3], bank[:, 0:48].re("p (c b) -> p c b", c=12))
                k.tt("dve", ext.v, ext.v, wcv.v.un(2).bto([128, 12, 4, 4]), ALU.mult)
                cpre = k.sb("cpre", [128, 12, 4], F32)
                k.reduce(cpre.v, ext.v)
                k.act(G.cv[:, :, 0:4], cpre.v, AF.Silu)
                l2norm_fm(G, 4)
                g_tok, beta = gdn_scalars(4)
                eg = sm(); k.act(eg[rows, :], g_tok[rows, :], AF.Exp)
                sm_b = Rot([k.sb(f"smb{i}", [128, 4, 4], F32) for i in range(3)])

                def bc_bh(src):
                    bank = psb()
                    for b in range(4):
                        k.mm(bank[:, b * 8:(b + 1) * 8], onehot4[:, b, :], src)
                    o = sm_b()
                    for hp in range(2):
                        RH = slice(hp * 64, hp * 64 + 64)
                        evac(o[RH, :, :], bank[RH, 0:32].re("p (b pr hp) -> p b pr hp", b=4, pr=4)[:, :, :, hp])
                    return o
                eg_b = bc_bh(eg[rows, :]); beta_b = bc_bh(beta[rows, :])
                st = k.sb("st_s", [128, 4, 4, 64], F32)
                for b in range(4):
                    for hp in range(2):
                        k.dma("sp", st[hp * 64:(hp + 1) * 64, b, :, :],
                              sgdn_d.v[b].re("(pr hp) k v -> hp k pr v", hp=2)[hp])
                B4 = lambda t: t.v.un(3).bto([128, 4, 4, 64])
                k.tt("dve", st.v, st.v, B4(eg_b), ALU.mult)
                tmp = k.sb("tmp_s", [128, 4, 4, 64], F32)
                kcol = G.kTg[:, :, 0:4].re("p pr b -> p b pr")
                k.tt("dve", tmp.v, st.v, kcol.un(3).bto([128, 4, 4, 64]), ALU.mult)
                kSB = k.sb("kSB_s", [128, 4, 4, 64], F32)
                for hf in range(2):
                    bank = psb()
                    k.mm(bank.v, headblk.v, tmp[:, hf * 2:hf * 2 + 2, :, :].re("p b pr v -> p (b pr v)"))
                    evac(kSB[:, hf * 2:hf * 2 + 2, :, :].re("p b pr v -> p (b pr v)"), bank.v)
                v_tk = k.sb("v_tk", [4, 512], F32)
                bank = psb()
                for c in range(4):
                    k.tr(bank[0:4, c * 128:(c + 1) * 128], G.cv[:, 8 + c, 0:4], identf.v)
                evac(v_tk.v, bank[0:4, :])
                vB = k.sb("vB_s", [128, 4, 4, 64], F32)
                for b in range(4):
                    bank = psb()
                    k.mm(bank.v, onehot4[:, b, :], v_tk.v)
                    for hp in range(2):
                        RH = slice(hp * 64, hp * 64 + 64)
                        evac(vB[RH, b, :, :], bank[RH, :].re("p (pr hp v) -> p pr hp v", pr=4, hp=2)[:, :, hp, :])
                k.tt("dve", vB.v, vB.v, kSB.v, ALU.subtract)
                k.tt("dve", vB.v, vB.v, B4(beta_b), ALU.mult)
                k.tt("dve", tmp.v, vB.v, kcol.un(3).bto([128, 4, 4, 64]), ALU.mult)
                k.tt("dve", st.v, st.v, tmp.v, ALU.add)
                for b in range(4):
                    for hp in range(2):
                        k.dma("sp", ogdns_d.v[b].re("(pr hp) k v -> hp k pr v", hp=2)[hp], st[hp * 64:(hp + 1) * 64, b, :, :])
                oT = k.sb("oT_s", [128, 16], F32)
                for hp in range(2):
                    bank = psb()
                    RH = slice(hp * 64, hp * 64 + 64)
                    for b in range(4):
                        for pr in range(4):
                            k.mm(bank[RH, pr * 4 + b:pr * 4 + b + 1], st[RH, b, pr, :], G.qTg[RH, pr, b:b + 1])
                    evac(oT[RH, :], bank[RH, 0:16])
                osq = k.sb("osq_s", [128, 16], F32)
                k.act(osq.v, oT.v, AF.Square)
                bank = psb()
                k.mm(bank[:, 0:16], headblk.v, osq.v)
                a_ = k.sb("a_s_", [128, 16], F32); r_ = k.sb("r_s_", [128, 16], F32)
                k.ts("dve", a_.v, bank[:, 0:16], 1.0 / 64, ALU.mult, EPS, ALU.add)
                k.act(a_.v, a_.v, AF.Ln)
                k.act(r_.v, a_.v, AF.Exp, scale=-0.5)
                k.tt("dve", oT.v, oT.v, r_.v, ALU.mult)
                ggo_col = k.sb("ggo_col", [128, 1], F32)
                for hp in range(2):
                    k.dma("sp", ggo_col[hp * 64:(hp + 1) * 64, :], W["g_gdn_out"].v.re("o d -> d o"))
                k.ts("dve", oT.v, oT.v, ggo_col[:, 0:1], ALU.mult)
                k.act(G.sz_t[rows, :], z_tok[rows, C_Z:C_Z + 512], AF.Silu)
                bank = psb()
                for c in range(4):
                    k.tr(bank[:, c * 4:c * 4 + 4], G.sz_t[rows, c * 128:(c + 1) * 128], identf[0:4, 0:4])
                k.tt("dve", oms[:, 0:4, :], oT.v.re("p (pr b) -> p pr b", pr=4), bank[:, 0:16].re("p (pr b) -> p pr b", pr=4), ALU.mult)
                k.barrier()
            k.es = es1_cur[0]
            mla_q(M, 4, C["cos_s"].v, C["sin_s"].v)
            mla_kv(M, 4, ockvs_d.v, okrs_d.v)
            krb = k.sb("krb_s", [4, 32], BF16)
            k.copy("dve", krb.v, M.kr_f[0:4, :])
            krT_new = k.sb("krT_new", [32, 4], BF16)
            bank = psb(); pb = bank.v.bc(BF16)
            k.tr(pb[0:32, 0:4], krb.v, identb[0:4, 0:4])
            evac(krT_new.v, pb[0:32, 0:4])
            if DBG == 7:
                return
            WkT = k.sb("WkT", [64, 8, 256], BF16)
            wk4 = w_kvb.v.re("p k (h d) -> p k h d", h=8)
            for kk in range(2):
                bank = psb(); pb = bank.v.bc(BF16)
                for h in range(8):
                    k.tr(pb[0:64, h * 128:(h + 1) * 128], wk4[:, kk, h, 0:64], identb.v)
                evac(WkT[:, :, kk * 128:(kk + 1) * 128], pb[0:64, 0:1024].re("p (h t) -> p h t", h=8))
            qg = k.sb("qg_s", [4, 8, 64], BF16)
            k.tt("dve", qg.v, M.hh[rows, :, 0:64], g_k_nope[rows, :].un(1).bto([4, 8, 64]), ALU.mult)
            qr = k.sb("qr_s", [4, 8, 32], BF16)
            k.copy("dve", qr.v, M.hh[rows, :, 64:96])
            bank = psb(); pb = bank.v.bc(BF16)
            for h in range(8):
                k.tr(pb[0:64, h * 4:h * 4 + 4], qg[:, h, :], identb[0:4, 0:4])
                k.tr(pb[0:32, 64 + h * 4:64 + h * 4 + 4], qr[:, h, :], identb[0:4, 0:4])
            qgT = k.sb("qgT_s", [64, 8, 4], BF16); qrT = k.sb("qrT_s", [32, 8, 4], BF16)
            evac(qgT.v, pb[0:64, 0:32].re("p (h b) -> p h b", h=8))
            evac(qrT.v, pb[0:32, 64:96].re("p (h b) -> p h b", h=8))
            bank = psb()
            for kk in range(2):
                for h in range(8):
                    k.mm(bank[:, kk * 32 + h * 4:kk * 32 + h * 4 + 4], WkT[:, h, kk * 128:(kk + 1) * 128], qgT[:, h, :])
            qpT = k.sb("qpT_s", [128, 2, 4, 8], BF16)
            evac(qpT.v, bank[:, 0:64].re("p (k h b) -> p k b h", k=2, h=8))
            if DBG == 8:
                return
            pti = k.sb("pti", [128, 4 * NPG], I32)
            k.dma("sp", pti.v, pt_d.v.re("(o b) j -> o (b j)", o=1).bto([128, 4 * NPG]))
            ptf = k.sb("ptf", [128, 4 * NPG], F32)
            k.copy("dve", ptf.v, pti.v)
            iot = k.sb("iot", [128, 1], F32)
            k.op("pool", lambda g: g.iota(iot.v.ap, pattern=[[0, 1]], base=0, channel_multiplier=1,
                                          allow_small_or_imprecise_dtypes=True), [], [iot.v])
            k.ts("dve", ptf.v, ptf.v, 128.0, ALU.mult, iot[:, 0:1], ALU.add)
            idx = pti
            k.copy("dve", idx.v, ptf.v)
            G_ = 4
            pg_r = Rot([k.sb(f"pg{i}", [128, 292], BF16) for i in range(2 * G_ + 2)])
            for t_ in pg_r.items:
                k.memset("dve", t_[:, 288:289], 1.0)
            sq_r = Rot([k.sb(f"sqp{i}", [128, G_, 512], BF16) for i in range(2)])
            sq1 = sq_r.items[0][:, 0, :]
            p_r = Rot([k.sb(f"pp{i}", [128, G_ * 8], BF16) for i in range(3)])
            sg_r = Rot([k.sb(f"sgp{i}", [128, G_ * 8], F32) for i in range(6)])
            Wkc = k.sb("Wkc", [128, 2, 512], BF16); Wvc = k.sb("Wvc", [128, 2, 512], BF16)
            for kk in range(2):
                k.copy("dve", Wkc[:, kk, :].re("p (h d) -> p h d", h=8), wk4[:, kk, :, 0:64])
                k.copy("dve", Wvc[:, kk, :].re("p (h d) -> p h d", h=8), wk4[:, kk, :, 64:128])
            wk_rhs = lambda kk: Wkc[:, kk, :]
            wv_rhs = lambda kk: Wvc[:, kk, :]
            acc_sb = k.sb("acc_sb", [8, 257], F32)
            accn = k.sb("accn", [8, 256], BF16)
            accT = k.sb("accT", [128, 2, 8], BF16)
            om_f = k.sb("om_f", [8, 512], F32)
            trb = Rot([banks[0]]); bankA = banks[1:5]; bB = banks[5]
            qr_f = k.sb("qr_f", [4, 256], F32)
            k.copy("dve", qr_f.v.re("p (h d) -> p h d", h=8), M.hh[rows, :, 64:96])
            qrB = k.sb("qrB", [128, 4, 256], BF16)
            for b in range(4):
                bank = trb()
                k.mm(bank[:, 0:256], onehot4[:, b, :], qr_f.v)
                evac(qrB[:, b, :], bank[:, 0:256])
            rp_r = Rot([k.sb(f"rp{i}", [128, G_, 256], BF16) for i in range(2)])

            def newtok(b):
                rws = slice(0, 4)
                bA = bankA[0]
                for kk in range(2):
                    k.mm(bA[rws, 0:512], M.ckvT[:, kk, 0:4], wk_rhs(kk), start=(kk == 0), stop=(kk == 1))
                for kk in range(2):
                    k.mm(bB[rws, 0:8], M.ckvT[:, kk, 0:4], qpT[:, kk, b, :], start=(kk == 0), stop=(kk == 1))
                k.mm(bB[rws, 8:16], krT_new[:, 0:4], qrT[:, :, b])
                k.act(sq1[rws, :], bA[rws, :], AF.Square)
                ss = sm()
                k.reduce(ss[rws, :], sq1[rws, :].re("p (h d) -> p h d", h=8))
                r = rstd_from_ss(ss[rws, :], 8, 64, rws)
                s1 = sm()
                k.tt("dve", s1[rws, :], bB[rws, 0:8], r, ALU.mult)
                k.tt("dve", s1[rws, :], s1[rws, :], bB[rws, 8:16], ALU.add)
                s2 = sm()
                k.act(s2[rws, :], s1[rws, :], AF.Exp)
                pp = p_r()
                k.ts("dve", pp[rws, 0:8], s2[rws, :], identf[0:4, b:b + 1], ALU.mult)
                return pp

            trb2 = Rot([banks[0], banks[6]])
            cT4_r = Rot([k.sb(f"cT4_{i}", [128, 4, 2, 128], BF16) for i in range(2)])

            def frontA(b, j0, g):
                pgs = []
                tb = trb2(); pb = tb.v.bc(BF16)
                for i in range(g):
                    pg = pg_r(); pgs.append(pg)
                    col = b * NPG + j0 + i
                    k.gather(pg[:, 0:288], ccat_d.v, idx[:, col:col + 1])
                for i in range(g):
                    k.tr(pb[:, i * 256:i * 256 + 128], pgs[i][:, 32:160], identb.v)
                    k.tr(pb[:, i * 256 + 128:i * 256 + 256], pgs[i][:, 160:288], identb.v)
                cT4 = cT4_r()
                k.act(cT4[:, 0:g, :, :], pb[:, 0:g * 256].re("p (g j t) -> p g j t", g=g, j=2), AF.Copy)
                return pgs, cT4

            def frontB(b, g, pgs, cT4):
                sq = sq_r(); rp = rp_r()
                for i in range(g):
                    for kk in range(2):
                        k.mm(bankA[i][:, 0:512], cT4[:, i, kk, :], wk_rhs(kk), start=(kk == 0), stop=(kk == 1))
                    for kk in range(2):
                        k.mm(bB[:, i * 8:i * 8 + 8], cT4[:, i, kk, :], qpT[:, kk, b, :], start=(kk == 0), stop=(kk == 1))
                    k.act(sq[:, i, :], bankA[i].v, AF.Square)
                    k.tt("dve", rp[:, i, :].re("p (h d) -> p h d", h=8), pgs[i][:, 0:32].un(1).bto([128, 8, 32]),
                         qrB[:, b, :].re("p (h d) -> p h d", h=8), ALU.mult)
                return sq, rp

            def small(g, sq, rp):
                n8 = g * 8
                sr = sg_r()
                k.reduce(sr[:, 0:n8], rp[:, 0:g, :].re("p g (h d) -> p (g h) d", h=8))
                ss = sg_r()
                k.reduce(ss[:, 0:n8], sq[:, 0:g, :].re("p g (h d) -> p (g h) d", h=8))
                a = sg_r()
                k.ts("dve", a[:, 0:n8], ss[:, 0:n8], 1.0 / 64, ALU.mult, EPS, ALU.add)
                k.act(a[:, 0:n8], a[:, 0:n8], AF.Ln)
                r = sg_r()
                k.act(r[:, 0:n8], a[:, 0:n8], AF.Exp, scale=-0.5)
                s1 = sg_r()
                k.tt("dve", s1[:, 0:n8], bB[:, 0:n8], r[:, 0:n8], ALU.mult)
                k.tt("dve", s1[:, 0:n8], s1[:, 0:n8], sr[:, 0:n8], ALU.add)
                pp = p_r()
                k.act(pp[:, 0:n8], s1[:, 0:n8], AF.Exp)
                return pp

            def accm(j0, g, pgs, pp):
                for i in range(g):
                    k.mm(accbank[0:8, 0:257], pp[:, i * 8:(i + 1) * 8], pgs[i][:, 32:289], start=False,
                         stop=(j0 + i == NPG - 1))

            for b in range(4):
                pp = newtok(b)
                k.mm(accbank[0:8, 0:257], pp[0:4, 0:8], M.ckv_bf[0:4, 0:257], start=True, stop=(NPG == 0))
                groups = [(j0, min(G_, NPG - j0)) for j0 in range(0, NPG, G_)]
                if groups:
                    pgs, cT4 = frontA(b, *groups[0])
                    sq, rp = frontB(b, groups[0][1], pgs, cT4)
                    ppg = small(groups[0][1], sq, rp)
                for gi, (j0, g) in enumerate(groups):
                    cur = (pgs, ppg)
                    if gi + 1 < len(groups):
                        pgs, cT4 = frontA(b, *groups[gi + 1])
                    accm(j0, g, *cur)
                    if gi + 1 < len(groups):
                        sq, rp = frontB(b, groups[gi + 1][1], pgs, cT4)
                        ppg = small(groups[gi + 1][1], sq, rp)
                evac(acc_sb.v, accbank[0:8, 0:257])
                rl = sm(); k.recip(rl[0:8, 0:1], acc_sb[:, 256:257])
                k.ts("dve", accn.v, acc_sb[:, 0:256], rl[0:8, 0:1], ALU.mult)
                bank = trb(); pb = bank.v.bc(BF16)
                for kk in range(2):
                    k.tr(pb[:, kk * 8:kk * 8 + 8], accn[:, kk * 128:(kk + 1) * 128], identb[0:8, 0:8])
                evac(accT.v, pb[:, 0:16].re("p (k h) -> p k h", k=2))
                bank = trb()
                for kk in range(2):
                    k.mm(bank[0:8, :], accT[:, kk, :], wv_rhs(kk), start=(kk == 0), stop=(kk == 1))
                k.tt("dve", om_f.v, bank[0:8, :], diagmask.v, ALU.mult)
                bank2 = trb()
                for pr in range(4):
                    k.mm(bank2[:, pr:pr + 1], om_f[:, pr * 128:(pr + 1) * 128], onesf[0:8, 0:1])
                evac(oms[:, 4:8, b], bank2[:, 0:4])
            k.dma("sp", omixs_d.v, oms.v)

        def pass_a():
            w_in = wload_cols("w_in", 1024, 0, 2064)
            G = alloc_gdn()
            S2 = k.sb("S2", [128, 4, 128], F32)
            k.memset("pool", S2.v, 0.0)
            zcT = k.sb("zcT", [128, 12, 131], F32)
            k.memset("pool", zcT.v, 0.0)
            omixA = k.sb("omixA", [128, 4, 512], BF16)
            acc_r = Rot([k.sb(f"acc_c{i}", [128, 128], F32) for i in range(2)])
            accp_r = Rot([k.sb(f"acc_p{i}", [128, 128], F32) for i in range(2)])
            tmp_p = k.sb("tmp_p", [128, 128], F32)
            kbT = k.sb("kbT", [128, 4, 128], BF16); nwT = k.sb("nwT", [128, 4, 128], BF16)
            qgT = k.sb("qgT", [128, 4, 128], BF16)
            kT_b = k.sb("kT_b", [128, 4, 128], BF16); qT_b = k.sb("qT_b", [128, 4, 128], BF16)
            S2b = k.sb("S2b", [128, 4, 128], BF16)
            k.memset("pool", S2b.v, 0.0)
            egc_fm = k.sb("egc_fm", [128, 4, 128], F32)
            k_tok = G.on_t
            v_tok = k.sb("v_tok", [128, 512], F32); u_tok = v_tok
            vb_tok = k.sb("vb_tok", [128, 512], BF16)
            kbg_tok = k.sb("kbg_tok", [128, 512], BF16)
            kdec_tok = k.sb("kdec_tok", [128, 512], BF16)
            vnew_tok = k.sb("vnew_tok", [128, 512], BF16)
            gcT8 = k.sb("gcT8", [8, 128], F32)
            betaT8 = k.sb("betaT8", [8, 128], F32)
            qkT_all = k.sb("qkT_all", [128, 8, 128], BF16)
            U_all = k.sb("U_all", [128, 8, 128], BF16)
            g4 = Rot([k.sb(f"g4_{i}", [128, 4, 128], F32) for i in range(3)])
            r4s = [Rot([k.sb(f"r4_{j}_{i}", [128, 4, 128], INV_DT) for i in range(6)]) for j in range(2)]
            t4 = k.sb("t4", [128, 4, 128], F32)
            v4 = lambda b_: b_.v.re("p (c t) -> p c t", c=4)

            def gdn_block(tl):
                cv = G.cv; kTg = G.kTg; qTg = G.qTg
                for g3 in range(3):
                    bank = psb()
                    for c in range(4):
                        k.tr(bank[:, c * 128:(c + 1) * 128], z_tok[:, (g3 * 4 + c) * 128:(g3 * 4 + c + 1) * 128], identf.v)
                    evac(zcT[:, g3 * 4:g3 * 4 + 4, 3:131], v4(bank))
                for c in range(12):
                    if c < 12:
                        ac = acc_r()
                        k.ts("dve", ac.v, zcT[:, c, 0:128], wcv[:, c, 0:1], ALU.mult)
                        for j in (1, 2, 3):
                            k.stt(ac.v, zcT[:, c, j:j + 128], wcv[:, c, j:j + 1], ac.v, ALU.mult, ALU.add)
                    else:
                        ac = accp_r()
                        k.ts("pool", ac.v, zcT[:, c, 0:128], wcv[:, c, 0:1], ALU.mult)
                        for j in (1, 2, 3):
                            k.ts("pool", tmp_p.v, zcT[:, c, j:j + 128], wcv[:, c, j:j + 1], ALU.mult)
                            k.tt("pool", ac.v, ac.v, tmp_p.v, ALU.add)
                    k.act(cv[:, c, :], ac.v, AF.Silu)
                k.copy("pool", zcT[:, :, 0:3], zcT[:, :, 128:131])
                l2norm_fm(G, 128)
                k.copy("pool", kT_b.v, kTg.v)
                k.copy("pool", qT_b.v, qTg.v)
                bank = psb()
                for c in range(4):
                    k.tr(bank[:, c * 128:(c + 1) * 128], kTg[:, c, :], identf.v)
                evac(k_tok.v, bank.v)
                bank = psb()
                for c in range(4):
                    k.tr(bank[:, c * 128:(c + 1) * 128], cv[:, 8 + c, :], identf.v)
                evac(v_tok.v, bank.v)
                g_tok, beta = gdn_scalars(128)
                bank = psb()
                k.mm(bank[:, 0:8], tri.v, g_tok.v)
                gc = sm(); evac(gc.v, bank[:, 0:8])
                bank = psb()
                k.mm(bank[:, 0:8], lastsel.v, gc.v)
                dd = sm(); k.tt("dve", dd.v, bank[:, 0:8], gc.v, ALU.subtract)
                edec = sm(); k.act(edec.v, dd.v, AF.Exp)
                egc = sm(); k.act(egc.v, gc.v, AF.Exp)
                bge = sm(); k.tt("dve", bge.v, beta.v, egc.v, ALU.mult)
                b3 = lambda t: t.v.un(2).bto([128, 8, 64])
                r3 = lambda t: t.v.re("p (h d) -> p h d", h=8)
                k.tt("dve", r3(vb_tok), r3(v_tok), b3(beta), ALU.mult)
                k.tt("pool", r3(kdec_tok), r3(k_tok), b3(edec), ALU.mult)
                k.tt("pool", r3(kbg_tok), r3(k_tok), b3(bge), ALU.mult)
                bank = psb()
                k.tr(bank[0:8, 0:128], gc.v, identf.v)
                k.tr(bank[0:8, 128:256], beta.v, identf.v)
                evac(gcT8.v, bank[0:8, 0:128]); evac(betaT8.v, bank[0:8, 128:256])
                bank = psb()
                for pr in range(4):
                    k.mm(bank[:, pr * 128:(pr + 1) * 128], selpair[:, pr, :], gcT8.v)
                k.act(egc_fm.v, v4(bank), AF.Exp)
                bank = psb()
                for pr in range(4):
                    k.mm(bank[:, pr * 128:(pr + 1) * 128], selpair[:, pr, :], betaT8.v)
                k.tt("dve", kbT.v, kTg.v, v4(bank), ALU.mult)
                k.tt("dve", qgT.v, qTg.v, egc_fm.v, ALU.mult)
                R = lambda h: slice((h % 2) * 64, (h % 2) * 64 + 64)
                st8 = []
                for hg in range(2):
                    hs = [hg * 4 + i for i in range(4)]
                    r4 = r4s[hg]
                    bcb = psb()
                    for i, h in enumerate(hs):
                        k.mm(bcb[:, i * 128:(i + 1) * 128], sel8[:, h, :], gcT8.v)
                    d1 = g4()
                    k.tt("dve", d1.v, v4(bcb), gc[:, hg * 4:hg * 4 + 4].un(2).bto([128, 4, 128]), ALU.subtract)
                    e1 = g4()
                    k.tt("dve", e1.v, d1.v, negs.v.un(1).bto([128, 4, 128]), ALU.max)
                    Dm = g4()
                    k.act(Dm.v, e1.v, AF.Exp, scale=-1.0)
                    k.tt("dve", d1.v, d1.v, negt.v.un(1).bto([128, 4, 128]), ALU.min)
                    DTm = e1
                    k.act(DTm.v, d1.v, AF.Exp)
                    Bt = r4(); Ct = r4(); St = r4()
                    k.tt("pool", d1.v, DTm.v, identf.v.un(1).bto([128, 4, 128]), ALU.add)
                    for hp_ in range(2):
                        bKB = psb(); bKBT = psb(); bQKT = psb()
                        for which in range(3):
                            for i, h in enumerate(hs):
                                if h % 2 != hp_:
                                    continue
                                pr = h // 2
                                cs_ = slice((i // 2) * 128, (i // 2 + 1) * 128)
                                if which == 0:
                                    k.mm(bKB[:, cs_], kbT[R(h), pr, :], kT_b[R(h), pr, :])
                                elif which == 1:
                                    k.mm(bKBT[:, cs_], kT_b[R(h), pr, :], kbT[R(h), pr, :])
                                else:
                                    k.mm(bQKT[:, cs_], kT_b[R(h), pr, :], qT_b[R(h), pr, :])
                        v2 = lambda b_: b_[:, 0:256].re("p (c t) -> p c t", c=2)
                        k.stt(Bt[:, hp_::2, :], v2(bKB), -1.0, Dm[:, hp_::2, :], ALU.mult, ALU.mult)
                        k.stt(Ct[:, hp_::2, :], v2(bKBT), -1.0, DTm[:, hp_::2, :], ALU.mult, ALU.mult)
                        k.tt("dve", qkT_all[:, hg * 4 + hp_:hg * 4 + 4:2, :], v2(bQKT), d1[:, hp_::2, :], ALU.mult)
                    k.tt("pool", St.v, Ct.v, identf.v.un(1).bto([128, 4, 128]), ALU.add)
                    st8.append([Bt, Ct, St])
                for lvl in range(1, 6):
                    nBs = []
                    for hg in range(2):
                        Bt, Ct, St = st8[hg]
                        r4 = r4s[hg]
                        bB = psb()
                        for i in range(4):
                            k.mm(bB[:, i * 128:(i + 1) * 128], Ct[:, i, :], Bt[:, i, :])
                        nB = r4()
                        k.act(nB.v, v4(bB), AF.Copy)
                        nC = None
                        if lvl < 5:
                            bC = psb()
                            for i in range(4):
                                k.mm(bC[:, i * 128:(i + 1) * 128], Bt[:, i, :], Ct[:, i, :])
                            nC = r4()
                            k.act(nC.v, v4(bC), AF.Copy)
                        nBs.append((nB, nC))
                    for hg in range(2):
                        Bt, Ct, St = st8[hg]
                        nB, nC = nBs[hg]
                        r4 = r4s[hg]
                        bS = psb()
                        for i in range(4):
                            k.mm(bS[:, i * 128:(i + 1) * 128], nB[:, i, :], St[:, i, :])
                        if lvl < 5:
                            nS = r4()
                            k.tt("dve", nS.v, v4(bS), St.v, ALU.add)
                            st8[hg] = [nB, nC, nS]
                        else:
                            k.tt("dve", U_all[:, hg * 4:hg * 4 + 4, :], v4(bS), St.v, ALU.add)
                ub = psb(); wb = psb()
                for h in range(8):
                    pr = h // 2; Rh = slice((h % 2) * 64, (h % 2) * 64 + 64)
                    k.mm(ub[:, h * 64:(h + 1) * 64], U_all[:, h, :], vb_tok[:, h * 64:(h + 1) * 64])
                    k.mm(wb[Rh, pr * 128:(pr + 1) * 128], kbg_tok[:, h * 64:(h + 1) * 64], U_all[:, h, :])
                evac(u_tok.v, ub.v)
                k.act(nwT.v, v4(wb), AF.Copy, scale=-1.0)
                for ci in range(2):
                    RR = slice(ci * 64, ci * 64 + 64)
                    vbk = psb()
                    for pr in range(4):
                        k.mm(vbk[RR, pr * 128:(pr + 1) * 128], nwT[:, pr, RR], S2b[:, pr, :])
                    k.tt("dve", vnew_tok[RR, :], vbk[RR, :], u_tok[RR, :], ALU.add)
                    obk = psb()
                    for pr in range(4):
                        k.mm(obk[RR, pr * 128:(pr + 1) * 128], qgT[:, pr, RR], S2b[:, pr, :], start=True, stop=False)
                        for hp in range(2):
                            h = 2 * pr + hp
                            k.mm(obk[RR, pr * 128 + hp * 64:pr * 128 + (hp + 1) * 64], qkT_all[RR, h, RR],
                                 vnew_tok[RR, h * 64:(h + 1) * 64], start=False, stop=(hp == 1))
                    k.act(G.o_tok[RR, :], obk[RR, :], AF.Copy)
                    sbk = psb()
                    for pr in range(4):
                        cs = slice(pr * 128, (pr + 1) * 128)
                        k.mm(sbk[:, cs], kdec_tok[RR, cs], vnew_tok[RR, cs])
                    k.tt("dve", t4.v, v4(sbk), headblk.v.un(1).bto([128, 4, 128]), ALU.mult)
                    k.tt("pool", S2.v, S2.v, egc_fm[:, :, ci * 64 + 63:ci * 64 + 64].bto([128, 4, 128]), ALU.mult)
                    k.tt("pool", S2.v, S2.v, t4.v, ALU.add)
                    k.copy("pool", S2b.v, S2.v)
                gdn_out_tok(G, 128)
                pb = transpose_bf(None, G.og_bf.v, 128, 4)
                evac(omixA[:, :, tl * 128:(tl + 1) * 128], pb[:, 0:512].re("p (j t) -> p j t", j=4))

            for bi in range(NBLK):
                tl = bi % 4
                t0 = bi * 128
                inproj(x_d.v[t0:t0 + 128, :], 128, w_in, 0, GROUPS_A)
                if bi == NBLK - 1:
                    k.dma("sp", oconv_d.v, z_tok[125:128, 0:1536])
                gdn_block(tl)
                if tl == 3:
                    t = bi // 4
                    k.dma("sp", omix_d.v[:, 0:4, t * 512:(t + 1) * 512], omixA.v)
            for pr in range(4):
                for hp in range(2):
                    RH = slice(hp * 64, hp * 64 + 64)
                    k.dma("sp", ogdn_d.v[2 * pr + hp], S2[RH, pr, hp * 64:hp * 64 + 64])

        def pass_b():
            psb.items = banks[:5]
            w_in = wload_cols("w_in", 1024, C_QA, IN_DIM)
            w_qb = wload_cols("w_q_b", 384, 0, 768)
            w_kvb = wload_cols("w_kv_b", 256, 0, 1024)
            M = alloc_mla(w_qb, w_kvb)
            kT = k.sb("kT", [128, 8, S], BF16)
            v_sb = k.sb("v_sb", [128, NBLK, 8, 64], BF16)
            qT_tile = k.sb("qT_tile", [128, 8, 512], BF16)
            omixB = k.sb("omixB", [128, 4, 512], BF16)
            pT_r = Rot([k.sb(f"pT{i}", [128, 512], BF16) for i in range(4)])
            rl_t = k.sb("rl_t", [128, 512], F32)

            def mla_block(bi, tl):
                t0 = bi * 128
                mla_q(M, 128, C["cos_p"].v[t0:t0 + 128, :], C["sin_p"].v[t0:t0 + 128, :])
                k.copy("pool", M.hh_bf.v, M.hh.v)
                bank = psb(); pb = bank.v.bc(BF16)
                for h in range(8):
                    k.tr(pb[0:96, h * 128:(h + 1) * 128], M.hh_bf[:, h, :], identb.v)
                evac(qT_tile[0:96, :, tl * 128:(tl + 1) * 128], pb[0:96, 0:1024].re("p (h t) -> p h t", h=8))
                mla_kv(M, 128, ockv_d.v[t0:t0 + 128, :], okr_d.v[t0:t0 + 128, :])
                for (c0, c1) in ((0, 512), (512, 1024)):
                    bank = psb()
                    for kk in range(2):
                        k.mm(bank[:, 0:512], M.ckvT[:, kk, :], w_kvb[:, kk, c0:c1], start=(kk == 0), stop=(kk == 1))
                    evac(M.qkv_tok[:, c0:c1], bank.v)
                kv3 = M.qkv_tok.v.re("p (h d) -> p h d", h=8)
                head_rms(M.hh[:, :, 0:64], kv3[:, :, 0:64], g_k_nope.v, 128, 64)
                k.copy("pool", M.hh[:, :, 64:96], M.kr_f.v.un(1).bto([128, 8, 32]))
                k.copy("pool", M.hh_bf.v, M.hh.v)
                k.copy("dve", v_sb[:, bi, :, :], kv3[:, :, 64:128])
                bank = psb(); pb = bank.v.bc(BF16)
                for h in range(8):
                    k.tr(pb[0:96, h * 128:(h + 1) * 128], M.hh_bf[:, h, :], identb.v)
                evac(kT[0:96, :, t0:t0 + 128], pb[0:96, 0:1024].re("p (h t) -> p h t", h=8))

            def attention_tile(t):
                for pr in range(4):
                    o_ps = banks[5]; l_ps = banks[6]
                    for hp in range(2):
                        h = 2 * pr + hp
                        RH = slice(hp * 64, hp * 64 + 64)
                        nkb = 4 * t + 4
                        def stepA(j):
                            qlo = max(0, j - 4 * t)
                            ncol = (4 - qlo) * 128
                            qc = slice(qlo * 128, 512)
                            sc = psb()
                            k.mm(sc[:, 0:ncol], kT[0:96, h, j * 128:(j + 1) * 128], qT_tile[0:96, h, qc])
                            pT = pT_r()
                            k.act(pT[:, 0:ncol], sc[:, 0:ncol], AF.Exp)
                            if j >= 4 * t:
                                k.tt("pool", pT[:, 0:128], pT[:, 0:128], cmask.v, ALU.mult)
                            return pT, ncol, qc

                        def stepB(j, pT, ncol, qc):
                            k.mm(o_ps[RH, qc], v_sb[:, j, h, :], pT[:, 0:ncol], start=(j == 0), stop=(j == nkb - 1))
                            k.mm(l_ps[RH, qc], onesb.v, pT[:, 0:ncol], start=(j == 0), stop=(j == nkb - 1))

                        pend = stepA(0)
                        for j in range(nkb):
                            cur = pend
                            if j + 1 < nkb:
                                pend = stepA(j + 1)
                            stepB(j, *cur)
                    k.recip(rl_t.v, l_ps.v)
                    k.tt("dve", omixB[:, pr, :], o_ps.v, rl_t.v, ALU.mult)

            for bi in range(NBLK):
                tl = bi % 4
                t0 = bi * 128
                inproj(x_d.v[t0:t0 + 128, :], 128, w_in, C_QA, GROUPS_B)
                mla_block(bi, tl)
                if tl == 3:
                    t = bi // 4
                    attention_tile(t)
                    k.dma("sp", omix_d.v[:, 4:8, t * 512:(t + 1) * 512], omixB.v)

        k.es_saved = esP1
        es1_cur = [None]
        for name_, fn_ in (("sample", sample_scope), ("a", pass_a), ("b", pass_b)):
            es1 = contextlib.ExitStack()
            with es1:
                k.es = es1
                es1_cur[0] = es1
                fn_()
                k.barrier()
            k.es = k.es_saved
            if stop_after == name_:
                break
        psb.items = banks[:7]
        esP1.__exit__(None, None, None)
        k.es = k.es_outer

        if stop_after is None:
            w_dn = wload("w_ffn_down", D_FF, 1024)
            wg_scr = k.dram("wg_scr", [128, 8, D_FF], BF16, "Internal")
            wu_scr = k.dram("wu_scr", [128, 8, D_FF], BF16, "Internal")
            first_stream = [True]
            g_ffn = bload("g_ffn", 1024); g_ple = bload("g_ple", 1024)
            wg_r = Rot([k.sb(f"wg{i}", [128, 8, 512], BF16) for i in range(2)])
            wu_r = Rot([k.sb(f"wu{i}", [128, 8, 512], BF16) for i in range(2)])
            om_t = k.sb("om_t", [128, 8, 512], BF16)
            h1 = k.sb("h1", [128, 4, 1024], F32)
            un = k.sb("un", [128, 1024], BF16)
            uT = k.sb("uT", [128, 8, 512], BF16)
            hT = k.sb("hT", [128, 22, 512], BF16)
            sg = k.sb("sg", [128, 512], F32)
            sg2 = k.sb("sg2", [128, 512], F32)
            sg_rot = Rot([sg, sg2])
            p_t = k.sb("p_t", [128, 256], F32)
            p_bf = k.sb("p_bf", [128, 256], BF16)
            pT2 = k.sb("pT2", [128, 2, 128], BF16)
            gate_t = sg
            wgs = W["w_ffn_gate"].v.re("(k p) n -> p k n", p=128)
            wus = W["w_ffn_up"].v.re("(k p) n -> p k n", p=128)

            def phase2_tile(nt, om_src, x_src, p_src, y_dst):
                nb = (nt + 127) // 128
                bt = min(nt, 128)
                rows = slice(0, bt)
                k.dma("sp", om_t[:, :, 0:nt], om_src)
                for b in range(nb):
                    k.dma("sp", h1[rows, b, :], x_src(b))
                for b in range(nb):
                    cb = slice(b * 128, b * 128 + bt)
                    for hf in range(2):
                        bank = psb()
                        for kk in range(8):
                            k.mm(bank[rows, :], om_t[:, kk, cb], w_o[:, kk, hf * 512:(hf + 1) * 512], start=(kk == 0), stop=(kk == 7))
                        k.tt("dve", h1[rows, b, hf * 512:(hf + 1) * 512], bank[rows, :], h1[rows, b, hf * 512:(hf + 1) * 512], ALU.add)
                    rmsnorm_rows(un[rows, :], h1[rows, b, :], g_ffn[rows, :], 1024, rows)
                    pb = transpose_bf(None, un[rows, :], bt, 8)
                    evac(uT[:, :, cb], pb[:, 0:1024].re("p (j t) -> p j t", j=8)[:, :, 0:bt])
                for c0 in range(0, D_FF, 512):
                    c1 = min(D_FF, c0 + 512)
                    wg = wg_r(); wu = wu_r()
                    if first_stream[0]:
                        k.dma("pool", wg[:, :, 0:c1 - c0], wgs[:, :, c0:c1])
                        k.dma("pool", wu[:, :, 0:c1 - c0], wus[:, :, c0:c1])
                        k.dma("sp", wg_scr.v[:, :, c0:c1], wg[:, :, 0:c1 - c0])
                        k.dma("sp", wu_scr.v[:, :, c0:c1], wu[:, :, 0:c1 - c0])
                    else:
                        k.dma("sp", wg[:, :, 0:c1 - c0], wg_scr.v[:, :, c0:c1])
                        k.dma("sp", wu[:, :, 0:c1 - c0], wu_scr.v[:, :, c0:c1])
                    for m in range(c0 // 128, c1 // 128):
                        ms_ = slice(m * 128 - c0, (m + 1) * 128 - c0)
                        bg = psb(); bu = psb()
                        for kk in range(8):
                            k.mm(bg[:, 0:nt], wg[:, kk, ms_], uT[:, kk, 0:nt], start=(kk == 0), stop=(kk == 7))
                        for kk in range(8):
                            k.mm(bu[:, 0:nt], wu[:, kk, ms_], uT[:, kk, 0:nt], start=(kk == 0), stop=(kk == 7))
                        sgt = sg_rot()
                        k.act(sgt[:, 0:nt], bg[:, 0:nt], AF.Silu)
                        k.tt("dve", hT[:, m, 0:nt], sgt[:, 0:nt], bu[:, 0:nt], ALU.mult)
                for b in range(nb):
                    cb = slice(b * 128, b * 128 + bt)
                    for hf in range(2):
                        bank = psb()
                        for m in range(22):
                            k.mm(bank[rows, :], hT[:, m, cb], w_dn[:, m, hf * 512:(hf + 1) * 512], start=(m == 0), stop=(m == 21))
                        k.tt("dve", h1[rows, b, hf * 512:(hf + 1) * 512], bank[rows, :], h1[rows, b, hf * 512:(hf + 1) * 512], ALU.add)
                    rmsnorm_rows(un[rows, :], h1[rows, b, :], g_ple[rows, :], 1024, rows)
                    pb = transpose_bf(None, un[rows, :], bt, 8)
                    evac(uT[:, :, cb], pb[:, 0:1024].re("p (j t) -> p j t", j=8)[:, :, 0:bt])
                    k.dma("sp", p_t[rows, :], p_src(b))
                    k.copy("pool", p_bf[rows, :], p_t[rows, :])
                    pb = transpose_bf(None, p_bf[rows, :], bt, 2)
                    evac(pT2[:, :, 0:bt], pb[:, 0:256].re("p (j t) -> p j t", j=2)[:, :, 0:bt])
                    for hf in range(2):
                        hs_ = slice(hf * 512, (hf + 1) * 512)
                        bank = psb()
                        for kk in range(8):
                            k.mm(bank[rows, :], uT[:, kk, cb], w_pg[:, kk, hs_], start=(kk == 0), stop=(kk == 7))
                        k.act(gate_t[rows, :], bank[rows, :], AF.Sigmoid)
                        bank2 = psb()
                        for kk in range(2):
                            k.mm(bank2[rows, :], pT2[:, kk, 0:bt], w_pp[:, kk, hs_], start=(kk == 0), stop=(kk == 1))
                        k.tt("dve", gate_t[rows, :], bank2[rows, :], gate_t[rows, :], ALU.mult)
                        k.tt("pool", h1[rows, b, hs_], gate_t[rows, :], h1[rows, b, hs_], ALU.add)
                    k.dma("sp", y_dst(b), h1[rows, b, :])

            phase2_tile(4, omixs_d.v, lambda b: xs_d.v, lambda b: psm_d.v, lambda b: ys_d.v)
            first_stream[0] = False
            for t in range(NT):
                q0 = t * 512
                phase2_tile(512, omix_d.v[:, :, q0:q0 + 512],
                            lambda b: x_d.v[q0 + b * 128:q0 + (b + 1) * 128, :],
                            lambda b: p_d.v[q0 + b * 128:q0 + (b + 1) * 128, :],
                            lambda b: y_d.v[q0 + b * 128:q0 + (b + 1) * 128, :])
        k.finish()
    return nc, k


_NC_CACHE = {}


def run_cores(inputs, S, NPG, NPHYS, stop_after=None, ncores=8):
    key = (S, NPG, NPHYS, stop_after)
    if key not in _NC_CACHE:
        _NC_CACHE[key] = build(S, NPG, NPHYS, stop_after)
    nc, kb = _NC_CACHE[key]
    f = lambda a: np.ascontiguousarray(np.asarray(a))
    consts = host_consts(S, NPG * 128)
    cache_cat = np.concatenate([np.asarray(inputs["cache_krope"][0]).reshape(NPHYS * 128, 32),
                                np.asarray(inputs["cache_ckv"][0]).reshape(NPHYS * 128, 256)], axis=1)
    cache_cat = np.ascontiguousarray(cache_cat, dtype=np.float32)
    wmap = {n: f(inputs[n][0]).reshape(s) for n, s in W_SHAPES.items()}
    in_maps = []
    for c in range(ncores):
        m = dict(wmap)
        m.update(consts)
        m["x"] = f(inputs["x_prompt"][c])
        m["xs"] = f(inputs["x_sample"][4 * c:4 * c + 4, 0])
        m["p"] = f(inputs["p_prompt"][0, c])
        m["psm"] = f(inputs["p_sample"][0, 4 * c:4 * c + 4, 0])
        m["cache_cat"] = cache_cat
        m["state_gdn"] = f(inputs["state_gdn"][0, 4 * c:4 * c + 4])
        m["state_conv"] = f(inputs["state_conv"][0, 4 * c:4 * c + 4]).reshape(4, 3 * 1536)
        m["pt"] = f(inputs["page_table"][4 * c:4 * c + 4]).astype(np.int32)
        in_maps.append(m)
    res = run_bass_kernel_spmd(nc, in_maps, core_ids=list(range(ncores))).results
    g = lambda n: np.stack([np.asarray(r[n]) for r in res])
    y = g("y"); ys = g("ys").reshape(4 * ncores, 1, 1024)
    return (y, ys, g("o_ckv")[None], g("o_kr")[None], g("o_gdn")[None], g("o_conv")[None],
            g("o_ckv_s").reshape(1, 4 * ncores, 1, 256), g("o_kr_s").reshape(1, 4 * ncores, 1, 32),
            g("o_gdn_s").reshape(1, 4 * ncores, 8, 64, 64), g("o_conv_s").reshape(1, 4 * ncores, 3, 1536))


def kernel(**inputs):
    S = inputs["x_prompt"].shape[1]
    NPG = inputs["page_table"].shape[1]
    NPHYS = inputs["cache_ckv"].shape[1]
    outs = run_cores(inputs, S, NPG, NPHYS)
    return tuple(np.ascontiguousarray(o.astype(np.float32)) for o in outs)
```

```python
import contextlib
import numpy as np
import concourse.bass as bass
import concourse.mybir as mybir

F32 = mybir.dt.float32
F32R = mybir.dt.float32r
BF16 = mybir.dt.bfloat16
I32 = mybir.dt.int32
AF = mybir.ActivationFunctionType
ALU = mybir.AluOpType
AX = mybir.AxisListType
SKIP_SELF_WAIT = False


class V:
    __slots__ = ("t", "ap")

    def __init__(self, t, ap):
        self.t = t
        self.ap = ap

    def __getitem__(self, idx):
        return V(self.t, self.ap[idx])

    def re(self, pat, **kw):
        return V(self.t, self.ap.rearrange(pat, **kw))

    def bc(self, dtype):
        return V(self.t, self.ap.bitcast(dtype))

    def un(self, axis):
        return V(self.t, self.ap.unsqueeze(axis))

    def bto(self, shape):
        return V(self.t, self.ap.broadcast_to(shape))

    def pb(self, n):
        return V(self.t, self.ap.partition_broadcast(n))


class T:
    def __init__(self, name, h):
        self.name = name
        self.h = h
        self.w = None
        self.r = {}
        self.dsem = None
        self.dtot = 0
        self.psum = False

    def __getitem__(self, idx):
        return V(self, self.h[idx])

    @property
    def v(self):
        return V(self, self.h[:])


class KB:
    def __init__(self, nc, es, needed=None):
        self.nc = nc
        self.es = es
        self.es_sem = es
        self.eng = {"pe": nc.tensor, "act": nc.scalar, "dve": nc.vector, "pool": nc.gpsimd, "sp": nc.sync}
        self.sem = {k: es.enter_context(nc.semaphore("s_" + k)) for k in self.eng}
        self.cnt = {k: 0 for k in self.eng}
        self.waited = {k: {} for k in self.eng}
        self.dma_sems = {}
        self.out_events = []
        self.ninst = 0
        self.needed = needed
        self.need_rec = {k: set() for k in self.eng}
        self.rank = {k: 0 for k in self.eng}
        self.rankmap = {k: {} for k in self.eng}
        self.engsem = {id(v): k for k, v in self.sem.items()}

    def sb(self, name, shape, dt):
        self.uid = getattr(self, "uid", 0) + 1
        name = f"{name}_u{self.uid}"
        return T(name, self.es.enter_context(self.nc.sbuf_tensor(name, list(shape), dt)))

    def ps(self, name, shape, dt):
        t = T(name, self.es.enter_context(self.nc.psum_tensor(name, list(shape), dt)))
        t.psum = True
        return t

    def dram(self, name, shape, dt, kind):
        return T(name, self.nc.dram_tensor(name, list(shape), dt, kind=kind).ap())

    def _wait(self, e, ev):
        sem, val = ev
        sid = id(sem)
        if e == "pe" and sem is self.sem["pe"]:
            return
        if SKIP_SELF_WAIT and sem is self.sem.get(e):
            return
        if sid in self.dma_sems:
            val = max(val, self.dma_sems[sid][1])
        w = self.waited[e]
        if w.get(sid, 0) >= val:
            return
        w[sid] = val
        f = self.engsem.get(sid)
        if f is not None:
            self.need_rec[f].add(val)
            if self.needed is not None:
                val = self.rankmap[f][val]
        self.eng[e].wait_ge(sem, val)

    def _deps(self, e, reads, writes):
        for v in reads:
            t = v.t
            if t.w is not None:
                self._wait(e, t.w)
            if t.psum:
                for ev in t.r.values():
                    if ev[0] is not self.sem.get(e):
                        self._wait(e, ev)
        for v in writes:
            t = v.t
            if t.w is not None:
                self._wait(e, t.w)
            for ev in t.r.values():
                self._wait(e, ev)

    def _record(self, ev, reads, writes):
        sem, val = ev
        for v in reads:
            v.t.r[id(sem)] = ev
        for v in writes:
            v.t.w = ev
            v.t.r = {}

    def op(self, e, fn, reads, writes):
        reads = [v for v in reads if isinstance(v, V)]
        self._deps(e, reads, writes)
        inst = fn(self.eng[e])
        self.cnt[e] += 1
        if self.needed is None:
            inst.then_inc(self.sem[e], 1)
        elif self.cnt[e] in self.needed[e]:
            inst.then_inc(self.sem[e], 1)
            self.rank[e] += 1
            self.rankmap[e][self.cnt[e]] = self.rank[e]
        self._record((self.sem[e], self.cnt[e]), reads, writes)
        self.ninst += 1
        return inst

    def dma(self, q, out, in_, **kw):
        self._deps(q, [in_], [out])
        own = out.t
        if own.dsem is None:
            own.dsem = self.es_sem.enter_context(self.nc.semaphore("d_" + own.name))
            self.dma_sems[id(own.dsem)] = [own.dsem, 0]
        inst = self.eng[q].dma_start(out=out.ap, in_=in_.ap, **kw)
        own.dtot += 16
        self.dma_sems[id(own.dsem)][1] = own.dtot
        inst.then_inc(own.dsem, 16)
        ev = (own.dsem, own.dtot)
        self._record(ev, [in_], [out])
        self.ninst += 1
        return ev

    def gather(self, out, in_, idx, **kw):
        q = "pool"
        self._deps(q, [in_, idx], [out])
        own = out.t
        if own.dsem is None:
            own.dsem = self.es_sem.enter_context(self.nc.semaphore("d_" + own.name))
            self.dma_sems[id(own.dsem)] = [own.dsem, 0]
        inst = self.nc.gpsimd.indirect_dma_start(
            out=out.ap, out_offset=None, in_=in_.ap,
            in_offset=bass.IndirectOffsetOnAxis(ap=idx.ap, axis=0), **kw)
        own.dtot += 16
        self.dma_sems[id(own.dsem)][1] = own.dtot
        inst.then_inc(own.dsem, 16)
        ev = (own.dsem, own.dtot)
        self._record(ev, [in_, idx], [out])
        return ev

    def barrier(self):
        for e in self.eng:
            for sem, tot in self.dma_sems.values():
                if tot:
                    self._wait(e, (sem, tot))
            for f in self.eng:
                if f != e and self.cnt[f]:
                    self._wait(e, (self.sem[f], self.cnt[f]))

    def finish(self):
        for sem, tot in self.dma_sems.values():
            if tot:
                self._wait("sp", (sem, tot))
        for e in self.eng:
            if e != "sp" and self.cnt[e]:
                self._wait("sp", (self.sem[e], self.cnt[e]))

    def mm(self, out, lhsT, rhs, start=True, stop=True):
        return self.op("pe", lambda e: e.matmul(out.ap, lhsT=lhsT.ap, rhs=rhs.ap, start=start, stop=stop),
                       [lhsT, rhs] + ([] if start else [out]), [out])

    def tr(self, out, in_, ident):
        return self.op("pe", lambda e: e.transpose(out.ap, in_.ap, ident.ap), [in_, ident], [out])

    def act(self, out, in_, func, scale=1.0, bias=0.0, accum=None, eng="act"):
        kw = {}
        if accum is not None:
            kw["accum_out"] = accum.ap
        sc = scale.ap if isinstance(scale, V) else scale
        bi = bias.ap if isinstance(bias, V) else bias
        return self.op("act", lambda e: e.activation(out=out.ap, in_=in_.ap, func=func, scale=sc, bias=bi, **kw),
                       [in_, scale, bias], [out] + ([accum] if accum is not None else []))

    def tt(self, e, out, in0, in1, op):
        return self.op(e, lambda g: g.tensor_tensor(out=out.ap, in0=in0.ap, in1=in1.ap, op=op), [in0, in1], [out])

    def ts(self, e, out, in0, s1, op0, s2=None, op1=None, accum=None):
        a1 = s1.ap if isinstance(s1, V) else s1
        a2 = s2.ap if isinstance(s2, V) else s2
        kw = {}
        if op1 is not None:
            kw["op1"] = op1
        if accum is not None:
            kw["accum_out"] = accum.ap
        return self.op(e, lambda g: g.tensor_scalar(out=out.ap, in0=in0.ap, scalar1=a1, scalar2=a2, op0=op0, **kw),
                       [in0, s1, s2], [out] + ([accum] if accum is not None else []))

    def stt(self, out, in0, scalar, in1, op0, op1, e="dve"):
        sc = scalar.ap if isinstance(scalar, V) else scalar
        return self.op(e, lambda g: g.scalar_tensor_tensor(out=out.ap, in0=in0.ap, scalar=sc, in1=in1.ap, op0=op0, op1=op1),
                       [in0, scalar, in1], [out])

    def copy(self, e, out, in_):
        if e == "act":
            return self.act(out, in_, AF.Copy)
        return self.op(e, lambda g: g.tensor_copy(out=out.ap, in_=in_.ap), [in_], [out])

    def memset(self, e, out, val):
        return self.op(e, lambda g: g.memset(out.ap, val), [], [out])

    def reduce(self, out, in_, op=ALU.add, axis=AX.X, e="dve"):
        return self.op(e, lambda g: g.tensor_reduce(out=out.ap, in_=in_.ap, axis=axis, op=op), [in_], [out])

    def recip(self, out, in_):
        return self.op("dve", lambda g: g.reciprocal(out=out.ap, in_=in_.ap), [in_], [out])

from concourse.bass_utils import run_bass_kernel_spmd

D_MODEL = 1024; PLE_DIM = 256
NH = 8; DK = 64; CONV_DIM = 1536
Q_LORA = 384; KV_LORA = 256; ROPE = 32; NOPE = 64; QK = 96
IN_DIM = 2736; D_FF = 2816
EPS = 1e-6
ATTN_SCALE = QK ** -0.5
C_AB = 1536; C_Z = 1552; C_QA = 2064; C_KVA = 2448
BIG = 1.0e4


class Rot:
    def __init__(self, items):
        self.items = items
        self.i = 0

    def __call__(self):
        t = self.items[self.i % len(self.items)]
        self.i += 1
        return t


def host_consts(S, past):
    c = {}
    i = np.arange(128)
    same = (i[:, None] // 64) == (i[None, :] // 64)
    c["ident"] = np.eye(128, dtype=np.float32)
    c["tri"] = (same & (i[:, None] <= i[None, :])).astype(np.float32)
    c["lastsel"] = (i[:, None] == (i[None, :] // 64) * 64 + 63).astype(np.float32)
    vis = same & (i[None, :] < i[:, None])
    c["negs"] = np.where(vis, 0.0, BIG).astype(np.float32)
    c["negt"] = np.where(vis.T, 0.0, -BIG).astype(np.float32)
    c["headblk"] = same.astype(np.float32)
    sel8 = np.zeros((8, 8, 128), np.float32)
    for h in range(8):
        sel8[h, h, :] = 1.0
    c["sel8"] = sel8
    selp = np.zeros((8, 4, 128), np.float32)
    for h in range(8):
        selp[h, h // 2, (h % 2) * 64:(h % 2) * 64 + 64] = 1.0
    c["selpair"] = selp
    c["cmask"] = (i[None, :] >= i[:, None]).astype(np.float32)
    dm = np.zeros((8, 8, 64), np.float32)
    for h in range(8):
        dm[h, h, :] = 1.0
    c["diagmask"] = dm.reshape(8, 512)
    oh = np.zeros((4, 4, 128), np.float32)
    for b in range(4):
        oh[b, b, :] = 1.0
    c["onehot4"] = oh
    half = ROPE // 2
    inv = (10000.0 ** (-np.arange(half, dtype=np.float32) / half)).astype(np.float32)
    pos = np.arange(S, dtype=np.float32)
    ang = pos[:, None] * inv[None, :]
    c["cos_p"] = np.cos(ang).astype(np.float32)
    c["sin_p"] = np.sin(ang).astype(np.float32)
    angs = (np.float32(past) * inv)[None, :].astype(np.float32)
    c["cos_s"] = np.repeat(np.cos(angs), 4, 0).astype(np.float32)
    c["sin_s"] = np.repeat(np.sin(angs), 4, 0).astype(np.float32)
    return c


CONST_SHAPES = dict(ident=[128, 128], tri=[128, 128], lastsel=[128, 128], negs=[128, 128], negt=[128, 128],
                    headblk=[128, 128], sel8=[8, 8, 128], selpair=[8, 4, 128], cmask=[128, 128],
                    diagmask=[8, 512], onehot4=[4, 4, 128], cos_s=[4, 16], sin_s=[4, 16])

W_SHAPES = dict(g_attn=[1, 1024], w_in=[1024, IN_DIM], w_conv=[4, 1536], gdn_a_log=[1, 8], gdn_dt_bias=[1, 8],
                g_gdn_out=[1, 64], g_q_a=[1, 384], w_q_b=[384, 768], g_q_nope=[1, 64], g_q_rope=[1, 32],
                g_kv_a=[1, 256], g_k_rope=[1, 32], w_kv_b=[256, 1024], g_k_nope=[1, 64], w_o=[1024, 1024],
                g_ffn=[1, 1024], w_ffn_gate=[1024, D_FF], w_ffn_up=[1024, D_FF], w_ffn_down=[D_FF, 1024],
                g_ple=[1, 1024], w_ple_gate=[1024, 1024], w_ple_proj=[256, 1024])


def build(S, NPG, NPHYS, stop_after=None):
    _, k1 = build1(S, NPG, NPHYS, stop_after, None)
    return build1(S, NPG, NPHYS, stop_after, k1.need_rec)


INV_DT = BF16
NDUMMY = 0


def build1(S, NPG, NPHYS, stop_after, needed):
    import os
    DBG = int(os.environ.get('KDBG', '0'))
    NBLK = S // 128
    NT = S // 512
    nc = bass.Bass("TRN2", target_bir_lowering=False)
    es = contextlib.ExitStack()
    with es:
        nc_lp = es.enter_context(nc.allow_low_precision("bf16 matmul operands by design"))
        es.enter_context(nc.allow_non_contiguous_dma("small strided state loads/stores"))
        k = KB(nc, es, needed)
        din = {}

        def DI(name, shape, dt=F32):
            din[name] = k.dram(name, shape, dt, "ExternalInput")
            return din[name]

        def DO(name, shape, dt=F32):
            return k.dram(name, shape, dt, "ExternalOutput")

        x_d = DI("x", [S, 1024]); xs_d = DI("xs", [4, 1024])
        p_d = DI("p", [S, 256]); psm_d = DI("psm", [4, 256])
        ccat_d = DI("cache_cat", [NPHYS * 128, 288])
        sgdn_d = DI("state_gdn", [4, 8, 64, 64]); sconv_d = DI("state_conv", [4, 3 * 1536])
        pt_d = DI("pt", [4, NPG], I32)
        W = {n: DI(n, s) for n, s in W_SHAPES.items()}
        C = {n: DI(n, s) for n, s in CONST_SHAPES.items()}
        C["cos_p"] = DI("cos_p", [S, 16]); C["sin_p"] = DI("sin_p", [S, 16])
        y_d = DO("y", [S, 1024]); ys_d = DO("ys", [4, 1024])
        ockv_d = DO("o_ckv", [S, 256]); okr_d = DO("o_kr", [S, 32])
        ogdn_d = DO("o_gdn", [8, 64, 64]); oconv_d = DO("o_conv", [3, 1536])
        ockvs_d = DO("o_ckv_s", [4, 256]); okrs_d = DO("o_kr_s", [4, 32])
        ogdns_d = DO("o_gdn_s", [4, 8, 64, 64]); oconvs_d = DO("o_conv_s", [4, 3 * 1536])
        omix_d = k.dram("omix_scr", [128, 8, S], BF16, "ExternalOutput" if DBG == 99 else "Internal")
        omixs_d = k.dram("omixs_scr", [128, 8, 4], BF16, "Internal")

        banks = [k.ps(f"ps{i}", [128, 512], F32) for i in range(8)]
        psb = Rot(banks[:7])
        accbank = banks[7]
        evi = [0]

        def evac(out, in_, scale=None):
            evi[0] += 1
            if scale is not None:
                return k.act(out, in_, AF.Copy, scale=scale)
            if evi[0] % 3:
                return k.act(out, in_, AF.Copy)
            return k.copy("dve", out, in_)

        def cload(name, shape, dt=F32, src=None, q="sp"):
            t = k.sb("c_" + name, shape, dt)
            k.dma(q, t.v, (src if src is not None else C[name].v))
            return t

        identf = cload("ident", [128, 128])
        identb = k.sb("identb", [128, 128], BF16); k.copy("dve", identb.v, identf.v)
        identr = k.sb("identr", [128, 128], F32R); k.copy("dve", identr.v, identf.v)
        tri = cload("tri", [128, 128]); lastsel = cload("lastsel", [128, 128])
        negs = cload("negs", [128, 128]); negt = cload("negt", [128, 128])
        headblk = cload("headblk", [128, 128])
        headblk_b = k.sb("headblk_b", [128, 128], BF16); k.copy("dve", headblk_b.v, headblk.v)
        sel8 = cload("sel8", [8, 8, 128]); selpair = cload("selpair", [8, 4, 128])
        cmaskf = cload("cmask", [128, 128])
        cmask = k.sb("cmaskb", [128, 128], BF16); k.copy("dve", cmask.v, cmaskf.v)
        diagmask = cload("diagmask", [8, 512]); onehot4 = cload("onehot4", [4, 4, 128])
        onesb = k.sb("onesb", [128, 64], BF16); k.memset("dve", onesb.v, 1.0)
        onesf = k.sb("onesf", [128, 8], F32); k.memset("dve", onesf.v, 1.0)

        def bload(name, F, q="sp"):
            t = k.sb("b_" + name, [128, F], F32)
            k.dma(q, t.v, W[name].v.bto([128, F]))
            return t

        g_attn = bload("g_attn", 1024); g_q_a = bload("g_q_a", 384); g_kv_a = bload("g_kv_a", 256)
        g_q_nope = bload("g_q_nope", 64); g_q_rope = bload("g_q_rope", 32)
        g_k_rope = bload("g_k_rope", 32); g_k_nope = bload("g_k_nope", 64)
        g_gdn_out = bload("g_gdn_out", 64)
        a_log = bload("gdn_a_log", 8); dtb = bload("gdn_dt_bias", 8)
        eA = k.sb("eA", [128, 8], F32); k.act(eA.v, a_log.v, AF.Exp)
        k.ts("dve", g_q_nope.v, g_q_nope.v, ATTN_SCALE, ALU.mult)
        k.ts("dve", g_q_rope.v, g_q_rope.v, ATTN_SCALE, ALU.mult)
        wcv = k.sb("wcv", [128, 12, 4], F32)
        for j in range(4):
            k.dma("sp", wcv[:, :, j], W["w_conv"].v[j:j + 1, :].re("o (c p) -> p (o c)", p=128))

        def wload(name, K, N, q="pool"):
            kc = K // 128
            t = k.sb("w_" + name, [128, kc, N], BF16)
            src = W[name].v.re("(k p) n -> p k n", p=128)
            for i in range(kc):
                for n0 in range(0, N, 1024):
                    n1 = min(N, n0 + 1024)
                    k.dma(q, t[:, i, n0:n1], src[:, i, n0:n1])
            return t

        sm = Rot([k.sb(f"sm{i}", [128, 8], F32) for i in range(24)])

        def rstd_from_ss(ss, n, F, rows):
            a = sm()
            k.ts("dve", a[rows, 0:n], ss, 1.0 / F, ALU.mult, EPS, ALU.add)
            k.act(a[rows, 0:n], a[rows, 0:n], AF.Ln)
            r = sm()
            k.act(r[rows, 0:n], a[rows, 0:n], AF.Exp, scale=-0.5)
            return r[rows, 0:n]

        junk = k.sb("junk", [128, 1024], BF16)

        def rmsnorm_rows(out, x, g, F, rows):
            ss = sm()
            k.act(junk[rows, 0:F], x, AF.Square, accum=ss[rows, 0:1])
            r = rstd_from_ss(ss[rows, 0:1], 1, F, rows)
            k.stt(out, x, r, g, ALU.mult, ALU.mult)

        def transpose_bf(dst_fn, src, nt, nch, width=128):
            bank = psb()
            pb = bank.v.bc(BF16)
            for j in range(nch):
                k.tr(pb[0:width, j * 128:j * 128 + nt], src[:, j * width:(j + 1) * width], identb[0:nt, 0:nt])
            return pb

        from types import SimpleNamespace as NS
        GROUPS_ALL = [(0, 512), (512, 1024), (1024, 1536), (1536, 1552), (1552, 2064), (2064, 2448), (2448, 2736)]
        GROUPS_A = GROUPS_ALL[:5]
        GROUPS_B = GROUPS_ALL[5:]
        if stop_after is None:
            w_o = wload("w_o", 1024, 1024)
            w_pg = wload("w_ple_gate", 1024, 1024)
            w_pp = wload("w_ple_proj", 256, 1024)
        esP1 = contextlib.ExitStack()
        esP1.__enter__()
        k.es_outer = k.es
        k.es = esP1
        x_blk = k.sb("x_blk", [128, 1024], F32)
        xn_t = k.sb("xn", [128, 1024], BF16)
        xnT = k.sb("xnT", [128, 8, 128], BF16)
        z_tok = k.sb("z_tok", [128, IN_DIM], F32)
        sq_t = k.sb("sq_t", [128, 512], F32)

        def inproj(x_src_v, nt, w_in, c_off, groups):
            rows = slice(0, nt)
            k.dma("sp", x_blk[rows, :], x_src_v)
            rmsnorm_rows(xn_t[rows, :], x_blk[rows, :], g_attn[rows, :], 1024, rows)
            pb = transpose_bf(None, xn_t[rows, :], nt, 8)
            evac(xnT[:, :, 0:nt], pb[:, 0:1024].re("p (j t) -> p j t", j=8)[:, :, 0:nt])
            for (c0, c1) in groups:
                bank = psb()
                for kk in range(8):
                    k.mm(bank[rows, 0:c1 - c0], xnT[:, kk, 0:nt], w_in[:, kk, c0 - c_off:c1 - c_off], start=(kk == 0), stop=(kk == 7))
                evac(z_tok[rows, c0:c1], bank[rows, 0:c1 - c0])

        def wload_cols(name, K, c0, c1, q="pool"):
            kc = K // 128
            t = k.sb("w_" + name + f"_{c0}", [128, kc, c1 - c0], BF16)
            src = W[name].v.re("(k p) n -> p k n", p=128)
            for i in range(kc):
                for n0 in range(c0, c1, 1024):
                    n1 = min(c1, n0 + 1024)
                    k.dma(q, t[:, i, n0 - c0:n1 - c0], src[:, i, n0:n1])
            return t

        def head_rms(out, xin, g, nt, width):
            rows = slice(0, nt)
            sq = sq_t[rows, 0:8 * width].re("p (h d) -> p h d", h=8)
            k.act(sq, xin, AF.Square)
            ss = sm()
            k.reduce(ss[rows, 0:8], sq)
            r = rstd_from_ss(ss[rows, 0:8], 8, width, rows)
            k.tt("dve", out, xin, r.un(2).bto([nt, 8, width]), ALU.mult)
            k.tt("pool", out, out, g.un(1).bto([nt, 8, width]), ALU.mult)

        def alloc_mla(w_qb, w_kvb):
            M = NS()
            M.w_qb = w_qb; M.w_kvb = w_kvb
            M.qa_n = k.sb("qa_n", [128, 384], BF16)
            M.qanT = k.sb("qanT", [128, 3, 128], BF16)
            M.qkv_tok = k.sb("qkv_tok", [128, 1024], F32)
            M.hh = k.sb("hh", [128, 8, 96], F32)
            M.hh_bf = k.sb("hh_bf", [128, 8, 96], BF16)
            M.rtmp = k.sb("rtmp", [128, 8, 32], F32)
            M.rtmp2 = k.sb("rtmp2", [128, 8, 16], F32)
            M.cos_t = k.sb("cos_t", [128, 16], F32); M.sin_t = k.sb("sin_t", [128, 16], F32)
            M.ckv_f = k.sb("ckv_f", [128, 256], F32)
            M.ckv_bf = k.sb("ckv_bf", [128, 260], BF16)
            k.memset("dve", M.ckv_bf[:, 256:257], 1.0)
            M.ckvT = k.sb("ckvT", [128, 2, 128], BF16)
            M.kr_f = k.sb("kr_f", [128, 32], F32)
            M.kr_n = k.sb("kr_n", [128, 32], F32)
            return M

        def rope(M, out, xin, nt, nh):
            rows = slice(0, nt)
            cb = M.cos_t[rows, :].un(1).bto([nt, nh, 16]); sb_ = M.sin_t[rows, :].un(1).bto([nt, nh, 16])
            x1 = xin[:, :, 0:16]; x2 = xin[:, :, 16:32]
            t2 = M.rtmp2[rows, 0:nh, :]
            k.tt("dve", out[:, :, 0:16], x1, cb, ALU.mult)
            k.tt("dve", t2, x2, sb_, ALU.mult)
            k.tt("dve", out[:, :, 0:16], out[:, :, 0:16], t2, ALU.subtract)
            k.tt("dve", out[:, :, 16:32], x1, sb_, ALU.mult)
            k.tt("dve", t2, x2, cb, ALU.mult)
            k.tt("dve", out[:, :, 16:32], out[:, :, 16:32], t2, ALU.add)

        def mla_q(M, nt, cos_src, sin_src):
            rows = slice(0, nt)
            k.dma("sp", M.cos_t[rows, :], cos_src); k.dma("sp", M.sin_t[rows, :], sin_src)
            rmsnorm_rows(M.qa_n[rows, :], z_tok[rows, C_QA:C_QA + 384], g_q_a[rows, :], 384, rows)
            pb = transpose_bf(None, M.qa_n[rows, :], nt, 3)
            evac(M.qanT[:, :, 0:nt], pb[:, 0:384].re("p (j t) -> p j t", j=3)[:, :, 0:nt])
            for (c0, c1) in ((0, 512), (512, 768)):
                bank = psb()
                for kk in range(3):
                    k.mm(bank[rows, 0:c1 - c0], M.qanT[:, kk, 0:nt], M.w_qb[:, kk, c0:c1], start=(kk == 0), stop=(kk == 2))
                evac(M.qkv_tok[rows, c0:c1], bank[rows, 0:c1 - c0])
            q3 = M.qkv_tok[rows, 0:768].re("p (h d) -> p h d", h=8)
            head_rms(M.hh[rows, :, 0:64], q3[:, :, 0:64], g_q_nope[rows, :], nt, 64)
            head_rms(M.rtmp[rows, :, :], q3[:, :, 64:96], g_q_rope[rows, :], nt, 32)
            rope(M, M.hh[rows, :, 64:96], M.rtmp[rows, :, :], nt, 8)

        def mla_kv(M, nt, ockv_v, okr_v):
            rows = slice(0, nt)
            rmsnorm_rows(M.ckv_f[rows, :], z_tok[rows, C_KVA:C_KVA + 256], g_kv_a[rows, :], 256, rows)
            k.dma("sp", ockv_v, M.ckv_f[rows, :])
            k.copy("pool", M.ckv_bf[rows, 0:256], M.ckv_f[rows, :])
            rmsnorm_rows(M.kr_n[rows, :], z_tok[rows, C_KVA + 256:C_KVA + 288], g_k_rope[rows, :], 32, rows)
            rope(M, M.kr_f[rows, :].un(1), M.kr_n[rows, :].un(1), nt, 1)
            k.dma("sp", okr_v, M.kr_f[rows, :])
            pb = transpose_bf(None, M.ckv_bf[rows, 0:256], nt, 2)
            evac(M.ckvT[:, :, 0:nt], pb[:, 0:256].re("p (j t) -> p j t", j=2)[:, :, 0:nt])

        def alloc_gdn(small=False):
            G = NS()
            G.cv = k.sb("cv", [128, 12, 128], F32)
            G.sqt = k.sb("sqt", [128, 4, 128], BF16)
            G.rst = k.sb("rst", [128, 4, 128], F32)
            G.qTg = k.sb("qTg", [128, 4, 128], F32)
            G.kTg = k.sb("kTg", [128, 4, 128], F32)
            G.tmp4 = k.sb("tmp4", [128, 4, 128], F32)
            G.sz_t = k.sb("sz_t", [128, 512], F32)
            if not small:
                G.o_tok = k.sb("o_tok", [128, 512], F32)
                G.on_t = k.sb("on_t", [128, 512], F32)
                G.og_bf = k.sb("og_bf", [128, 512], BF16)
            return G

        def gdn_scalars(nt):
            rows = slice(0, nt)
            ta = sm(); k.tt("dve", ta[rows, :], z_tok[rows, C_AB:C_AB + 8], dtb[rows, :], ALU.add)
            e = sm(); k.act(e[rows, :], ta[rows, :], AF.Exp)
            sp_ = sm(); k.act(sp_[rows, :], e[rows, :], AF.Ln, bias=1.0)
            g_tok = sm(); k.stt(g_tok[rows, :], sp_[rows, :], -1.0, eA[rows, :], ALU.mult, ALU.mult)
            beta = sm(); k.act(beta[rows, :], z_tok[rows, C_AB + 8:C_AB + 16], AF.Sigmoid)
            return g_tok, beta

        def l2norm_fm(G, nt):
            for half in range(2):
                k.act(G.sqt[:, :, 0:nt], G.cv[:, half * 4:half * 4 + 4, 0:nt], AF.Square)
                bank = psb()
                for c in range(4):
                    k.mm(bank[:, c * 128:c * 128 + nt], headblk_b.v, G.sqt[:, c, 0:nt])
                k.ts("dve", G.tmp4[:, :, 0:nt], bank.v.re("p (c t) -> p c t", c=4)[:, :, 0:nt], EPS, ALU.add)
                k.act(G.tmp4[:, :, 0:nt], G.tmp4[:, :, 0:nt], AF.Ln)
                k.act(G.rst[:, :, 0:nt], G.tmp4[:, :, 0:nt], AF.Exp, scale=-0.5)
                if half == 0:
                    k.stt(G.qTg[:, :, 0:nt], G.cv[:, 0:4, 0:nt], DK ** -0.5, G.rst[:, :, 0:nt], ALU.mult, ALU.mult)
                else:
                    k.tt("dve", G.kTg[:, :, 0:nt], G.cv[:, 4:8, 0:nt], G.rst[:, :, 0:nt], ALU.mult)

        def gdn_out_tok(G, nt):
            rows = slice(0, nt)
            o3 = G.o_tok[rows, :].re("p (h d) -> p h d", h=8)
            on3 = G.on_t[rows, :].re("p (h d) -> p h d", h=8)
            head_rms(on3, o3, g_gdn_out[rows, :], nt, 64)
            k.tt("dve", G.og_bf[rows, :], G.on_t[rows, :], G.sz_t[rows, :], ALU.mult)

        def sample_scope():
            w_in = wload_cols("w_in", 1024, 0, IN_DIM)
            w_qb = wload_cols("w_q_b", 384, 0, 768)
            w_kvb = wload_cols("w_kv_b", 256, 0, 1024)
            M = alloc_mla(w_qb, w_kvb)
            G = alloc_gdn(small=True)
            nt = 4; rows = slice(0, 4)
            inproj(xs_d.v, 4, w_in, 0, GROUPS_ALL)
            if DBG == 1:
                return
            oms = k.sb("oms", [128, 8, 4], BF16)
            esg = contextlib.ExitStack()
            with esg:
                k.es = esg
                st_tok = k.sb("st_tok", [12, 1536], F32)
                k.dma("sp", st_tok.v, sconv_d.v.re("b (j c) -> (b j) c", j=3))
                k.dma("sp", oconvs_d.v.re("b (j c) -> b j c", j=3)[:, 0:2, :], sconv_d.v.re("b (j c) -> b j c", j=3)[:, 1:3, :])
                k.dma("sp", oconvs_d.v.re("b (j c) -> b j c", j=3)[:, 2, :], z_tok[rows, 0:1536])
                ext = k.sb("ext_fm", [128, 12, 4, 4], F32)
                bank = psb()
                for c in range(12):
                    k.tr(bank[:, c * 12:(c + 1) * 12], st_tok[:, c * 128:(c + 1) * 128], identf[0:12, 0:12])
                evac(ext[:, :, :, 0:3], bank[:, 0:144].re("p (c b j) -> p c b j", c=12, b=4))
                bank = psb()
                for c in range(12):
                    k.tr(bank[:, c * 4:(c + 1) * 4], z_tok[rows, c * 128:(c + 1) * 128], identf[0:4, 0:4])
                evac(ext[:, :, :, 3], bank[:, 0:48].re("p (c b) -> p c b", c=12))
                k.tt("dve", ext.v, ext.v, wcv.v.un(2).bto([128, 12, 4, 4]), ALU.mult)
                cpre = k.sb("cpre", [128, 12, 4], F32)
                k.reduce(cpre.v, ext.v)
                k.act(G.cv[:, :, 0:4], cpre.v, AF.Silu)
                l2norm_fm(G, 4)
                g_tok, beta = gdn_scalars(4)
                eg = sm(); k.act(eg[rows, :], g_tok[rows, :], AF.Exp)
                sm_b = Rot([k.sb(f"smb{i}", [128, 4, 4], F32) for i in range(3)])

                def bc_bh(src):
                    bank = psb()
                    for b in range(4):
                        k.mm(bank[:, b * 8:(b + 1) * 8], onehot4[:, b, :], src)
                    o = sm_b()
                    for hp in range(2):
                        RH = slice(hp * 64, hp * 64 + 64)
                        evac(o[RH, :, :], bank[RH, 0:32].re("p (b pr hp) -> p b pr hp", b=4, pr=4)[:, :, :, hp])
                    return o
                eg_b = bc_bh(eg[rows, :]); beta_b = bc_bh(beta[rows, :])
                st = k.sb("st_s", [128, 4, 4, 64], F32)
                for b in range(4):
                    for hp in range(2):
                        k.dma("sp", st[hp * 64:(hp + 1) * 64, b, :, :],
                              sgdn_d.v[b].re("(pr hp) k v -> hp k pr v", hp=2)[hp])
                B4 = lambda t: t.v.un(3).bto([128, 4, 4, 64])
                k.tt("dve", st.v, st.v, B4(eg_b), ALU.mult)
                tmp = k.sb("tmp_s", [128, 4, 4, 64], F32)
                kcol = G.kTg[:, :, 0:4].re("p pr b -> p b pr")
                k.tt("dve", tmp.v, st.v, kcol.un(3).bto([128, 4, 4, 64]), ALU.mult)
                kSB = k.sb("kSB_s", [128, 4, 4, 64], F32)
                for hf in range(2):
                    bank = psb()
                    k.mm(bank.v, headblk.v, tmp[:, hf * 2:hf * 2 + 2, :, :].re("p b pr v -> p (b pr v)"))
                    evac(kSB[:, hf * 2:hf * 2 + 2, :, :].re("p b pr v -> p (b pr v)"), bank.v)
                v_tk = k.sb("v_tk", [4, 512], F32)
                bank = psb()
                for c in range(4):
                    k.tr(bank[0:4, c * 128:(c + 1) * 128], G.cv[:, 8 + c, 0:4], identf.v)
                evac(v_tk.v, bank[0:4, :])
                vB = k.sb("vB_s", [128, 4, 4, 64], F32)
                for b in range(4):
                    bank = psb()
                    k.mm(bank.v, onehot4[:, b, :], v_tk.v)
                    for hp in range(2):
                        RH = slice(hp * 64, hp * 64 + 64)
                        evac(vB[RH, b, :, :], bank[RH, :].re("p (pr hp v) -> p pr hp v", pr=4, hp=2)[:, :, hp, :])
                k.tt("dve", vB.v, vB.v, kSB.v, ALU.subtract)
                k.tt("dve", vB.v, vB.v, B4(beta_b), ALU.mult)
                k.tt("dve", tmp.v, vB.v, kcol.un(3).bto([128, 4, 4, 64]), ALU.mult)
                k.tt("dve", st.v, st.v, tmp.v, ALU.add)
                for b in range(4):
                    for hp in range(2):
                        k.dma("sp", ogdns_d.v[b].re("(pr hp) k v -> hp k pr v", hp=2)[hp], st[hp * 64:(hp + 1) * 64, b, :, :])
                oT = k.sb("oT_s", [128, 16], F32)
                for hp in range(2):
                    bank = psb()
                    RH = slice(hp * 64, hp * 64 + 64)
                    for b in range(4):
                        for pr in range(4):
                            k.mm(bank[RH, pr * 4 + b:pr * 4 + b + 1], st[RH, b, pr, :], G.qTg[RH, pr, b:b + 1])
                    evac(oT[RH, :], bank[RH, 0:16])
                osq = k.sb("osq_s", [128, 16], F32)
                k.act(osq.v, oT.v, AF.Square)
                bank = psb()
                k.mm(bank[:, 0:16], headblk.v, osq.v)
                a_ = k.sb("a_s_", [128, 16], F32); r_ = k.sb("r_s_", [128, 16], F32)
                k.ts("dve", a_.v, bank[:, 0:16], 1.0 / 64, ALU.mult, EPS, ALU.add)
                k.act(a_.v, a_.v, AF.Ln)
                k.act(r_.v, a_.v, AF.Exp, scale=-0.5)
                k.tt("dve", oT.v, oT.v, r_.v, ALU.mult)
                ggo_col = k.sb("ggo_col", [128, 1], F32)
                for hp in range(2):
                    k.dma("sp", ggo_col[hp * 64:(hp + 1) * 64, :], W["g_gdn_out"].v.re("o d -> d o"))
                k.ts("dve", oT.v, oT.v, ggo_col[:, 0:1], ALU.mult)
                k.act(G.sz_t[rows, :], z_tok[rows, C_Z:C_Z + 512], AF.Silu)
                bank = psb()
                for c in range(4):
                    k.tr(bank[:, c * 4:c * 4 + 4], G.sz_t[rows, c * 128:(c + 1) * 128], identf[0:4, 0:4])
                k.tt("dve", oms[:, 0:4, :], oT.v.re("p (pr b) -> p pr b", pr=4), bank[:, 0:16].re("p (pr b) -> p pr b", pr=4), ALU.mult)
                k.barrier()
            k.es = es1_cur[0]
            mla_q(M, 4, C["cos_s"].v, C["sin_s"].v)
            mla_kv(M, 4, ockvs_d.v, okrs_d.v)
            krb = k.sb("krb_s", [4, 32], BF16)
            k.copy("dve", krb.v, M.kr_f[0:4, :])
            krT_new = k.sb("krT_new", [32, 4], BF16)
            bank = psb(); pb = bank.v.bc(BF16)
            k.tr(pb[0:32, 0:4], krb.v, identb[0:4, 0:4])
            evac(krT_new.v, pb[0:32, 0:4])
            if DBG == 7:
                return
            WkT = k.sb("WkT", [64, 8, 256], BF16)
            wk4 = w_kvb.v.re("p k (h d) -> p k h d", h=8)
            for kk in range(2):
                bank = psb(); pb = bank.v.bc(BF16)
                for h in range(8):
                    k.tr(pb[0:64, h * 128:(h + 1) * 128], wk4[:, kk, h, 0:64], identb.v)
                evac(WkT[:, :, kk * 128:(kk + 1) * 128], pb[0:64, 0:1024].re("p (h t) -> p h t", h=8))
            qg = k.sb("qg_s", [4, 8, 64], BF16)
            k.tt("dve", qg.v, M.hh[rows, :, 0:64], g_k_nope[rows, :].un(1).bto([4, 8, 64]), ALU.mult)
            qr = k.sb("qr_s", [4, 8, 32], BF16)
            k.copy("dve", qr.v, M.hh[rows, :, 64:96])
            bank = psb(); pb = bank.v.bc(BF16)
            for h in range(8):
                k.tr(pb[0:64, h * 4:h * 4 + 4], qg[:, h, :], identb[0:4, 0:4])
                k.tr(pb[0:32, 64 + h * 4:64 + h * 4 + 4], qr[:, h, :], identb[0:4, 0:4])
            qgT = k.sb("qgT_s", [64, 8, 4], BF16); qrT = k.sb("qrT_s", [32, 8, 4], BF16)
            evac(qgT.v, pb[0:64, 0:32].re("p (h b) -> p h b", h=8))
            evac(qrT.v, pb[0:32, 64:96].re("p (h b) -> p h b", h=8))
            bank = psb()
            for kk in range(2):
                for h in range(8):
                    k.mm(bank[:, kk * 32 + h * 4:kk * 32 + h * 4 + 4], WkT[:, h, kk * 128:(kk + 1) * 128], qgT[:, h, :])
            qpT = k.sb("qpT_s", [128, 2, 4, 8], BF16)
            evac(qpT.v, bank[:, 0:64].re("p (k h b) -> p k b h", k=2, h=8))
            if DBG == 8:
                return
            pti = k.sb("pti", [128, 4 * NPG], I32)
            k.dma("sp", pti.v, pt_d.v.re("(o b) j -> o (b j)", o=1).bto([128, 4 * NPG]))
            ptf = k.sb("ptf", [128, 4 * NPG], F32)
            k.copy("dve", ptf.v, pti.v)
            iot = k.sb("iot", [128, 1], F32)
            k.op("pool", lambda g: g.iota(iot.v.ap, pattern=[[0, 1]], base=0, channel_multiplier=1,
                                          allow_small_or_imprecise_dtypes=True), [], [iot.v])
            k.ts("dve", ptf.v, ptf.v, 128.0, ALU.mult, iot[:, 0:1], ALU.add)
            idx = pti
            k.copy("dve", idx.v, ptf.v)
            G_ = 4
            pg_r = Rot([k.sb(f"pg{i}", [128, 292], BF16) for i in range(2 * G_ + 2)])
            for t_ in pg_r.items:
                k.memset("dve", t_[:, 288:289], 1.0)
            sq_r = Rot([k.sb(f"sqp{i}", [128, G_, 512], BF16) for i in range(2)])
            sq1 = sq_r.items[0][:, 0, :]
            p_r = Rot([k.sb(f"pp{i}", [128, G_ * 8], BF16) for i in range(3)])
            sg_r = Rot([k.sb(f"sgp{i}", [128, G_ * 8], F32) for i in range(6)])
            Wkc = k.sb("Wkc", [128, 2, 512], BF16); Wvc = k.sb("Wvc", [128, 2, 512], BF16)
            for kk in range(2):
                k.copy("dve", Wkc[:, kk, :].re("p (h d) -> p h d", h=8), wk4[:, kk, :, 0:64])
                k.copy("dve", Wvc[:, kk, :].re("p (h d) -> p h d", h=8), wk4[:, kk, :, 64:128])
            wk_rhs = lambda kk: Wkc[:, kk, :]
            wv_rhs = lambda kk: Wvc[:, kk, :]
            acc_sb = k.sb("acc_sb", [8, 257], F32)
            accn = k.sb("accn", [8, 256], BF16)
            accT = k.sb("accT", [128, 2, 8], BF16)
            om_f = k.sb("om_f", [8, 512], F32)
            trb = Rot([banks[0]]); bankA = banks[1:5]; bB = banks[5]
            qr_f = k.sb("qr_f", [4, 256], F32)
            k.copy("dve", qr_f.v.re("p (h d) -> p h d", h=8), M.hh[rows, :, 64:96])
            qrB = k.sb("qrB", [128, 4, 256], BF16)
            for b in range(4):
                bank = trb()
                k.mm(bank[:, 0:256], onehot4[:, b, :], qr_f.v)
                evac(qrB[:, b, :], bank[:, 0:256])
            rp_r = Rot([k.sb(f"rp{i}", [128, G_, 256], BF16) for i in range(2)])

            def newtok(b):
                rws = slice(0, 4)
                bA = bankA[0]
                for kk in range(2):
                    k.mm(bA[rws, 0:512], M.ckvT[:, kk, 0:4], wk_rhs(kk), start=(kk == 0), stop=(kk == 1))
                for kk in range(2):
                    k.mm(bB[rws, 0:8], M.ckvT[:, kk, 0:4], qpT[:, kk, b, :], start=(kk == 0), stop=(kk == 1))
                k.mm(bB[rws, 8:16], krT_new[:, 0:4], qrT[:, :, b])
                k.act(sq1[rws, :], bA[rws, :], AF.Square)
                ss = sm()
                k.reduce(ss[rws, :], sq1[rws, :].re("p (h d) -> p h d", h=8))
                r = rstd_from_ss(ss[rws, :], 8, 64, rws)
                s1 = sm()
                k.tt("dve", s1[rws, :], bB[rws, 0:8], r, ALU.mult)
                k.tt("dve", s1[rws, :], s1[rws, :], bB[rws, 8:16], ALU.add)
                s2 = sm()
                k.act(s2[rws, :], s1[rws, :], AF.Exp)
                pp = p_r()
                k.ts("dve", pp[rws, 0:8], s2[rws, :], identf[0:4, b:b + 1], ALU.mult)
                return pp

            trb2 = Rot([banks[0], banks[6]])
            cT4_r = Rot([k.sb(f"cT4_{i}", [128, 4, 2, 128], BF16) for i in range(2)])

            def frontA(b, j0, g):
                pgs = []
                tb = trb2(); pb = tb.v.bc(BF16)
                for i in range(g):
                    pg = pg_r(); pgs.append(pg)
                    col = b * NPG + j0 + i
                    k.gather(pg[:, 0:288], ccat_d.v, idx[:, col:col + 1])
                for i in range(g):
                    k.tr(pb[:, i * 256:i * 256 + 128], pgs[i][:, 32:160], identb.v)
                    k.tr(pb[:, i * 256 + 128:i * 256 + 256], pgs[i][:, 160:288], identb.v)
                cT4 = cT4_r()
                k.act(cT4[:, 0:g, :, :], pb[:, 0:g * 256].re("p (g j t) -> p g j t", g=g, j=2), AF.Copy)
                return pgs, cT4

            def frontB(b, g, pgs, cT4):
                sq = sq_r(); rp = rp_r()
                for i in range(g):
                    for kk in range(2):
                        k.mm(bankA[i][:, 0:512], cT4[:, i, kk, :], wk_rhs(kk), start=(kk == 0), stop=(kk == 1))
                    for kk in range(2):
                        k.mm(bB[:, i * 8:i * 8 + 8], cT4[:, i, kk, :], qpT[:, kk, b, :], start=(kk == 0), stop=(kk == 1))
                    k.act(sq[:, i, :], bankA[i].v, AF.Square)
                    for _d in range(NDUMMY):
                        k.op("pe", lambda e: e.matmul(accbank[64:128, 0:512].ap, lhsT=Wkc[:, 0, 0:64].ap, rhs=Wkc[:, 1, :].ap,
                                                      start=True, stop=True), [], [])
                    k.tt("dve", rp[:, i, :].re("p (h d) -> p h d", h=8), pgs[i][:, 0:32].un(1).bto([128, 8, 32]),
                         qrB[:, b, :].re("p (h d) -> p h d", h=8), ALU.mult)
                return sq, rp

            def small(g, sq, rp):
                n8 = g * 8
                sr = sg_r()
                k.reduce(sr[:, 0:n8], rp[:, 0:g, :].re("p g (h d) -> p (g h) d", h=8))
                ss = sg_r()
                k.reduce(ss[:, 0:n8], sq[:, 0:g, :].re("p g (h d) -> p (g h) d", h=8))
                a = sg_r()
                k.ts("dve", a[:, 0:n8], ss[:, 0:n8], 1.0 / 64, ALU.mult, EPS, ALU.add)
                k.act(a[:, 0:n8], a[:, 0:n8], AF.Ln)
                r = sg_r()
                k.act(r[:, 0:n8], a[:, 0:n8], AF.Exp, scale=-0.5)
                s1 = sg_r()
                k.tt("dve", s1[:, 0:n8], bB[:, 0:n8], r[:, 0:n8], ALU.mult)
                k.tt("dve", s1[:, 0:n8], s1[:, 0:n8], sr[:, 0:n8], ALU.add)
                pp = p_r()
                k.act(pp[:, 0:n8], s1[:, 0:n8], AF.Exp)
                return pp

            def accm(j0, g, pgs, pp):
                for i in range(g):
                    k.mm(accbank[0:8, 0:257], pp[:, i * 8:(i + 1) * 8], pgs[i][:, 32:289], start=False,
                         stop=(j0 + i == NPG - 1))

            for b in range(4):
                pp = newtok(b)
                k.mm(accbank[0:8, 0:257], pp[0:4, 0:8], M.ckv_bf[0:4, 0:257], start=True, stop=(NPG == 0))
                groups = [(j0, min(G_, NPG - j0)) for j0 in range(0, NPG, G_)]
                if groups:
                    pgs, cT4 = frontA(b, *groups[0])
                    sq, rp = frontB(b, groups[0][1], pgs, cT4)
                    ppg = small(groups[0][1], sq, rp)
                for gi, (j0, g) in enumerate(groups):
                    cur = (pgs, ppg)
                    if gi + 1 < len(groups):
                        pgs, cT4 = frontA(b, *groups[gi + 1])
                    accm(j0, g, *cur)
                    if gi + 1 < len(groups):
                        sq, rp = frontB(b, groups[gi + 1][1], pgs, cT4)
                        ppg = small(groups[gi + 1][1], sq, rp)
                evac(acc_sb.v, accbank[0:8, 0:257])
                rl = sm(); k.recip(rl[0:8, 0:1], acc_sb[:, 256:257])
                k.ts("dve", accn.v, acc_sb[:, 0:256], rl[0:8, 0:1], ALU.mult)
                bank = trb(); pb = bank.v.bc(BF16)
                for kk in range(2):
                    k.tr(pb[:, kk * 8:kk * 8 + 8], accn[:, kk * 128:(kk + 1) * 128], identb[0:8, 0:8])
                evac(accT.v, pb[:, 0:16].re("p (k h) -> p k h", k=2))
                bank = trb()
                for kk in range(2):
                    k.mm(bank[0:8, :], accT[:, kk, :], wv_rhs(kk), start=(kk == 0), stop=(kk == 1))
                k.tt("dve", om_f.v, bank[0:8, :], diagmask.v, ALU.mult)
                bank2 = trb()
                for pr in range(4):
                    k.mm(bank2[:, pr:pr + 1], om_f[:, pr * 128:(pr + 1) * 128], onesf[0:8, 0:1])
                evac(oms[:, 4:8, b], bank2[:, 0:4])
            k.dma("sp", omixs_d.v, oms.v)

        def pass_a():
            w_in = wload_cols("w_in", 1024, 0, 2064)
            G = alloc_gdn()
            S2 = k.sb("S2", [128, 4, 128], F32)
            k.memset("pool", S2.v, 0.0)
            zcT = k.sb("zcT", [128, 12, 131], F32)
            k.memset("pool", zcT.v, 0.0)
            omixA = k.sb("omixA", [128, 4, 512], BF16)
            acc_r = Rot([k.sb(f"acc_c{i}", [128, 128], F32) for i in range(2)])
            accp_r = Rot([k.sb(f"acc_p{i}", [128, 128], F32) for i in range(2)])
            tmp_p = k.sb("tmp_p", [128, 128], F32)
            kbT = k.sb("kbT", [128, 4, 128], BF16); nwT = k.sb("nwT", [128, 4, 128], BF16)
            qgT = k.sb("qgT", [128, 4, 128], BF16)
            kT_b = k.sb("kT_b", [128, 4, 128], BF16); qT_b = k.sb("qT_b", [128, 4, 128], BF16)
            S2b = k.sb("S2b", [128, 4, 128], BF16)
            k.memset("pool", S2b.v, 0.0)
            egc_fm = k.sb("egc_fm", [128, 4, 128], F32)
            k_tok = G.on_t
            v_tok = k.sb("v_tok", [128, 512], F32); u_tok = v_tok
            vb_tok = k.sb("vb_tok", [128, 512], BF16)
            kbg_tok = k.sb("kbg_tok", [128, 512], BF16)
            kdec_tok = k.sb("kdec_tok", [128, 512], BF16)
            vnew_tok = k.sb("vnew_tok", [128, 512], BF16)
            gcT8 = k.sb("gcT8", [8, 128], F32)
            betaT8 = k.sb("betaT8", [8, 128], F32)
            qkT_all = k.sb("qkT_all", [128, 8, 128], BF16)
            U_all = k.sb("U_all", [128, 8, 128], BF16)
            g4 = Rot([k.sb(f"g4_{i}", [128, 4, 128], F32) for i in range(3)])
            r4s = [Rot([k.sb(f"r4_{j}_{i}", [128, 4, 128], INV_DT) for i in range(6)]) for j in range(2)]
            t4 = k.sb("t4", [128, 4, 128], F32)
            v4 = lambda b_: b_.v.re("p (c t) -> p c t", c=4)

            def f1_steps(bi):
                steps = []
                t0 = bi * 128

                def s_in():
                    inproj(x_d.v[t0:t0 + 128, :], 128, w_in, 0, GROUPS_A)
                    if bi == NBLK - 1:
                        k.dma("sp", oconv_d.v, z_tok[125:128, 0:1536])
                steps.append(s_in)

                def s_tr(g3):
                    def f():
                        bank = psb()
                        for c in range(4):
                            k.tr(bank[:, c * 128:(c + 1) * 128], z_tok[:, (g3 * 4 + c) * 128:(g3 * 4 + c + 1) * 128], identf.v)
                        evac(zcT[:, g3 * 4:g3 * 4 + 4, 3:131], v4(bank))
                    return f
                for g3 in range(3):
                    steps.append(s_tr(g3))

                def s_cv(c):
                    def f():
                        ac = acc_r()
                        k.ts("dve", ac.v, zcT[:, c, 0:128], wcv[:, c, 0:1], ALU.mult)
                        for j in (1, 2, 3):
                            k.stt(ac.v, zcT[:, c, j:j + 128], wcv[:, c, j:j + 1], ac.v, ALU.mult, ALU.add)
                        k.act(G.cv[:, c, :], ac.v, AF.Silu)
                    return f
                for c in range(12):
                    steps.append(s_cv(c))
                steps.append(lambda: k.copy("pool", zcT[:, :, 0:3], zcT[:, :, 128:131]))
                return steps

            def gdn_block(tl, inj):
                cv = G.cv; kTg = G.kTg; qTg = G.qTg

                def pump(n):
                    for _ in range(n):
                        if inj:
                            inj.pop(0)()
                l2norm_fm(G, 128)
                k.act(G.sz_t.v, z_tok[:, C_Z:C_Z + 512], AF.Silu)
                k.copy("pool", kT_b.v, kTg.v)
                k.copy("pool", qT_b.v, qTg.v)
                bank = psb()
                for c in range(4):
                    k.tr(bank[:, c * 128:(c + 1) * 128], kTg[:, c, :], identf.v)
                evac(k_tok.v, bank.v)
                bank = psb()
                for c in range(4):
                    k.tr(bank[:, c * 128:(c + 1) * 128], cv[:, 8 + c, :], identf.v)
                evac(v_tok.v, bank.v)
                g_tok, beta = gdn_scalars(128)
                bank = psb()
                k.mm(bank[:, 0:8], tri.v, g_tok.v)
                gc = sm(); evac(gc.v, bank[:, 0:8])
                bank = psb()
                k.mm(bank[:, 0:8], lastsel.v, gc.v)
                dd = sm(); k.tt("dve", dd.v, bank[:, 0:8], gc.v, ALU.subtract)
                edec = sm(); k.act(edec.v, dd.v, AF.Exp)
                egc = sm(); k.act(egc.v, gc.v, AF.Exp)
                bge = sm(); k.tt("dve", bge.v, beta.v, egc.v, ALU.mult)
                b3 = lambda t: t.v.un(2).bto([128, 8, 64])
                r3 = lambda t: t.v.re("p (h d) -> p h d", h=8)
                k.tt("dve", r3(vb_tok), r3(v_tok), b3(beta), ALU.mult)
                k.tt("pool", r3(kdec_tok), r3(k_tok), b3(edec), ALU.mult)
                k.tt("pool", r3(kbg_tok), r3(k_tok), b3(bge), ALU.mult)
                bank = psb()
                k.tr(bank[0:8, 0:128], gc.v, identf.v)
                k.tr(bank[0:8, 128:256], beta.v, identf.v)
                evac(gcT8.v, bank[0:8, 0:128]); evac(betaT8.v, bank[0:8, 128:256])
                bank = psb()
                for pr in range(4):
                    k.mm(bank[:, pr * 128:(pr + 1) * 128], selpair[:, pr, :], gcT8.v)
                k.act(egc_fm.v, v4(bank), AF.Exp)
                bank = psb()
                for pr in range(4):
                    k.mm(bank[:, pr * 128:(pr + 1) * 128], selpair[:, pr, :], betaT8.v)
                k.tt("dve", kbT.v, kTg.v, v4(bank), ALU.mult)
                k.tt("dve", qgT.v, qTg.v, egc_fm.v, ALU.mult)
                R = lambda h: slice((h % 2) * 64, (h % 2) * 64 + 64)
                st8 = []
                for hg in range(2):
                    hs = [hg * 4 + i for i in range(4)]
                    r4 = r4s[hg]
                    bcb = psb()
                    for i, h in enumerate(hs):
                        k.mm(bcb[:, i * 128:(i + 1) * 128], sel8[:, h, :], gcT8.v)
                    d1 = g4()
                    k.tt("dve", d1.v, v4(bcb), gc[:, hg * 4:hg * 4 + 4].un(2).bto([128, 4, 128]), ALU.subtract)
                    e1 = g4()
                    k.tt("dve", e1.v, d1.v, negs.v.un(1).bto([128, 4, 128]), ALU.max)
                    Dm = g4()
                    k.act(Dm.v, e1.v, AF.Exp, scale=-1.0)
                    k.tt("dve", d1.v, d1.v, negt.v.un(1).bto([128, 4, 128]), ALU.min)
                    DTm = e1
                    k.act(DTm.v, d1.v, AF.Exp)
                    Bt = r4(); Ct = r4(); St = r4()
                    k.tt("pool", d1.v, DTm.v, identf.v.un(1).bto([128, 4, 128]), ALU.add)
                    for hp_ in range(2):
                        bKB = psb(); bKBT = psb(); bQKT = psb()
                        for which in range(3):
                            for i, h in enumerate(hs):
                                if h % 2 != hp_:
                                    continue
                                pr = h // 2
                                cs_ = slice((i // 2) * 128, (i // 2 + 1) * 128)
                                if which == 0:
                                    k.mm(bKB[:, cs_], kbT[R(h), pr, :], kT_b[R(h), pr, :])
                                elif which == 1:
                                    k.mm(bKBT[:, cs_], kT_b[R(h), pr, :], kbT[R(h), pr, :])
                                else:
                                    k.mm(bQKT[:, cs_], kT_b[R(h), pr, :], qT_b[R(h), pr, :])
                        v2 = lambda b_: b_[:, 0:256].re("p (c t) -> p c t", c=2)
                        k.stt(Bt[:, hp_::2, :], v2(bKB), -1.0, Dm[:, hp_::2, :], ALU.mult, ALU.mult)
                        k.stt(Ct[:, hp_::2, :], v2(bKBT), -1.0, DTm[:, hp_::2, :], ALU.mult, ALU.mult)
                        k.tt("dve", qkT_all[:, hg * 4 + hp_:hg * 4 + 4:2, :], v2(bQKT), d1[:, hp_::2, :], ALU.mult)
                    k.tt("pool", St.v, Ct.v, identf.v.un(1).bto([128, 4, 128]), ALU.add)
                    st8.append([Bt, Ct, St])
                    if hg == 1:
                        pump(1)
                for lvl in range(1, 6):
                    nBs = []
                    for hg in range(2):
                        Bt, Ct, St = st8[hg]
                        r4 = r4s[hg]
                        bB = psb()
                        for i in range(4):
                            k.mm(bB[:, i * 128:(i + 1) * 128], Ct[:, i, :], Bt[:, i, :])
                        nB = r4()
                        k.act(nB.v, v4(bB), AF.Copy)
                        nC = None
                        if lvl < 5:
                            bC = psb()
                            for i in range(4):
                                k.mm(bC[:, i * 128:(i + 1) * 128], Bt[:, i, :], Ct[:, i, :])
                            nC = r4()
                            k.act(nC.v, v4(bC), AF.Copy)
                        nBs.append((nB, nC))
                        pump(1)
                    for hg in range(2):
                        Bt, Ct, St = st8[hg]
                        nB, nC = nBs[hg]
                        r4 = r4s[hg]
                        bS = psb()
                        for i in range(4):
                            k.mm(bS[:, i * 128:(i + 1) * 128], nB[:, i, :], St[:, i, :])
                        if lvl < 5:
                            nS = r4()
                            k.tt("dve", nS.v, v4(bS), St.v, ALU.add)
                            st8[hg] = [nB, nC, nS]
                        else:
                            k.tt("dve", U_all[:, hg * 4:hg * 4 + 4, :], v4(bS), St.v, ALU.add)
                    pump(1)
                ub = psb(); wb = psb()
                for h in range(8):
                    pr = h // 2; Rh = slice((h % 2) * 64, (h % 2) * 64 + 64)
                    k.mm(ub[:, h * 64:(h + 1) * 64], U_all[:, h, :], vb_tok[:, h * 64:(h + 1) * 64])
                    k.mm(wb[Rh, pr * 128:(pr + 1) * 128], kbg_tok[:, h * 64:(h + 1) * 64], U_all[:, h, :])
                evac(u_tok.v, ub.v)
                k.act(nwT.v, v4(wb), AF.Copy, scale=-1.0)
                for ci in range(2):
                    RR = slice(ci * 64, ci * 64 + 64)
                    vbk = psb()
                    for pr in range(4):
                        k.mm(vbk[RR, pr * 128:(pr + 1) * 128], nwT[:, pr, RR], S2b[:, pr, :])
                    k.tt("dve", vnew_tok[RR, :], vbk[RR, :], u_tok[RR, :], ALU.add)
                    obk = psb()
                    for pr in range(4):
                        k.mm(obk[RR, pr * 128:(pr + 1) * 128], qgT[:, pr, RR], S2b[:, pr, :], start=True, stop=False)
                        for hp in range(2):
                            h = 2 * pr + hp
                            k.mm(obk[RR, pr * 128 + hp * 64:pr * 128 + (hp + 1) * 64], qkT_all[RR, h, RR],
                                 vnew_tok[RR, h * 64:(h + 1) * 64], start=False, stop=(hp == 1))
                    k.act(G.o_tok[RR, :], obk[RR, :], AF.Copy)
                    sbk = psb()
                    for pr in range(4):
                        cs = slice(pr * 128, (pr + 1) * 128)
                        k.mm(sbk[:, cs], kdec_tok[RR, cs], vnew_tok[RR, cs])
                    k.tt("dve", t4.v, v4(sbk), headblk.v.un(1).bto([128, 4, 128]), ALU.mult)
                    k.tt("pool", S2.v, S2.v, egc_fm[:, :, ci * 64 + 63:ci * 64 + 64].bto([128, 4, 128]), ALU.mult)
                    k.tt("pool", S2.v, S2.v, t4.v, ALU.add)
                    k.copy("pool", S2b.v, S2.v)
                    pump(1)
                gdn_out_tok(G, 128)
                pb = transpose_bf(None, G.og_bf.v, 128, 4)
                evac(omixA[:, :, tl * 128:(tl + 1) * 128], pb[:, 0:512].re("p (j t) -> p j t", j=4))
                pump(len(inj))

            for st_ in f1_steps(0):
                st_()
            for bi in range(NBLK):
                tl = bi % 4
                gdn_block(tl, f1_steps(bi + 1) if bi + 1 < NBLK else [])
                if tl == 3:
                    t = bi // 4
                    k.dma("sp", omix_d.v[:, 0:4, t * 512:(t + 1) * 512], omixA.v)
            for pr in range(4):
                for hp in range(2):
                    RH = slice(hp * 64, hp * 64 + 64)
                    k.dma("sp", ogdn_d.v[2 * pr + hp], S2[RH, pr, hp * 64:hp * 64 + 64])

        def pass_b():
            psb.items = banks[:5]
            w_in = wload_cols("w_in", 1024, C_QA, IN_DIM)
            w_qb = wload_cols("w_q_b", 384, 0, 768)
            w_kvb = wload_cols("w_kv_b", 256, 0, 1024)
            M = alloc_mla(w_qb, w_kvb)
            kT = k.sb("kT", [128, 8, S], BF16)
            v_sb = k.sb("v_sb", [128, NBLK, 8, 64], BF16)
            qT_tile = k.sb("qT_tile", [128, 8, 512], BF16)
            omixB = k.sb("omixB", [128, 4, 512], BF16)
            pT_r = Rot([k.sb(f"pT{i}", [128, 512], BF16) for i in range(4)])
            rl_t = k.sb("rl_t", [128, 512], F32)

            def mla_block(bi, tl):
                t0 = bi * 128
                mla_q(M, 128, C["cos_p"].v[t0:t0 + 128, :], C["sin_p"].v[t0:t0 + 128, :])
                k.copy("pool", M.hh_bf.v, M.hh.v)
                bank = psb(); pb = bank.v.bc(BF16)
                for h in range(8):
                    k.tr(pb[0:96, h * 128:(h + 1) * 128], M.hh_bf[:, h, :], identb.v)
                evac(qT_tile[0:96, :, tl * 128:(tl + 1) * 128], pb[0:96, 0:1024].re("p (h t) -> p h t", h=8))
                mla_kv(M, 128, ockv_d.v[t0:t0 + 128, :], okr_d.v[t0:t0 + 128, :])
                for (c0, c1) in ((0, 512), (512, 1024)):
                    bank = psb()
                    for kk in range(2):
                        k.mm(bank[:, 0:512], M.ckvT[:, kk, :], w_kvb[:, kk, c0:c1], start=(kk == 0), stop=(kk == 1))
                    evac(M.qkv_tok[:, c0:c1], bank.v)
                kv3 = M.qkv_tok.v.re("p (h d) -> p h d", h=8)
                head_rms(M.hh[:, :, 0:64], kv3[:, :, 0:64], g_k_nope.v, 128, 64)
                k.copy("pool", M.hh[:, :, 64:96], M.kr_f.v.un(1).bto([128, 8, 32]))
                k.copy("pool", M.hh_bf.v, M.hh.v)
                k.copy("dve", v_sb[:, bi, :, :], kv3[:, :, 64:128])
                bank = psb(); pb = bank.v.bc(BF16)
                for h in range(8):
                    k.tr(pb[0:96, h * 128:(h + 1) * 128], M.hh_bf[:, h, :], identb.v)
                evac(kT[0:96, :, t0:t0 + 128], pb[0:96, 0:1024].re("p (h t) -> p h t", h=8))

            def attention_tile(t):
                for pr in range(4):
                    o_ps = banks[5]; l_ps = banks[6]
                    for hp in range(2):
                        h = 2 * pr + hp
                        RH = slice(hp * 64, hp * 64 + 64)
                        nkb = 4 * t + 4
                        def stepA(j):
                            qlo = max(0, j - 4 * t)
                            ncol = (4 - qlo) * 128
                            qc = slice(qlo * 128, 512)
                            sc = psb()
                            k.mm(sc[:, 0:ncol], kT[0:96, h, j * 128:(j + 1) * 128], qT_tile[0:96, h, qc])
                            pT = pT_r()
                            k.act(pT[:, 0:ncol], sc[:, 0:ncol], AF.Exp)
                            if j >= 4 * t:
                                k.tt("pool", pT[:, 0:128], pT[:, 0:128], cmask.v, ALU.mult)
                            return pT, ncol, qc

                        def stepB(j, pT, ncol, qc):
                            k.mm(o_ps[RH, qc], v_sb[:, j, h, :], pT[:, 0:ncol], start=(j == 0), stop=(j == nkb - 1))
                            k.mm(l_ps[RH, qc], onesb.v, pT[:, 0:ncol], start=(j == 0), stop=(j == nkb - 1))

                        pend = stepA(0)
                        for j in range(nkb):
                            cur = pend
                            if j + 1 < nkb:
                                pend = stepA(j + 1)
                            stepB(j, *cur)
                    k.recip(rl_t.v, l_ps.v)
                    k.tt("dve", omixB[:, pr, :], o_ps.v, rl_t.v, ALU.mult)

            for bi in range(NBLK):
                tl = bi % 4
                t0 = bi * 128
                inproj(x_d.v[t0:t0 + 128, :], 128, w_in, C_QA, GROUPS_B)
                mla_block(bi, tl)
                if tl == 3:
                    t = bi // 4
                    attention_tile(t)
                    k.dma("sp", omix_d.v[:, 4:8, t * 512:(t + 1) * 512], omixB.v)

        k.es_saved = esP1
        es1_cur = [None]
        for name_, fn_ in (("sample", sample_scope), ("a", pass_a), ("b", pass_b)):
            es1 = contextlib.ExitStack()
            with es1:
                k.es = es1
                es1_cur[0] = es1
                fn_()
                k.barrier()
            k.es = k.es_saved
            if stop_after == name_:
                break
        psb.items = banks[:7]
        esP1.__exit__(None, None, None)
        k.es = k.es_outer

        if stop_after is None:
            w_dn = wload("w_ffn_down", D_FF, 1024)
            wg_scr = k.dram("wg_scr", [128, 8, D_FF], BF16, "Internal")
            wu_scr = k.dram("wu_scr", [128, 8, D_FF], BF16, "Internal")
            first_stream = [True]
            g_ffn = bload("g_ffn", 1024); g_ple = bload("g_ple", 1024)
            wg_r = Rot([k.sb(f"wg{i}", [128, 8, 512], BF16) for i in range(2)])
            wu_r = Rot([k.sb(f"wu{i}", [128, 8, 512], BF16) for i in range(2)])
            om_t = k.sb("om_t", [128, 8, 512], BF16)
            h1 = k.sb("h1", [128, 4, 1024], F32)
            un = k.sb("un", [128, 1024], BF16)
            uT = k.sb("uT", [128, 8, 512], BF16)
            hT = k.sb("hT", [128, 22, 512], BF16)
            sg = k.sb("sg", [128, 512], F32)
            sg2 = k.sb("sg2", [128, 512], F32)
            sg_rot = Rot([sg, sg2])
            p_t = k.sb("p_t", [128, 256], F32)
            p_bf = k.sb("p_bf", [128, 256], BF16)
            pT2 = k.sb("pT2", [128, 2, 128], BF16)
            gate_t = sg
            wgs = W["w_ffn_gate"].v.re("(k p) n -> p k n", p=128)
            wus = W["w_ffn_up"].v.re("(k p) n -> p k n", p=128)

            def phase2_tile(nt, om_src, x_src, p_src, y_dst):
                nb = (nt + 127) // 128
                bt = min(nt, 128)
                rows = slice(0, bt)
                k.dma("sp", om_t[:, :, 0:nt], om_src)
                for b in range(nb):
                    k.dma("sp", h1[rows, b, :], x_src(b))
                for b in range(nb):
                    cb = slice(b * 128, b * 128 + bt)
                    for hf in range(2):
                        bank = psb()
                        for kk in range(8):
                            k.mm(bank[rows, :], om_t[:, kk, cb], w_o[:, kk, hf * 512:(hf + 1) * 512], start=(kk == 0), stop=(kk == 7))
                        k.tt("dve", h1[rows, b, hf * 512:(hf + 1) * 512], bank[rows, :], h1[rows, b, hf * 512:(hf + 1) * 512], ALU.add)
                    rmsnorm_rows(un[rows, :], h1[rows, b, :], g_ffn[rows, :], 1024, rows)
                    pb = transpose_bf(None, un[rows, :], bt, 8)
                    evac(uT[:, :, cb], pb[:, 0:1024].re("p (j t) -> p j t", j=8)[:, :, 0:bt])
                for c0 in range(0, D_FF, 512):
                    c1 = min(D_FF, c0 + 512)
                    wg = wg_r(); wu = wu_r()
                    if first_stream[0]:
                        k.dma("pool", wg[:, :, 0:c1 - c0], wgs[:, :, c0:c1])
                        k.dma("pool", wu[:, :, 0:c1 - c0], wus[:, :, c0:c1])
                        k.dma("sp", wg_scr.v[:, :, c0:c1], wg[:, :, 0:c1 - c0])
                        k.dma("sp", wu_scr.v[:, :, c0:c1], wu[:, :, 0:c1 - c0])
                    else:
                        k.dma("sp", wg[:, :, 0:c1 - c0], wg_scr.v[:, :, c0:c1])
                        k.dma("sp", wu[:, :, 0:c1 - c0], wu_scr.v[:, :, c0:c1])
                    for m in range(c0 // 128, c1 // 128):
                        ms_ = slice(m * 128 - c0, (m + 1) * 128 - c0)
                        bg = psb(); bu = psb()
                        for kk in range(8):
                            k.mm(bg[:, 0:nt], wg[:, kk, ms_], uT[:, kk, 0:nt], start=(kk == 0), stop=(kk == 7))
                        for kk in range(8):
                            k.mm(bu[:, 0:nt], wu[:, kk, ms_], uT[:, kk, 0:nt], start=(kk == 0), stop=(kk == 7))
                        sgt = sg_rot()
                        k.act(sgt[:, 0:nt], bg[:, 0:nt], AF.Silu)
                        k.tt("dve", hT[:, m, 0:nt], sgt[:, 0:nt], bu[:, 0:nt], ALU.mult)
                for b in range(nb):
                    cb = slice(b * 128, b * 128 + bt)
                    for hf in range(2):
                        bank = psb()
                        for m in range(22):
                            k.mm(bank[rows, :], hT[:, m, cb], w_dn[:, m, hf * 512:(hf + 1) * 512], start=(m == 0), stop=(m == 21))
                        k.tt("dve", h1[rows, b, hf * 512:(hf + 1) * 512], bank[rows, :], h1[rows, b, hf * 512:(hf + 1) * 512], ALU.add)
                    rmsnorm_rows(un[rows, :], h1[rows, b, :], g_ple[rows, :], 1024, rows)
                    pb = transpose_bf(None, un[rows, :], bt, 8)
                    evac(uT[:, :, cb], pb[:, 0:1024].re("p (j t) -> p j t", j=8)[:, :, 0:bt])
                    k.dma("sp", p_t[rows, :], p_src(b))
                    k.copy("pool", p_bf[rows, :], p_t[rows, :])
                    pb = transpose_bf(None, p_bf[rows, :], bt, 2)
                    evac(pT2[:, :, 0:bt], pb[:, 0:256].re("p (j t) -> p j t", j=2)[:, :, 0:bt])
                    for hf in range(2):
                        hs_ = slice(hf * 512, (hf + 1) * 512)
                        bank = psb()
                        for kk in range(8):
                            k.mm(bank[rows, :], uT[:, kk, cb], w_pg[:, kk, hs_], start=(kk == 0), stop=(kk == 7))
                        k.act(gate_t[rows, :], bank[rows, :], AF.Sigmoid)
                        bank2 = psb()
                        for kk in range(2):
                            k.mm(bank2[rows, :], pT2[:, kk, 0:bt], w_pp[:, kk, hs_], start=(kk == 0), stop=(kk == 1))
                        k.tt("dve", gate_t[rows, :], bank2[rows, :], gate_t[rows, :], ALU.mult)
                        k.tt("pool", h1[rows, b, hs_], gate_t[rows, :], h1[rows, b, hs_], ALU.add)
                    k.dma("sp", y_dst(b), h1[rows, b, :])

            phase2_tile(4, omixs_d.v, lambda b: xs_d.v, lambda b: psm_d.v, lambda b: ys_d.v)
            first_stream[0] = False
            for t in range(NT):
                q0 = t * 512
                phase2_tile(512, omix_d.v[:, :, q0:q0 + 512],
                            lambda b: x_d.v[q0 + b * 128:q0 + (b + 1) * 128, :],
                            lambda b: p_d.v[q0 + b * 128:q0 + (b + 1) * 128, :],
                            lambda b: y_d.v[q0 + b * 128:q0 + (b + 1) * 128, :])
        k.finish()
    return nc, k


_NC_CACHE = {}


def run_cores(inputs, S, NPG, NPHYS, stop_after=None, ncores=8):
    key = (S, NPG, NPHYS, stop_after)
    if key not in _NC_CACHE:
        _NC_CACHE[key] = build(S, NPG, NPHYS, stop_after)
    nc, kb = _NC_CACHE[key]
    f = lambda a: np.ascontiguousarray(np.asarray(a))
    consts = host_consts(S, NPG * 128)
    cache_cat = np.concatenate([np.asarray(inputs["cache_krope"][0]).reshape(NPHYS * 128, 32),
                                np.asarray(inputs["cache_ckv"][0]).reshape(NPHYS * 128, 256)], axis=1)
    cache_cat = np.ascontiguousarray(cache_cat, dtype=np.float32)
    wmap = {n: f(inputs[n][0]).reshape(s) for n, s in W_SHAPES.items()}
    in_maps = []
    for c in range(ncores):
        m = dict(wmap)
        m.update(consts)
        m["x"] = f(inputs["x_prompt"][c])
        m["xs"] = f(inputs["x_sample"][4 * c:4 * c + 4, 0])
        m["p"] = f(inputs["p_prompt"][0, c])
        m["psm"] = f(inputs["p_sample"][0, 4 * c:4 * c + 4, 0])
        m["cache_cat"] = cache_cat
        m["state_gdn"] = f(inputs["state_gdn"][0, 4 * c:4 * c + 4])
        m["state_conv"] = f(inputs["state_conv"][0, 4 * c:4 * c + 4]).reshape(4, 3 * 1536)
        m["pt"] = f(inputs["page_table"][4 * c:4 * c + 4]).astype(np.int32)
        in_maps.append(m)
    res = run_bass_kernel_spmd(nc, in_maps, core_ids=list(range(ncores))).results
    g = lambda n: np.stack([np.asarray(r[n]) for r in res])
    y = g("y"); ys = g("ys").reshape(4 * ncores, 1, 1024)
    return (y, ys, g("o_ckv")[None], g("o_kr")[None], g("o_gdn")[None], g("o_conv")[None],
            g("o_ckv_s").reshape(1, 4 * ncores, 1, 256), g("o_kr_s").reshape(1, 4 * ncores, 1, 32),
            g("o_gdn_s").reshape(1, 4 * ncores, 8, 64, 64), g("o_conv_s").reshape(1, 4 * ncores, 3, 1536))


def kernel(**inputs):
    S = inputs["x_prompt"].shape[1]
    NPG = inputs["page_table"].shape[1]
    NPHYS = inputs["cache_ckv"].shape[1]
    outs = run_cores(inputs, S, NPG, NPHYS)
    return tuple(np.ascontiguousarray(o.astype(np.float32)) for o in outs)
```

```python
import contextlib
import numpy as np
import concourse.bass as bass
import concourse.mybir as mybir

F32 = mybir.dt.float32
F32R = mybir.dt.float32r
BF16 = mybir.dt.bfloat16
I32 = mybir.dt.int32
AF = mybir.ActivationFunctionType
ALU = mybir.AluOpType
AX = mybir.AxisListType
SKIP_SELF_WAIT = False


class V:
    __slots__ = ("t", "ap")

    def __init__(self, t, ap):
        self.t = t
        self.ap = ap

    def __getitem__(self, idx):
        return V(self.t, self.ap[idx])

    def re(self, pat, **kw):
        return V(self.t, self.ap.rearrange(pat, **kw))

    def bc(self, dtype):
        return V(self.t, self.ap.bitcast(dtype))

    def un(self, axis):
        return V(self.t, self.ap.unsqueeze(axis))

    def bto(self, shape):
        return V(self.t, self.ap.broadcast_to(shape))

    def pb(self, n):
        return V(self.t, self.ap.partition_broadcast(n))


class T:
    def __init__(self, name, h):
        self.name = name
        self.h = h
        self.w = None
        self.r = {}
        self.dsem = None
        self.dtot = 0
        self.psum = False

    def __getitem__(self, idx):
        return V(self, self.h[idx])

    @property
    def v(self):
        return V(self, self.h[:])


class KB:
    def __init__(self, nc, es, needed=None):
        self.nc = nc
        self.es = es
        self.es_sem = es
        self.eng = {"pe": nc.tensor, "act": nc.scalar, "dve": nc.vector, "pool": nc.gpsimd, "sp": nc.sync}
        self.sem = {k: es.enter_context(nc.semaphore("s_" + k)) for k in self.eng}
        self.cnt = {k: 0 for k in self.eng}
        self.waited = {k: {} for k in self.eng}
        self.dma_sems = {}
        self.out_events = []
        self.ninst = 0
        self.needed = needed
        self.rec = None
        self.need_rec = {k: set() for k in self.eng}
        self.rank = {k: 0 for k in self.eng}
        self.rankmap = {k: {} for k in self.eng}
        self.engsem = {id(v): k for k, v in self.sem.items()}

    def sb(self, name, shape, dt):
        self.uid = getattr(self, "uid", 0) + 1
        name = f"{name}_u{self.uid}"
        return T(name, self.es.enter_context(self.nc.sbuf_tensor(name, list(shape), dt)))

    def ps(self, name, shape, dt):
        t = T(name, self.es.enter_context(self.nc.psum_tensor(name, list(shape), dt)))
        t.psum = True
        return t

    def dram(self, name, shape, dt, kind):
        return T(name, self.nc.dram_tensor(name, list(shape), dt, kind=kind).ap())

    def _wait(self, e, ev):
        sem, val = ev
        sid = id(sem)
        if e == "pe" and sem is self.sem["pe"]:
            return
        if SKIP_SELF_WAIT and sem is self.sem.get(e):
            return
        if sid in self.dma_sems:
            val = max(val, self.dma_sems[sid][1])
        w = self.waited[e]
        if w.get(sid, 0) >= val:
            return
        w[sid] = val
        f = self.engsem.get(sid)
        if f is not None:
            self.need_rec[f].add(val)
            if self.needed is not None:
                val = self.rankmap[f][val]
        self.eng[e].wait_ge(sem, val)

    def _deps(self, e, reads, writes):
        for v in reads:
            t = v.t
            if t.w is not None:
                self._wait(e, t.w)
            if t.psum:
                for ev in t.r.values():
                    if ev[0] is not self.sem.get(e):
                        self._wait(e, ev)
        for v in writes:
            t = v.t
            if t.w is not None:
                self._wait(e, t.w)
            for ev in t.r.values():
                self._wait(e, ev)

    def _record(self, ev, reads, writes):
        sem, val = ev
        for v in reads:
            v.t.r[id(sem)] = ev
        for v in writes:
            v.t.w = ev
            v.t.r = {}

    def play(self, items):
        for it in items:
            if it[0] == "op":
                self.op(*it[1:])
            else:
                self.dma(it[1], it[2], it[3], **it[4])

    def op(self, e, fn, reads, writes):
        if self.rec is not None:
            self.rec.append(("op", e, fn, reads, writes))
            return None
        reads = [v for v in reads if isinstance(v, V)]
        self._deps(e, reads, writes)
        inst = fn(self.eng[e])
        self.cnt[e] += 1
        if self.needed is None:
            inst.then_inc(self.sem[e], 1)
        elif self.cnt[e] in self.needed[e]:
            inst.then_inc(self.sem[e], 1)
            self.rank[e] += 1
            self.rankmap[e][self.cnt[e]] = self.rank[e]
        self._record((self.sem[e], self.cnt[e]), reads, writes)
        self.ninst += 1
        return inst

    def dma(self, q, out, in_, **kw):
        if self.rec is not None:
            self.rec.append(("dma", q, out, in_, kw))
            return None
        self._deps(q, [in_], [out])
        own = out.t
        if own.dsem is None:
            own.dsem = self.es_sem.enter_context(self.nc.semaphore("d_" + own.name))
            self.dma_sems[id(own.dsem)] = [own.dsem, 0]
        inst = self.eng[q].dma_start(out=out.ap, in_=in_.ap, **kw)
        own.dtot += 16
        self.dma_sems[id(own.dsem)][1] = own.dtot
        inst.then_inc(own.dsem, 16)
        ev = (own.dsem, own.dtot)
        self._record(ev, [in_], [out])
        self.ninst += 1
        return ev

    def gather(self, out, in_, idx, **kw):
        q = "pool"
        self._deps(q, [in_, idx], [out])
        own = out.t
        if own.dsem is None:
            own.dsem = self.es_sem.enter_context(self.nc.semaphore("d_" + own.name))
            self.dma_sems[id(own.dsem)] = [own.dsem, 0]
        inst = self.nc.gpsimd.indirect_dma_start(
            out=out.ap, out_offset=None, in_=in_.ap,
            in_offset=bass.IndirectOffsetOnAxis(ap=idx.ap, axis=0), **kw)
        own.dtot += 16
        self.dma_sems[id(own.dsem)][1] = own.dtot
        inst.then_inc(own.dsem, 16)
        ev = (own.dsem, own.dtot)
        self._record(ev, [in_, idx], [out])
        return ev

    def barrier(self):
        for e in self.eng:
            for sem, tot in self.dma_sems.values():
                if tot:
                    self._wait(e, (sem, tot))
            for f in self.eng:
                if f != e and self.cnt[f]:
                    self._wait(e, (self.sem[f], self.cnt[f]))

    def finish(self):
        for sem, tot in self.dma_sems.values():
            if tot:
                self._wait("sp", (sem, tot))
        for e in self.eng:
            if e != "sp" and self.cnt[e]:
                self._wait("sp", (self.sem[e], self.cnt[e]))

    def mm(self, out, lhsT, rhs, start=True, stop=True):
        return self.op("pe", lambda e: e.matmul(out.ap, lhsT=lhsT.ap, rhs=rhs.ap, start=start, stop=stop),
                       [lhsT, rhs] + ([] if start else [out]), [out])

    def tr(self, out, in_, ident):
        return self.op("pe", lambda e: e.transpose(out.ap, in_.ap, ident.ap), [in_, ident], [out])

    def act(self, out, in_, func, scale=1.0, bias=0.0, accum=None, eng="act"):
        kw = {}
        if accum is not None:
            kw["accum_out"] = accum.ap
        sc = scale.ap if isinstance(scale, V) else scale
        bi = bias.ap if isinstance(bias, V) else bias
        return self.op("act", lambda e: e.activation(out=out.ap, in_=in_.ap, func=func, scale=sc, bias=bi, **kw),
                       [in_, scale, bias], [out] + ([accum] if accum is not None else []))

    def tt(self, e, out, in0, in1, op):
        return self.op(e, lambda g: g.tensor_tensor(out=out.ap, in0=in0.ap, in1=in1.ap, op=op), [in0, in1], [out])

    def ts(self, e, out, in0, s1, op0, s2=None, op1=None, accum=None):
        a1 = s1.ap if isinstance(s1, V) else s1
        a2 = s2.ap if isinstance(s2, V) else s2
        kw = {}
        if op1 is not None:
            kw["op1"] = op1
        if accum is not None:
            kw["accum_out"] = accum.ap
        return self.op(e, lambda g: g.tensor_scalar(out=out.ap, in0=in0.ap, scalar1=a1, scalar2=a2, op0=op0, **kw),
                       [in0, s1, s2], [out] + ([accum] if accum is not None else []))

    def stt(self, out, in0, scalar, in1, op0, op1, e="dve"):
        sc = scalar.ap if isinstance(scalar, V) else scalar
        return self.op(e, lambda g: g.scalar_tensor_tensor(out=out.ap, in0=in0.ap, scalar=sc, in1=in1.ap, op0=op0, op1=op1),
                       [in0, scalar, in1], [out])

    def copy(self, e, out, in_):
        if e == "act":
            return self.act(out, in_, AF.Copy)
        return self.op(e, lambda g: g.tensor_copy(out=out.ap, in_=in_.ap), [in_], [out])

    def memset(self, e, out, val):
        return self.op(e, lambda g: g.memset(out.ap, val), [], [out])

    def reduce(self, out, in_, op=ALU.add, axis=AX.X, e="dve"):
        return self.op(e, lambda g: g.tensor_reduce(out=out.ap, in_=in_.ap, axis=axis, op=op), [in_], [out])

    def recip(self, out, in_):
        return self.op("dve", lambda g: g.reciprocal(out=out.ap, in_=in_.ap), [in_], [out])

from concourse.bass_utils import run_bass_kernel_spmd

D_MODEL = 1024; PLE_DIM = 256
NH = 8; DK = 64; CONV_DIM = 1536
Q_LORA = 384; KV_LORA = 256; ROPE = 32; NOPE = 64; QK = 96
IN_DIM = 2736; D_FF = 2816
EPS = 1e-6
ATTN_SCALE = QK ** -0.5
C_AB = 1536; C_Z = 1552; C_QA = 2064; C_KVA = 2448
BIG = 1.0e4


class Rot:
    def __init__(self, items):
        self.items = items
        self.i = 0

    def __call__(self):
        t = self.items[self.i % len(self.items)]
        self.i += 1
        return t


def host_consts(S, past):
    c = {}
    i = np.arange(128)
    same = (i[:, None] // 64) == (i[None, :] // 64)
    c["ident"] = np.eye(128, dtype=np.float32)
    c["tri"] = (same & (i[:, None] <= i[None, :])).astype(np.float32)
    c["lastsel"] = (i[:, None] == (i[None, :] // 64) * 64 + 63).astype(np.float32)
    vis = same & (i[None, :] < i[:, None])
    c["negs"] = np.where(vis, 0.0, BIG).astype(np.float32)
    c["negt"] = np.where(vis.T, 0.0, -BIG).astype(np.float32)
    c["headblk"] = same.astype(np.float32)
    sel8 = np.zeros((8, 8, 128), np.float32)
    for h in range(8):
        sel8[h, h, :] = 1.0
    c["sel8"] = sel8
    selp = np.zeros((8, 4, 128), np.float32)
    for h in range(8):
        selp[h, h // 2, (h % 2) * 64:(h % 2) * 64 + 64] = 1.0
    c["selpair"] = selp
    c["cmask"] = (i[None, :] >= i[:, None]).astype(np.float32)
    dm = np.zeros((8, 8, 64), np.float32)
    for h in range(8):
        dm[h, h, :] = 1.0
    c["diagmask"] = dm.reshape(8, 512)
    oh = np.zeros((4, 4, 128), np.float32)
    for b in range(4):
        oh[b, b, :] = 1.0
    c["onehot4"] = oh
    half = ROPE // 2
    inv = (10000.0 ** (-np.arange(half, dtype=np.float32) / half)).astype(np.float32)
    pos = np.arange(S, dtype=np.float32)
    ang = pos[:, None] * inv[None, :]
    c["cos_p"] = np.cos(ang).astype(np.float32)
    c["sin_p"] = np.sin(ang).astype(np.float32)
    angs = (np.float32(past) * inv)[None, :].astype(np.float32)
    c["cos_s"] = np.repeat(np.cos(angs), 4, 0).astype(np.float32)
    c["sin_s"] = np.repeat(np.sin(angs), 4, 0).astype(np.float32)
    return c


CONST_SHAPES = dict(ident=[128, 128], tri=[128, 128], lastsel=[128, 128], negs=[128, 128], negt=[128, 128],
                    headblk=[128, 128], sel8=[8, 8, 128], selpair=[8, 4, 128], cmask=[128, 128],
                    diagmask=[8, 512], onehot4=[4, 4, 128], cos_s=[4, 16], sin_s=[4, 16])

W_SHAPES = dict(g_attn=[1, 1024], w_in=[1024, IN_DIM], w_conv=[4, 1536], gdn_a_log=[1, 8], gdn_dt_bias=[1, 8],
                g_gdn_out=[1, 64], g_q_a=[1, 384], w_q_b=[384, 768], g_q_nope=[1, 64], g_q_rope=[1, 32],
                g_kv_a=[1, 256], g_k_rope=[1, 32], w_kv_b=[256, 1024], g_k_nope=[1, 64], w_o=[1024, 1024],
                g_ffn=[1, 1024], w_ffn_gate=[1024, D_FF], w_ffn_up=[1024, D_FF], w_ffn_down=[D_FF, 1024],
                g_ple=[1, 1024], w_ple_gate=[1024, 1024], w_ple_proj=[256, 1024])


def build(S, NPG, NPHYS, stop_after=None):
    _, k1 = build1(S, NPG, NPHYS, stop_after, None)
    return build1(S, NPG, NPHYS, stop_after, k1.need_rec)


INV_DT = BF16
NDUMMY = 0


def build1(S, NPG, NPHYS, stop_after, needed):
    import os
    DBG = int(os.environ.get('KDBG', '0'))
    NBLK = S // 128
    NT = S // 512
    nc = bass.Bass("TRN2", target_bir_lowering=False)
    es = contextlib.ExitStack()
    with es:
        nc_lp = es.enter_context(nc.allow_low_precision("bf16 matmul operands by design"))
        es.enter_context(nc.allow_non_contiguous_dma("small strided state loads/stores"))
        k = KB(nc, es, needed)
        din = {}

        def DI(name, shape, dt=F32):
            din[name] = k.dram(name, shape, dt, "ExternalInput")
            return din[name]

        def DO(name, shape, dt=F32):
            return k.dram(name, shape, dt, "ExternalOutput")

        x_d = DI("x", [S, 1024]); xs_d = DI("xs", [4, 1024])
        p_d = DI("p", [S, 256]); psm_d = DI("psm", [4, 256])
        ccat_d = DI("cache_cat", [NPHYS * 128, 288])
        sgdn_d = DI("state_gdn", [4, 8, 64, 64]); sconv_d = DI("state_conv", [4, 3 * 1536])
        pt_d = DI("pt", [4, NPG], I32)
        W = {n: DI(n, s) for n, s in W_SHAPES.items()}
        C = {n: DI(n, s) for n, s in CONST_SHAPES.items()}
        C["cos_p"] = DI("cos_p", [S, 16]); C["sin_p"] = DI("sin_p", [S, 16])
        y_d = DO("y", [S, 1024]); ys_d = DO("ys", [4, 1024])
        ockv_d = DO("o_ckv", [S, 256]); okr_d = DO("o_kr", [S, 32])
        ogdn_d = DO("o_gdn", [8, 64, 64]); oconv_d = DO("o_conv", [3, 1536])
        ockvs_d = DO("o_ckv_s", [4, 256]); okrs_d = DO("o_kr_s", [4, 32])
        ogdns_d = DO("o_gdn_s", [4, 8, 64, 64]); oconvs_d = DO("o_conv_s", [4, 3 * 1536])
        omix_d = k.dram("omix_scr", [128, 8, S], BF16, "ExternalOutput" if DBG == 99 else "Internal")
        omixs_d = k.dram("omixs_scr", [128, 8, 4], BF16, "Internal")

        banks = [k.ps(f"ps{i}", [128, 512], F32) for i in range(8)]
        psb = Rot(banks[:7])
        accbank = banks[7]
        evi = [0]

        def evac(out, in_, scale=None):
            evi[0] += 1
            if scale is not None:
                return k.act(out, in_, AF.Copy, scale=scale)
            if evi[0] % 3:
                return k.act(out, in_, AF.Copy)
            return k.copy("dve", out, in_)

        def cload(name, shape, dt=F32, src=None, q="sp"):
            t = k.sb("c_" + name, shape, dt)
            k.dma(q, t.v, (src if src is not None else C[name].v))
            return t

        identf = cload("ident", [128, 128])
        identb = k.sb("identb", [128, 128], BF16); k.copy("dve", identb.v, identf.v)
        identr = k.sb("identr", [128, 128], F32R); k.copy("dve", identr.v, identf.v)
        tri = cload("tri", [128, 128]); lastsel = cload("lastsel", [128, 128])
        negs = cload("negs", [128, 128]); negt = cload("negt", [128, 128])
        headblk = cload("headblk", [128, 128])
        headblk_b = k.sb("headblk_b", [128, 128], BF16); k.copy("dve", headblk_b.v, headblk.v)
        sel8 = cload("sel8", [8, 8, 128]); selpair = cload("selpair", [8, 4, 128])
        cmaskf = cload("cmask", [128, 128])
        cmask = k.sb("cmaskb", [128, 128], BF16); k.copy("dve", cmask.v, cmaskf.v)
        diagmask = cload("diagmask", [8, 512]); onehot4 = cload("onehot4", [4, 4, 128])
        onesb = k.sb("onesb", [128, 64], BF16); k.memset("dve", onesb.v, 1.0)
        onesf = k.sb("onesf", [128, 8], F32); k.memset("dve", onesf.v, 1.0)

        def bload(name, F, q="sp"):
            t = k.sb("b_" + name, [128, F], F32)
            k.dma(q, t.v, W[name].v.bto([128, F]))
            return t

        g_attn = bload("g_attn", 1024); g_q_a = bload("g_q_a", 384); g_kv_a = bload("g_kv_a", 256)
        g_q_nope = bload("g_q_nope", 64); g_q_rope = bload("g_q_rope", 32)
        g_k_rope = bload("g_k_rope", 32); g_k_nope = bload("g_k_nope", 64)
        g_gdn_out = bload("g_gdn_out", 64)
        a_log = bload("gdn_a_log", 8); dtb = bload("gdn_dt_bias", 8)
        eA = k.sb("eA", [128, 8], F32); k.act(eA.v, a_log.v, AF.Exp)
        k.ts("dve", g_q_nope.v, g_q_nope.v, ATTN_SCALE, ALU.mult)
        k.ts("dve", g_q_rope.v, g_q_rope.v, ATTN_SCALE, ALU.mult)
        wcv = k.sb("wcv", [128, 12, 4], F32)
        for j in range(4):
            k.dma("sp", wcv[:, :, j], W["w_conv"].v[j:j + 1, :].re("o (c p) -> p (o c)", p=128))

        def wload(name, K, N, q="pool"):
            kc = K // 128
            t = k.sb("w_" + name, [128, kc, N], BF16)
            src = W[name].v.re("(k p) n -> p k n", p=128)
            for i in range(kc):
                for n0 in range(0, N, 1024):
                    n1 = min(N, n0 + 1024)
                    k.dma(q, t[:, i, n0:n1], src[:, i, n0:n1])
            return t

        sm = Rot([k.sb(f"sm{i}", [128, 8], F32) for i in range(24)])

        def rstd_from_ss(ss, n, F, rows):
            a = sm()
            k.ts("dve", a[rows, 0:n], ss, 1.0 / F, ALU.mult, EPS, ALU.add)
            k.act(a[rows, 0:n], a[rows, 0:n], AF.Ln)
            r = sm()
            k.act(r[rows, 0:n], a[rows, 0:n], AF.Exp, scale=-0.5)
            return r[rows, 0:n]

        junk = k.sb("junk", [128, 1024], BF16)

        def rmsnorm_rows(out, x, g, F, rows):
            ss = sm()
            k.act(junk[rows, 0:F], x, AF.Square, accum=ss[rows, 0:1])
            r = rstd_from_ss(ss[rows, 0:1], 1, F, rows)
            k.stt(out, x, r, g, ALU.mult, ALU.mult)

        def transpose_bf(dst_fn, src, nt, nch, width=128):
            bank = psb()
            pb = bank.v.bc(BF16)
            for j in range(nch):
                k.tr(pb[0:width, j * 128:j * 128 + nt], src[:, j * width:(j + 1) * width], identb[0:nt, 0:nt])
            return pb

        from types import SimpleNamespace as NS
        GROUPS_ALL = [(0, 512), (512, 1024), (1024, 1536), (1536, 1552), (1552, 2064), (2064, 2448), (2448, 2736)]
        GROUPS_A = GROUPS_ALL[:5]
        GROUPS_B = GROUPS_ALL[5:]
        if stop_after is None:
            w_o = wload("w_o", 1024, 1024)
            w_pg = wload("w_ple_gate", 1024, 1024)
            w_pp = wload("w_ple_proj", 256, 1024)
        esP1 = contextlib.ExitStack()
        esP1.__enter__()
        k.es_outer = k.es
        k.es = esP1
        x_blk = k.sb("x_blk", [128, 1024], F32)
        xn_t = k.sb("xn", [128, 1024], BF16)
        xnT = k.sb("xnT", [128, 8, 128], BF16)
        z_tok = k.sb("z_tok", [128, IN_DIM], F32)
        sq_t = k.sb("sq_t", [128, 512], F32)

        def inproj(x_src_v, nt, w_in, c_off, groups):
            rows = slice(0, nt)
            k.dma("sp", x_blk[rows, :], x_src_v)
            rmsnorm_rows(xn_t[rows, :], x_blk[rows, :], g_attn[rows, :], 1024, rows)
            pb = transpose_bf(None, xn_t[rows, :], nt, 8)
            evac(xnT[:, :, 0:nt], pb[:, 0:1024].re("p (j t) -> p j t", j=8)[:, :, 0:nt])
            for (c0, c1) in groups:
                bank = psb()
                for kk in range(8):
                    k.mm(bank[rows, 0:c1 - c0], xnT[:, kk, 0:nt], w_in[:, kk, c0 - c_off:c1 - c_off], start=(kk == 0), stop=(kk == 7))
                evac(z_tok[rows, c0:c1], bank[rows, 0:c1 - c0])

        def wload_cols(name, K, c0, c1, q="pool"):
            kc = K // 128
            t = k.sb("w_" + name + f"_{c0}", [128, kc, c1 - c0], BF16)
            src = W[name].v.re("(k p) n -> p k n", p=128)
            for i in range(kc):
                for n0 in range(c0, c1, 1024):
                    n1 = min(c1, n0 + 1024)
                    k.dma(q, t[:, i, n0 - c0:n1 - c0], src[:, i, n0:n1])
            return t

        def head_rms(out, xin, g, nt, width, scratch=None):
            rows = slice(0, nt)
            sq = (scratch or sq_t)[rows, 0:8 * width].re("p (h d) -> p h d", h=8)
            k.act(sq, xin, AF.Square)
            ss = sm()
            k.reduce(ss[rows, 0:8], sq)
            r = rstd_from_ss(ss[rows, 0:8], 8, width, rows)
            k.tt("dve", out, xin, r.un(2).bto([nt, 8, width]), ALU.mult)
            k.tt("pool", out, out, g.un(1).bto([nt, 8, width]), ALU.mult)

        def alloc_mla(w_qb, w_kvb):
            M = NS()
            M.w_qb = w_qb; M.w_kvb = w_kvb
            M.qa_n = k.sb("qa_n", [128, 384], BF16)
            M.qanT = k.sb("qanT", [128, 3, 128], BF16)
            M.qkv_tok = k.sb("qkv_tok", [128, 1024], F32)
            M.hh = k.sb("hh", [128, 8, 96], F32)
            M.hh_bf = k.sb("hh_bf", [128, 8, 96], BF16)
            M.rtmp = k.sb("rtmp", [128, 8, 32], F32)
            M.rtmp2 = k.sb("rtmp2", [128, 8, 16], F32)
            M.cos_t = k.sb("cos_t", [128, 16], F32); M.sin_t = k.sb("sin_t", [128, 16], F32)
            M.ckv_f = k.sb("ckv_f", [128, 256], F32)
            M.ckv_bf = k.sb("ckv_bf", [128, 260], BF16)
            k.memset("dve", M.ckv_bf[:, 256:257], 1.0)
            M.ckvT = k.sb("ckvT", [128, 2, 128], BF16)
            M.kr_f = k.sb("kr_f", [128, 32], F32)
            M.kr_n = k.sb("kr_n", [128, 32], F32)
            return M

        def rope(M, out, xin, nt, nh):
            rows = slice(0, nt)
            cb = M.cos_t[rows, :].un(1).bto([nt, nh, 16]); sb_ = M.sin_t[rows, :].un(1).bto([nt, nh, 16])
            x1 = xin[:, :, 0:16]; x2 = xin[:, :, 16:32]
            t2 = M.rtmp2[rows, 0:nh, :]
            k.tt("dve", out[:, :, 0:16], x1, cb, ALU.mult)
            k.tt("dve", t2, x2, sb_, ALU.mult)
            k.tt("dve", out[:, :, 0:16], out[:, :, 0:16], t2, ALU.subtract)
            k.tt("dve", out[:, :, 16:32], x1, sb_, ALU.mult)
            k.tt("dve", t2, x2, cb, ALU.mult)
            k.tt("dve", out[:, :, 16:32], out[:, :, 16:32], t2, ALU.add)

        def mla_q(M, nt, cos_src, sin_src):
            rows = slice(0, nt)
            k.dma("sp", M.cos_t[rows, :], cos_src); k.dma("sp", M.sin_t[rows, :], sin_src)
            rmsnorm_rows(M.qa_n[rows, :], z_tok[rows, C_QA:C_QA + 384], g_q_a[rows, :], 384, rows)
            pb = transpose_bf(None, M.qa_n[rows, :], nt, 3)
            evac(M.qanT[:, :, 0:nt], pb[:, 0:384].re("p (j t) -> p j t", j=3)[:, :, 0:nt])
            for (c0, c1) in ((0, 512), (512, 768)):
                bank = psb()
                for kk in range(3):
                    k.mm(bank[rows, 0:c1 - c0], M.qanT[:, kk, 0:nt], M.w_qb[:, kk, c0:c1], start=(kk == 0), stop=(kk == 2))
                evac(M.qkv_tok[rows, c0:c1], bank[rows, 0:c1 - c0])
            q3 = M.qkv_tok[rows, 0:768].re("p (h d) -> p h d", h=8)
            head_rms(M.hh[rows, :, 0:64], q3[:, :, 0:64], g_q_nope[rows, :], nt, 64)
            head_rms(M.rtmp[rows, :, :], q3[:, :, 64:96], g_q_rope[rows, :], nt, 32)
            rope(M, M.hh[rows, :, 64:96], M.rtmp[rows, :, :], nt, 8)

        def mla_kv(M, nt, ockv_v, okr_v):
            rows = slice(0, nt)
            rmsnorm_rows(M.ckv_f[rows, :], z_tok[rows, C_KVA:C_KVA + 256], g_kv_a[rows, :], 256, rows)
            k.dma("sp", ockv_v, M.ckv_f[rows, :])
            k.copy("pool", M.ckv_bf[rows, 0:256], M.ckv_f[rows, :])
            rmsnorm_rows(M.kr_n[rows, :], z_tok[rows, C_KVA + 256:C_KVA + 288], g_k_rope[rows, :], 32, rows)
            rope(M, M.kr_f[rows, :].un(1), M.kr_n[rows, :].un(1), nt, 1)
            k.dma("sp", okr_v, M.kr_f[rows, :])
            pb = transpose_bf(None, M.ckv_bf[rows, 0:256], nt, 2)
            evac(M.ckvT[:, :, 0:nt], pb[:, 0:256].re("p (j t) -> p j t", j=2)[:, :, 0:nt])

        def alloc_gdn(small=False):
            G = NS()
            G.cv = k.sb("cv", [128, 12, 128], F32)
            G.sqt = k.sb("sqt", [128, 4, 128], BF16)
            G.rst = k.sb("rst", [128, 4, 128], F32)
            G.qTg = k.sb("qTg", [128, 4, 128], F32)
            G.kTg = k.sb("kTg", [128, 4, 128], F32)
            G.tmp4 = k.sb("tmp4", [128, 4, 128], F32)
            G.sz_t = k.sb("sz_t", [128, 512], F32)
            if not small:
                G.o_tok = k.sb("o_tok", [128, 512], F32)
                G.on_t = k.sb("on_t", [128, 512], F32)
                G.og_bf = k.sb("og_bf", [128, 512], BF16)
            return G

        def gdn_scalars(nt):
            rows = slice(0, nt)
            ta = sm(); k.tt("dve", ta[rows, :], z_tok[rows, C_AB:C_AB + 8], dtb[rows, :], ALU.add)
            e = sm(); k.act(e[rows, :], ta[rows, :], AF.Exp)
            sp_ = sm(); k.act(sp_[rows, :], e[rows, :], AF.Ln, bias=1.0)
            g_tok = sm(); k.stt(g_tok[rows, :], sp_[rows, :], -1.0, eA[rows, :], ALU.mult, ALU.mult)
            beta = sm(); k.act(beta[rows, :], z_tok[rows, C_AB + 8:C_AB + 16], AF.Sigmoid)
            return g_tok, beta

        def l2norm_fm(G, nt):
            for half in range(2):
                k.act(G.sqt[:, :, 0:nt], G.cv[:, half * 4:half * 4 + 4, 0:nt], AF.Square)
                bank = psb()
                for c in range(4):
                    k.mm(bank[:, c * 128:c * 128 + nt], headblk_b.v, G.sqt[:, c, 0:nt])
                k.ts("dve", G.tmp4[:, :, 0:nt], bank.v.re("p (c t) -> p c t", c=4)[:, :, 0:nt], EPS, ALU.add)
                k.act(G.tmp4[:, :, 0:nt], G.tmp4[:, :, 0:nt], AF.Ln)
                k.act(G.rst[:, :, 0:nt], G.tmp4[:, :, 0:nt], AF.Exp, scale=-0.5)
                if half == 0:
                    k.stt(G.qTg[:, :, 0:nt], G.cv[:, 0:4, 0:nt], DK ** -0.5, G.rst[:, :, 0:nt], ALU.mult, ALU.mult)
                else:
                    k.tt("dve", G.kTg[:, :, 0:nt], G.cv[:, 4:8, 0:nt], G.rst[:, :, 0:nt], ALU.mult)

        def gdn_out_tok(G, nt):
            rows = slice(0, nt)
            o3 = G.o_tok[rows, :].re("p (h d) -> p h d", h=8)
            on3 = G.on_t[rows, :].re("p (h d) -> p h d", h=8)
            head_rms(on3, o3, g_gdn_out[rows, :], nt, 64)
            k.tt("dve", G.og_bf[rows, :], G.on_t[rows, :], G.sz_t[rows, :], ALU.mult)

        def sample_scope():
            w_in = wload_cols("w_in", 1024, 0, IN_DIM)
            w_qb = wload_cols("w_q_b", 384, 0, 768)
            w_kvb = wload_cols("w_kv_b", 256, 0, 1024)
            M = alloc_mla(w_qb, w_kvb)
            G = alloc_gdn(small=True)
            nt = 4; rows = slice(0, 4)
            inproj(xs_d.v, 4, w_in, 0, GROUPS_ALL)
            if DBG == 1:
                return
            oms = k.sb("oms", [128, 8, 4], BF16)
            esg = contextlib.ExitStack()
            with esg:
                k.es = esg
                st_tok = k.sb("st_tok", [12, 1536], F32)
                k.dma("sp", st_tok.v, sconv_d.v.re("b (j c) -> (b j) c", j=3))
                k.dma("sp", oconvs_d.v.re("b (j c) -> b j c", j=3)[:, 0:2, :], sconv_d.v.re("b (j c) -> b j c", j=3)[:, 1:3, :])
                k.dma("sp", oconvs_d.v.re("b (j c) -> b j c", j=3)[:, 2, :], z_tok[rows, 0:1536])
                ext = k.sb("ext_fm", [128, 12, 4, 4], F32)
                bank = psb()
                for c in range(12):
                    k.tr(bank[:, c * 12:(c + 1) * 12], st_tok[:, c * 128:(c + 1) * 128], identf[0:12, 0:12])
                evac(ext[:, :, :, 0:3], bank[:, 0:144].re("p (c b j) -> p c b j", c=12, b=4))
                bank = psb()
                for c in range(12):
                    k.tr(bank[:, c * 4:(c + 1) * 4], z_tok[rows, c * 128:(c + 1) * 128], identf[0:4, 0:4])
                evac(ext[:, :, :, 3], bank[:, 0:48].re("p (c b) -> p c b", c=12))
                k.tt("dve", ext.v, ext.v, wcv.v.un(2).bto([128, 12, 4, 4]), ALU.mult)
                cpre = k.sb("cpre", [128, 12, 4], F32)
                k.reduce(cpre.v, ext.v)
                k.act(G.cv[:, :, 0:4], cpre.v, AF.Silu)
                l2norm_fm(G, 4)
                g_tok, beta = gdn_scalars(4)
                eg = sm(); k.act(eg[rows, :], g_tok[rows, :], AF.Exp)
                sm_b = Rot([k.sb(f"smb{i}", [128, 4, 4], F32) for i in range(3)])

                def bc_bh(src):
                    bank = psb()
                    for b in range(4):
                        k.mm(bank[:, b * 8:(b + 1) * 8], onehot4[:, b, :], src)
                    o = sm_b()
                    for hp in range(2):
                        RH = slice(hp * 64, hp * 64 + 64)
                        evac(o[RH, :, :], bank[RH, 0:32].re("p (b pr hp) -> p b pr hp", b=4, pr=4)[:, :, :, hp])
                    return o
                eg_b = bc_bh(eg[rows, :]); beta_b = bc_bh(beta[rows, :])
                st = k.sb("st_s", [128, 4, 4, 64], F32)
                for b in range(4):
                    for hp in range(2):
                        k.dma("sp", st[hp * 64:(hp + 1) * 64, b, :, :],
                              sgdn_d.v[b].re("(pr hp) k v -> hp k pr v", hp=2)[hp])
                B4 = lambda t: t.v.un(3).bto([128, 4, 4, 64])
                k.tt("dve", st.v, st.v, B4(eg_b), ALU.mult)
                tmp = k.sb("tmp_s", [128, 4, 4, 64], F32)
                kcol = G.kTg[:, :, 0:4].re("p pr b -> p b pr")
                k.tt("dve", tmp.v, st.v, kcol.un(3).bto([128, 4, 4, 64]), ALU.mult)
                kSB = k.sb("kSB_s", [128, 4, 4, 64], F32)
                for hf in range(2):
                    bank = psb()
                    k.mm(bank.v, headblk.v, tmp[:, hf * 2:hf * 2 + 2, :, :].re("p b pr v -> p (b pr v)"))
                    evac(kSB[:, hf * 2:hf * 2 + 2, :, :].re("p b pr v -> p (b pr v)"), bank.v)
                v_tk = k.sb("v_tk", [4, 512], F32)
                bank = psb()
                for c in range(4):
                    k.tr(bank[0:4, c * 128:(c + 1) * 128], G.cv[:, 8 + c, 0:4], identf.v)
                evac(v_tk.v, bank[0:4, :])
                vB = k.sb("vB_s", [128, 4, 4, 64], F32)
                for b in range(4):
                    bank = psb()
                    k.mm(bank.v, onehot4[:, b, :], v_tk.v)
                    for hp in range(2):
                        RH = slice(hp * 64, hp * 64 + 64)
                        evac(vB[RH, b, :, :], bank[RH, :].re("p (pr hp v) -> p pr hp v", pr=4, hp=2)[:, :, hp, :])
                k.tt("dve", vB.v, vB.v, kSB.v, ALU.subtract)
                k.tt("dve", vB.v, vB.v, B4(beta_b), ALU.mult)
                k.tt("dve", tmp.v, vB.v, kcol.un(3).bto([128, 4, 4, 64]), ALU.mult)
                k.tt("dve", st.v, st.v, tmp.v, ALU.add)
                for b in range(4):
                    for hp in range(2):
                        k.dma("sp", ogdns_d.v[b].re("(pr hp) k v -> hp k pr v", hp=2)[hp], st[hp * 64:(hp + 1) * 64, b, :, :])
                oT = k.sb("oT_s", [128, 16], F32)
                for hp in range(2):
                    bank = psb()
                    RH = slice(hp * 64, hp * 64 + 64)
                    for b in range(4):
                        for pr in range(4):
                            k.mm(bank[RH, pr * 4 + b:pr * 4 + b + 1], st[RH, b, pr, :], G.qTg[RH, pr, b:b + 1])
                    evac(oT[RH, :], bank[RH, 0:16])
                osq = k.sb("osq_s", [128, 16], F32)
                k.act(osq.v, oT.v, AF.Square)
                bank = psb()
                k.mm(bank[:, 0:16], headblk.v, osq.v)
                a_ = k.sb("a_s_", [128, 16], F32); r_ = k.sb("r_s_", [128, 16], F32)
                k.ts("dve", a_.v, bank[:, 0:16], 1.0 / 64, ALU.mult, EPS, ALU.add)
                k.act(a_.v, a_.v, AF.Ln)
                k.act(r_.v, a_.v, AF.Exp, scale=-0.5)
                k.tt("dve", oT.v, oT.v, r_.v, ALU.mult)
                ggo_col = k.sb("ggo_col", [128, 1], F32)
                for hp in range(2):
                    k.dma("sp", ggo_col[hp * 64:(hp + 1) * 64, :], W["g_gdn_out"].v.re("o d -> d o"))
                k.ts("dve", oT.v, oT.v, ggo_col[:, 0:1], ALU.mult)
                k.act(G.sz_t[rows, :], z_tok[rows, C_Z:C_Z + 512], AF.Silu)
                bank = psb()
                for c in range(4):
                    k.tr(bank[:, c * 4:c * 4 + 4], G.sz_t[rows, c * 128:(c + 1) * 128], identf[0:4, 0:4])
                k.tt("dve", oms[:, 0:4, :], oT.v.re("p (pr b) -> p pr b", pr=4), bank[:, 0:16].re("p (pr b) -> p pr b", pr=4), ALU.mult)
                k.barrier()
            k.es = es1_cur[0]
            mla_q(M, 4, C["cos_s"].v, C["sin_s"].v)
            mla_kv(M, 4, ockvs_d.v, okrs_d.v)
            krb = k.sb("krb_s", [4, 32], BF16)
            k.copy("dve", krb.v, M.kr_f[0:4, :])
            krT_new = k.sb("krT_new", [32, 4], BF16)
            bank = psb(); pb = bank.v.bc(BF16)
            k.tr(pb[0:32, 0:4], krb.v, identb[0:4, 0:4])
            evac(krT_new.v, pb[0:32, 0:4])
            if DBG == 7:
                return
            WkT = k.sb("WkT", [64, 8, 256], BF16)
            wk4 = w_kvb.v.re("p k (h d) -> p k h d", h=8)
            for kk in range(2):
                bank = psb(); pb = bank.v.bc(BF16)
                for h in range(8):
                    k.tr(pb[0:64, h * 128:(h + 1) * 128], wk4[:, kk, h, 0:64], identb.v)
                evac(WkT[:, :, kk * 128:(kk + 1) * 128], pb[0:64, 0:1024].re("p (h t) -> p h t", h=8))
            qg = k.sb("qg_s", [4, 8, 64], BF16)
            k.tt("dve", qg.v, M.hh[rows, :, 0:64], g_k_nope[rows, :].un(1).bto([4, 8, 64]), ALU.mult)
            qr = k.sb("qr_s", [4, 8, 32], BF16)
            k.copy("dve", qr.v, M.hh[rows, :, 64:96])
            bank = psb(); pb = bank.v.bc(BF16)
            for h in range(8):
                k.tr(pb[0:64, h * 4:h * 4 + 4], qg[:, h, :], identb[0:4, 0:4])
                k.tr(pb[0:32, 64 + h * 4:64 + h * 4 + 4], qr[:, h, :], identb[0:4, 0:4])
            qgT = k.sb("qgT_s", [64, 8, 4], BF16); qrT = k.sb("qrT_s", [32, 8, 4], BF16)
            evac(qgT.v, pb[0:64, 0:32].re("p (h b) -> p h b", h=8))
            evac(qrT.v, pb[0:32, 64:96].re("p (h b) -> p h b", h=8))
            bank = psb()
            for kk in range(2):
                for h in range(8):
                    k.mm(bank[:, kk * 32 + h * 4:kk * 32 + h * 4 + 4], WkT[:, h, kk * 128:(kk + 1) * 128], qgT[:, h, :])
            qpT = k.sb("qpT_s", [128, 2, 4, 8], BF16)
            evac(qpT.v, bank[:, 0:64].re("p (k h b) -> p k b h", k=2, h=8))
            if DBG == 8:
                return
            pti = k.sb("pti", [128, 4 * NPG], I32)
            k.dma("sp", pti.v, pt_d.v.re("(o b) j -> o (b j)", o=1).bto([128, 4 * NPG]))
            ptf = k.sb("ptf", [128, 4 * NPG], F32)
            k.copy("dve", ptf.v, pti.v)
            iot = k.sb("iot", [128, 1], F32)
            k.op("pool", lambda g: g.iota(iot.v.ap, pattern=[[0, 1]], base=0, channel_multiplier=1,
                                          allow_small_or_imprecise_dtypes=True), [], [iot.v])
            k.ts("dve", ptf.v, ptf.v, 128.0, ALU.mult, iot[:, 0:1], ALU.add)
            idx = pti
            k.copy("dve", idx.v, ptf.v)
            G_ = 4
            pg_r = Rot([k.sb(f"pg{i}", [128, 292], BF16) for i in range(2 * G_ + 2)])
            for t_ in pg_r.items:
                k.memset("dve", t_[:, 288:289], 1.0)
            sq_r = Rot([k.sb(f"sqp{i}", [128, G_, 512], BF16) for i in range(2)])
            sq1 = sq_r.items[0][:, 0, :]
            p_r = Rot([k.sb(f"pp{i}", [128, G_ * 8], BF16) for i in range(3)])
            sg_r = Rot([k.sb(f"sgp{i}", [128, G_ * 8], F32) for i in range(6)])
            Wkc = k.sb("Wkc", [128, 2, 512], BF16); Wvc = k.sb("Wvc", [128, 2, 512], BF16)
            for kk in range(2):
                k.copy("dve", Wkc[:, kk, :].re("p (h d) -> p h d", h=8), wk4[:, kk, :, 0:64])
                k.copy("dve", Wvc[:, kk, :].re("p (h d) -> p h d", h=8), wk4[:, kk, :, 64:128])
            wk_rhs = lambda kk: Wkc[:, kk, :]
            wv_rhs = lambda kk: Wvc[:, kk, :]
            acc_sb = k.sb("acc_sb", [8, 257], F32)
            accn = k.sb("accn", [8, 256], BF16)
            accT = k.sb("accT", [128, 2, 8], BF16)
            om_f = k.sb("om_f", [8, 512], F32)
            trb = Rot([banks[0]]); bankA = banks[1:5]; bB = banks[5]
            qr_f = k.sb("qr_f", [4, 256], F32)
            k.copy("dve", qr_f.v.re("p (h d) -> p h d", h=8), M.hh[rows, :, 64:96])
            qrB = k.sb("qrB", [128, 4, 256], BF16)
            for b in range(4):
                bank = trb()
                k.mm(bank[:, 0:256], onehot4[:, b, :], qr_f.v)
                evac(qrB[:, b, :], bank[:, 0:256])
            rp_r = Rot([k.sb(f"rp{i}", [128, G_, 256], BF16) for i in range(2)])

            def newtok(b):
                rws = slice(0, 4)
                bA = bankA[0]
                for kk in range(2):
                    k.mm(bA[rws, 0:512], M.ckvT[:, kk, 0:4], wk_rhs(kk), start=(kk == 0), stop=(kk == 1))
                for kk in range(2):
                    k.mm(bB[rws, 0:8], M.ckvT[:, kk, 0:4], qpT[:, kk, b, :], start=(kk == 0), stop=(kk == 1))
                k.mm(bB[rws, 8:16], krT_new[:, 0:4], qrT[:, :, b])
                k.act(sq1[rws, :], bA[rws, :], AF.Square)
                ss = sm()
                k.reduce(ss[rws, :], sq1[rws, :].re("p (h d) -> p h d", h=8))
                r = rstd_from_ss(ss[rws, :], 8, 64, rws)
                s1 = sm()
                k.tt("dve", s1[rws, :], bB[rws, 0:8], r, ALU.mult)
                k.tt("dve", s1[rws, :], s1[rws, :], bB[rws, 8:16], ALU.add)
                s2 = sm()
                k.act(s2[rws, :], s1[rws, :], AF.Exp)
                pp = p_r()
                k.ts("dve", pp[rws, 0:8], s2[rws, :], identf[0:4, b:b + 1], ALU.mult)
                return pp

            trb2 = Rot([banks[0], banks[6]])
            cT4_r = Rot([k.sb(f"cT4_{i}", [128, 4, 2, 128], BF16) for i in range(2)])

            def frontA(b, j0, g):
                pgs = []
                tb = trb2(); pb = tb.v.bc(BF16)
                for i in range(g):
                    pg = pg_r(); pgs.append(pg)
                    col = b * NPG + j0 + i
                    k.gather(pg[:, 0:288], ccat_d.v, idx[:, col:col + 1])
                for i in range(g):
                    k.tr(pb[:, i * 256:i * 256 + 128], pgs[i][:, 32:160], identb.v)
                    k.tr(pb[:, i * 256 + 128:i * 256 + 256], pgs[i][:, 160:288], identb.v)
                cT4 = cT4_r()
                k.act(cT4[:, 0:g, :, :], pb[:, 0:g * 256].re("p (g j t) -> p g j t", g=g, j=2), AF.Copy)
                return pgs, cT4

            def frontB(b, g, pgs, cT4):
                sq = sq_r(); rp = rp_r()
                for i in range(g):
                    for kk in range(2):
                        k.mm(bankA[i][:, 0:512], cT4[:, i, kk, :], wk_rhs(kk), start=(kk == 0), stop=(kk == 1))
                    for kk in range(2):
                        k.mm(bB[:, i * 8:i * 8 + 8], cT4[:, i, kk, :], qpT[:, kk, b, :], start=(kk == 0), stop=(kk == 1))
                    k.act(sq[:, i, :], bankA[i].v, AF.Square)
                    for _d in range(NDUMMY):
                        k.op("pe", lambda e: e.matmul(accbank[64:128, 0:512].ap, lhsT=Wkc[:, 0, 0:64].ap, rhs=Wkc[:, 1, :].ap,
                                                      start=True, stop=True), [], [])
                    k.tt("dve", rp[:, i, :].re("p (h d) -> p h d", h=8), pgs[i][:, 0:32].un(1).bto([128, 8, 32]),
                         qrB[:, b, :].re("p (h d) -> p h d", h=8), ALU.mult)
                return sq, rp

            def small(g, sq, rp):
                n8 = g * 8
                sr = sg_r()
                k.reduce(sr[:, 0:n8], rp[:, 0:g, :].re("p g (h d) -> p (g h) d", h=8))
                ss = sg_r()
                k.reduce(ss[:, 0:n8], sq[:, 0:g, :].re("p g (h d) -> p (g h) d", h=8))
                a = sg_r()
                k.ts("dve", a[:, 0:n8], ss[:, 0:n8], 1.0 / 64, ALU.mult, EPS, ALU.add)
                k.act(a[:, 0:n8], a[:, 0:n8], AF.Ln)
                r = sg_r()
                k.act(r[:, 0:n8], a[:, 0:n8], AF.Exp, scale=-0.5)
                s1 = sg_r()
                k.tt("dve", s1[:, 0:n8], bB[:, 0:n8], r[:, 0:n8], ALU.mult)
                k.tt("dve", s1[:, 0:n8], s1[:, 0:n8], sr[:, 0:n8], ALU.add)
                pp = p_r()
                k.act(pp[:, 0:n8], s1[:, 0:n8], AF.Exp)
                return pp

            def accm(j0, g, pgs, pp):
                for i in range(g):
                    k.mm(accbank[0:8, 0:257], pp[:, i * 8:(i + 1) * 8], pgs[i][:, 32:289], start=False,
                         stop=(j0 + i == NPG - 1))

            for b in range(4):
                pp = newtok(b)
                k.mm(accbank[0:8, 0:257], pp[0:4, 0:8], M.ckv_bf[0:4, 0:257], start=True, stop=(NPG == 0))
                groups = [(j0, min(G_, NPG - j0)) for j0 in range(0, NPG, G_)]
                if groups:
                    pgs, cT4 = frontA(b, *groups[0])
                    sq, rp = frontB(b, groups[0][1], pgs, cT4)
                    ppg = small(groups[0][1], sq, rp)
                for gi, (j0, g) in enumerate(groups):
                    cur = (pgs, ppg)
                    if gi + 1 < len(groups):
                        pgs, cT4 = frontA(b, *groups[gi + 1])
                    accm(j0, g, *cur)
                    if gi + 1 < len(groups):
                        sq, rp = frontB(b, groups[gi + 1][1], pgs, cT4)
                        ppg = small(groups[gi + 1][1], sq, rp)
                evac(acc_sb.v, accbank[0:8, 0:257])
                rl = sm(); k.recip(rl[0:8, 0:1], acc_sb[:, 256:257])
                k.ts("dve", accn.v, acc_sb[:, 0:256], rl[0:8, 0:1], ALU.mult)
                bank = trb(); pb = bank.v.bc(BF16)
                for kk in range(2):
                    k.tr(pb[:, kk * 8:kk * 8 + 8], accn[:, kk * 128:(kk + 1) * 128], identb[0:8, 0:8])
                evac(accT.v, pb[:, 0:16].re("p (k h) -> p k h", k=2))
                bank = trb()
                for kk in range(2):
                    k.mm(bank[0:8, :], accT[:, kk, :], wv_rhs(kk), start=(kk == 0), stop=(kk == 1))
                k.tt("dve", om_f.v, bank[0:8, :], diagmask.v, ALU.mult)
                bank2 = trb()
                for pr in range(4):
                    k.mm(bank2[:, pr:pr + 1], om_f[:, pr * 128:(pr + 1) * 128], onesf[0:8, 0:1])
                evac(oms[:, 4:8, b], bank2[:, 0:4])
            k.dma("sp", omixs_d.v, oms.v)

        def pass_a():
            w_in = wload_cols("w_in", 1024, 0, 2064)
            G = alloc_gdn()
            S2 = k.sb("S2", [128, 4, 128], F32)
            k.memset("pool", S2.v, 0.0)
            zcT = k.sb("zcT", [128, 12, 131], F32)
            k.memset("pool", zcT.v, 0.0)
            omixA = k.sb("omixA", [128, 4, 512], BF16)
            acc_r = Rot([k.sb(f"acc_c{i}", [128, 128], F32) for i in range(2)])
            accp_r = Rot([k.sb(f"acc_p{i}", [128, 128], F32) for i in range(2)])
            tmp_p = k.sb("tmp_p", [128, 128], F32)
            kbT = k.sb("kbT", [128, 4, 128], BF16); nwT = k.sb("nwT", [128, 4, 128], BF16)
            qgT = k.sb("qgT", [128, 4, 128], BF16)
            kT_b = k.sb("kT_b", [128, 4, 128], BF16); qT_b = k.sb("qT_b", [128, 4, 128], BF16)
            S2b = k.sb("S2b", [128, 4, 128], BF16)
            k.memset("pool", S2b.v, 0.0)
            egc_fm = k.sb("egc_fm", [128, 4, 128], F32)
            k_tok = G.on_t
            v_tok = k.sb("v_tok", [128, 512], F32); u_tok = v_tok
            vb_tok = k.sb("vb_tok", [128, 512], BF16)
            kbg_tok = k.sb("kbg_tok", [128, 512], BF16)
            kdec_tok = k.sb("kdec_tok", [128, 512], BF16)
            vnew_tok = k.sb("vnew_tok", [128, 512], BF16)
            gcT8 = k.sb("gcT8", [8, 128], F32)
            betaT8 = k.sb("betaT8", [8, 128], F32)
            qkT_all = k.sb("qkT_all", [128, 8, 128], BF16)
            U_all = k.sb("U_all", [128, 8, 128], BF16)
            g4 = Rot([k.sb(f"g4_{i}", [128, 4, 128], F32) for i in range(3)])
            r4s = [Rot([k.sb(f"r4_{j}_{i}", [128, 4, 128], INV_DT) for i in range(6)]) for j in range(2)]
            t4 = k.sb("t4", [128, 4, 128], F32)
            v4 = lambda b_: b_.v.re("p (c t) -> p c t", c=4)

            def f1_steps(bi):
                steps = []
                t0 = bi * 128

                def s_in():
                    inproj(x_d.v[t0:t0 + 128, :], 128, w_in, 0, GROUPS_A)
                    if bi == NBLK - 1:
                        k.dma("sp", oconv_d.v, z_tok[125:128, 0:1536])
                steps.append(s_in)

                def s_tr(g3):
                    def f():
                        bank = psb()
                        for c in range(4):
                            k.tr(bank[:, c * 128:(c + 1) * 128], z_tok[:, (g3 * 4 + c) * 128:(g3 * 4 + c + 1) * 128], identf.v)
                        evac(zcT[:, g3 * 4:g3 * 4 + 4, 3:131], v4(bank))
                    return f
                for g3 in range(3):
                    steps.append(s_tr(g3))

                def s_cv(c):
                    def f():
                        ac = acc_r()
                        k.ts("dve", ac.v, zcT[:, c, 0:128], wcv[:, c, 0:1], ALU.mult)
                        for j in (1, 2, 3):
                            k.stt(ac.v, zcT[:, c, j:j + 128], wcv[:, c, j:j + 1], ac.v, ALU.mult, ALU.add)
                        k.act(G.cv[:, c, :], ac.v, AF.Silu)
                    return f
                for c in range(12):
                    steps.append(s_cv(c))
                steps.append(lambda: k.copy("pool", zcT[:, :, 0:3], zcT[:, :, 128:131]))
                return steps

            def gdn_block(tl, inj):
                cv = G.cv; kTg = G.kTg; qTg = G.qTg

                def pump(n):
                    for _ in range(n):
                        if inj:
                            inj.pop(0)()
                l2norm_fm(G, 128)
                k.act(G.sz_t.v, z_tok[:, C_Z:C_Z + 512], AF.Silu)
                k.copy("pool", kT_b.v, kTg.v)
                k.copy("pool", qT_b.v, qTg.v)
                bank = psb()
                for c in range(4):
                    k.tr(bank[:, c * 128:(c + 1) * 128], kTg[:, c, :], identf.v)
                evac(k_tok.v, bank.v)
                bank = psb()
                for c in range(4):
                    k.tr(bank[:, c * 128:(c + 1) * 128], cv[:, 8 + c, :], identf.v)
                evac(v_tok.v, bank.v)
                g_tok, beta = gdn_scalars(128)
                bank = psb()
                k.mm(bank[:, 0:8], tri.v, g_tok.v)
                gc = sm(); evac(gc.v, bank[:, 0:8])
                bank = psb()
                k.mm(bank[:, 0:8], lastsel.v, gc.v)
                dd = sm(); k.tt("dve", dd.v, bank[:, 0:8], gc.v, ALU.subtract)
                edec = sm(); k.act(edec.v, dd.v, AF.Exp)
                egc = sm(); k.act(egc.v, gc.v, AF.Exp)
                bge = sm(); k.tt("dve", bge.v, beta.v, egc.v, ALU.mult)
                b3 = lambda t: t.v.un(2).bto([128, 8, 64])
                r3 = lambda t: t.v.re("p (h d) -> p h d", h=8)
                k.tt("dve", r3(vb_tok), r3(v_tok), b3(beta), ALU.mult)
                k.tt("pool", r3(kdec_tok), r3(k_tok), b3(edec), ALU.mult)
                k.tt("pool", r3(kbg_tok), r3(k_tok), b3(bge), ALU.mult)
                bank = psb()
                k.tr(bank[0:8, 0:128], gc.v, identf.v)
                k.tr(bank[0:8, 128:256], beta.v, identf.v)
                evac(gcT8.v, bank[0:8, 0:128]); evac(betaT8.v, bank[0:8, 128:256])
                bank = psb()
                for pr in range(4):
                    k.mm(bank[:, pr * 128:(pr + 1) * 128], selpair[:, pr, :], gcT8.v)
                k.act(egc_fm.v, v4(bank), AF.Exp)
                bank = psb()
                for pr in range(4):
                    k.mm(bank[:, pr * 128:(pr + 1) * 128], selpair[:, pr, :], betaT8.v)
                k.tt("dve", kbT.v, kTg.v, v4(bank), ALU.mult)
                k.tt("dve", qgT.v, qTg.v, egc_fm.v, ALU.mult)
                R = lambda h: slice((h % 2) * 64, (h % 2) * 64 + 64)
                st8 = []
                for hg in range(2):
                    hs = [hg * 4 + i for i in range(4)]
                    r4 = r4s[hg]
                    bcb = psb()
                    for i, h in enumerate(hs):
                        k.mm(bcb[:, i * 128:(i + 1) * 128], sel8[:, h, :], gcT8.v)
                    d1 = g4()
                    k.tt("dve", d1.v, v4(bcb), gc[:, hg * 4:hg * 4 + 4].un(2).bto([128, 4, 128]), ALU.subtract)
                    e1 = g4()
                    k.tt("dve", e1.v, d1.v, negs.v.un(1).bto([128, 4, 128]), ALU.max)
                    Dm = g4()
                    k.act(Dm.v, e1.v, AF.Exp, scale=-1.0)
                    k.tt("dve", d1.v, d1.v, negt.v.un(1).bto([128, 4, 128]), ALU.min)
                    DTm = e1
                    k.act(DTm.v, d1.v, AF.Exp)
                    Bt = r4(); Ct = r4(); St = r4()
                    k.tt("pool", d1.v, DTm.v, identf.v.un(1).bto([128, 4, 128]), ALU.add)
                    for hp_ in range(2):
                        bKB = psb(); bKBT = psb(); bQKT = psb()
                        for which in range(3):
                            for i, h in enumerate(hs):
                                if h % 2 != hp_:
                                    continue
                                pr = h // 2
                                cs_ = slice((i // 2) * 128, (i // 2 + 1) * 128)
                                if which == 0:
                                    k.mm(bKB[:, cs_], kbT[R(h), pr, :], kT_b[R(h), pr, :])
                                elif which == 1:
                                    k.mm(bKBT[:, cs_], kT_b[R(h), pr, :], kbT[R(h), pr, :])
                                else:
                                    k.mm(bQKT[:, cs_], kT_b[R(h), pr, :], qT_b[R(h), pr, :])
                        v2 = lambda b_: b_[:, 0:256].re("p (c t) -> p c t", c=2)
                        k.stt(Bt[:, hp_::2, :], v2(bKB), -1.0, Dm[:, hp_::2, :], ALU.mult, ALU.mult)
                        k.stt(Ct[:, hp_::2, :], v2(bKBT), -1.0, DTm[:, hp_::2, :], ALU.mult, ALU.mult)
                        k.tt("dve", qkT_all[:, hg * 4 + hp_:hg * 4 + 4:2, :], v2(bQKT), d1[:, hp_::2, :], ALU.mult)
                    k.tt("pool", St.v, Ct.v, identf.v.un(1).bto([128, 4, 128]), ALU.add)
                    st8.append([Bt, Ct, St])
                    if hg == 1:
                        pump(1)
                for lvl in range(1, 6):
                    nBs = []
                    for hg in range(2):
                        Bt, Ct, St = st8[hg]
                        r4 = r4s[hg]
                        bB = psb()
                        for i in range(4):
                            k.mm(bB[:, i * 128:(i + 1) * 128], Ct[:, i, :], Bt[:, i, :])
                        nB = r4()
                        k.act(nB.v, v4(bB), AF.Copy)
                        nC = None
                        if lvl < 5:
                            bC = psb()
                            for i in range(4):
                                k.mm(bC[:, i * 128:(i + 1) * 128], Bt[:, i, :], Ct[:, i, :])
                            nC = r4()
                            k.act(nC.v, v4(bC), AF.Copy)
                        nBs.append((nB, nC))
                        pump(1)
                    for hg in range(2):
                        Bt, Ct, St = st8[hg]
                        nB, nC = nBs[hg]
                        r4 = r4s[hg]
                        bS = psb()
                        for i in range(4):
                            k.mm(bS[:, i * 128:(i + 1) * 128], nB[:, i, :], St[:, i, :])
                        if lvl < 5:
                            nS = r4()
                            k.tt("dve", nS.v, v4(bS), St.v, ALU.add)
                            st8[hg] = [nB, nC, nS]
                        else:
                            k.tt("dve", U_all[:, hg * 4:hg * 4 + 4, :], v4(bS), St.v, ALU.add)
                    pump(1)
                ub = psb(); wb = psb()
                for h in range(8):
                    pr = h // 2; Rh = slice((h % 2) * 64, (h % 2) * 64 + 64)
                    k.mm(ub[:, h * 64:(h + 1) * 64], U_all[:, h, :], vb_tok[:, h * 64:(h + 1) * 64])
                    k.mm(wb[Rh, pr * 128:(pr + 1) * 128], kbg_tok[:, h * 64:(h + 1) * 64], U_all[:, h, :])
                evac(u_tok.v, ub.v)
                k.act(nwT.v, v4(wb), AF.Copy, scale=-1.0)
                for ci in range(2):
                    RR = slice(ci * 64, ci * 64 + 64)
                    vbk = psb()
                    for pr in range(4):
                        k.mm(vbk[RR, pr * 128:(pr + 1) * 128], nwT[:, pr, RR], S2b[:, pr, :])
                    k.tt("dve", vnew_tok[RR, :], vbk[RR, :], u_tok[RR, :], ALU.add)
                    obk = psb()
                    for pr in range(4):
                        k.mm(obk[RR, pr * 128:(pr + 1) * 128], qgT[:, pr, RR], S2b[:, pr, :], start=True, stop=False)
                        for hp in range(2):
                            h = 2 * pr + hp
                            k.mm(obk[RR, pr * 128 + hp * 64:pr * 128 + (hp + 1) * 64], qkT_all[RR, h, RR],
                                 vnew_tok[RR, h * 64:(h + 1) * 64], start=False, stop=(hp == 1))
                    k.act(G.o_tok[RR, :], obk[RR, :], AF.Copy)
                    sbk = psb()
                    for pr in range(4):
                        cs = slice(pr * 128, (pr + 1) * 128)
                        k.mm(sbk[:, cs], kdec_tok[RR, cs], vnew_tok[RR, cs])
                    k.tt("dve", t4.v, v4(sbk), headblk.v.un(1).bto([128, 4, 128]), ALU.mult)
                    k.tt("pool", S2.v, S2.v, egc_fm[:, :, ci * 64 + 63:ci * 64 + 64].bto([128, 4, 128]), ALU.mult)
                    k.tt("pool", S2.v, S2.v, t4.v, ALU.add)
                    k.copy("pool", S2b.v, S2.v)
                    pump(1)
                gdn_out_tok(G, 128)
                pb = transpose_bf(None, G.og_bf.v, 128, 4)
                evac(omixA[:, :, tl * 128:(tl + 1) * 128], pb[:, 0:512].re("p (j t) -> p j t", j=4))
                pump(len(inj))

            for st_ in f1_steps(0):
                st_()
            for bi in range(NBLK):
                tl = bi % 4
                gdn_block(tl, f1_steps(bi + 1) if bi + 1 < NBLK else [])
                if tl == 3:
                    t = bi // 4
                    k.dma("sp", omix_d.v[:, 0:4, t * 512:(t + 1) * 512], omixA.v)
            for pr in range(4):
                for hp in range(2):
                    RH = slice(hp * 64, hp * 64 + 64)
                    k.dma("sp", ogdn_d.v[2 * pr + hp], S2[RH, pr, hp * 64:hp * 64 + 64])

        def pass_b():
            psb.items = banks[:5]
            w_in = wload_cols("w_in", 1024, C_QA, IN_DIM)
            w_qb = wload_cols("w_q_b", 384, 0, 768)
            w_kvb = wload_cols("w_kv_b", 256, 0, 1024)
            M = alloc_mla(w_qb, w_kvb)
            Mk = alloc_mla(w_qb, w_kvb)
            Mk.cos_t = M.cos_t; Mk.sin_t = M.sin_t
            sq_t2 = k.sb("sq_t2", [128, 512], F32)
            kT = k.sb("kT", [128, 8, S], BF16)
            v_sb = k.sb("v_sb", [128, NBLK, 8, 64], BF16)
            qT_tile = k.sb("qT_tile", [128, 8, 512], BF16)
            omixB = k.sb("omixB", [128, 4, 512], BF16)
            pT_r = Rot([k.sb(f"pT{i}", [128, 512], BF16) for i in range(4)])
            rl_t = k.sb("rl_t", [128, 512], F32)

            def mla_block(bi, tl):
                t0 = bi * 128

                def chain_q():
                    mla_q(M, 128, C["cos_p"].v[t0:t0 + 128, :], C["sin_p"].v[t0:t0 + 128, :])
                    k.copy("pool", M.hh_bf.v, M.hh.v)
                    bank = psb(); pb = bank.v.bc(BF16)
                    for h in range(8):
                        k.tr(pb[0:96, h * 128:(h + 1) * 128], M.hh_bf[:, h, :], identb.v)
                    evac(qT_tile[0:96, :, tl * 128:(tl + 1) * 128], pb[0:96, 0:1024].re("p (h t) -> p h t", h=8))

                def chain_k():
                    mla_kv(Mk, 128, ockv_d.v[t0:t0 + 128, :], okr_d.v[t0:t0 + 128, :])
                    for (c0, c1) in ((0, 512), (512, 1024)):
                        bank = psb()
                        for kk in range(2):
                            k.mm(bank[:, 0:512], Mk.ckvT[:, kk, :], w_kvb[:, kk, c0:c1], start=(kk == 0), stop=(kk == 1))
                        evac(Mk.qkv_tok[:, c0:c1], bank.v)
                    kv3 = Mk.qkv_tok.v.re("p (h d) -> p h d", h=8)
                    head_rms(Mk.hh[:, :, 0:64], kv3[:, :, 0:64], g_k_nope.v, 128, 64, scratch=sq_t2)
                    k.copy("pool", Mk.hh[:, :, 64:96], Mk.kr_f.v.un(1).bto([128, 8, 32]))
                    k.copy("pool", Mk.hh_bf.v, Mk.hh.v)
                    k.copy("dve", v_sb[:, bi, :, :], kv3[:, :, 64:128])
                    bank = psb(); pb = bank.v.bc(BF16)
                    for h in range(8):
                        k.tr(pb[0:96, h * 128:(h + 1) * 128], Mk.hh_bf[:, h, :], identb.v)
                    evac(kT[0:96, :, t0:t0 + 128], pb[0:96, 0:1024].re("p (h t) -> p h t", h=8))

                k.rec = []
                chain_q()
                rq = k.rec
                k.rec = []
                chain_k()
                rk = k.rec
                k.rec = None
                merged = []
                nq, nk = len(rq), len(rk)
                iq = ik = 0
                while iq < nq or ik < nk:
                    if ik >= nk or (iq < nq and iq * nk <= ik * nq):
                        merged.append(rq[iq]); iq += 1
                    else:
                        merged.append(rk[ik]); ik += 1
                k.play(merged)

            def attention_tile(t):
                for pr in range(4):
                    o_ps = banks[5]; l_ps = banks[6]
                    for hp in range(2):
                        h = 2 * pr + hp
                        RH = slice(hp * 64, hp * 64 + 64)
                        nkb = 4 * t + 4
                        def stepA(j):
                            qlo = max(0, j - 4 * t)
                            ncol = (4 - qlo) * 128
                            qc = slice(qlo * 128, 512)
                            sc = psb()
                            k.mm(sc[:, 0:ncol], kT[0:96, h, j * 128:(j + 1) * 128], qT_tile[0:96, h, qc])
                            pT = pT_r()
                            k.act(pT[:, 0:ncol], sc[:, 0:ncol], AF.Exp)
                            if j >= 4 * t:
                                k.tt("pool", pT[:, 0:128], pT[:, 0:128], cmask.v, ALU.mult)
                            return pT, ncol, qc

                        def stepB(j, pT, ncol, qc):
                            k.mm(o_ps[RH, qc], v_sb[:, j, h, :], pT[:, 0:ncol], start=(j == 0), stop=(j == nkb - 1))
                            k.mm(l_ps[RH, qc], onesb.v, pT[:, 0:ncol], start=(j == 0), stop=(j == nkb - 1))

                        pend = stepA(0)
                        for j in range(nkb):
                            cur = pend
                            if j + 1 < nkb:
                                pend = stepA(j + 1)
                            stepB(j, *cur)
                    k.recip(rl_t.v, l_ps.v)
                    k.tt("dve", omixB[:, pr, :], o_ps.v, rl_t.v, ALU.mult)

            for bi in range(NBLK):
                tl = bi % 4
                t0 = bi * 128
                inproj(x_d.v[t0:t0 + 128, :], 128, w_in, C_QA, GROUPS_B)
                mla_block(bi, tl)
                if tl == 3:
                    t = bi // 4
                    attention_tile(t)
                    k.dma("sp", omix_d.v[:, 4:8, t * 512:(t + 1) * 512], omixB.v)

        k.es_saved = esP1
        es1_cur = [None]
        for name_, fn_ in (("sample", sample_scope), ("a", pass_a), ("b", pass_b)):
            es1 = contextlib.ExitStack()
            with es1:
                k.es = es1
                es1_cur[0] = es1
                fn_()
                k.barrier()
            k.es = k.es_saved
            if stop_after == name_:
                break
        psb.items = banks[:7]
        esP1.__exit__(None, None, None)
        k.es = k.es_outer

        if stop_after is None:
            w_dn = wload("w_ffn_down", D_FF, 1024)
            wg_scr = k.dram("wg_scr", [128, 8, D_FF], BF16, "Internal")
            wu_scr = k.dram("wu_scr", [128, 8, D_FF], BF16, "Internal")
            first_stream = [True]
            g_ffn = bload("g_ffn", 1024); g_ple = bload("g_ple", 1024)
            wg_r = Rot([k.sb(f"wg{i}", [128, 8, 512], BF16) for i in range(2)])
            wu_r = Rot([k.sb(f"wu{i}", [128, 8, 512], BF16) for i in range(2)])
            om_t = k.sb("om_t", [128, 8, 512], BF16)
            h1 = k.sb("h1", [128, 4, 1024], F32)
            un = k.sb("un", [128, 1024], BF16)
            uT = k.sb("uT", [128, 8, 512], BF16)
            hT = k.sb("hT", [128, 22, 512], BF16)
            sg = k.sb("sg", [128, 512], F32)
            sg2 = k.sb("sg2", [128, 512], F32)
            sg_rot = Rot([sg, sg2])
            p_t = k.sb("p_t", [128, 256], F32)
            p_bf = k.sb("p_bf", [128, 256], BF16)
            pT2 = k.sb("pT2", [128, 2, 128], BF16)
            gate_t = sg
            wgs = W["w_ffn_gate"].v.re("(k p) n -> p k n", p=128)
            wus = W["w_ffn_up"].v.re("(k p) n -> p k n", p=128)

            def phase2_tile(nt, om_src, x_src, p_src, y_dst):
                nb = (nt + 127) // 128
                bt = min(nt, 128)
                rows = slice(0, bt)
                k.dma("sp", om_t[:, :, 0:nt], om_src)
                for b in range(nb):
                    k.dma("sp", h1[rows, b, :], x_src(b))
                for b in range(nb):
                    cb = slice(b * 128, b * 128 + bt)
                    for hf in range(2):
                        bank = psb()
                        for kk in range(8):
                            k.mm(bank[rows, :], om_t[:, kk, cb], w_o[:, kk, hf * 512:(hf + 1) * 512], start=(kk == 0), stop=(kk == 7))
                        k.tt("dve", h1[rows, b, hf * 512:(hf + 1) * 512], bank[rows, :], h1[rows, b, hf * 512:(hf + 1) * 512], ALU.add)
                    rmsnorm_rows(un[rows, :], h1[rows, b, :], g_ffn[rows, :], 1024, rows)
                    pb = transpose_bf(None, un[rows, :], bt, 8)
                    evac(uT[:, :, cb], pb[:, 0:1024].re("p (j t) -> p j t", j=8)[:, :, 0:bt])
                for c0 in range(0, D_FF, 512):
                    c1 = min(D_FF, c0 + 512)
                    wg = wg_r(); wu = wu_r()
                    if first_stream[0]:
                        k.dma("pool", wg[:, :, 0:c1 - c0], wgs[:, :, c0:c1])
                        k.dma("pool", wu[:, :, 0:c1 - c0], wus[:, :, c0:c1])
                        k.dma("sp", wg_scr.v[:, :, c0:c1], wg[:, :, 0:c1 - c0])
                        k.dma("sp", wu_scr.v[:, :, c0:c1], wu[:, :, 0:c1 - c0])
                    else:
                        k.dma("sp", wg[:, :, 0:c1 - c0], wg_scr.v[:, :, c0:c1])
                        k.dma("sp", wu[:, :, 0:c1 - c0], wu_scr.v[:, :, c0:c1])
                    for m in range(c0 // 128, c1 // 128):
                        ms_ = slice(m * 128 - c0, (m + 1) * 128 - c0)
                        bg = psb(); bu = psb()
                        for kk in range(8):
                            k.mm(bg[:, 0:nt], wg[:, kk, ms_], uT[:, kk, 0:nt], start=(kk == 0), stop=(kk == 7))
                        for kk in range(8):
                            k.mm(bu[:, 0:nt], wu[:, kk, ms_], uT[:, kk, 0:nt], start=(kk == 0), stop=(kk == 7))
                        sgt = sg_rot()
                        k.act(sgt[:, 0:nt], bg[:, 0:nt], AF.Silu)
                        k.tt("dve", hT[:, m, 0:nt], sgt[:, 0:nt], bu[:, 0:nt], ALU.mult)
                for b in range(nb):
                    cb = slice(b * 128, b * 128 + bt)
                    for hf in range(2):
                        bank = psb()
                        for m in range(22):
                            k.mm(bank[rows, :], hT[:, m, cb], w_dn[:, m, hf * 512:(hf + 1) * 512], start=(m == 0), stop=(m == 21))
                        k.tt("dve", h1[rows, b, hf * 512:(hf + 1) * 512], bank[rows, :], h1[rows, b, hf * 512:(hf + 1) * 512], ALU.add)
                    rmsnorm_rows(un[rows, :], h1[rows, b, :], g_ple[rows, :], 1024, rows)
                    pb = transpose_bf(None, un[rows, :], bt, 8)
                    evac(uT[:, :, cb], pb[:, 0:1024].re("p (j t) -> p j t", j=8)[:, :, 0:bt])
                    k.dma("sp", p_t[rows, :], p_src(b))
                    k.copy("pool", p_bf[rows, :], p_t[rows, :])
                    pb = transpose_bf(None, p_bf[rows, :], bt, 2)
                    evac(pT2[:, :, 0:bt], pb[:, 0:256].re("p (j t) -> p j t", j=2)[:, :, 0:bt])
                    for hf in range(2):
                        hs_ = slice(hf * 512, (hf + 1) * 512)
                        bank = psb()
                        for kk in range(8):
                            k.mm(bank[rows, :], uT[:, kk, cb], w_pg[:, kk, hs_], start=(kk == 0), stop=(kk == 7))
                        k.act(gate_t[rows, :], bank[rows, :], AF.Sigmoid)
                        bank2 = psb()
                        for kk in range(2):
                            k.mm(bank2[rows, :], pT2[:, kk, 0:bt], w_pp[:, kk, hs_], start=(kk == 0), stop=(kk == 1))
                        k.tt("dve", gate_t[rows, :], bank2[rows, :], gate_t[rows, :], ALU.mult)
                        k.tt("pool", h1[rows, b, hs_], gate_t[rows, :], h1[rows, b, hs_], ALU.add)
                    k.dma("sp", y_dst(b), h1[rows, b, :])

            phase2_tile(4, omixs_d.v, lambda b: xs_d.v, lambda b: psm_d.v, lambda b: ys_d.v)
            first_stream[0] = False
            for t in range(NT):
                q0 = t * 512
                phase2_tile(512, omix_d.v[:, :, q0:q0 + 512],
                            lambda b: x_d.v[q0 + b * 128:q0 + (b + 1) * 128, :],
                            lambda b: p_d.v[q0 + b * 128:q0 + (b + 1) * 128, :],
                            lambda b: y_d.v[q0 + b * 128:q0 + (b + 1) * 128, :])
        k.finish()
    return nc, k


_NC_CACHE = {}


def run_cores(inputs, S, NPG, NPHYS, stop_after=None, ncores=8):
    key = (S, NPG, NPHYS, stop_after)
    if key not in _NC_CACHE:
        _NC_CACHE[key] = build(S, NPG, NPHYS, stop_after)
    nc, kb = _NC_CACHE[key]
    f = lambda a: np.ascontiguousarray(np.asarray(a))
    consts = host_consts(S, NPG * 128)
    cache_cat = np.concatenate([np.asarray(inputs["cache_krope"][0]).reshape(NPHYS * 128, 32),
                                np.asarray(inputs["cache_ckv"][0]).reshape(NPHYS * 128, 256)], axis=1)
    cache_cat = np.ascontiguousarray(cache_cat, dtype=np.float32)
    wmap = {n: f(inputs[n][0]).reshape(s) for n, s in W_SHAPES.items()}
    in_maps = []
    for c in range(ncores):
        m = dict(wmap)
        m.update(consts)
        m["x"] = f(inputs["x_prompt"][c])
        m["xs"] = f(inputs["x_sample"][4 * c:4 * c + 4, 0])
        m["p"] = f(inputs["p_prompt"][0, c])
        m["psm"] = f(inputs["p_sample"][0, 4 * c:4 * c + 4, 0])
        m["cache_cat"] = cache_cat
        m["state_gdn"] = f(inputs["state_gdn"][0, 4 * c:4 * c + 4])
        m["state_conv"] = f(inputs["state_conv"][0, 4 * c:4 * c + 4]).reshape(4, 3 * 1536)
        m["pt"] = f(inputs["page_table"][4 * c:4 * c + 4]).astype(np.int32)
        in_maps.append(m)
    res = run_bass_kernel_spmd(nc, in_maps, core_ids=list(range(ncores))).results
    g = lambda n: np.stack([np.asarray(r[n]) for r in res])
    y = g("y"); ys = g("ys").reshape(4 * ncores, 1, 1024)
    return (y, ys, g("o_ckv")[None], g("o_kr")[None], g("o_gdn")[None], g("o_conv")[None],
            g("o_ckv_s").reshape(1, 4 * ncores, 1, 256), g("o_kr_s").reshape(1, 4 * ncores, 1, 32),
            g("o_gdn_s").reshape(1, 4 * ncores, 8, 64, 64), g("o_conv_s").reshape(1, 4 * ncores, 3, 1536))


def kernel(**inputs):
    S = inputs["x_prompt"].shape[1]
    NPG = inputs["page_table"].shape[1]
    NPHYS = inputs["cache_ckv"].shape[1]
    outs = run_cores(inputs, S, NPG, NPHYS)
    return tuple(np.ascontiguousarray(o.astype(np.float32)) for o in outs)
```

```python
import contextlib
import numpy as np
import concourse.bass as bass
import concourse.mybir as mybir

F32 = mybir.dt.float32
F32R = mybir.dt.float32r
BF16 = mybir.dt.bfloat16
I32 = mybir.dt.int32
AF = mybir.ActivationFunctionType
ALU = mybir.AluOpType
AX = mybir.AxisListType
SKIP_SELF_WAIT = False


class V:
    __slots__ = ("t", "ap")

    def __init__(self, t, ap):
        self.t = t
        self.ap = ap

    def __getitem__(self, idx):
        return V(self.t, self.ap[idx])

    def re(self, pat, **kw):
        return V(self.t, self.ap.rearrange(pat, **kw))

    def bc(self, dtype):
        return V(self.t, self.ap.bitcast(dtype))

    def un(self, axis):
        return V(self.t, self.ap.unsqueeze(axis))

    def bto(self, shape):
        return V(self.t, self.ap.broadcast_to(shape))

    def pb(self, n):
        return V(self.t, self.ap.partition_broadcast(n))


class T:
    def __init__(self, name, h):
        self.name = name
        self.h = h
        self.w = None
        self.r = {}
        self.dsem = None
        self.dtot = 0
        self.psum = False

    def __getitem__(self, idx):
        return V(self, self.h[idx])

    @property
    def v(self):
        return V(self, self.h[:])


class KB:
    def __init__(self, nc, es, needed=None):
        self.nc = nc
        self.es = es
        self.es_sem = es
        self.eng = {"pe": nc.tensor, "act": nc.scalar, "dve": nc.vector, "pool": nc.gpsimd, "sp": nc.sync}
        self.sem = {k: es.enter_context(nc.semaphore("s_" + k)) for k in self.eng}
        self.cnt = {k: 0 for k in self.eng}
        self.waited = {k: {} for k in self.eng}
        self.dma_sems = {}
        self.out_events = []
        self.ninst = 0
        self.needed = needed
        self.rec = None
        self.need_rec = {k: set() for k in self.eng}
        self.rank = {k: 0 for k in self.eng}
        self.rankmap = {k: {} for k in self.eng}
        self.engsem = {id(v): k for k, v in self.sem.items()}

    def sb(self, name, shape, dt):
        self.uid = getattr(self, "uid", 0) + 1
        name = f"{name}_u{self.uid}"
        return T(name, self.es.enter_context(self.nc.sbuf_tensor(name, list(shape), dt)))

    def ps(self, name, shape, dt):
        t = T(name, self.es.enter_context(self.nc.psum_tensor(name, list(shape), dt)))
        t.psum = True
        return t

    def dram(self, name, shape, dt, kind):
        return T(name, self.nc.dram_tensor(name, list(shape), dt, kind=kind).ap())

    def _wait(self, e, ev):
        sem, val = ev
        sid = id(sem)
        if e == "pe" and sem is self.sem["pe"]:
            return
        if SKIP_SELF_WAIT and sem is self.sem.get(e):
            return
        if sid in self.dma_sems:
            val = max(val, self.dma_sems[sid][1])
        w = self.waited[e]
        if w.get(sid, 0) >= val:
            return
        w[sid] = val
        f = self.engsem.get(sid)
        if f is not None:
            self.need_rec[f].add(val)
            if self.needed is not None:
                val = self.rankmap[f][val]
        self.eng[e].wait_ge(sem, val)

    def _deps(self, e, reads, writes):
        for v in reads:
            t = v.t
            if t.w is not None:
                self._wait(e, t.w)
            if t.psum:
                for ev in t.r.values():
                    if ev[0] is not self.sem.get(e):
                        self._wait(e, ev)
        for v in writes:
            t = v.t
            if t.w is not None:
                self._wait(e, t.w)
            for ev in t.r.values():
                self._wait(e, ev)

    def _record(self, ev, reads, writes):
        sem, val = ev
        for v in reads:
            v.t.r[id(sem)] = ev
        for v in writes:
            v.t.w = ev
            v.t.r = {}

    def play(self, items):
        for it in items:
            if it[0] == "op":
                self.op(*it[1:])
            else:
                self.dma(it[1], it[2], it[3], **it[4])

    def op(self, e, fn, reads, writes):
        if self.rec is not None:
            self.rec.append(("op", e, fn, reads, writes))
            return None
        reads = [v for v in reads if isinstance(v, V)]
        self._deps(e, reads, writes)
        inst = fn(self.eng[e])
        self.cnt[e] += 1
        if self.needed is None:
            inst.then_inc(self.sem[e], 1)
        elif self.cnt[e] in self.needed[e]:
            inst.then_inc(self.sem[e], 1)
            self.rank[e] += 1
            self.rankmap[e][self.cnt[e]] = self.rank[e]
        self._record((self.sem[e], self.cnt[e]), reads, writes)
        self.ninst += 1
        return inst

    def dma(self, q, out, in_, **kw):
        if self.rec is not None:
            self.rec.append(("dma", q, out, in_, kw))
            return None
        self._deps(q, [in_], [out])
        own = out.t
        if own.dsem is None:
            own.dsem = self.es_sem.enter_context(self.nc.semaphore("d_" + own.name))
            self.dma_sems[id(own.dsem)] = [own.dsem, 0]
        inst = self.eng[q].dma_start(out=out.ap, in_=in_.ap, **kw)
        own.dtot += 16
        self.dma_sems[id(own.dsem)][1] = own.dtot
        inst.then_inc(own.dsem, 16)
        ev = (own.dsem, own.dtot)
        self._record(ev, [in_], [out])
        self.ninst += 1
        return ev

    def gather(self, out, in_, idx, **kw):
        q = "pool"
        self._deps(q, [in_, idx], [out])
        own = out.t
        if own.dsem is None:
            own.dsem = self.es_sem.enter_context(self.nc.semaphore("d_" + own.name))
            self.dma_sems[id(own.dsem)] = [own.dsem, 0]
        inst = self.nc.gpsimd.indirect_dma_start(
            out=out.ap, out_offset=None, in_=in_.ap,
            in_offset=bass.IndirectOffsetOnAxis(ap=idx.ap, axis=0), **kw)
        own.dtot += 16
        self.dma_sems[id(own.dsem)][1] = own.dtot
        inst.then_inc(own.dsem, 16)
        ev = (own.dsem, own.dtot)
        self._record(ev, [in_, idx], [out])
        return ev

    def barrier(self):
        for e in self.eng:
            for sem, tot in self.dma_sems.values():
                if tot:
                    self._wait(e, (sem, tot))
            for f in self.eng:
                if f != e and self.cnt[f]:
                    self._wait(e, (self.sem[f], self.cnt[f]))

    def finish(self):
        for sem, tot in self.dma_sems.values():
            if tot:
                self._wait("sp", (sem, tot))
        for e in self.eng:
            if e != "sp" and self.cnt[e]:
                self._wait("sp", (self.sem[e], self.cnt[e]))

    def mm(self, out, lhsT, rhs, start=True, stop=True):
        return self.op("pe", lambda e: e.matmul(out.ap, lhsT=lhsT.ap, rhs=rhs.ap, start=start, stop=stop),
                       [lhsT, rhs] + ([] if start else [out]), [out])

    def tr(self, out, in_, ident):
        return self.op("pe", lambda e: e.transpose(out.ap, in_.ap, ident.ap), [in_, ident], [out])

    def act(self, out, in_, func, scale=1.0, bias=0.0, accum=None, eng="act"):
        kw = {}
        if accum is not None:
            kw["accum_out"] = accum.ap
        sc = scale.ap if isinstance(scale, V) else scale
        bi = bias.ap if isinstance(bias, V) else bias
        return self.op("act", lambda e: e.activation(out=out.ap, in_=in_.ap, func=func, scale=sc, bias=bi, **kw),
                       [in_, scale, bias], [out] + ([accum] if accum is not None else []))

    def tt(self, e, out, in0, in1, op):
        return self.op(e, lambda g: g.tensor_tensor(out=out.ap, in0=in0.ap, in1=in1.ap, op=op), [in0, in1], [out])

    def ts(self, e, out, in0, s1, op0, s2=None, op1=None, accum=None):
        a1 = s1.ap if isinstance(s1, V) else s1
        a2 = s2.ap if isinstance(s2, V) else s2
        kw = {}
        if op1 is not None:
            kw["op1"] = op1
        if accum is not None:
            kw["accum_out"] = accum.ap
        return self.op(e, lambda g: g.tensor_scalar(out=out.ap, in0=in0.ap, scalar1=a1, scalar2=a2, op0=op0, **kw),
                       [in0, s1, s2], [out] + ([accum] if accum is not None else []))

    def stt(self, out, in0, scalar, in1, op0, op1, e="dve"):
        sc = scalar.ap if isinstance(scalar, V) else scalar
        return self.op(e, lambda g: g.scalar_tensor_tensor(out=out.ap, in0=in0.ap, scalar=sc, in1=in1.ap, op0=op0, op1=op1),
                       [in0, scalar, in1], [out])

    def copy(self, e, out, in_):
        if e == "act":
            return self.act(out, in_, AF.Copy)
        return self.op(e, lambda g: g.tensor_copy(out=out.ap, in_=in_.ap), [in_], [out])

    def memset(self, e, out, val):
        return self.op(e, lambda g: g.memset(out.ap, val), [], [out])

    def reduce(self, out, in_, op=ALU.add, axis=AX.X, e="dve"):
        return self.op(e, lambda g: g.tensor_reduce(out=out.ap, in_=in_.ap, axis=axis, op=op), [in_], [out])

    def recip(self, out, in_):
        return self.op("dve", lambda g: g.reciprocal(out=out.ap, in_=in_.ap), [in_], [out])

from concourse.bass_utils import run_bass_kernel_spmd

D_MODEL = 1024; PLE_DIM = 256
NH = 8; DK = 64; CONV_DIM = 1536
Q_LORA = 384; KV_LORA = 256; ROPE = 32; NOPE = 64; QK = 96
IN_DIM = 2736; D_FF = 2816
EPS = 1e-6
ATTN_SCALE = QK ** -0.5
C_AB = 1536; C_Z = 1552; C_QA = 2064; C_KVA = 2448
BIG = 1.0e4


class Rot:
    def __init__(self, items):
        self.items = items
        self.i = 0

    def __call__(self):
        t = self.items[self.i % len(self.items)]
        self.i += 1
        return t


def host_consts(S, past):
    c = {}
    i = np.arange(128)
    same = (i[:, None] // 64) == (i[None, :] // 64)
    c["ident"] = np.eye(128, dtype=np.float32)
    c["tri"] = (same & (i[:, None] <= i[None, :])).astype(np.float32)
    c["lastsel"] = (i[:, None] == (i[None, :] // 64) * 64 + 63).astype(np.float32)
    vis = same & (i[None, :] < i[:, None])
    c["negs"] = np.where(vis, 0.0, BIG).astype(np.float32)
    c["negt"] = np.where(vis.T, 0.0, -BIG).astype(np.float32)
    c["headblk"] = same.astype(np.float32)
    sel8 = np.zeros((8, 8, 128), np.float32)
    for h in range(8):
        sel8[h, h, :] = 1.0
    c["sel8"] = sel8
    selp = np.zeros((8, 4, 128), np.float32)
    for h in range(8):
        selp[h, h // 2, (h % 2) * 64:(h % 2) * 64 + 64] = 1.0
    c["selpair"] = selp
    c["cmask"] = (i[None, :] >= i[:, None]).astype(np.float32)
    dm = np.zeros((8, 8, 64), np.float32)
    for h in range(8):
        dm[h, h, :] = 1.0
    c["diagmask"] = dm.reshape(8, 512)
    oh = np.zeros((4, 4, 128), np.float32)
    for b in range(4):
        oh[b, b, :] = 1.0
    c["onehot4"] = oh
    half = ROPE // 2
    inv = (10000.0 ** (-np.arange(half, dtype=np.float32) / half)).astype(np.float32)
    pos = np.arange(S, dtype=np.float32)
    ang = pos[:, None] * inv[None, :]
    c["cos_p"] = np.cos(ang).astype(np.float32)
    c["sin_p"] = np.sin(ang).astype(np.float32)
    angs = (np.float32(past) * inv)[None, :].astype(np.float32)
    c["cos_s"] = np.repeat(np.cos(angs), 4, 0).astype(np.float32)
    c["sin_s"] = np.repeat(np.sin(angs), 4, 0).astype(np.float32)
    return c


CONST_SHAPES = dict(ident=[128, 128], tri=[128, 128], lastsel=[128, 128], negs=[128, 128], negt=[128, 128],
                    headblk=[128, 128], sel8=[8, 8, 128], selpair=[8, 4, 128], cmask=[128, 128],
                    diagmask=[8, 512], onehot4=[4, 4, 128], cos_s=[4, 16], sin_s=[4, 16])

W_SHAPES = dict(g_attn=[1, 1024], w_in=[1024, IN_DIM], w_conv=[4, 1536], gdn_a_log=[1, 8], gdn_dt_bias=[1, 8],
                g_gdn_out=[1, 64], g_q_a=[1, 384], w_q_b=[384, 768], g_q_nope=[1, 64], g_q_rope=[1, 32],
                g_kv_a=[1, 256], g_k_rope=[1, 32], w_kv_b=[256, 1024], g_k_nope=[1, 64], w_o=[1024, 1024],
                g_ffn=[1, 1024], w_ffn_gate=[1024, D_FF], w_ffn_up=[1024, D_FF], w_ffn_down=[D_FF, 1024],
                g_ple=[1, 1024], w_ple_gate=[1024, 1024], w_ple_proj=[256, 1024])


def build(S, NPG, NPHYS, stop_after=None):
    _, k1 = build1(S, NPG, NPHYS, stop_after, None)
    return build1(S, NPG, NPHYS, stop_after, k1.need_rec)


INV_DT = BF16
NDUMMY = 0


def build1(S, NPG, NPHYS, stop_after, needed):
    import os
    DBG = int(os.environ.get('KDBG', '0'))
    NBLK = S // 128
    NT = S // 512
    nc = bass.Bass("TRN2", target_bir_lowering=False)
    es = contextlib.ExitStack()
    with es:
        nc_lp = es.enter_context(nc.allow_low_precision("bf16 matmul operands by design"))
        es.enter_context(nc.allow_non_contiguous_dma("small strided state loads/stores"))
        k = KB(nc, es, needed)
        din = {}

        def DI(name, shape, dt=F32):
            din[name] = k.dram(name, shape, dt, "ExternalInput")
            return din[name]

        def DO(name, shape, dt=F32):
            return k.dram(name, shape, dt, "ExternalOutput")

        x_d = DI("x", [S, 1024]); xs_d = DI("xs", [4, 1024])
        p_d = DI("p", [S, 256]); psm_d = DI("psm", [4, 256])
        ccat_d = DI("cache_cat", [NPHYS * 128, 288])
        sgdn_d = DI("state_gdn", [4, 8, 64, 64]); sconv_d = DI("state_conv", [4, 3 * 1536])
        pt_d = DI("pt", [4, NPG], I32)
        W = {n: DI(n, s) for n, s in W_SHAPES.items()}
        C = {n: DI(n, s) for n, s in CONST_SHAPES.items()}
        C["cos_p"] = DI("cos_p", [S, 16]); C["sin_p"] = DI("sin_p", [S, 16])
        y_d = DO("y", [S, 1024]); ys_d = DO("ys", [4, 1024])
        ockv_d = DO("o_ckv", [S, 256]); okr_d = DO("o_kr", [S, 32])
        ogdn_d = DO("o_gdn", [8, 64, 64]); oconv_d = DO("o_conv", [3, 1536])
        ockvs_d = DO("o_ckv_s", [4, 256]); okrs_d = DO("o_kr_s", [4, 32])
        ogdns_d = DO("o_gdn_s", [4, 8, 64, 64]); oconvs_d = DO("o_conv_s", [4, 3 * 1536])
        omix_d = k.dram("omix_scr", [128, 8, S], BF16, "ExternalOutput" if DBG == 99 else "Internal")
        omixs_d = k.dram("omixs_scr", [128, 8, 4], BF16, "Internal")

        banks = [k.ps(f"ps{i}", [128, 512], F32) for i in range(8)]
        psb = Rot(banks[:7])
        accbank = banks[7]
        evi = [0]

        def evac(out, in_, scale=None):
            evi[0] += 1
            if scale is not None:
                return k.act(out, in_, AF.Copy, scale=scale)
            if evi[0] % 3:
                return k.act(out, in_, AF.Copy)
            return k.copy("dve", out, in_)

        def cload(name, shape, dt=F32, src=None, q="sp"):
            t = k.sb("c_" + name, shape, dt)
            k.dma(q, t.v, (src if src is not None else C[name].v))
            return t

        identf = cload("ident", [128, 128])
        identb = k.sb("identb", [128, 128], BF16); k.copy("dve", identb.v, identf.v)
        identr = k.sb("identr", [128, 128], F32R); k.copy("dve", identr.v, identf.v)
        tri = cload("tri", [128, 128]); lastsel = cload("lastsel", [128, 128])
        negs = cload("negs", [128, 128]); negt = cload("negt", [128, 128])
        headblk = cload("headblk", [128, 128])
        headblk_b = k.sb("headblk_b", [128, 128], BF16); k.copy("dve", headblk_b.v, headblk.v)
        sel8 = cload("sel8", [8, 8, 128]); selpair = cload("selpair", [8, 4, 128])
        cmaskf = cload("cmask", [128, 128])
        cmask = k.sb("cmaskb", [128, 128], BF16); k.copy("dve", cmask.v, cmaskf.v)
        diagmask = cload("diagmask", [8, 512]); onehot4 = cload("onehot4", [4, 4, 128])
        onesb = k.sb("onesb", [128, 64], BF16); k.memset("dve", onesb.v, 1.0)
        onesf = k.sb("onesf", [128, 8], F32); k.memset("dve", onesf.v, 1.0)

        def bload(name, F, q="sp"):
            t = k.sb("b_" + name, [128, F], F32)
            k.dma(q, t.v, W[name].v.bto([128, F]))
            return t

        g_attn = bload("g_attn", 1024); g_q_a = bload("g_q_a", 384); g_kv_a = bload("g_kv_a", 256)
        g_q_nope = bload("g_q_nope", 64); g_q_rope = bload("g_q_rope", 32)
        g_k_rope = bload("g_k_rope", 32); g_k_nope = bload("g_k_nope", 64)
        g_gdn_out = bload("g_gdn_out", 64)
        a_log = bload("gdn_a_log", 8); dtb = bload("gdn_dt_bias", 8)
        eA = k.sb("eA", [128, 8], F32); k.act(eA.v, a_log.v, AF.Exp)
        k.ts("dve", g_q_nope.v, g_q_nope.v, ATTN_SCALE, ALU.mult)
        k.ts("dve", g_q_rope.v, g_q_rope.v, ATTN_SCALE, ALU.mult)
        wcv = k.sb("wcv", [128, 12, 4], F32)
        for j in range(4):
            k.dma("sp", wcv[:, :, j], W["w_conv"].v[j:j + 1, :].re("o (c p) -> p (o c)", p=128))

        def wload(name, K, N, q="pool"):
            kc = K // 128
            t = k.sb("w_" + name, [128, kc, N], BF16)
            src = W[name].v.re("(k p) n -> p k n", p=128)
            for i in range(kc):
                for n0 in range(0, N, 1024):
                    n1 = min(N, n0 + 1024)
                    k.dma(q, t[:, i, n0:n1], src[:, i, n0:n1])
            return t

        sm = Rot([k.sb(f"sm{i}", [128, 8], F32) for i in range(24)])

        def rstd_from_ss(ss, n, F, rows):
            a = sm()
            k.ts("dve", a[rows, 0:n], ss, 1.0 / F, ALU.mult, EPS, ALU.add)
            k.act(a[rows, 0:n], a[rows, 0:n], AF.Ln)
            r = sm()
            k.act(r[rows, 0:n], a[rows, 0:n], AF.Exp, scale=-0.5)
            return r[rows, 0:n]

        junk = k.sb("junk", [128, 1024], BF16)

        def rmsnorm_rows(out, x, g, F, rows):
            ss = sm()
            k.act(junk[rows, 0:F], x, AF.Square, accum=ss[rows, 0:1])
            r = rstd_from_ss(ss[rows, 0:1], 1, F, rows)
            k.stt(out, x, r, g, ALU.mult, ALU.mult)

        def transpose_bf(dst_fn, src, nt, nch, width=128):
            bank = psb()
            pb = bank.v.bc(BF16)
            for j in range(nch):
                k.tr(pb[0:width, j * 128:j * 128 + nt], src[:, j * width:(j + 1) * width], identb[0:nt, 0:nt])
            return pb

        from types import SimpleNamespace as NS
        GROUPS_ALL = [(0, 512), (512, 1024), (1024, 1536), (1536, 1552), (1552, 2064), (2064, 2448), (2448, 2736)]
        GROUPS_A = GROUPS_ALL[:5]
        GROUPS_B = GROUPS_ALL[5:]
        if stop_after is None:
            w_o = wload("w_o", 1024, 1024)
            w_pg = wload("w_ple_gate", 1024, 1024)
            w_pp = wload("w_ple_proj", 256, 1024)
        esP1 = contextlib.ExitStack()
        esP1.__enter__()
        k.es_outer = k.es
        k.es = esP1
        x_blk = k.sb("x_blk", [128, 1024], F32)
        xn_t = k.sb("xn", [128, 1024], BF16)
        xnT = k.sb("xnT", [128, 8, 128], BF16)
        z_tok = k.sb("z_tok", [128, IN_DIM], F32)
        sq_t = k.sb("sq_t", [128, 512], F32)

        def inproj(x_src_v, nt, w_in, c_off, groups):
            rows = slice(0, nt)
            k.dma("sp", x_blk[rows, :], x_src_v)
            rmsnorm_rows(xn_t[rows, :], x_blk[rows, :], g_attn[rows, :], 1024, rows)
            pb = transpose_bf(None, xn_t[rows, :], nt, 8)
            evac(xnT[:, :, 0:nt], pb[:, 0:1024].re("p (j t) -> p j t", j=8)[:, :, 0:nt])
            for (c0, c1) in groups:
                bank = psb()
                for kk in range(8):
                    k.mm(bank[rows, 0:c1 - c0], xnT[:, kk, 0:nt], w_in[:, kk, c0 - c_off:c1 - c_off], start=(kk == 0), stop=(kk == 7))
                evac(z_tok[rows, c0:c1], bank[rows, 0:c1 - c0])

        def wload_cols(name, K, c0, c1, q="pool"):
            kc = K // 128
            t = k.sb("w_" + name + f"_{c0}", [128, kc, c1 - c0], BF16)
            src = W[name].v.re("(k p) n -> p k n", p=128)
            for i in range(kc):
                for n0 in range(c0, c1, 1024):
                    n1 = min(c1, n0 + 1024)
                    k.dma(q, t[:, i, n0 - c0:n1 - c0], src[:, i, n0:n1])
            return t

        def head_rms(out, xin, g, nt, width, scratch=None):
            rows = slice(0, nt)
            sq = (scratch or sq_t)[rows, 0:8 * width].re("p (h d) -> p h d", h=8)
            k.act(sq, xin, AF.Square)
            ss = sm()
            k.reduce(ss[rows, 0:8], sq)
            r = rstd_from_ss(ss[rows, 0:8], 8, width, rows)
            k.tt("dve", out, xin, r.un(2).bto([nt, 8, width]), ALU.mult)
            k.tt("pool", out, out, g.un(1).bto([nt, 8, width]), ALU.mult)

        def alloc_mla(w_qb, w_kvb):
            M = NS()
            M.w_qb = w_qb; M.w_kvb = w_kvb
            M.qa_n = k.sb("qa_n", [128, 384], BF16)
            M.qanT = k.sb("qanT", [128, 3, 128], BF16)
            M.qkv_tok = k.sb("qkv_tok", [128, 1024], F32)
            M.hh = k.sb("hh", [128, 8, 96], F32)
            M.hh_bf = k.sb("hh_bf", [128, 8, 96], BF16)
            M.rtmp = k.sb("rtmp", [128, 8, 32], F32)
            M.rtmp2 = k.sb("rtmp2", [128, 8, 16], F32)
            M.cos_t = k.sb("cos_t", [128, 16], F32); M.sin_t = k.sb("sin_t", [128, 16], F32)
            M.ckv_f = k.sb("ckv_f", [128, 256], F32)
            M.ckv_bf = k.sb("ckv_bf", [128, 260], BF16)
            k.memset("dve", M.ckv_bf[:, 256:257], 1.0)
            M.ckvT = k.sb("ckvT", [128, 2, 128], BF16)
            M.kr_f = k.sb("kr_f", [128, 32], F32)
            M.kr_n = k.sb("kr_n", [128, 32], F32)
            return M

        def rope(M, out, xin, nt, nh):
            rows = slice(0, nt)
            cb = M.cos_t[rows, :].un(1).bto([nt, nh, 16]); sb_ = M.sin_t[rows, :].un(1).bto([nt, nh, 16])
            x1 = xin[:, :, 0:16]; x2 = xin[:, :, 16:32]
            t2 = M.rtmp2[rows, 0:nh, :]
            k.tt("dve", out[:, :, 0:16], x1, cb, ALU.mult)
            k.tt("dve", t2, x2, sb_, ALU.mult)
            k.tt("dve", out[:, :, 0:16], out[:, :, 0:16], t2, ALU.subtract)
            k.tt("dve", out[:, :, 16:32], x1, sb_, ALU.mult)
            k.tt("dve", t2, x2, cb, ALU.mult)
            k.tt("dve", out[:, :, 16:32], out[:, :, 16:32], t2, ALU.add)

        def mla_q(M, nt, cos_src, sin_src):
            rows = slice(0, nt)
            k.dma("sp", M.cos_t[rows, :], cos_src); k.dma("sp", M.sin_t[rows, :], sin_src)
            rmsnorm_rows(M.qa_n[rows, :], z_tok[rows, C_QA:C_QA + 384], g_q_a[rows, :], 384, rows)
            pb = transpose_bf(None, M.qa_n[rows, :], nt, 3)
            evac(M.qanT[:, :, 0:nt], pb[:, 0:384].re("p (j t) -> p j t", j=3)[:, :, 0:nt])
            for (c0, c1) in ((0, 512), (512, 768)):
                bank = psb()
                for kk in range(3):
                    k.mm(bank[rows, 0:c1 - c0], M.qanT[:, kk, 0:nt], M.w_qb[:, kk, c0:c1], start=(kk == 0), stop=(kk == 2))
                evac(M.qkv_tok[rows, c0:c1], bank[rows, 0:c1 - c0])
            q3 = M.qkv_tok[rows, 0:768].re("p (h d) -> p h d", h=8)
            head_rms(M.hh[rows, :, 0:64], q3[:, :, 0:64], g_q_nope[rows, :], nt, 64)
            head_rms(M.rtmp[rows, :, :], q3[:, :, 64:96], g_q_rope[rows, :], nt, 32)
            rope(M, M.hh[rows, :, 64:96], M.rtmp[rows, :, :], nt, 8)

        def mla_kv(M, nt, ockv_v, okr_v):
            rows = slice(0, nt)
            rmsnorm_rows(M.ckv_f[rows, :], z_tok[rows, C_KVA:C_KVA + 256], g_kv_a[rows, :], 256, rows)
            k.dma("sp", ockv_v, M.ckv_f[rows, :])
            k.copy("pool", M.ckv_bf[rows, 0:256], M.ckv_f[rows, :])
            rmsnorm_rows(M.kr_n[rows, :], z_tok[rows, C_KVA + 256:C_KVA + 288], g_k_rope[rows, :], 32, rows)
            rope(M, M.kr_f[rows, :].un(1), M.kr_n[rows, :].un(1), nt, 1)
            k.dma("sp", okr_v, M.kr_f[rows, :])
            pb = transpose_bf(None, M.ckv_bf[rows, 0:256], nt, 2)
            evac(M.ckvT[:, :, 0:nt], pb[:, 0:256].re("p (j t) -> p j t", j=2)[:, :, 0:nt])

        def alloc_gdn(small=False):
            G = NS()
            G.cv = k.sb("cv", [128, 12, 128], F32)
            G.sqt = k.sb("sqt", [128, 4, 128], BF16)
            G.rst = k.sb("rst", [128, 4, 128], F32)
            G.qTg = k.sb("qTg", [128, 4, 128], F32)
            G.kTg = k.sb("kTg", [128, 4, 128], F32)
            G.tmp4 = k.sb("tmp4", [128, 4, 128], F32)
            G.sz_t = k.sb("sz_t", [128, 512], F32)
            if not small:
                G.o_tok = k.sb("o_tok", [128, 512], F32)
                G.on_t = k.sb("on_t", [128, 512], F32)
                G.og_bf = k.sb("og_bf", [128, 512], BF16)
            return G

        def gdn_scalars(nt):
            rows = slice(0, nt)
            ta = sm(); k.tt("dve", ta[rows, :], z_tok[rows, C_AB:C_AB + 8], dtb[rows, :], ALU.add)
            e = sm(); k.act(e[rows, :], ta[rows, :], AF.Exp)
            sp_ = sm(); k.act(sp_[rows, :], e[rows, :], AF.Ln, bias=1.0)
            g_tok = sm(); k.stt(g_tok[rows, :], sp_[rows, :], -1.0, eA[rows, :], ALU.mult, ALU.mult)
            beta = sm(); k.act(beta[rows, :], z_tok[rows, C_AB + 8:C_AB + 16], AF.Sigmoid)
            return g_tok, beta

        def l2norm_fm(G, nt):
            for half in range(2):
                k.act(G.sqt[:, :, 0:nt], G.cv[:, half * 4:half * 4 + 4, 0:nt], AF.Square)
                bank = psb()
                for c in range(4):
                    k.mm(bank[:, c * 128:c * 128 + nt], headblk_b.v, G.sqt[:, c, 0:nt])
                k.ts("dve", G.tmp4[:, :, 0:nt], bank.v.re("p (c t) -> p c t", c=4)[:, :, 0:nt], EPS, ALU.add)
                k.act(G.tmp4[:, :, 0:nt], G.tmp4[:, :, 0:nt], AF.Ln)
                k.act(G.rst[:, :, 0:nt], G.tmp4[:, :, 0:nt], AF.Exp, scale=-0.5)
                if half == 0:
                    k.stt(G.qTg[:, :, 0:nt], G.cv[:, 0:4, 0:nt], DK ** -0.5, G.rst[:, :, 0:nt], ALU.mult, ALU.mult)
                else:
                    k.tt("dve", G.kTg[:, :, 0:nt], G.cv[:, 4:8, 0:nt], G.rst[:, :, 0:nt], ALU.mult)

        def gdn_out_tok(G, nt):
            rows = slice(0, nt)
            o3 = G.o_tok[rows, :].re("p (h d) -> p h d", h=8)
            on3 = G.on_t[rows, :].re("p (h d) -> p h d", h=8)
            head_rms(on3, o3, g_gdn_out[rows, :], nt, 64)
            k.tt("dve", G.og_bf[rows, :], G.on_t[rows, :], G.sz_t[rows, :], ALU.mult)

        def sample_scope():
            w_in = wload_cols("w_in", 1024, 0, IN_DIM)
            w_qb = wload_cols("w_q_b", 384, 0, 768)
            w_kvb = wload_cols("w_kv_b", 256, 0, 1024)
            M = alloc_mla(w_qb, w_kvb)
            G = alloc_gdn(small=True)
            nt = 4; rows = slice(0, 4)
            inproj(xs_d.v, 4, w_in, 0, GROUPS_ALL)
            if DBG == 1:
                return
            oms = k.sb("oms", [128, 8, 4], BF16)
            esg = contextlib.ExitStack()
            with esg:
                k.es = esg
                st_tok = k.sb("st_tok", [12, 1536], F32)
                k.dma("sp", st_tok.v, sconv_d.v.re("b (j c) -> (b j) c", j=3))
                k.dma("sp", oconvs_d.v.re("b (j c) -> b j c", j=3)[:, 0:2, :], sconv_d.v.re("b (j c) -> b j c", j=3)[:, 1:3, :])
                k.dma("sp", oconvs_d.v.re("b (j c) -> b j c", j=3)[:, 2, :], z_tok[rows, 0:1536])
                ext = k.sb("ext_fm", [128, 12, 4, 4], F32)
                bank = psb()
                for c in range(12):
                    k.tr(bank[:, c * 12:(c + 1) * 12], st_tok[:, c * 128:(c + 1) * 128], identf[0:12, 0:12])
                evac(ext[:, :, :, 0:3], bank[:, 0:144].re("p (c b j) -> p c b j", c=12, b=4))
                bank = psb()
                for c in range(12):
                    k.tr(bank[:, c * 4:(c + 1) * 4], z_tok[rows, c * 128:(c + 1) * 128], identf[0:4, 0:4])
                evac(ext[:, :, :, 3], bank[:, 0:48].re("p (c b) -> p c b", c=12))
                k.tt("dve", ext.v, ext.v, wcv.v.un(2).bto([128, 12, 4, 4]), ALU.mult)
                cpre = k.sb("cpre", [128, 12, 4], F32)
                k.reduce(cpre.v, ext.v)
                k.act(G.cv[:, :, 0:4], cpre.v, AF.Silu)
                l2norm_fm(G, 4)
                g_tok, beta = gdn_scalars(4)
                eg = sm(); k.act(eg[rows, :], g_tok[rows, :], AF.Exp)
                sm_b = Rot([k.sb(f"smb{i}", [128, 4, 4], F32) for i in range(3)])

                def bc_bh(src):
                    bank = psb()
                    for b in range(4):
                        k.mm(bank[:, b * 8:(b + 1) * 8], onehot4[:, b, :], src)
                    o = sm_b()
                    for hp in range(2):
                        RH = slice(hp * 64, hp * 64 + 64)
                        evac(o[RH, :, :], bank[RH, 0:32].re("p (b pr hp) -> p b pr hp", b=4, pr=4)[:, :, :, hp])
                    return o
                eg_b = bc_bh(eg[rows, :]); beta_b = bc_bh(beta[rows, :])
                st = k.sb("st_s", [128, 4, 4, 64], F32)
                for b in range(4):
                    for hp in range(2):
                        k.dma("sp", st[hp * 64:(hp + 1) * 64, b, :, :],
                              sgdn_d.v[b].re("(pr hp) k v -> hp k pr v", hp=2)[hp])
                B4 = lambda t: t.v.un(3).bto([128, 4, 4, 64])
                k.tt("dve", st.v, st.v, B4(eg_b), ALU.mult)
                tmp = k.sb("tmp_s", [128, 4, 4, 64], F32)
                kcol = G.kTg[:, :, 0:4].re("p pr b -> p b pr")
                k.tt("dve", tmp.v, st.v, kcol.un(3).bto([128, 4, 4, 64]), ALU.mult)
                kSB = k.sb("kSB_s", [128, 4, 4, 64], F32)
                for hf in range(2):
                    bank = psb()
                    k.mm(bank.v, headblk.v, tmp[:, hf * 2:hf * 2 + 2, :, :].re("p b pr v -> p (b pr v)"))
                    evac(kSB[:, hf * 2:hf * 2 + 2, :, :].re("p b pr v -> p (b pr v)"), bank.v)
                v_tk = k.sb("v_tk", [4, 512], F32)
                bank = psb()
                for c in range(4):
                    k.tr(bank[0:4, c * 128:(c + 1) * 128], G.cv[:, 8 + c, 0:4], identf.v)
                evac(v_tk.v, bank[0:4, :])
                vB = k.sb("vB_s", [128, 4, 4, 64], F32)
                for b in range(4):
                    bank = psb()
                    k.mm(bank.v, onehot4[:, b, :], v_tk.v)
                    for hp in range(2):
                        RH = slice(hp * 64, hp * 64 + 64)
                        evac(vB[RH, b, :, :], bank[RH, :].re("p (pr hp v) -> p pr hp v", pr=4, hp=2)[:, :, hp, :])
                k.tt("dve", vB.v, vB.v, kSB.v, ALU.subtract)
                k.tt("dve", vB.v, vB.v, B4(beta_b), ALU.mult)
                k.tt("dve", tmp.v, vB.v, kcol.un(3).bto([128, 4, 4, 64]), ALU.mult)
                k.tt("dve", st.v, st.v, tmp.v, ALU.add)
                for b in range(4):
                    for hp in range(2):
                        k.dma("sp", ogdns_d.v[b].re("(pr hp) k v -> hp k pr v", hp=2)[hp], st[hp * 64:(hp + 1) * 64, b, :, :])
                oT = k.sb("oT_s", [128, 16], F32)
                for hp in range(2):
                    bank = psb()
                    RH = slice(hp * 64, hp * 64 + 64)
                    for b in range(4):
                        for pr in range(4):
                            k.mm(bank[RH, pr * 4 + b:pr * 4 + b + 1], st[RH, b, pr, :], G.qTg[RH, pr, b:b + 1])
                    evac(oT[RH, :], bank[RH, 0:16])
                osq = k.sb("osq_s", [128, 16], F32)
                k.act(osq.v, oT.v, AF.Square)
                bank = psb()
                k.mm(bank[:, 0:16], headblk.v, osq.v)
                a_ = k.sb("a_s_", [128, 16], F32); r_ = k.sb("r_s_", [128, 16], F32)
                k.ts("dve", a_.v, bank[:, 0:16], 1.0 / 64, ALU.mult, EPS, ALU.add)
                k.act(a_.v, a_.v, AF.Ln)
                k.act(r_.v, a_.v, AF.Exp, scale=-0.5)
                k.tt("dve", oT.v, oT.v, r_.v, ALU.mult)
                ggo_col = k.sb("ggo_col", [128, 1], F32)
                for hp in range(2):
                    k.dma("sp", ggo_col[hp * 64:(hp + 1) * 64, :], W["g_gdn_out"].v.re("o d -> d o"))
                k.ts("dve", oT.v, oT.v, ggo_col[:, 0:1], ALU.mult)
                k.act(G.sz_t[rows, :], z_tok[rows, C_Z:C_Z + 512], AF.Silu)
                bank = psb()
                for c in range(4):
                    k.tr(bank[:, c * 4:c * 4 + 4], G.sz_t[rows, c * 128:(c + 1) * 128], identf[0:4, 0:4])
                k.tt("dve", oms[:, 0:4, :], oT.v.re("p (pr b) -> p pr b", pr=4), bank[:, 0:16].re("p (pr b) -> p pr b", pr=4), ALU.mult)
                k.barrier()
            k.es = es1_cur[0]
            mla_q(M, 4, C["cos_s"].v, C["sin_s"].v)
            mla_kv(M, 4, ockvs_d.v, okrs_d.v)
            krb = k.sb("krb_s", [4, 32], BF16)
            k.copy("dve", krb.v, M.kr_f[0:4, :])
            krT_new = k.sb("krT_new", [32, 4], BF16)
            bank = psb(); pb = bank.v.bc(BF16)
            k.tr(pb[0:32, 0:4], krb.v, identb[0:4, 0:4])
            evac(krT_new.v, pb[0:32, 0:4])
            if DBG == 7:
                return
            WkT = k.sb("WkT", [64, 8, 256], BF16)
            wk4 = w_kvb.v.re("p k (h d) -> p k h d", h=8)
            for kk in range(2):
                bank = psb(); pb = bank.v.bc(BF16)
                for h in range(8):
                    k.tr(pb[0:64, h * 128:(h + 1) * 128], wk4[:, kk, h, 0:64], identb.v)
                evac(WkT[:, :, kk * 128:(kk + 1) * 128], pb[0:64, 0:1024].re("p (h t) -> p h t", h=8))
            qg = k.sb("qg_s", [4, 8, 64], BF16)
            k.tt("dve", qg.v, M.hh[rows, :, 0:64], g_k_nope[rows, :].un(1).bto([4, 8, 64]), ALU.mult)
            qr = k.sb("qr_s", [4, 8, 32], BF16)
            k.copy("dve", qr.v, M.hh[rows, :, 64:96])
            bank = psb(); pb = bank.v.bc(BF16)
            for h in range(8):
                k.tr(pb[0:64, h * 4:h * 4 + 4], qg[:, h, :], identb[0:4, 0:4])
                k.tr(pb[0:32, 64 + h * 4:64 + h * 4 + 4], qr[:, h, :], identb[0:4, 0:4])
            qgT = k.sb("qgT_s", [64, 8, 4], BF16); qrT = k.sb("qrT_s", [32, 8, 4], BF16)
            evac(qgT.v, pb[0:64, 0:32].re("p (h b) -> p h b", h=8))
            evac(qrT.v, pb[0:32, 64:96].re("p (h b) -> p h b", h=8))
            bank = psb()
            for kk in range(2):
                for h in range(8):
                    k.mm(bank[:, kk * 32 + h * 4:kk * 32 + h * 4 + 4], WkT[:, h, kk * 128:(kk + 1) * 128], qgT[:, h, :])
            qpT = k.sb("qpT_s", [128, 2, 4, 8], BF16)
            evac(qpT.v, bank[:, 0:64].re("p (k h b) -> p k b h", k=2, h=8))
            if DBG == 8:
                return
            pti = k.sb("pti", [128, 4 * NPG], I32)
            k.dma("sp", pti.v, pt_d.v.re("(o b) j -> o (b j)", o=1).bto([128, 4 * NPG]))
            ptf = k.sb("ptf", [128, 4 * NPG], F32)
            k.copy("dve", ptf.v, pti.v)
            iot = k.sb("iot", [128, 1], F32)
            k.op("pool", lambda g: g.iota(iot.v.ap, pattern=[[0, 1]], base=0, channel_multiplier=1,
                                          allow_small_or_imprecise_dtypes=True), [], [iot.v])
            k.ts("dve", ptf.v, ptf.v, 128.0, ALU.mult, iot[:, 0:1], ALU.add)
            idx = pti
            k.copy("dve", idx.v, ptf.v)
            G_ = 4
            pg_r = Rot([k.sb(f"pg{i}", [128, 292], BF16) for i in range(2 * G_ + 2)])
            for t_ in pg_r.items:
                k.memset("dve", t_[:, 288:289], 1.0)
            sq_r = Rot([k.sb(f"sqp{i}", [128, G_, 512], BF16) for i in range(2)])
            sq1 = sq_r.items[0][:, 0, :]
            p_r = Rot([k.sb(f"pp{i}", [128, G_ * 8], BF16) for i in range(3)])
            sg_r = Rot([k.sb(f"sgp{i}", [128, G_ * 8], F32) for i in range(6)])
            Wkc = k.sb("Wkc", [128, 2, 512], BF16); Wvc = k.sb("Wvc", [128, 2, 512], BF16)
            for kk in range(2):
                k.copy("dve", Wkc[:, kk, :].re("p (h d) -> p h d", h=8), wk4[:, kk, :, 0:64])
                k.copy("dve", Wvc[:, kk, :].re("p (h d) -> p h d", h=8), wk4[:, kk, :, 64:128])
            wk_rhs = lambda kk: Wkc[:, kk, :]
            wv_rhs = lambda kk: Wvc[:, kk, :]
            acc_sb = k.sb("acc_sb", [8, 257], F32)
            accn = k.sb("accn", [8, 256], BF16)
            accT = k.sb("accT", [128, 2, 8], BF16)
            om_f = k.sb("om_f", [8, 512], F32)
            trb = Rot([banks[0]]); bankA = banks[1:5]; bB = banks[5]
            qr_f = k.sb("qr_f", [4, 256], F32)
            k.copy("dve", qr_f.v.re("p (h d) -> p h d", h=8), M.hh[rows, :, 64:96])
            qrB = k.sb("qrB", [128, 4, 256], BF16)
            for b in range(4):
                bank = trb()
                k.mm(bank[:, 0:256], onehot4[:, b, :], qr_f.v)
                evac(qrB[:, b, :], bank[:, 0:256])
            rp_r = Rot([k.sb(f"rp{i}", [128, G_, 256], BF16) for i in range(2)])

            def newtok(b):
                rws = slice(0, 4)
                bA = bankA[0]
                for kk in range(2):
                    k.mm(bA[rws, 0:512], M.ckvT[:, kk, 0:4], wk_rhs(kk), start=(kk == 0), stop=(kk == 1))
                for kk in range(2):
                    k.mm(bB[rws, 0:8], M.ckvT[:, kk, 0:4], qpT[:, kk, b, :], start=(kk == 0), stop=(kk == 1))
                k.mm(bB[rws, 8:16], krT_new[:, 0:4], qrT[:, :, b])
                k.act(sq1[rws, :], bA[rws, :], AF.Square)
                ss = sm()
                k.reduce(ss[rws, :], sq1[rws, :].re("p (h d) -> p h d", h=8))
                r = rstd_from_ss(ss[rws, :], 8, 64, rws)
                s1 = sm()
                k.tt("dve", s1[rws, :], bB[rws, 0:8], r, ALU.mult)
                k.tt("dve", s1[rws, :], s1[rws, :], bB[rws, 8:16], ALU.add)
                s2 = sm()
                k.act(s2[rws, :], s1[rws, :], AF.Exp)
                pp = p_r()
                k.ts("dve", pp[rws, 0:8], s2[rws, :], identf[0:4, b:b + 1], ALU.mult)
                return pp

            trb2 = Rot([banks[0], banks[6]])
            cT4_r = Rot([k.sb(f"cT4_{i}", [128, 4, 2, 128], BF16) for i in range(2)])

            def frontA(b, j0, g):
                pgs = []
                tb = trb2(); pb = tb.v.bc(BF16)
                for i in range(g):
                    pg = pg_r(); pgs.append(pg)
                    col = b * NPG + j0 + i
                    k.gather(pg[:, 0:288], ccat_d.v, idx[:, col:col + 1])
                for i in range(g):
                    k.tr(pb[:, i * 256:i * 256 + 128], pgs[i][:, 32:160], identb.v)
                    k.tr(pb[:, i * 256 + 128:i * 256 + 256], pgs[i][:, 160:288], identb.v)
                cT4 = cT4_r()
                k.act(cT4[:, 0:g, :, :], pb[:, 0:g * 256].re("p (g j t) -> p g j t", g=g, j=2), AF.Copy)
                return pgs, cT4

            def frontB(b, g, pgs, cT4):
                sq = sq_r(); rp = rp_r()
                for i in range(g):
                    for kk in range(2):
                        k.mm(bankA[i][:, 0:512], cT4[:, i, kk, :], wk_rhs(kk), start=(kk == 0), stop=(kk == 1))
                    for kk in range(2):
                        k.mm(bB[:, i * 8:i * 8 + 8], cT4[:, i, kk, :], qpT[:, kk, b, :], start=(kk == 0), stop=(kk == 1))
                    k.act(sq[:, i, :], bankA[i].v, AF.Square)
                    for _d in range(NDUMMY):
                        k.op("pe", lambda e: e.matmul(accbank[64:128, 0:512].ap, lhsT=Wkc[:, 0, 0:64].ap, rhs=Wkc[:, 1, :].ap,
                                                      start=True, stop=True), [], [])
                    k.tt("dve", rp[:, i, :].re("p (h d) -> p h d", h=8), pgs[i][:, 0:32].un(1).bto([128, 8, 32]),
                         qrB[:, b, :].re("p (h d) -> p h d", h=8), ALU.mult)
                return sq, rp

            def small(g, sq, rp):
                n8 = g * 8
                sr = sg_r()
                k.reduce(sr[:, 0:n8], rp[:, 0:g, :].re("p g (h d) -> p (g h) d", h=8))
                ss = sg_r()
                k.reduce(ss[:, 0:n8], sq[:, 0:g, :].re("p g (h d) -> p (g h) d", h=8))
                a = sg_r()
                k.ts("dve", a[:, 0:n8], ss[:, 0:n8], 1.0 / 64, ALU.mult, EPS, ALU.add)
                k.act(a[:, 0:n8], a[:, 0:n8], AF.Ln)
                r = sg_r()
                k.act(r[:, 0:n8], a[:, 0:n8], AF.Exp, scale=-0.5)
                s1 = sg_r()
                k.tt("dve", s1[:, 0:n8], bB[:, 0:n8], r[:, 0:n8], ALU.mult)
                k.tt("dve", s1[:, 0:n8], s1[:, 0:n8], sr[:, 0:n8], ALU.add)
                pp = p_r()
                k.act(pp[:, 0:n8], s1[:, 0:n8], AF.Exp)
                return pp

            def accm(j0, g, pgs, pp):
                for i in range(g):
                    k.mm(accbank[0:8, 0:257], pp[:, i * 8:(i + 1) * 8], pgs[i][:, 32:289], start=False,
                         stop=(j0 + i == NPG - 1))

            for b in range(4):
                pp = newtok(b)
                k.mm(accbank[0:8, 0:257], pp[0:4, 0:8], M.ckv_bf[0:4, 0:257], start=True, stop=(NPG == 0))
                groups = [(j0, min(G_, NPG - j0)) for j0 in range(0, NPG, G_)]
                if groups:
                    pgs, cT4 = frontA(b, *groups[0])
                    sq, rp = frontB(b, groups[0][1], pgs, cT4)
                    ppg = small(groups[0][1], sq, rp)
                for gi, (j0, g) in enumerate(groups):
                    cur = (pgs, ppg)
                    if gi + 1 < len(groups):
                        pgs, cT4 = frontA(b, *groups[gi + 1])
                    accm(j0, g, *cur)
                    if gi + 1 < len(groups):
                        sq, rp = frontB(b, groups[gi + 1][1], pgs, cT4)
                        ppg = small(groups[gi + 1][1], sq, rp)
                evac(acc_sb.v, accbank[0:8, 0:257])
                rl = sm(); k.recip(rl[0:8, 0:1], acc_sb[:, 256:257])
                k.ts("dve", accn.v, acc_sb[:, 0:256], rl[0:8, 0:1], ALU.mult)
                bank = trb(); pb = bank.v.bc(BF16)
                for kk in range(2):
                    k.tr(pb[:, kk * 8:kk * 8 + 8], accn[:, kk * 128:(kk + 1) * 128], identb[0:8, 0:8])
                evac(accT.v, pb[:, 0:16].re("p (k h) -> p k h", k=2))
                bank = trb()
                for kk in range(2):
                    k.mm(bank[0:8, :], accT[:, kk, :], wv_rhs(kk), start=(kk == 0), stop=(kk == 1))
                k.tt("dve", om_f.v, bank[0:8, :], diagmask.v, ALU.mult)
                bank2 = trb()
                for pr in range(4):
                    k.mm(bank2[:, pr:pr + 1], om_f[:, pr * 128:(pr + 1) * 128], onesf[0:8, 0:1])
                evac(oms[:, 4:8, b], bank2[:, 0:4])
            k.dma("sp", omixs_d.v, oms.v)

        def pass_a():
            w_in = wload_cols("w_in", 1024, 0, 2064)
            G = alloc_gdn()
            S2 = k.sb("S2", [128, 4, 128], F32)
            k.memset("pool", S2.v, 0.0)
            zcT = k.sb("zcT", [128, 12, 131], F32)
            k.memset("pool", zcT.v, 0.0)
            omixA = k.sb("omixA", [128, 4, 512], BF16)
            acc_r = Rot([k.sb(f"acc_c{i}", [128, 128], F32) for i in range(2)])
            accp_r = Rot([k.sb(f"acc_p{i}", [128, 128], F32) for i in range(2)])
            tmp_p = k.sb("tmp_p", [128, 128], F32)
            kbT = k.sb("kbT", [128, 4, 128], BF16); nwT = k.sb("nwT", [128, 4, 128], BF16)
            qgT = k.sb("qgT", [128, 4, 128], BF16)
            kT_b = k.sb("kT_b", [128, 4, 128], BF16); qT_b = k.sb("qT_b", [128, 4, 128], BF16)
            S2b = k.sb("S2b", [128, 4, 128], BF16)
            k.memset("pool", S2b.v, 0.0)
            egc_fm = k.sb("egc_fm", [128, 4, 128], F32)
            k_tok = G.on_t
            v_tok = k.sb("v_tok", [128, 512], F32); u_tok = v_tok
            vb_tok = k.sb("vb_tok", [128, 512], BF16)
            kbg_tok = k.sb("kbg_tok", [128, 512], BF16)
            kdec_tok = k.sb("kdec_tok", [128, 512], BF16)
            vnew_tok = k.sb("vnew_tok", [128, 512], BF16)
            gcT8 = k.sb("gcT8", [8, 128], F32)
            betaT8 = k.sb("betaT8", [8, 128], F32)
            qkT_all = k.sb("qkT_all", [128, 8, 128], BF16)
            U_all = k.sb("U_all", [128, 8, 128], BF16)
            g4 = Rot([k.sb(f"g4_{i}", [128, 4, 128], F32) for i in range(3)])
            r4s = [Rot([k.sb(f"r4_{j}_{i}", [128, 4, 128], INV_DT) for i in range(6)]) for j in range(2)]
            t4 = k.sb("t4", [128, 4, 128], F32)
            v4 = lambda b_: b_.v.re("p (c t) -> p c t", c=4)

            def f1_steps(bi):
                steps = []
                t0 = bi * 128

                def s_in():
                    inproj(x_d.v[t0:t0 + 128, :], 128, w_in, 0, GROUPS_A)
                    if bi == NBLK - 1:
                        k.dma("sp", oconv_d.v, z_tok[125:128, 0:1536])
                steps.append(s_in)

                def s_tr(g3):
                    def f():
                        bank = psb()
                        for c in range(4):
                            k.tr(bank[:, c * 128:(c + 1) * 128], z_tok[:, (g3 * 4 + c) * 128:(g3 * 4 + c + 1) * 128], identf.v)
                        evac(zcT[:, g3 * 4:g3 * 4 + 4, 3:131], v4(bank))
                    return f
                for g3 in range(3):
                    steps.append(s_tr(g3))

                def s_cv(c):
                    def f():
                        ac = acc_r()
                        k.ts("dve", ac.v, zcT[:, c, 0:128], wcv[:, c, 0:1], ALU.mult)
                        for j in (1, 2, 3):
                            k.stt(ac.v, zcT[:, c, j:j + 128], wcv[:, c, j:j + 1], ac.v, ALU.mult, ALU.add)
                        k.act(G.cv[:, c, :], ac.v, AF.Silu)
                    return f
                for c in range(12):
                    steps.append(s_cv(c))
                steps.append(lambda: k.copy("pool", zcT[:, :, 0:3], zcT[:, :, 128:131]))
                return steps

            def gdn_block(tl, inj):
                cv = G.cv; kTg = G.kTg; qTg = G.qTg

                def pump(n):
                    for _ in range(n):
                        if inj:
                            inj.pop(0)()
                l2norm_fm(G, 128)
                k.act(G.sz_t.v, z_tok[:, C_Z:C_Z + 512], AF.Silu)
                k.copy("pool", kT_b.v, kTg.v)
                k.copy("pool", qT_b.v, qTg.v)
                bank = psb()
                for c in range(4):
                    k.tr(bank[:, c * 128:(c + 1) * 128], kTg[:, c, :], identf.v)
                evac(k_tok.v, bank.v)
                bank = psb()
                for c in range(4):
                    k.tr(bank[:, c * 128:(c + 1) * 128], cv[:, 8 + c, :], identf.v)
                evac(v_tok.v, bank.v)
                g_tok, beta = gdn_scalars(128)
                bank = psb()
                k.mm(bank[:, 0:8], tri.v, g_tok.v)
                gc = sm(); evac(gc.v, bank[:, 0:8])
                bank = psb()
                k.mm(bank[:, 0:8], lastsel.v, gc.v)
                dd = sm(); k.tt("dve", dd.v, bank[:, 0:8], gc.v, ALU.subtract)
                edec = sm(); k.act(edec.v, dd.v, AF.Exp)
                egc = sm(); k.act(egc.v, gc.v, AF.Exp)
                bge = sm(); k.tt("dve", bge.v, beta.v, egc.v, ALU.mult)
                b3 = lambda t: t.v.un(2).bto([128, 8, 64])
                r3 = lambda t: t.v.re("p (h d) -> p h d", h=8)
                k.tt("dve", r3(vb_tok), r3(v_tok), b3(beta), ALU.mult)
                k.tt("pool", r3(kdec_tok), r3(k_tok), b3(edec), ALU.mult)
                k.tt("pool", r3(kbg_tok), r3(k_tok), b3(bge), ALU.mult)
                bank = psb()
                k.tr(bank[0:8, 0:128], gc.v, identf.v)
                k.tr(bank[0:8, 128:256], beta.v, identf.v)
                evac(gcT8.v, bank[0:8, 0:128]); evac(betaT8.v, bank[0:8, 128:256])
                bank = psb()
                for pr in range(4):
                    k.mm(bank[:, pr * 128:(pr + 1) * 128], selpair[:, pr, :], gcT8.v)
                k.act(egc_fm.v, v4(bank), AF.Exp)
                bank = psb()
                for pr in range(4):
                    k.mm(bank[:, pr * 128:(pr + 1) * 128], selpair[:, pr, :], betaT8.v)
                k.tt("dve", kbT.v, kTg.v, v4(bank), ALU.mult)
                k.tt("dve", qgT.v, qTg.v, egc_fm.v, ALU.mult)
                R = lambda h: slice((h % 2) * 64, (h % 2) * 64 + 64)
                st8 = []
                for hg in range(2):
                    hs = [hg * 4 + i for i in range(4)]
                    r4 = r4s[hg]
                    bcb = psb()
                    for i, h in enumerate(hs):
                        k.mm(bcb[:, i * 128:(i + 1) * 128], sel8[:, h, :], gcT8.v)
                    d1 = g4()
                    k.tt("dve", d1.v, v4(bcb), gc[:, hg * 4:hg * 4 + 4].un(2).bto([128, 4, 128]), ALU.subtract)
                    e1 = g4()
                    k.tt("dve", e1.v, d1.v, negs.v.un(1).bto([128, 4, 128]), ALU.max)
                    Dm = g4()
                    k.act(Dm.v, e1.v, AF.Exp, scale=-1.0)
                    k.tt("dve", d1.v, d1.v, negt.v.un(1).bto([128, 4, 128]), ALU.min)
                    DTm = e1
                    k.act(DTm.v, d1.v, AF.Exp)
                    Bt = r4(); Ct = r4(); St = r4()
                    k.tt("pool", d1.v, DTm.v, identf.v.un(1).bto([128, 4, 128]), ALU.add)
                    for hp_ in range(2):
                        bKB = psb(); bKBT = psb(); bQKT = psb()
                        for which in range(3):
                            for i, h in enumerate(hs):
                                if h % 2 != hp_:
                                    continue
                                pr = h // 2
                                cs_ = slice((i // 2) * 128, (i // 2 + 1) * 128)
                                if which == 0:
                                    k.mm(bKB[:, cs_], kbT[R(h), pr, :], kT_b[R(h), pr, :])
                                elif which == 1:
                                    k.mm(bKBT[:, cs_], kT_b[R(h), pr, :], kbT[R(h), pr, :])
                                else:
                                    k.mm(bQKT[:, cs_], kT_b[R(h), pr, :], qT_b[R(h), pr, :])
                        v2 = lambda b_: b_[:, 0:256].re("p (c t) -> p c t", c=2)
                        k.stt(Bt[:, hp_::2, :], v2(bKB), -1.0, Dm[:, hp_::2, :], ALU.mult, ALU.mult)
                        k.stt(Ct[:, hp_::2, :], v2(bKBT), -1.0, DTm[:, hp_::2, :], ALU.mult, ALU.mult)
                        k.tt("dve", qkT_all[:, hg * 4 + hp_:hg * 4 + 4:2, :], v2(bQKT), d1[:, hp_::2, :], ALU.mult)
                    k.tt("pool", St.v, Ct.v, identf.v.un(1).bto([128, 4, 128]), ALU.add)
                    st8.append([Bt, Ct, St])
                    if hg == 1:
                        pump(1)
                for lvl in range(1, 6):
                    nBs = []
                    for hg in range(2):
                        Bt, Ct, St = st8[hg]
                        r4 = r4s[hg]
                        bB = psb()
                        for i in range(4):
                            k.mm(bB[:, i * 128:(i + 1) * 128], Ct[:, i, :], Bt[:, i, :])
                        nB = r4()
                        k.act(nB.v, v4(bB), AF.Copy)
                        nC = None
                        if lvl < 5:
                            bC = psb()
                            for i in range(4):
                                k.mm(bC[:, i * 128:(i + 1) * 128], Bt[:, i, :], Ct[:, i, :])
                            nC = r4()
                            k.act(nC.v, v4(bC), AF.Copy)
                        nBs.append((nB, nC))
                        pump(1)
                    for hg in range(2):
                        Bt, Ct, St = st8[hg]
                        nB, nC = nBs[hg]
                        r4 = r4s[hg]
                        bS = psb()
                        for i in range(4):
                            k.mm(bS[:, i * 128:(i + 1) * 128], nB[:, i, :], St[:, i, :])
                        if lvl < 5:
                            nS = r4()
                            k.tt("dve", nS.v, v4(bS), St.v, ALU.add)
                            st8[hg] = [nB, nC, nS]
                        else:
                            k.tt("dve", U_all[:, hg * 4:hg * 4 + 4, :], v4(bS), St.v, ALU.add)
                    pump(1)
                ub = psb(); wb = psb()
                for h in range(8):
                    pr = h // 2; Rh = slice((h % 2) * 64, (h % 2) * 64 + 64)
                    k.mm(ub[:, h * 64:(h + 1) * 64], U_all[:, h, :], vb_tok[:, h * 64:(h + 1) * 64])
                    k.mm(wb[Rh, pr * 128:(pr + 1) * 128], kbg_tok[:, h * 64:(h + 1) * 64], U_all[:, h, :])
                evac(u_tok.v, ub.v)
                k.act(nwT.v, v4(wb), AF.Copy, scale=-1.0)
                for ci in range(2):
                    RR = slice(ci * 64, ci * 64 + 64)
                    vbk = psb()
                    for pr in range(4):
                        k.mm(vbk[RR, pr * 128:(pr + 1) * 128], nwT[:, pr, RR], S2b[:, pr, :])
                    k.tt("dve", vnew_tok[RR, :], vbk[RR, :], u_tok[RR, :], ALU.add)
                    obk = psb()
                    for pr in range(4):
                        k.mm(obk[RR, pr * 128:(pr + 1) * 128], qgT[:, pr, RR], S2b[:, pr, :], start=True, stop=False)
                        for hp in range(2):
                            h = 2 * pr + hp
                            k.mm(obk[RR, pr * 128 + hp * 64:pr * 128 + (hp + 1) * 64], qkT_all[RR, h, RR],
                                 vnew_tok[RR, h * 64:(h + 1) * 64], start=False, stop=(hp == 1))
                    k.act(G.o_tok[RR, :], obk[RR, :], AF.Copy)
                    sbk = psb()
                    for pr in range(4):
                        cs = slice(pr * 128, (pr + 1) * 128)
                        k.mm(sbk[:, cs], kdec_tok[RR, cs], vnew_tok[RR, cs])
                    k.tt("dve", t4.v, v4(sbk), headblk.v.un(1).bto([128, 4, 128]), ALU.mult)
                    k.tt("pool", S2.v, S2.v, egc_fm[:, :, ci * 64 + 63:ci * 64 + 64].bto([128, 4, 128]), ALU.mult)
                    k.tt("pool", S2.v, S2.v, t4.v, ALU.add)
                    k.copy("pool", S2b.v, S2.v)
                    pump(1)
                gdn_out_tok(G, 128)
                pb = transpose_bf(None, G.og_bf.v, 128, 4)
                evac(omixA[:, :, tl * 128:(tl + 1) * 128], pb[:, 0:512].re("p (j t) -> p j t", j=4))
                pump(len(inj))

            for st_ in f1_steps(0):
                st_()
            for bi in range(NBLK):
                tl = bi % 4
                gdn_block(tl, f1_steps(bi + 1) if bi + 1 < NBLK else [])
                if tl == 3:
                    t = bi // 4
                    k.dma("sp", omix_d.v[:, 0:4, t * 512:(t + 1) * 512], omixA.v)
            for pr in range(4):
                for hp in range(2):
                    RH = slice(hp * 64, hp * 64 + 64)
                    k.dma("sp", ogdn_d.v[2 * pr + hp], S2[RH, pr, hp * 64:hp * 64 + 64])

        def pass_b():
            psb.items = banks[:3]
            scb = Rot([banks[3], banks[4]])
            w_in = wload_cols("w_in", 1024, C_QA, IN_DIM)
            w_qb = wload_cols("w_q_b", 384, 0, 768)
            w_kvb = wload_cols("w_kv_b", 256, 0, 1024)
            M = alloc_mla(w_qb, w_kvb)
            Mk = alloc_mla(w_qb, w_kvb)
            Mk.cos_t = M.cos_t; Mk.sin_t = M.sin_t
            sq_t2 = k.sb("sq_t2", [128, 512], F32)
            kT = k.sb("kT", [128, 8, S], BF16)
            v_sb = k.sb("v_sb", [128, NBLK, 8, 64], BF16)
            qT_tiles = [k.sb(f"qT_tile{i}", [128, 8, 512], BF16) for i in range(2)]
            omixB = k.sb("omixB", [128, 4, 512], BF16)
            pT_r = Rot([k.sb(f"pT{i}", [128, 512], BF16) for i in range(4)])
            rl_t = k.sb("rl_t", [128, 512], F32)

            def mla_block(bi, tl, qT_tile):
                t0 = bi * 128

                def chain_q():
                    mla_q(M, 128, C["cos_p"].v[t0:t0 + 128, :], C["sin_p"].v[t0:t0 + 128, :])
                    k.copy("pool", M.hh_bf.v, M.hh.v)
                    bank = psb(); pb = bank.v.bc(BF16)
                    for h in range(8):
                        k.tr(pb[0:96, h * 128:(h + 1) * 128], M.hh_bf[:, h, :], identb.v)
                    evac(qT_tile[0:96, :, tl * 128:(tl + 1) * 128], pb[0:96, 0:1024].re("p (h t) -> p h t", h=8))

                def chain_k():
                    mla_kv(Mk, 128, ockv_d.v[t0:t0 + 128, :], okr_d.v[t0:t0 + 128, :])
                    for (c0, c1) in ((0, 512), (512, 1024)):
                        bank = psb()
                        for kk in range(2):
                            k.mm(bank[:, 0:512], Mk.ckvT[:, kk, :], w_kvb[:, kk, c0:c1], start=(kk == 0), stop=(kk == 1))
                        evac(Mk.qkv_tok[:, c0:c1], bank.v)
                    kv3 = Mk.qkv_tok.v.re("p (h d) -> p h d", h=8)
                    head_rms(Mk.hh[:, :, 0:64], kv3[:, :, 0:64], g_k_nope.v, 128, 64, scratch=sq_t2)
                    k.copy("pool", Mk.hh[:, :, 64:96], Mk.kr_f.v.un(1).bto([128, 8, 32]))
                    k.copy("pool", Mk.hh_bf.v, Mk.hh.v)
                    k.copy("dve", v_sb[:, bi, :, :], kv3[:, :, 64:128])
                    bank = psb(); pb = bank.v.bc(BF16)
                    for h in range(8):
                        k.tr(pb[0:96, h * 128:(h + 1) * 128], Mk.hh_bf[:, h, :], identb.v)
                    evac(kT[0:96, :, t0:t0 + 128], pb[0:96, 0:1024].re("p (h t) -> p h t", h=8))

                outer = k.rec
                k.rec = []
                psb.items = [banks[0]]
                chain_q()
                rq = k.rec
                k.rec = []
                psb.items = [banks[1], banks[2]]
                chain_k()
                rk = k.rec
                psb.items = banks[:3]
                k.rec = outer
                merged = []
                nq, nk = len(rq), len(rk)
                iq = ik = 0
                while iq < nq or ik < nk:
                    if ik >= nk or (iq < nq and iq * nk <= ik * nq):
                        merged.append(rq[iq]); iq += 1
                    else:
                        merged.append(rk[ik]); ik += 1
                if outer is not None:
                    outer.extend(merged)
                else:
                    k.play(merged)

            def attention_tile(t):
                qT_tile = qT_tiles[t % 2]
                for pr in range(4):
                    o_ps = banks[5]; l_ps = banks[6]
                    for hp in range(2):
                        h = 2 * pr + hp
                        RH = slice(hp * 64, hp * 64 + 64)
                        nkb = 4 * t + 4
                        def stepA(j):
                            qlo = max(0, j - 4 * t)
                            ncol = (4 - qlo) * 128
                            qc = slice(qlo * 128, 512)
                            sc = scb()
                            k.mm(sc[:, 0:ncol], kT[0:96, h, j * 128:(j + 1) * 128], qT_tile[0:96, h, qc])
                            pT = pT_r()
                            k.act(pT[:, 0:ncol], sc[:, 0:ncol], AF.Exp)
                            if j >= 4 * t:
                                k.tt("pool", pT[:, 0:128], pT[:, 0:128], cmask.v, ALU.mult)
                            return pT, ncol, qc

                        def stepB(j, pT, ncol, qc):
                            k.mm(o_ps[RH, qc], v_sb[:, j, h, :], pT[:, 0:ncol], start=(j == 0), stop=(j == nkb - 1))
                            k.mm(l_ps[RH, qc], onesb.v, pT[:, 0:ncol], start=(j == 0), stop=(j == nkb - 1))

                        pend = stepA(0)
                        for j in range(nkb):
                            cur = pend
                            if j + 1 < nkb:
                                pend = stepA(j + 1)
                            stepB(j, *cur)
                    k.recip(rl_t.v, l_ps.v)
                    k.tt("dve", omixB[:, pr, :], o_ps.v, rl_t.v, ALU.mult)

            def merge2(ra, rb):
                out = []
                na, nb_ = len(ra), len(rb)
                ia = ib = 0
                while ia < na or ib < nb_:
                    if ib >= nb_ or (ia < na and ia * nb_ <= ib * na):
                        out.append(ra[ia]); ia += 1
                    else:
                        out.append(rb[ib]); ib += 1
                return out

            att_rec = None
            for t in range(NT):
                k.rec = []
                for bi in range(4 * t, 4 * t + 4):
                    t0 = bi * 128
                    inproj(x_d.v[t0:t0 + 128, :], 128, w_in, C_QA, GROUPS_B)
                    mla_block(bi, bi % 4, qT_tiles[t % 2])
                blk_rec = k.rec
                k.rec = None
                k.play(blk_rec if att_rec is None else merge2(att_rec, blk_rec))
                k.rec = []
                attention_tile(t)
                k.dma("sp", omix_d.v[:, 4:8, t * 512:(t + 1) * 512], omixB.v)
                att_rec = k.rec
                k.rec = None
            k.play(att_rec)

        k.es_saved = esP1
        es1_cur = [None]
        for name_, fn_ in (("sample", sample_scope), ("a", pass_a), ("b", pass_b)):
            es1 = contextlib.ExitStack()
            with es1:
                k.es = es1
                es1_cur[0] = es1
                fn_()
                k.barrier()
            k.es = k.es_saved
            if stop_after == name_:
                break
        psb.items = banks[:7]
        esP1.__exit__(None, None, None)
        k.es = k.es_outer

        if stop_after is None:
            w_dn = wload("w_ffn_down", D_FF, 1024)
            wg_scr = k.dram("wg_scr", [128, 8, D_FF], BF16, "Internal")
            wu_scr = k.dram("wu_scr", [128, 8, D_FF], BF16, "Internal")
            first_stream = [True]
            g_ffn = bload("g_ffn", 1024); g_ple = bload("g_ple", 1024)
            wg_r = Rot([k.sb(f"wg{i}", [128, 8, 512], BF16) for i in range(2)])
            wu_r = Rot([k.sb(f"wu{i}", [128, 8, 512], BF16) for i in range(2)])
            om_t = k.sb("om_t", [128, 8, 512], BF16)
            h1 = k.sb("h1", [128, 4, 1024], F32)
            un = k.sb("un", [128, 1024], BF16)
            uT = k.sb("uT", [128, 8, 512], BF16)
            hT = k.sb("hT", [128, 22, 512], BF16)
            sg = k.sb("sg", [128, 512], F32)
            sg2 = k.sb("sg2", [128, 512], F32)
            sg_rot = Rot([sg, sg2])
            p_t = k.sb("p_t", [128, 256], F32)
            p_bf = k.sb("p_bf", [128, 256], BF16)
            pT2 = k.sb("pT2", [128, 2, 128], BF16)
            gate_t = sg
            wgs = W["w_ffn_gate"].v.re("(k p) n -> p k n", p=128)
            wus = W["w_ffn_up"].v.re("(k p) n -> p k n", p=128)

            def phase2_tile(nt, om_src, x_src, p_src, y_dst):
                nb = (nt + 127) // 128
                bt = min(nt, 128)
                rows = slice(0, bt)
                k.dma("sp", om_t[:, :, 0:nt], om_src)
                for b in range(nb):
                    k.dma("sp", h1[rows, b, :], x_src(b))
                for b in range(nb):
                    cb = slice(b * 128, b * 128 + bt)
                    for hf in range(2):
                        bank = psb()
                        for kk in range(8):
                            k.mm(bank[rows, :], om_t[:, kk, cb], w_o[:, kk, hf * 512:(hf + 1) * 512], start=(kk == 0), stop=(kk == 7))
                        k.tt("dve", h1[rows, b, hf * 512:(hf + 1) * 512], bank[rows, :], h1[rows, b, hf * 512:(hf + 1) * 512], ALU.add)
                    rmsnorm_rows(un[rows, :], h1[rows, b, :], g_ffn[rows, :], 1024, rows)
                    pb = transpose_bf(None, un[rows, :], bt, 8)
                    evac(uT[:, :, cb], pb[:, 0:1024].re("p (j t) -> p j t", j=8)[:, :, 0:bt])
                for c0 in range(0, D_FF, 512):
                    c1 = min(D_FF, c0 + 512)
                    wg = wg_r(); wu = wu_r()
                    if first_stream[0]:
                        k.dma("pool", wg[:, :, 0:c1 - c0], wgs[:, :, c0:c1])
                        k.dma("pool", wu[:, :, 0:c1 - c0], wus[:, :, c0:c1])
                        k.dma("sp", wg_scr.v[:, :, c0:c1], wg[:, :, 0:c1 - c0])
                        k.dma("sp", wu_scr.v[:, :, c0:c1], wu[:, :, 0:c1 - c0])
                    else:
                        k.dma("sp", wg[:, :, 0:c1 - c0], wg_scr.v[:, :, c0:c1])
                        k.dma("sp", wu[:, :, 0:c1 - c0], wu_scr.v[:, :, c0:c1])
                    for m in range(c0 // 128, c1 // 128):
                        ms_ = slice(m * 128 - c0, (m + 1) * 128 - c0)
                        bg = psb(); bu = psb()
                        for kk in range(8):
                            k.mm(bg[:, 0:nt], wg[:, kk, ms_], uT[:, kk, 0:nt], start=(kk == 0), stop=(kk == 7))
                        for kk in range(8):
                            k.mm(bu[:, 0:nt], wu[:, kk, ms_], uT[:, kk, 0:nt], start=(kk == 0), stop=(kk == 7))
                        sgt = sg_rot()
                        k.act(sgt[:, 0:nt], bg[:, 0:nt], AF.Silu)
                        k.tt("dve", hT[:, m, 0:nt], sgt[:, 0:nt], bu[:, 0:nt], ALU.mult)
                for b in range(nb):
                    cb = slice(b * 128, b * 128 + bt)
                    for hf in range(2):
                        bank = psb()
                        for m in range(22):
                            k.mm(bank[rows, :], hT[:, m, cb], w_dn[:, m, hf * 512:(hf + 1) * 512], start=(m == 0), stop=(m == 21))
                        k.tt("dve", h1[rows, b, hf * 512:(hf + 1) * 512], bank[rows, :], h1[rows, b, hf * 512:(hf + 1) * 512], ALU.add)
                    rmsnorm_rows(un[rows, :], h1[rows, b, :], g_ple[rows, :], 1024, rows)
                    pb = transpose_bf(None, un[rows, :], bt, 8)
                    evac(uT[:, :, cb], pb[:, 0:1024].re("p (j t) -> p j t", j=8)[:, :, 0:bt])
                    k.dma("sp", p_t[rows, :], p_src(b))
                    k.copy("pool", p_bf[rows, :], p_t[rows, :])
                    pb = transpose_bf(None, p_bf[rows, :], bt, 2)
                    evac(pT2[:, :, 0:bt], pb[:, 0:256].re("p (j t) -> p j t", j=2)[:, :, 0:bt])
                    for hf in range(2):
                        hs_ = slice(hf * 512, (hf + 1) * 512)
                        bank = psb()
                        for kk in range(8):
                            k.mm(bank[rows, :], uT[:, kk, cb], w_pg[:, kk, hs_], start=(kk == 0), stop=(kk == 7))
                        k.act(gate_t[rows, :], bank[rows, :], AF.Sigmoid)
                        bank2 = psb()
                        for kk in range(2):
                            k.mm(bank2[rows, :], pT2[:, kk, 0:bt], w_pp[:, kk, hs_], start=(kk == 0), stop=(kk == 1))
                        k.tt("dve", gate_t[rows, :], bank2[rows, :], gate_t[rows, :], ALU.mult)
                        k.tt("pool", h1[rows, b, hs_], gate_t[rows, :], h1[rows, b, hs_], ALU.add)
                    k.dma("sp", y_dst(b), h1[rows, b, :])

            phase2_tile(4, omixs_d.v, lambda b: xs_d.v, lambda b: psm_d.v, lambda b: ys_d.v)
            first_stream[0] = False
            for t in range(NT):
                q0 = t * 512
                phase2_tile(512, omix_d.v[:, :, q0:q0 + 512],
                            lambda b: x_d.v[q0 + b * 128:q0 + (b + 1) * 128, :],
                            lambda b: p_d.v[q0 + b * 128:q0 + (b + 1) * 128, :],
                            lambda b: y_d.v[q0 + b * 128:q0 + (b + 1) * 128, :])
        k.finish()
    return nc, k


_NC_CACHE = {}


def run_cores(inputs, S, NPG, NPHYS, stop_after=None, ncores=8):
    key = (S, NPG, NPHYS, stop_after)
    if key not in _NC_CACHE:
        _NC_CACHE[key] = build(S, NPG, NPHYS, stop_after)
    nc, kb = _NC_CACHE[key]
    f = lambda a: np.ascontiguousarray(np.asarray(a))
    consts = host_consts(S, NPG * 128)
    cache_cat = np.concatenate([np.asarray(inputs["cache_krope"][0]).reshape(NPHYS * 128, 32),
                                np.asarray(inputs["cache_ckv"][0]).reshape(NPHYS * 128, 256)], axis=1)
    cache_cat = np.ascontiguousarray(cache_cat, dtype=np.float32)
    wmap = {n: f(inputs[n][0]).reshape(s) for n, s in W_SHAPES.items()}
    in_maps = []
    for c in range(ncores):
        m = dict(wmap)
        m.update(consts)
        m["x"] = f(inputs["x_prompt"][c])
        m["xs"] = f(inputs["x_sample"][4 * c:4 * c + 4, 0])
        m["p"] = f(inputs["p_prompt"][0, c])
        m["psm"] = f(inputs["p_sample"][0, 4 * c:4 * c + 4, 0])
        m["cache_cat"] = cache_cat
        m["state_gdn"] = f(inputs["state_gdn"][0, 4 * c:4 * c + 4])
        m["state_conv"] = f(inputs["state_conv"][0, 4 * c:4 * c + 4]).reshape(4, 3 * 1536)
        m["pt"] = f(inputs["page_table"][4 * c:4 * c + 4]).astype(np.int32)
        in_maps.append(m)
    res = run_bass_kernel_spmd(nc, in_maps, core_ids=list(range(ncores))).results
    g = lambda n: np.stack([np.asarray(r[n]) for r in res])
    y = g("y"); ys = g("ys").reshape(4 * ncores, 1, 1024)
    return (y, ys, g("o_ckv")[None], g("o_kr")[None], g("o_gdn")[None], g("o_conv")[None],
            g("o_ckv_s").reshape(1, 4 * ncores, 1, 256), g("o_kr_s").reshape(1, 4 * ncores, 1, 32),
            g("o_gdn_s").reshape(1, 4 * ncores, 8, 64, 64), g("o_conv_s").reshape(1, 4 * ncores, 3, 1536))


def kernel(**inputs):
    S = inputs["x_prompt"].shape[1]
    NPG = inputs["page_table"].shape[1]
    NPHYS = inputs["cache_ckv"].shape[1]
    outs = run_cores(inputs, S, NPG, NPHYS)
    return tuple(np.ascontiguousarray(o.astype(np.float32)) for o in outs)
```

```python
import contextlib
import numpy as np
import concourse.bass as bass
import concourse.mybir as mybir

F32 = mybir.dt.float32
F32R = mybir.dt.float32r
BF16 = mybir.dt.bfloat16
I32 = mybir.dt.int32
AF = mybir.ActivationFunctionType
ALU = mybir.AluOpType
AX = mybir.AxisListType
SKIP_SELF_WAIT = False


class V:
    __slots__ = ("t", "ap")

    def __init__(self, t, ap):
        self.t = t
        self.ap = ap

    def __getitem__(self, idx):
        return V(self.t, self.ap[idx])

    def re(self, pat, **kw):
        return V(self.t, self.ap.rearrange(pat, **kw))

    def bc(self, dtype):
        return V(self.t, self.ap.bitcast(dtype))

    def un(self, axis):
        return V(self.t, self.ap.unsqueeze(axis))

    def bto(self, shape):
        return V(self.t, self.ap.broadcast_to(shape))

    def pb(self, n):
        return V(self.t, self.ap.partition_broadcast(n))


class T:
    def __init__(self, name, h):
        self.name = name
        self.h = h
        self.w = None
        self.r = {}
        self.dsem = None
        self.dtot = 0
        self.psum = False

    def __getitem__(self, idx):
        return V(self, self.h[idx])

    @property
    def v(self):
        return V(self, self.h[:])


class KB:
    def __init__(self, nc, es, needed=None):
        self.nc = nc
        self.es = es
        self.es_sem = es
        self.eng = {"pe": nc.tensor, "act": nc.scalar, "dve": nc.vector, "pool": nc.gpsimd, "sp": nc.sync}
        self.sem = {k: es.enter_context(nc.semaphore("s_" + k)) for k in self.eng}
        self.cnt = {k: 0 for k in self.eng}
        self.waited = {k: {} for k in self.eng}
        self.dma_sems = {}
        self.out_events = []
        self.ninst = 0
        self.needed = needed
        self.rec = None
        self.need_rec = {k: set() for k in self.eng}
        self.rank = {k: 0 for k in self.eng}
        self.rankmap = {k: {} for k in self.eng}
        self.engsem = {id(v): k for k, v in self.sem.items()}

    def sb(self, name, shape, dt):
        self.uid = getattr(self, "uid", 0) + 1
        name = f"{name}_u{self.uid}"
        return T(name, self.es.enter_context(self.nc.sbuf_tensor(name, list(shape), dt)))

    def ps(self, name, shape, dt):
        t = T(name, self.es.enter_context(self.nc.psum_tensor(name, list(shape), dt)))
        t.psum = True
        return t

    def dram(self, name, shape, dt, kind):
        return T(name, self.nc.dram_tensor(name, list(shape), dt, kind=kind).ap())

    def _wait(self, e, ev):
        sem, val = ev
        sid = id(sem)
        if e == "pe" and sem is self.sem["pe"]:
            return
        if SKIP_SELF_WAIT and sem is self.sem.get(e):
            return
        if sid in self.dma_sems:
            val = max(val, self.dma_sems[sid][1])
        w = self.waited[e]
        if w.get(sid, 0) >= val:
            return
        w[sid] = val
        f = self.engsem.get(sid)
        if f is not None:
            self.need_rec[f].add(val)
            if self.needed is not None:
                val = self.rankmap[f][val]
        self.eng[e].wait_ge(sem, val)

    def _deps(self, e, reads, writes):
        for v in reads:
            t = v.t
            if t.w is not None:
                self._wait(e, t.w)
            if t.psum:
                for ev in t.r.values():
                    if ev[0] is not self.sem.get(e):
                        self._wait(e, ev)
        for v in writes:
            t = v.t
            if t.w is not None:
                self._wait(e, t.w)
            for ev in t.r.values():
                self._wait(e, ev)

    def _record(self, ev, reads, writes):
        sem, val = ev
        for v in reads:
            v.t.r[id(sem)] = ev
        for v in writes:
            v.t.w = ev
            v.t.r = {}

    def play(self, items):
        for it in items:
            if it[0] == "op":
                self.op(*it[1:])
            else:
                self.dma(it[1], it[2], it[3], **it[4])

    def op(self, e, fn, reads, writes):
        if self.rec is not None:
            self.rec.append(("op", e, fn, reads, writes))
            return None
        reads = [v for v in reads if isinstance(v, V)]
        self._deps(e, reads, writes)
        inst = fn(self.eng[e])
        self.cnt[e] += 1
        if self.needed is None:
            inst.then_inc(self.sem[e], 1)
        elif self.cnt[e] in self.needed[e]:
            inst.then_inc(self.sem[e], 1)
            self.rank[e] += 1
            self.rankmap[e][self.cnt[e]] = self.rank[e]
        self._record((self.sem[e], self.cnt[e]), reads, writes)
        self.ninst += 1
        return inst

    def dma(self, q, out, in_, **kw):
        if self.rec is not None:
            self.rec.append(("dma", q, out, in_, kw))
            return None
        self._deps(q, [in_], [out])
        own = out.t
        if own.dsem is None:
            own.dsem = self.es_sem.enter_context(self.nc.semaphore("d_" + own.name))
            self.dma_sems[id(own.dsem)] = [own.dsem, 0]
        inst = self.eng[q].dma_start(out=out.ap, in_=in_.ap, **kw)
        own.dtot += 16
        self.dma_sems[id(own.dsem)][1] = own.dtot
        inst.then_inc(own.dsem, 16)
        ev = (own.dsem, own.dtot)
        self._record(ev, [in_], [out])
        self.ninst += 1
        return ev

    def gather(self, out, in_, idx, **kw):
        q = "pool"
        self._deps(q, [in_, idx], [out])
        own = out.t
        if own.dsem is None:
            own.dsem = self.es_sem.enter_context(self.nc.semaphore("d_" + own.name))
            self.dma_sems[id(own.dsem)] = [own.dsem, 0]
        inst = self.nc.gpsimd.indirect_dma_start(
            out=out.ap, out_offset=None, in_=in_.ap,
            in_offset=bass.IndirectOffsetOnAxis(ap=idx.ap, axis=0), **kw)
        own.dtot += 16
        self.dma_sems[id(own.dsem)][1] = own.dtot
        inst.then_inc(own.dsem, 16)
        ev = (own.dsem, own.dtot)
        self._record(ev, [in_, idx], [out])
        return ev

    def barrier(self):
        for e in self.eng:
            for sem, tot in self.dma_sems.values():
                if tot:
                    self._wait(e, (sem, tot))
            for f in self.eng:
                if f != e and self.cnt[f]:
                    self._wait(e, (self.sem[f], self.cnt[f]))

    def finish(self):
        for sem, tot in self.dma_sems.values():
            if tot:
                self._wait("sp", (sem, tot))
        for e in self.eng:
            if e != "sp" and self.cnt[e]:
                self._wait("sp", (self.sem[e], self.cnt[e]))

    def mm(self, out, lhsT, rhs, start=True, stop=True):
        return self.op("pe", lambda e: e.matmul(out.ap, lhsT=lhsT.ap, rhs=rhs.ap, start=start, stop=stop),
                       [lhsT, rhs] + ([] if start else [out]), [out])

    def tr(self, out, in_, ident):
        return self.op("pe", lambda e: e.transpose(out.ap, in_.ap, ident.ap), [in_, ident], [out])

    def act(self, out, in_, func, scale=1.0, bias=0.0, accum=None, eng="act"):
        kw = {}
        if accum is not None:
            kw["accum_out"] = accum.ap
        sc = scale.ap if isinstance(scale, V) else scale
        bi = bias.ap if isinstance(bias, V) else bias
        return self.op("act", lambda e: e.activation(out=out.ap, in_=in_.ap, func=func, scale=sc, bias=bi, **kw),
                       [in_, scale, bias], [out] + ([accum] if accum is not None else []))

    def tt(self, e, out, in0, in1, op):
        return self.op(e, lambda g: g.tensor_tensor(out=out.ap, in0=in0.ap, in1=in1.ap, op=op), [in0, in1], [out])

    def ts(self, e, out, in0, s1, op0, s2=None, op1=None, accum=None):
        a1 = s1.ap if isinstance(s1, V) else s1
        a2 = s2.ap if isinstance(s2, V) else s2
        kw = {}
        if op1 is not None:
            kw["op1"] = op1
        if accum is not None:
            kw["accum_out"] = accum.ap
        return self.op(e, lambda g: g.tensor_scalar(out=out.ap, in0=in0.ap, scalar1=a1, scalar2=a2, op0=op0, **kw),
                       [in0, s1, s2], [out] + ([accum] if accum is not None else []))

    def stt(self, out, in0, scalar, in1, op0, op1, e="dve"):
        sc = scalar.ap if isinstance(scalar, V) else scalar
        return self.op(e, lambda g: g.scalar_tensor_tensor(out=out.ap, in0=in0.ap, scalar=sc, in1=in1.ap, op0=op0, op1=op1),
                       [in0, scalar, in1], [out])

    def copy(self, e, out, in_):
        if e == "act":
            return self.act(out, in_, AF.Copy)
        return self.op(e, lambda g: g.tensor_copy(out=out.ap, in_=in_.ap), [in_], [out])

    def memset(self, e, out, val):
        return self.op(e, lambda g: g.memset(out.ap, val), [], [out])

    def reduce(self, out, in_, op=ALU.add, axis=AX.X, e="dve"):
        return self.op(e, lambda g: g.tensor_reduce(out=out.ap, in_=in_.ap, axis=axis, op=op), [in_], [out])

    def recip(self, out, in_):
        return self.op("dve", lambda g: g.reciprocal(out=out.ap, in_=in_.ap), [in_], [out])

from concourse.bass_utils import run_bass_kernel_spmd

D_MODEL = 1024; PLE_DIM = 256
NH = 8; DK = 64; CONV_DIM = 1536
Q_LORA = 384; KV_LORA = 256; ROPE = 32; NOPE = 64; QK = 96
IN_DIM = 2736; D_FF = 2816
EPS = 1e-6
ATTN_SCALE = QK ** -0.5
C_AB = 1536; C_Z = 1552; C_QA = 2064; C_KVA = 2448
BIG = 1.0e4


class Rot:
    def __init__(self, items):
        self.items = items
        self.i = 0

    def __call__(self):
        t = self.items[self.i % len(self.items)]
        self.i += 1
        return t


def host_consts(S, past):
    c = {}
    i = np.arange(128)
    same = (i[:, None] // 64) == (i[None, :] // 64)
    c["ident"] = np.eye(128, dtype=np.float32)
    c["tri"] = (same & (i[:, None] <= i[None, :])).astype(np.float32)
    c["lastsel"] = (i[:, None] == (i[None, :] // 64) * 64 + 63).astype(np.float32)
    vis = same & (i[None, :] < i[:, None])
    c["negs"] = np.where(vis, 0.0, BIG).astype(np.float32)
    c["negt"] = np.where(vis.T, 0.0, -BIG).astype(np.float32)
    c["headblk"] = same.astype(np.float32)
    sel8 = np.zeros((8, 8, 128), np.float32)
    for h in range(8):
        sel8[h, h, :] = 1.0
    c["sel8"] = sel8
    selp = np.zeros((8, 4, 128), np.float32)
    for h in range(8):
        selp[h, h // 2, (h % 2) * 64:(h % 2) * 64 + 64] = 1.0
    c["selpair"] = selp
    c["cmask"] = (i[None, :] >= i[:, None]).astype(np.float32)
    dm = np.zeros((8, 8, 64), np.float32)
    for h in range(8):
        dm[h, h, :] = 1.0
    c["diagmask"] = dm.reshape(8, 512)
    oh = np.zeros((4, 4, 128), np.float32)
    for b in range(4):
        oh[b, b, :] = 1.0
    c["onehot4"] = oh
    half = ROPE // 2
    inv = (10000.0 ** (-np.arange(half, dtype=np.float32) / half)).astype(np.float32)
    pos = np.arange(S, dtype=np.float32)
    ang = pos[:, None] * inv[None, :]
    c["cos_p"] = np.cos(ang).astype(np.float32)
    c["sin_p"] = np.sin(ang).astype(np.float32)
    angs = (np.float32(past) * inv)[None, :].astype(np.float32)
    c["cos_s"] = np.repeat(np.cos(angs), 4, 0).astype(np.float32)
    c["sin_s"] = np.repeat(np.sin(angs), 4, 0).astype(np.float32)
    return c


CONST_SHAPES = dict(ident=[128, 128], tri=[128, 128], lastsel=[128, 128], negs=[128, 128], negt=[128, 128],
                    headblk=[128, 128], sel8=[8, 8, 128], selpair=[8, 4, 128], cmask=[128, 128],
                    diagmask=[8, 512], onehot4=[4, 4, 128], cos_s=[4, 16], sin_s=[4, 16])

W_SHAPES = dict(g_attn=[1, 1024], w_in=[1024, IN_DIM], w_conv=[4, 1536], gdn_a_log=[1, 8], gdn_dt_bias=[1, 8],
                g_gdn_out=[1, 64], g_q_a=[1, 384], w_q_b=[384, 768], g_q_nope=[1, 64], g_q_rope=[1, 32],
                g_kv_a=[1, 256], g_k_rope=[1, 32], w_kv_b=[256, 1024], g_k_nope=[1, 64], w_o=[1024, 1024],
                g_ffn=[1, 1024], w_ffn_gate=[1024, D_FF], w_ffn_up=[1024, D_FF], w_ffn_down=[D_FF, 1024],
                g_ple=[1, 1024], w_ple_gate=[1024, 1024], w_ple_proj=[256, 1024])


def build(S, NPG, NPHYS, stop_after=None):
    _, k1 = build1(S, NPG, NPHYS, stop_after, None)
    return build1(S, NPG, NPHYS, stop_after, k1.need_rec)


INV_DT = BF16
NDUMMY = 0


def build1(S, NPG, NPHYS, stop_after, needed):
    DBG = 0
    NBLK = S // 128
    NT = S // 512
    nc = bass.Bass("TRN2", target_bir_lowering=False)
    es = contextlib.ExitStack()
    with es:
        nc_lp = es.enter_context(nc.allow_low_precision("bf16 matmul operands by design"))
        es.enter_context(nc.allow_non_contiguous_dma("small strided state loads/stores"))
        k = KB(nc, es, needed)
        din = {}

        def DI(name, shape, dt=F32):
            din[name] = k.dram(name, shape, dt, "ExternalInput")
            return din[name]

        def DO(name, shape, dt=F32):
            return k.dram(name, shape, dt, "ExternalOutput")

        x_d = DI("x", [S, 1024]); xs_d = DI("xs", [4, 1024])
        p_d = DI("p", [S, 256]); psm_d = DI("psm", [4, 256])
        ccat_d = DI("cache_cat", [NPHYS * 128, 288])
        sgdn_d = DI("state_gdn", [4, 8, 64, 64]); sconv_d = DI("state_conv", [4, 3 * 1536])
        pt_d = DI("pt", [4, NPG], I32)
        W = {n: DI(n, s) for n, s in W_SHAPES.items()}
        C = {n: DI(n, s) for n, s in CONST_SHAPES.items()}
        C["cos_p"] = DI("cos_p", [S, 16]); C["sin_p"] = DI("sin_p", [S, 16])
        y_d = DO("y", [S, 1024]); ys_d = DO("ys", [4, 1024])
        ockv_d = DO("o_ckv", [S, 256]); okr_d = DO("o_kr", [S, 32])
        ogdn_d = DO("o_gdn", [8, 64, 64]); oconv_d = DO("o_conv", [3, 1536])
        ockvs_d = DO("o_ckv_s", [4, 256]); okrs_d = DO("o_kr_s", [4, 32])
        ogdns_d = DO("o_gdn_s", [4, 8, 64, 64]); oconvs_d = DO("o_conv_s", [4, 3 * 1536])
        omix_d = k.dram("omix_scr", [128, 8, S], BF16, "ExternalOutput" if DBG == 99 else "Internal")
        omixs_d = k.dram("omixs_scr", [128, 8, 4], BF16, "Internal")

        banks = [k.ps(f"ps{i}", [128, 512], F32) for i in range(8)]
        psb = Rot(banks[:7])
        accbank = banks[7]
        evi = [0]

        def evac(out, in_, scale=None):
            evi[0] += 1
            if scale is not None:
                return k.act(out, in_, AF.Copy, scale=scale)
            if evi[0] % 3:
                return k.act(out, in_, AF.Copy)
            return k.copy("dve", out, in_)

        def cload(name, shape, dt=F32, src=None, q="sp"):
            t = k.sb("c_" + name, shape, dt)
            k.dma(q, t.v, (src if src is not None else C[name].v))
            return t

        identf = cload("ident", [128, 128])
        identb = k.sb("identb", [128, 128], BF16); k.copy("dve", identb.v, identf.v)
        identr = k.sb("identr", [128, 128], F32R); k.copy("dve", identr.v, identf.v)
        tri = cload("tri", [128, 128]); lastsel = cload("lastsel", [128, 128])
        negs = cload("negs", [128, 128]); negt = cload("negt", [128, 128])
        headblk = cload("headblk", [128, 128])
        headblk_b = k.sb("headblk_b", [128, 128], BF16); k.copy("dve", headblk_b.v, headblk.v)
        sel8 = cload("sel8", [8, 8, 128]); selpair = cload("selpair", [8, 4, 128])
        cmaskf = cload("cmask", [128, 128])
        cmask = k.sb("cmaskb", [128, 128], BF16); k.copy("dve", cmask.v, cmaskf.v)
        diagmask = cload("diagmask", [8, 512]); onehot4 = cload("onehot4", [4, 4, 128])
        onesb = k.sb("onesb", [128, 64], BF16); k.memset("dve", onesb.v, 1.0)
        onesf = k.sb("onesf", [128, 8], F32); k.memset("dve", onesf.v, 1.0)

        def bload(name, F, q="sp"):
            t = k.sb("b_" + name, [128, F], F32)
            k.dma(q, t.v, W[name].v.bto([128, F]))
            return t

        g_attn = bload("g_attn", 1024); g_q_a = bload("g_q_a", 384); g_kv_a = bload("g_kv_a", 256)
        g_q_nope = bload("g_q_nope", 64); g_q_rope = bload("g_q_rope", 32)
        g_k_rope = bload("g_k_rope", 32); g_k_nope = bload("g_k_nope", 64)
        g_gdn_out = bload("g_gdn_out", 64)
        a_log = bload("gdn_a_log", 8); dtb = bload("gdn_dt_bias", 8)
        eA = k.sb("eA", [128, 8], F32); k.act(eA.v, a_log.v, AF.Exp)
        k.ts("dve", g_q_nope.v, g_q_nope.v, ATTN_SCALE, ALU.mult)
        k.ts("dve", g_q_rope.v, g_q_rope.v, ATTN_SCALE, ALU.mult)
        wcv = k.sb("wcv", [128, 12, 4], F32)
        for j in range(4):
            k.dma("sp", wcv[:, :, j], W["w_conv"].v[j:j + 1, :].re("o (c p) -> p (o c)", p=128))

        def wload(name, K, N, q="pool"):
            kc = K // 128
            t = k.sb("w_" + name, [128, kc, N], BF16)
            src = W[name].v.re("(k p) n -> p k n", p=128)
            for i in range(kc):
                for n0 in range(0, N, 1024):
                    n1 = min(N, n0 + 1024)
                    k.dma(q, t[:, i, n0:n1], src[:, i, n0:n1])
            return t

        sm = Rot([k.sb(f"sm{i}", [128, 8], F32) for i in range(24)])

        def rstd_from_ss(ss, n, F, rows):
            a = sm()
            k.ts("dve", a[rows, 0:n], ss, 1.0 / F, ALU.mult, EPS, ALU.add)
            k.act(a[rows, 0:n], a[rows, 0:n], AF.Ln)
            r = sm()
            k.act(r[rows, 0:n], a[rows, 0:n], AF.Exp, scale=-0.5)
            return r[rows, 0:n]

        junk = k.sb("junk", [128, 1024], BF16)

        def rmsnorm_rows(out, x, g, F, rows):
            ss = sm()
            k.act(junk[rows, 0:F], x, AF.Square, accum=ss[rows, 0:1])
            r = rstd_from_ss(ss[rows, 0:1], 1, F, rows)
            k.stt(out, x, r, g, ALU.mult, ALU.mult)

        def transpose_bf(dst_fn, src, nt, nch, width=128):
            bank = psb()
            pb = bank.v.bc(BF16)
            for j in range(nch):
                k.tr(pb[0:width, j * 128:j * 128 + nt], src[:, j * width:(j + 1) * width], identb[0:nt, 0:nt])
            return pb

        from types import SimpleNamespace as NS
        GROUPS_ALL = [(0, 512), (512, 1024), (1024, 1536), (1536, 1552), (1552, 2064), (2064, 2448), (2448, 2736)]
        GROUPS_A = GROUPS_ALL[:5]
        GROUPS_B = GROUPS_ALL[5:]
        if stop_after is None:
            w_o = wload("w_o", 1024, 1024)
            w_pg = wload("w_ple_gate", 1024, 1024)
            w_pp = wload("w_ple_proj", 256, 1024)
        esP1 = contextlib.ExitStack()
        esP1.__enter__()
        k.es_outer = k.es
        k.es = esP1
        x_blk = k.sb("x_blk", [128, 1024], F32)
        xn_t = k.sb("xn", [128, 1024], BF16)
        xnT = k.sb("xnT", [128, 8, 128], BF16)
        z_tok = k.sb("z_tok", [128, IN_DIM], F32)
        sq_t = k.sb("sq_t", [128, 512], F32)

        def inproj(x_src_v, nt, w_in, c_off, groups):
            rows = slice(0, nt)
            k.dma("sp", x_blk[rows, :], x_src_v)
            rmsnorm_rows(xn_t[rows, :], x_blk[rows, :], g_attn[rows, :], 1024, rows)
            pb = transpose_bf(None, xn_t[rows, :], nt, 8)
            evac(xnT[:, :, 0:nt], pb[:, 0:1024].re("p (j t) -> p j t", j=8)[:, :, 0:nt])
            for (c0, c1) in groups:
                bank = psb()
                for kk in range(8):
                    k.mm(bank[rows, 0:c1 - c0], xnT[:, kk, 0:nt], w_in[:, kk, c0 - c_off:c1 - c_off], start=(kk == 0), stop=(kk == 7))
                evac(z_tok[rows, c0:c1], bank[rows, 0:c1 - c0])

        def wload_cols(name, K, c0, c1, q="pool"):
            kc = K // 128
            t = k.sb("w_" + name + f"_{c0}", [128, kc, c1 - c0], BF16)
            src = W[name].v.re("(k p) n -> p k n", p=128)
            for i in range(kc):
                for n0 in range(c0, c1, 1024):
                    n1 = min(c1, n0 + 1024)
                    k.dma(q, t[:, i, n0 - c0:n1 - c0], src[:, i, n0:n1])
            return t

        def head_rms(out, xin, g, nt, width, scratch=None):
            rows = slice(0, nt)
            sq = (scratch or sq_t)[rows, 0:8 * width].re("p (h d) -> p h d", h=8)
            k.act(sq, xin, AF.Square)
            ss = sm()
            k.reduce(ss[rows, 0:8], sq)
            r = rstd_from_ss(ss[rows, 0:8], 8, width, rows)
            k.tt("dve", out, xin, r.un(2).bto([nt, 8, width]), ALU.mult)
            k.tt("pool", out, out, g.un(1).bto([nt, 8, width]), ALU.mult)

        def alloc_mla(w_qb, w_kvb):
            M = NS()
            M.w_qb = w_qb; M.w_kvb = w_kvb
            M.qa_n = k.sb("qa_n", [128, 384], BF16)
            M.qanT = k.sb("qanT", [128, 3, 128], BF16)
            M.qkv_tok = k.sb("qkv_tok", [128, 1024], F32)
            M.hh = k.sb("hh", [128, 8, 96], F32)
            M.hh_bf = k.sb("hh_bf", [128, 8, 96], BF16)
            M.rtmp = k.sb("rtmp", [128, 8, 32], F32)
            M.rtmp2 = k.sb("rtmp2", [128, 8, 16], F32)
            M.cos_t = k.sb("cos_t", [128, 16], F32); M.sin_t = k.sb("sin_t", [128, 16], F32)
            M.ckv_f = k.sb("ckv_f", [128, 256], F32)
            M.ckv_bf = k.sb("ckv_bf", [128, 260], BF16)
            k.memset("dve", M.ckv_bf[:, 256:257], 1.0)
            M.ckvT = k.sb("ckvT", [128, 2, 128], BF16)
            M.kr_f = k.sb("kr_f", [128, 32], F32)
            M.kr_n = k.sb("kr_n", [128, 32], F32)
            return M

        def rope(M, out, xin, nt, nh):
            rows = slice(0, nt)
            cb = M.cos_t[rows, :].un(1).bto([nt, nh, 16]); sb_ = M.sin_t[rows, :].un(1).bto([nt, nh, 16])
            x1 = xin[:, :, 0:16]; x2 = xin[:, :, 16:32]
            t2 = M.rtmp2[rows, 0:nh, :]
            k.tt("dve", out[:, :, 0:16], x1, cb, ALU.mult)
            k.tt("dve", t2, x2, sb_, ALU.mult)
            k.tt("dve", out[:, :, 0:16], out[:, :, 0:16], t2, ALU.subtract)
            k.tt("dve", out[:, :, 16:32], x1, sb_, ALU.mult)
            k.tt("dve", t2, x2, cb, ALU.mult)
            k.tt("dve", out[:, :, 16:32], out[:, :, 16:32], t2, ALU.add)

        def mla_q(M, nt, cos_src, sin_src):
            rows = slice(0, nt)
            k.dma("sp", M.cos_t[rows, :], cos_src); k.dma("sp", M.sin_t[rows, :], sin_src)
            rmsnorm_rows(M.qa_n[rows, :], z_tok[rows, C_QA:C_QA + 384], g_q_a[rows, :], 384, rows)
            pb = transpose_bf(None, M.qa_n[rows, :], nt, 3)
            evac(M.qanT[:, :, 0:nt], pb[:, 0:384].re("p (j t) -> p j t", j=3)[:, :, 0:nt])
            for (c0, c1) in ((0, 512), (512, 768)):
                bank = psb()
                for kk in range(3):
                    k.mm(bank[rows, 0:c1 - c0], M.qanT[:, kk, 0:nt], M.w_qb[:, kk, c0:c1], start=(kk == 0), stop=(kk == 2))
                evac(M.qkv_tok[rows, c0:c1], bank[rows, 0:c1 - c0])
            q3 = M.qkv_tok[rows, 0:768].re("p (h d) -> p h d", h=8)
            head_rms(M.hh[rows, :, 0:64], q3[:, :, 0:64], g_q_nope[rows, :], nt, 64)
            head_rms(M.rtmp[rows, :, :], q3[:, :, 64:96], g_q_rope[rows, :], nt, 32)
            rope(M, M.hh[rows, :, 64:96], M.rtmp[rows, :, :], nt, 8)

        def mla_kv(M, nt, ockv_v, okr_v):
            rows = slice(0, nt)
            rmsnorm_rows(M.ckv_f[rows, :], z_tok[rows, C_KVA:C_KVA + 256], g_kv_a[rows, :], 256, rows)
            k.dma("sp", ockv_v, M.ckv_f[rows, :])
            k.copy("pool", M.ckv_bf[rows, 0:256], M.ckv_f[rows, :])
            rmsnorm_rows(M.kr_n[rows, :], z_tok[rows, C_KVA + 256:C_KVA + 288], g_k_rope[rows, :], 32, rows)
            rope(M, M.kr_f[rows, :].un(1), M.kr_n[rows, :].un(1), nt, 1)
            k.dma("sp", okr_v, M.kr_f[rows, :])
            pb = transpose_bf(None, M.ckv_bf[rows, 0:256], nt, 2)
            evac(M.ckvT[:, :, 0:nt], pb[:, 0:256].re("p (j t) -> p j t", j=2)[:, :, 0:nt])

        def alloc_gdn(small=False):
            G = NS()
            G.cv = k.sb("cv", [128, 12, 128], F32)
            G.sqt = k.sb("sqt", [128, 4, 128], BF16)
            G.rst = k.sb("rst", [128, 4, 128], F32)
            G.qTg = k.sb("qTg", [128, 4, 128], F32)
            G.kTg = k.sb("kTg", [128, 4, 128], F32)
            G.tmp4 = k.sb("tmp4", [128, 4, 128], F32)
            G.sz_t = k.sb("sz_t", [128, 512], F32)
            if not small:
                G.o_tok = k.sb("o_tok", [128, 512], F32)
                G.on_t = k.sb("on_t", [128, 512], F32)
                G.og_bf = k.sb("og_bf", [128, 512], BF16)
            return G

        def gdn_scalars(nt):
            rows = slice(0, nt)
            ta = sm(); k.tt("dve", ta[rows, :], z_tok[rows, C_AB:C_AB + 8], dtb[rows, :], ALU.add)
            e = sm(); k.act(e[rows, :], ta[rows, :], AF.Exp)
            sp_ = sm(); k.act(sp_[rows, :], e[rows, :], AF.Ln, bias=1.0)
            g_tok = sm(); k.stt(g_tok[rows, :], sp_[rows, :], -1.0, eA[rows, :], ALU.mult, ALU.mult)
            beta = sm(); k.act(beta[rows, :], z_tok[rows, C_AB + 8:C_AB + 16], AF.Sigmoid)
            return g_tok, beta

        def l2norm_fm(G, nt):
            for half in range(2):
                k.act(G.sqt[:, :, 0:nt], G.cv[:, half * 4:half * 4 + 4, 0:nt], AF.Square)
                bank = psb()
                for c in range(4):
                    k.mm(bank[:, c * 128:c * 128 + nt], headblk_b.v, G.sqt[:, c, 0:nt])
                k.ts("dve", G.tmp4[:, :, 0:nt], bank.v.re("p (c t) -> p c t", c=4)[:, :, 0:nt], EPS, ALU.add)
                k.act(G.tmp4[:, :, 0:nt], G.tmp4[:, :, 0:nt], AF.Ln)
                k.act(G.rst[:, :, 0:nt], G.tmp4[:, :, 0:nt], AF.Exp, scale=-0.5)
                if half == 0:
                    k.stt(G.qTg[:, :, 0:nt], G.cv[:, 0:4, 0:nt], DK ** -0.5, G.rst[:, :, 0:nt], ALU.mult, ALU.mult)
                else:
                    k.tt("dve", G.kTg[:, :, 0:nt], G.cv[:, 4:8, 0:nt], G.rst[:, :, 0:nt], ALU.mult)

        def gdn_out_tok(G, nt):
            rows = slice(0, nt)
            o3 = G.o_tok[rows, :].re("p (h d) -> p h d", h=8)
            on3 = G.on_t[rows, :].re("p (h d) -> p h d", h=8)
            head_rms(on3, o3, g_gdn_out[rows, :], nt, 64)
            k.tt("dve", G.og_bf[rows, :], G.on_t[rows, :], G.sz_t[rows, :], ALU.mult)

        def sample_scope():
            w_in = wload_cols("w_in", 1024, 0, IN_DIM)
            w_qb = wload_cols("w_q_b", 384, 0, 768)
            w_kvb = wload_cols("w_kv_b", 256, 0, 1024)
            M = alloc_mla(w_qb, w_kvb)
            G = alloc_gdn(small=True)
            nt = 4; rows = slice(0, 4)
            inproj(xs_d.v, 4, w_in, 0, GROUPS_ALL)
            if DBG == 1:
                return
            oms = k.sb("oms", [128, 8, 4], BF16)
            esg = contextlib.ExitStack()
            with esg:
                k.es = esg
                st_tok = k.sb("st_tok", [12, 1536], F32)
                k.dma("sp", st_tok.v, sconv_d.v.re("b (j c) -> (b j) c", j=3))
                k.dma("sp", oconvs_d.v.re("b (j c) -> b j c", j=3)[:, 0:2, :], sconv_d.v.re("b (j c) -> b j c", j=3)[:, 1:3, :])
                k.dma("sp", oconvs_d.v.re("b (j c) -> b j c", j=3)[:, 2, :], z_tok[rows, 0:1536])
                ext = k.sb("ext_fm", [128, 12, 4, 4], F32)
                bank = psb()
                for c in range(12):
                    k.tr(bank[:, c * 12:(c + 1) * 12], st_tok[:, c * 128:(c + 1) * 128], identf[0:12, 0:12])
                evac(ext[:, :, :, 0:3], bank[:, 0:144].re("p (c b j) -> p c b j", c=12, b=4))
                bank = psb()
                for c in range(12):
                    k.tr(bank[:, c * 4:(c + 1) * 4], z_tok[rows, c * 128:(c + 1) * 128], identf[0:4, 0:4])
                evac(ext[:, :, :, 3], bank[:, 0:48].re("p (c b) -> p c b", c=12))
                k.tt("dve", ext.v, ext.v, wcv.v.un(2).bto([128, 12, 4, 4]), ALU.mult)
                cpre = k.sb("cpre", [128, 12, 4], F32)
                k.reduce(cpre.v, ext.v)
                k.act(G.cv[:, :, 0:4], cpre.v, AF.Silu)
                l2norm_fm(G, 4)
                g_tok, beta = gdn_scalars(4)
                eg = sm(); k.act(eg[rows, :], g_tok[rows, :], AF.Exp)
                sm_b = Rot([k.sb(f"smb{i}", [128, 4, 4], F32) for i in range(3)])

                def bc_bh(src):
                    bank = psb()
                    for b in range(4):
                        k.mm(bank[:, b * 8:(b + 1) * 8], onehot4[:, b, :], src)
                    o = sm_b()
                    for hp in range(2):
                        RH = slice(hp * 64, hp * 64 + 64)
                        evac(o[RH, :, :], bank[RH, 0:32].re("p (b pr hp) -> p b pr hp", b=4, pr=4)[:, :, :, hp])
                    return o
                eg_b = bc_bh(eg[rows, :]); beta_b = bc_bh(beta[rows, :])
                st = k.sb("st_s", [128, 4, 4, 64], F32)
                for b in range(4):
                    for hp in range(2):
                        k.dma("sp", st[hp * 64:(hp + 1) * 64, b, :, :],
                              sgdn_d.v[b].re("(pr hp) k v -> hp k pr v", hp=2)[hp])
                B4 = lambda t: t.v.un(3).bto([128, 4, 4, 64])
                k.tt("dve", st.v, st.v, B4(eg_b), ALU.mult)
                tmp = k.sb("tmp_s", [128, 4, 4, 64], F32)
                kcol = G.kTg[:, :, 0:4].re("p pr b -> p b pr")
                k.tt("dve", tmp.v, st.v, kcol.un(3).bto([128, 4, 4, 64]), ALU.mult)
                kSB = k.sb("kSB_s", [128, 4, 4, 64], F32)
                for hf in range(2):
                    bank = psb()
                    k.mm(bank.v, headblk.v, tmp[:, hf * 2:hf * 2 + 2, :, :].re("p b pr v -> p (b pr v)"))
                    evac(kSB[:, hf * 2:hf * 2 + 2, :, :].re("p b pr v -> p (b pr v)"), bank.v)
                v_tk = k.sb("v_tk", [4, 512], F32)
                bank = psb()
                for c in range(4):
                    k.tr(bank[0:4, c * 128:(c + 1) * 128], G.cv[:, 8 + c, 0:4], identf.v)
                evac(v_tk.v, bank[0:4, :])
                vB = k.sb("vB_s", [128, 4, 4, 64], F32)
                for b in range(4):
                    bank = psb()
                    k.mm(bank.v, onehot4[:, b, :], v_tk.v)
                    for hp in range(2):
                        RH = slice(hp * 64, hp * 64 + 64)
                        evac(vB[RH, b, :, :], bank[RH, :].re("p (pr hp v) -> p pr hp v", pr=4, hp=2)[:, :, hp, :])
                k.tt("dve", vB.v, vB.v, kSB.v, ALU.subtract)
                k.tt("dve", vB.v, vB.v, B4(beta_b), ALU.mult)
                k.tt("dve", tmp.v, vB.v, kcol.un(3).bto([128, 4, 4, 64]), ALU.mult)
                k.tt("dve", st.v, st.v, tmp.v, ALU.add)
                for b in range(4):
                    for hp in range(2):
                        k.dma("sp", ogdns_d.v[b].re("(pr hp) k v -> hp k pr v", hp=2)[hp], st[hp * 64:(hp + 1) * 64, b, :, :])
                oT = k.sb("oT_s", [128, 16], F32)
                for hp in range(2):
                    bank = psb()
                    RH = slice(hp * 64, hp * 64 + 64)
                    for b in range(4):
                        for pr in range(4):
                            k.mm(bank[RH, pr * 4 + b:pr * 4 + b + 1], st[RH, b, pr, :], G.qTg[RH, pr, b:b + 1])
                    evac(oT[RH, :], bank[RH, 0:16])
                osq = k.sb("osq_s", [128, 16], F32)
                k.act(osq.v, oT.v, AF.Square)
                bank = psb()
                k.mm(bank[:, 0:16], headblk.v, osq.v)
                a_ = k.sb("a_s_", [128, 16], F32); r_ = k.sb("r_s_", [128, 16], F32)
                k.ts("dve", a_.v, bank[:, 0:16], 1.0 / 64, ALU.mult, EPS, ALU.add)
                k.act(a_.v, a_.v, AF.Ln)
                k.act(r_.v, a_.v, AF.Exp, scale=-0.5)
                k.tt("dve", oT.v, oT.v, r_.v, ALU.mult)
                ggo_col = k.sb("ggo_col", [128, 1], F32)
                for hp in range(2):
                    k.dma("sp", ggo_col[hp * 64:(hp + 1) * 64, :], W["g_gdn_out"].v.re("o d -> d o"))
                k.ts("dve", oT.v, oT.v, ggo_col[:, 0:1], ALU.mult)
                k.act(G.sz_t[rows, :], z_tok[rows, C_Z:C_Z + 512], AF.Silu)
                bank = psb()
                for c in range(4):
                    k.tr(bank[:, c * 4:c * 4 + 4], G.sz_t[rows, c * 128:(c + 1) * 128], identf[0:4, 0:4])
                k.tt("dve", oms[:, 0:4, :], oT.v.re("p (pr b) -> p pr b", pr=4), bank[:, 0:16].re("p (pr b) -> p pr b", pr=4), ALU.mult)
                k.barrier()
            k.es = es1_cur[0]
            mla_q(M, 4, C["cos_s"].v, C["sin_s"].v)
            mla_kv(M, 4, ockvs_d.v, okrs_d.v)
            krb = k.sb("krb_s", [4, 32], BF16)
            k.copy("dve", krb.v, M.kr_f[0:4, :])
            krT_new = k.sb("krT_new", [32, 4], BF16)
            bank = psb(); pb = bank.v.bc(BF16)
            k.tr(pb[0:32, 0:4], krb.v, identb[0:4, 0:4])
            evac(krT_new.v, pb[0:32, 0:4])
            if DBG == 7:
                return
            WkT = k.sb("WkT", [64, 8, 256], BF16)
            wk4 = w_kvb.v.re("p k (h d) -> p k h d", h=8)
            for kk in range(2):
                bank = psb(); pb = bank.v.bc(BF16)
                for h in range(8):
                    k.tr(pb[0:64, h * 128:(h + 1) * 128], wk4[:, kk, h, 0:64], identb.v)
                evac(WkT[:, :, kk * 128:(kk + 1) * 128], pb[0:64, 0:1024].re("p (h t) -> p h t", h=8))
            qg = k.sb("qg_s", [4, 8, 64], BF16)
            k.tt("dve", qg.v, M.hh[rows, :, 0:64], g_k_nope[rows, :].un(1).bto([4, 8, 64]), ALU.mult)
            qr = k.sb("qr_s", [4, 8, 32], BF16)
            k.copy("dve", qr.v, M.hh[rows, :, 64:96])
            bank = psb(); pb = bank.v.bc(BF16)
            for h in range(8):
                k.tr(pb[0:64, h * 4:h * 4 + 4], qg[:, h, :], identb[0:4, 0:4])
                k.tr(pb[0:32, 64 + h * 4:64 + h * 4 + 4], qr[:, h, :], identb[0:4, 0:4])
            qgT = k.sb("qgT_s", [64, 8, 4], BF16); qrT = k.sb("qrT_s", [32, 8, 4], BF16)
            evac(qgT.v, pb[0:64, 0:32].re("p (h b) -> p h b", h=8))
            evac(qrT.v, pb[0:32, 64:96].re("p (h b) -> p h b", h=8))
            bank = psb()
            for kk in range(2):
                for h in range(8):
                    k.mm(bank[:, kk * 32 + h * 4:kk * 32 + h * 4 + 4], WkT[:, h, kk * 128:(kk + 1) * 128], qgT[:, h, :])
            qpT = k.sb("qpT_s", [128, 2, 4, 8], BF16)
            evac(qpT.v, bank[:, 0:64].re("p (k h b) -> p k b h", k=2, h=8))
            if DBG == 8:
                return
            pti = k.sb("pti", [128, 4 * NPG], I32)
            k.dma("sp", pti.v, pt_d.v.re("(o b) j -> o (b j)", o=1).bto([128, 4 * NPG]))
            ptf = k.sb("ptf", [128, 4 * NPG], F32)
            k.copy("dve", ptf.v, pti.v)
            iot = k.sb("iot", [128, 1], F32)
            k.op("pool", lambda g: g.iota(iot.v.ap, pattern=[[0, 1]], base=0, channel_multiplier=1,
                                          allow_small_or_imprecise_dtypes=True), [], [iot.v])
            k.ts("dve", ptf.v, ptf.v, 128.0, ALU.mult, iot[:, 0:1], ALU.add)
            idx = pti
            k.copy("dve", idx.v, ptf.v)
            G_ = 4
            pg_r = Rot([k.sb(f"pg{i}", [128, 292], BF16) for i in range(2 * G_ + 2)])
            for t_ in pg_r.items:
                k.memset("dve", t_[:, 288:289], 1.0)
            sq_r = Rot([k.sb(f"sqp{i}", [128, G_, 512], BF16) for i in range(2)])
            sq1 = sq_r.items[0][:, 0, :]
            p_r = Rot([k.sb(f"pp{i}", [128, G_ * 8], BF16) for i in range(3)])
            sg_r = Rot([k.sb(f"sgp{i}", [128, G_ * 8], F32) for i in range(6)])
            Wkc = k.sb("Wkc", [128, 2, 512], BF16); Wvc = k.sb("Wvc", [128, 2, 512], BF16)
            for kk in range(2):
                k.copy("dve", Wkc[:, kk, :].re("p (h d) -> p h d", h=8), wk4[:, kk, :, 0:64])
                k.copy("dve", Wvc[:, kk, :].re("p (h d) -> p h d", h=8), wk4[:, kk, :, 64:128])
            wk_rhs = lambda kk: Wkc[:, kk, :]
            wv_rhs = lambda kk: Wvc[:, kk, :]
            acc_sb = k.sb("acc_sb", [8, 257], F32)
            accn = k.sb("accn", [8, 256], BF16)
            accT = k.sb("accT", [128, 2, 8], BF16)
            om_f = k.sb("om_f", [8, 512], F32)
            trb = Rot([banks[0]]); bankA = banks[1:5]; bB = banks[5]
            qr_f = k.sb("qr_f", [4, 256], F32)
            k.copy("dve", qr_f.v.re("p (h d) -> p h d", h=8), M.hh[rows, :, 64:96])
            qrB = k.sb("qrB", [128, 4, 256], BF16)
            for b in range(4):
                bank = trb()
                k.mm(bank[:, 0:256], onehot4[:, b, :], qr_f.v)
                evac(qrB[:, b, :], bank[:, 0:256])
            rp_r = Rot([k.sb(f"rp{i}", [128, G_, 256], BF16) for i in range(2)])

            def newtok(b):
                rws = slice(0, 4)
                bA = bankA[0]
                for kk in range(2):
                    k.mm(bA[rws, 0:512], M.ckvT[:, kk, 0:4], wk_rhs(kk), start=(kk == 0), stop=(kk == 1))
                for kk in range(2):
                    k.mm(bB[rws, 0:8], M.ckvT[:, kk, 0:4], qpT[:, kk, b, :], start=(kk == 0), stop=(kk == 1))
                k.mm(bB[rws, 8:16], krT_new[:, 0:4], qrT[:, :, b])
                k.act(sq1[rws, :], bA[rws, :], AF.Square)
                ss = sm()
                k.reduce(ss[rws, :], sq1[rws, :].re("p (h d) -> p h d", h=8))
                r = rstd_from_ss(ss[rws, :], 8, 64, rws)
                s1 = sm()
                k.tt("dve", s1[rws, :], bB[rws, 0:8], r, ALU.mult)
                k.tt("dve", s1[rws, :], s1[rws, :], bB[rws, 8:16], ALU.add)
                s2 = sm()
                k.act(s2[rws, :], s1[rws, :], AF.Exp)
                pp = p_r()
                k.ts("dve", pp[rws, 0:8], s2[rws, :], identf[0:4, b:b + 1], ALU.mult)
                return pp

            trb2 = Rot([banks[0], banks[6]])
            cT4_r = Rot([k.sb(f"cT4_{i}", [128, 4, 2, 128], BF16) for i in range(2)])

            def frontA(b, j0, g):
                pgs = []
                tb = trb2(); pb = tb.v.bc(BF16)
                for i in range(g):
                    pg = pg_r(); pgs.append(pg)
                    col = b * NPG + j0 + i
                    k.gather(pg[:, 0:288], ccat_d.v, idx[:, col:col + 1])
                for i in range(g):
                    k.tr(pb[:, i * 256:i * 256 + 128], pgs[i][:, 32:160], identb.v)
                    k.tr(pb[:, i * 256 + 128:i * 256 + 256], pgs[i][:, 160:288], identb.v)
                cT4 = cT4_r()
                k.act(cT4[:, 0:g, :, :], pb[:, 0:g * 256].re("p (g j t) -> p g j t", g=g, j=2), AF.Copy)
                return pgs, cT4

            def frontB(b, g, pgs, cT4):
                sq = sq_r(); rp = rp_r()
                for i in range(g):
                    for kk in range(2):
                        k.mm(bankA[i][:, 0:512], cT4[:, i, kk, :], wk_rhs(kk), start=(kk == 0), stop=(kk == 1))
                    for kk in range(2):
                        k.mm(bB[:, i * 8:i * 8 + 8], cT4[:, i, kk, :], qpT[:, kk, b, :], start=(kk == 0), stop=(kk == 1))
                    k.act(sq[:, i, :], bankA[i].v, AF.Square)
                    for _d in range(NDUMMY):
                        k.op("pe", lambda e: e.matmul(accbank[64:128, 0:512].ap, lhsT=Wkc[:, 0, 0:64].ap, rhs=Wkc[:, 1, :].ap,
                                                      start=True, stop=True), [], [])
                    k.tt("dve", rp[:, i, :].re("p (h d) -> p h d", h=8), pgs[i][:, 0:32].un(1).bto([128, 8, 32]),
                         qrB[:, b, :].re("p (h d) -> p h d", h=8), ALU.mult)
                return sq, rp

            def small(g, sq, rp):
                n8 = g * 8
                sr = sg_r()
                k.reduce(sr[:, 0:n8], rp[:, 0:g, :].re("p g (h d) -> p (g h) d", h=8))
                ss = sg_r()
                k.reduce(ss[:, 0:n8], sq[:, 0:g, :].re("p g (h d) -> p (g h) d", h=8))
                a = sg_r()
                k.ts("dve", a[:, 0:n8], ss[:, 0:n8], 1.0 / 64, ALU.mult, EPS, ALU.add)
                k.act(a[:, 0:n8], a[:, 0:n8], AF.Ln)
                r = sg_r()
                k.act(r[:, 0:n8], a[:, 0:n8], AF.Exp, scale=-0.5)
                s1 = sg_r()
                k.tt("dve", s1[:, 0:n8], bB[:, 0:n8], r[:, 0:n8], ALU.mult)
                k.tt("dve", s1[:, 0:n8], s1[:, 0:n8], sr[:, 0:n8], ALU.add)
                pp = p_r()
                k.act(pp[:, 0:n8], s1[:, 0:n8], AF.Exp)
                return pp

            def accm(j0, g, pgs, pp):
                for i in range(g):
                    k.mm(accbank[0:8, 0:257], pp[:, i * 8:(i + 1) * 8], pgs[i][:, 32:289], start=False,
                         stop=(j0 + i == NPG - 1))

            for b in range(4):
                pp = newtok(b)
                k.mm(accbank[0:8, 0:257], pp[0:4, 0:8], M.ckv_bf[0:4, 0:257], start=True, stop=(NPG == 0))
                groups = [(j0, min(G_, NPG - j0)) for j0 in range(0, NPG, G_)]
                if groups:
                    pgs, cT4 = frontA(b, *groups[0])
                    sq, rp = frontB(b, groups[0][1], pgs, cT4)
                    ppg = small(groups[0][1], sq, rp)
                for gi, (j0, g) in enumerate(groups):
                    cur = (pgs, ppg)
                    if gi + 1 < len(groups):
                        pgs, cT4 = frontA(b, *groups[gi + 1])
                    accm(j0, g, *cur)
                    if gi + 1 < len(groups):
                        sq, rp = frontB(b, groups[gi + 1][1], pgs, cT4)
                        ppg = small(groups[gi + 1][1], sq, rp)
                evac(acc_sb.v, accbank[0:8, 0:257])
                rl = sm(); k.recip(rl[0:8, 0:1], acc_sb[:, 256:257])
                k.ts("dve", accn.v, acc_sb[:, 0:256], rl[0:8, 0:1], ALU.mult)
                bank = trb(); pb = bank.v.bc(BF16)
                for kk in range(2):
                    k.tr(pb[:, kk * 8:kk * 8 + 8], accn[:, kk * 128:(kk + 1) * 128], identb[0:8, 0:8])
                evac(accT.v, pb[:, 0:16].re("p (k h) -> p k h", k=2))
                bank = trb()
                for kk in range(2):
                    k.mm(bank[0:8, :], accT[:, kk, :], wv_rhs(kk), start=(kk == 0), stop=(kk == 1))
                k.tt("dve", om_f.v, bank[0:8, :], diagmask.v, ALU.mult)
                bank2 = trb()
                for pr in range(4):
                    k.mm(bank2[:, pr:pr + 1], om_f[:, pr * 128:(pr + 1) * 128], onesf[0:8, 0:1])
                evac(oms[:, 4:8, b], bank2[:, 0:4])
            k.dma("sp", omixs_d.v, oms.v)

        def pass_a():
            w_in = wload_cols("w_in", 1024, 0, 2064)
            G = alloc_gdn()
            S2 = k.sb("S2", [128, 4, 128], F32)
            k.memset("pool", S2.v, 0.0)
            zcT = k.sb("zcT", [128, 12, 131], F32)
            k.memset("pool", zcT.v, 0.0)
            omixA = k.sb("omixA", [128, 4, 512], BF16)
            acc_r = Rot([k.sb(f"acc_c{i}", [128, 128], F32) for i in range(2)])
            accp_r = Rot([k.sb(f"acc_p{i}", [128, 128], F32) for i in range(2)])
            tmp_p = k.sb("tmp_p", [128, 128], F32)
            kbT = k.sb("kbT", [128, 4, 128], BF16); nwT = k.sb("nwT", [128, 4, 128], BF16)
            qgT = k.sb("qgT", [128, 4, 128], BF16)
            kT_b = k.sb("kT_b", [128, 4, 128], BF16); qT_b = k.sb("qT_b", [128, 4, 128], BF16)
            S2b = k.sb("S2b", [128, 4, 128], BF16)
            k.memset("pool", S2b.v, 0.0)
            egc_fm = k.sb("egc_fm", [128, 4, 128], F32)
            k_tok = G.on_t
            v_tok = k.sb("v_tok", [128, 512], F32); u_tok = v_tok
            vb_tok = k.sb("vb_tok", [128, 512], BF16)
            kbg_tok = k.sb("kbg_tok", [128, 512], BF16)
            kdec_tok = k.sb("kdec_tok", [128, 512], BF16)
            vnew_tok = k.sb("vnew_tok", [128, 512], BF16)
            gcT8 = k.sb("gcT8", [8, 128], F32)
            betaT8 = k.sb("betaT8", [8, 128], F32)
            qkT_all = k.sb("qkT_all", [128, 8, 128], BF16)
            U_all = k.sb("U_all", [128, 8, 128], BF16)
            g4 = Rot([k.sb(f"g4_{i}", [128, 4, 128], F32) for i in range(3)])
            r4s = [Rot([k.sb(f"r4_{j}_{i}", [128, 4, 128], INV_DT) for i in range(6)]) for j in range(2)]
            t4 = k.sb("t4", [128, 4, 128], F32)
            v4 = lambda b_: b_.v.re("p (c t) -> p c t", c=4)

            def f1_steps(bi):
                steps = []
                t0 = bi * 128

                def s_in():
                    inproj(x_d.v[t0:t0 + 128, :], 128, w_in, 0, GROUPS_A)
                    if bi == NBLK - 1:
                        k.dma("sp", oconv_d.v, z_tok[125:128, 0:1536])
                steps.append(s_in)

                def s_tr(g3):
                    def f():
                        bank = psb()
                        for c in range(4):
                            k.tr(bank[:, c * 128:(c + 1) * 128], z_tok[:, (g3 * 4 + c) * 128:(g3 * 4 + c + 1) * 128], identf.v)
                        evac(zcT[:, g3 * 4:g3 * 4 + 4, 3:131], v4(bank))
                    return f
                for g3 in range(3):
                    steps.append(s_tr(g3))

                def s_cv(c):
                    def f():
                        ac = acc_r()
                        k.ts("dve", ac.v, zcT[:, c, 0:128], wcv[:, c, 0:1], ALU.mult)
                        for j in (1, 2, 3):
                            k.stt(ac.v, zcT[:, c, j:j + 128], wcv[:, c, j:j + 1], ac.v, ALU.mult, ALU.add)
                        k.act(G.cv[:, c, :], ac.v, AF.Silu)
                    return f
                for c in range(12):
                    steps.append(s_cv(c))
                steps.append(lambda: k.copy("pool", zcT[:, :, 0:3], zcT[:, :, 128:131]))
                return steps

            def gdn_block(tl, inj):
                cv = G.cv; kTg = G.kTg; qTg = G.qTg

                def pump(n):
                    for _ in range(n):
                        if inj:
                            inj.pop(0)()
                l2norm_fm(G, 128)
                k.act(G.sz_t.v, z_tok[:, C_Z:C_Z + 512], AF.Silu)
                k.copy("pool", kT_b.v, kTg.v)
                k.copy("pool", qT_b.v, qTg.v)
                bank = psb()
                for c in range(4):
                    k.tr(bank[:, c * 128:(c + 1) * 128], kTg[:, c, :], identf.v)
                evac(k_tok.v, bank.v)
                bank = psb()
                for c in range(4):
                    k.tr(bank[:, c * 128:(c + 1) * 128], cv[:, 8 + c, :], identf.v)
                evac(v_tok.v, bank.v)
                g_tok, beta = gdn_scalars(128)
                bank = psb()
                k.mm(bank[:, 0:8], tri.v, g_tok.v)
                gc = sm(); evac(gc.v, bank[:, 0:8])
                bank = psb()
                k.mm(bank[:, 0:8], lastsel.v, gc.v)
                dd = sm(); k.tt("dve", dd.v, bank[:, 0:8], gc.v, ALU.subtract)
                edec = sm(); k.act(edec.v, dd.v, AF.Exp)
                egc = sm(); k.act(egc.v, gc.v, AF.Exp)
                bge = sm(); k.tt("dve", bge.v, beta.v, egc.v, ALU.mult)
                b3 = lambda t: t.v.un(2).bto([128, 8, 64])
                r3 = lambda t: t.v.re("p (h d) -> p h d", h=8)
                k.tt("dve", r3(vb_tok), r3(v_tok), b3(beta), ALU.mult)
                k.tt("pool", r3(kdec_tok), r3(k_tok), b3(edec), ALU.mult)
                k.tt("pool", r3(kbg_tok), r3(k_tok), b3(bge), ALU.mult)
                bank = psb()
                k.tr(bank[0:8, 0:128], gc.v, identf.v)
                k.tr(bank[0:8, 128:256], beta.v, identf.v)
                evac(gcT8.v, bank[0:8, 0:128]); evac(betaT8.v, bank[0:8, 128:256])
                bank = psb()
                for pr in range(4):
                    k.mm(bank[:, pr * 128:(pr + 1) * 128], selpair[:, pr, :], gcT8.v)
                k.act(egc_fm.v, v4(bank), AF.Exp)
                bank = psb()
                for pr in range(4):
                    k.mm(bank[:, pr * 128:(pr + 1) * 128], selpair[:, pr, :], betaT8.v)
                k.tt("dve", kbT.v, kTg.v, v4(bank), ALU.mult)
                k.tt("dve", qgT.v, qTg.v, egc_fm.v, ALU.mult)
                R = lambda h: slice((h % 2) * 64, (h % 2) * 64 + 64)
                st8 = []
                for hg in range(2):
                    hs = [hg * 4 + i for i in range(4)]
                    r4 = r4s[hg]
                    bcb = psb()
                    for i, h in enumerate(hs):
                        k.mm(bcb[:, i * 128:(i + 1) * 128], sel8[:, h, :], gcT8.v)
                    d1 = g4()
                    k.tt("dve", d1.v, v4(bcb), gc[:, hg * 4:hg * 4 + 4].un(2).bto([128, 4, 128]), ALU.subtract)
                    e1 = g4()
                    k.tt("dve", e1.v, d1.v, negs.v.un(1).bto([128, 4, 128]), ALU.max)
                    Dm = g4()
                    k.act(Dm.v, e1.v, AF.Exp, scale=-1.0)
                    k.tt("dve", d1.v, d1.v, negt.v.un(1).bto([128, 4, 128]), ALU.min)
                    DTm = e1
                    k.act(DTm.v, d1.v, AF.Exp)
                    Bt = r4(); Ct = r4(); St = r4()
                    k.tt("pool", d1.v, DTm.v, identf.v.un(1).bto([128, 4, 128]), ALU.add)
                    for hp_ in range(2):
                        bKB = psb(); bKBT = psb(); bQKT = psb()
                        for which in range(3):
                            for i, h in enumerate(hs):
                                if h % 2 != hp_:
                                    continue
                                pr = h // 2
                                cs_ = slice((i // 2) * 128, (i // 2 + 1) * 128)
                                if which == 0:
                                    k.mm(bKB[:, cs_], kbT[R(h), pr, :], kT_b[R(h), pr, :])
                                elif which == 1:
                                    k.mm(bKBT[:, cs_], kT_b[R(h), pr, :], kbT[R(h), pr, :])
                                else:
                                    k.mm(bQKT[:, cs_], kT_b[R(h), pr, :], qT_b[R(h), pr, :])
                        v2 = lambda b_: b_[:, 0:256].re("p (c t) -> p c t", c=2)
                        k.stt(Bt[:, hp_::2, :], v2(bKB), -1.0, Dm[:, hp_::2, :], ALU.mult, ALU.mult)
                        k.stt(Ct[:, hp_::2, :], v2(bKBT), -1.0, DTm[:, hp_::2, :], ALU.mult, ALU.mult)
                        k.tt("dve", qkT_all[:, hg * 4 + hp_:hg * 4 + 4:2, :], v2(bQKT), d1[:, hp_::2, :], ALU.mult)
                    k.tt("pool", St.v, Ct.v, identf.v.un(1).bto([128, 4, 128]), ALU.add)
                    st8.append([Bt, Ct, St])
                    if hg == 1:
                        pump(1)
                for lvl in range(1, 6):
                    nBs = []
                    for hg in range(2):
                        Bt, Ct, St = st8[hg]
                        r4 = r4s[hg]
                        bB = psb()
                        for i in range(4):
                            k.mm(bB[:, i * 128:(i + 1) * 128], Ct[:, i, :], Bt[:, i, :])
                        nB = r4()
                        k.act(nB.v, v4(bB), AF.Copy)
                        nC = None
                        if lvl < 5:
                            bC = psb()
                            for i in range(4):
                                k.mm(bC[:, i * 128:(i + 1) * 128], Bt[:, i, :], Ct[:, i, :])
                            nC = r4()
                            k.act(nC.v, v4(bC), AF.Copy)
                        nBs.append((nB, nC))
                        pump(1)
                    for hg in range(2):
                        Bt, Ct, St = st8[hg]
                        nB, nC = nBs[hg]
                        r4 = r4s[hg]
                        bS = psb()
                        for i in range(4):
                            k.mm(bS[:, i * 128:(i + 1) * 128], nB[:, i, :], St[:, i, :])
                        if lvl < 5:
                            nS = r4()
                            k.tt("dve", nS.v, v4(bS), St.v, ALU.add)
                            st8[hg] = [nB, nC, nS]
                        else:
                            k.tt("dve", U_all[:, hg * 4:hg * 4 + 4, :], v4(bS), St.v, ALU.add)
                    pump(1)
                ub = psb(); wb = psb()
                for h in range(8):
                    pr = h // 2; Rh = slice((h % 2) * 64, (h % 2) * 64 + 64)
                    k.mm(ub[:, h * 64:(h + 1) * 64], U_all[:, h, :], vb_tok[:, h * 64:(h + 1) * 64])
                    k.mm(wb[Rh, pr * 128:(pr + 1) * 128], kbg_tok[:, h * 64:(h + 1) * 64], U_all[:, h, :])
                evac(u_tok.v, ub.v)
                k.act(nwT.v, v4(wb), AF.Copy, scale=-1.0)
                for ci in range(2):
                    RR = slice(ci * 64, ci * 64 + 64)
                    vbk = psb()
                    for pr in range(4):
                        k.mm(vbk[RR, pr * 128:(pr + 1) * 128], nwT[:, pr, RR], S2b[:, pr, :])
                    k.tt("dve", vnew_tok[RR, :], vbk[RR, :], u_tok[RR, :], ALU.add)
                    obk = psb()
                    for pr in range(4):
                        k.mm(obk[RR, pr * 128:(pr + 1) * 128], qgT[:, pr, RR], S2b[:, pr, :], start=True, stop=False)
                        for hp in range(2):
                            h = 2 * pr + hp
                            k.mm(obk[RR, pr * 128 + hp * 64:pr * 128 + (hp + 1) * 64], qkT_all[RR, h, RR],
                                 vnew_tok[RR, h * 64:(h + 1) * 64], start=False, stop=(hp == 1))
                    k.act(G.o_tok[RR, :], obk[RR, :], AF.Copy)
                    sbk = psb()
                    for pr in range(4):
                        cs = slice(pr * 128, (pr + 1) * 128)
                        k.mm(sbk[:, cs], kdec_tok[RR, cs], vnew_tok[RR, cs])
                    k.tt("dve", t4.v, v4(sbk), headblk.v.un(1).bto([128, 4, 128]), ALU.mult)
                    k.tt("pool", S2.v, S2.v, egc_fm[:, :, ci * 64 + 63:ci * 64 + 64].bto([128, 4, 128]), ALU.mult)
                    k.tt("pool", S2.v, S2.v, t4.v, ALU.add)
                    k.copy("pool", S2b.v, S2.v)
                    pump(1)
                gdn_out_tok(G, 128)
                pb = transpose_bf(None, G.og_bf.v, 128, 4)
                evac(omixA[:, :, tl * 128:(tl + 1) * 128], pb[:, 0:512].re("p (j t) -> p j t", j=4))
                pump(len(inj))

            for st_ in f1_steps(0):
                st_()
            for bi in range(NBLK):
                tl = bi % 4
                gdn_block(tl, f1_steps(bi + 1) if bi + 1 < NBLK else [])
                if tl == 3:
                    t = bi // 4
                    k.dma("sp", omix_d.v[:, 0:4, t * 512:(t + 1) * 512], omixA.v)
            for pr in range(4):
                for hp in range(2):
                    RH = slice(hp * 64, hp * 64 + 64)
                    k.dma("sp", ogdn_d.v[2 * pr + hp], S2[RH, pr, hp * 64:hp * 64 + 64])

        def pass_b():
            psb.items = banks[:3]
            scb = Rot([banks[3], banks[4]])
            w_in = wload_cols("w_in", 1024, C_QA, IN_DIM)
            w_qb = wload_cols("w_q_b", 384, 0, 768)
            w_kvb = wload_cols("w_kv_b", 256, 0, 1024)
            M = alloc_mla(w_qb, w_kvb)
            Mk = alloc_mla(w_qb, w_kvb)
            Mk.cos_t = M.cos_t; Mk.sin_t = M.sin_t
            sq_t2 = k.sb("sq_t2", [128, 512], F32)
            kT = k.sb("kT", [128, 8, S], BF16)
            v_sb = k.sb("v_sb", [128, NBLK, 8, 64], BF16)
            qT_tiles = [k.sb(f"qT_tile{i}", [128, 8, 512], BF16) for i in range(2)]
            omixB = k.sb("omixB", [128, 4, 512], BF16)
            pT_r = Rot([k.sb(f"pT{i}", [128, 512], BF16) for i in range(4)])
            rl_t = k.sb("rl_t", [128, 512], F32)

            def mla_block(bi, tl, qT_tile):
                t0 = bi * 128

                def chain_q():
                    mla_q(M, 128, C["cos_p"].v[t0:t0 + 128, :], C["sin_p"].v[t0:t0 + 128, :])
                    k.copy("pool", M.hh_bf.v, M.hh.v)
                    bank = psb(); pb = bank.v.bc(BF16)
                    for h in range(8):
                        k.tr(pb[0:96, h * 128:(h + 1) * 128], M.hh_bf[:, h, :], identb.v)
                    evac(qT_tile[0:96, :, tl * 128:(tl + 1) * 128], pb[0:96, 0:1024].re("p (h t) -> p h t", h=8))

                def chain_k():
                    mla_kv(Mk, 128, ockv_d.v[t0:t0 + 128, :], okr_d.v[t0:t0 + 128, :])
                    for (c0, c1) in ((0, 512), (512, 1024)):
                        bank = psb()
                        for kk in range(2):
                            k.mm(bank[:, 0:512], Mk.ckvT[:, kk, :], w_kvb[:, kk, c0:c1], start=(kk == 0), stop=(kk == 1))
                        evac(Mk.qkv_tok[:, c0:c1], bank.v)
                    kv3 = Mk.qkv_tok.v.re("p (h d) -> p h d", h=8)
                    head_rms(Mk.hh[:, :, 0:64], kv3[:, :, 0:64], g_k_nope.v, 128, 64, scratch=sq_t2)
                    k.copy("pool", Mk.hh[:, :, 64:96], Mk.kr_f.v.un(1).bto([128, 8, 32]))
                    k.copy("pool", Mk.hh_bf.v, Mk.hh.v)
                    k.copy("dve", v_sb[:, bi, :, :], kv3[:, :, 64:128])
                    bank = psb(); pb = bank.v.bc(BF16)
                    for h in range(8):
                        k.tr(pb[0:96, h * 128:(h + 1) * 128], Mk.hh_bf[:, h, :], identb.v)
                    evac(kT[0:96, :, t0:t0 + 128], pb[0:96, 0:1024].re("p (h t) -> p h t", h=8))

                outer = k.rec
                k.rec = []
                psb.items = [banks[0]]
                chain_q()
                rq = k.rec
                k.rec = []
                psb.items = [banks[1], banks[2]]
                chain_k()
                rk = k.rec
                psb.items = banks[:3]
                k.rec = outer
                merged = []
                nq, nk = len(rq), len(rk)
                iq = ik = 0
                while iq < nq or ik < nk:
                    if ik >= nk or (iq < nq and iq * nk <= ik * nq):
                        merged.append(rq[iq]); iq += 1
                    else:
                        merged.append(rk[ik]); ik += 1
                if outer is not None:
                    outer.extend(merged)
                else:
                    k.play(merged)

            def attention_tile(t):
                qT_tile = qT_tiles[t % 2]
                for pr in range(4):
                    o_ps = banks[5]; l_ps = banks[6]
                    for hp in range(2):
                        h = 2 * pr + hp
                        RH = slice(hp * 64, hp * 64 + 64)
                        nkb = 4 * t + 4
                        def stepA(j):
                            qlo = max(0, j - 4 * t)
                            ncol = (4 - qlo) * 128
                            qc = slice(qlo * 128, 512)
                            sc = scb()
                            k.mm(sc[:, 0:ncol], kT[0:96, h, j * 128:(j + 1) * 128], qT_tile[0:96, h, qc])
                            pT = pT_r()
                            k.act(pT[:, 0:ncol], sc[:, 0:ncol], AF.Exp)
                            if j >= 4 * t:
                                k.tt("pool", pT[:, 0:128], pT[:, 0:128], cmask.v, ALU.mult)
                            return pT, ncol, qc

                        def stepB(j, pT, ncol, qc):
                            k.mm(o_ps[RH, qc], v_sb[:, j, h, :], pT[:, 0:ncol], start=(j == 0), stop=(j == nkb - 1))
                            k.mm(l_ps[RH, qc], onesb.v, pT[:, 0:ncol], start=(j == 0), stop=(j == nkb - 1))

                        pend = stepA(0)
                        for j in range(nkb):
                            cur = pend
                            if j + 1 < nkb:
                                pend = stepA(j + 1)
                            stepB(j, *cur)
                    k.recip(rl_t.v, l_ps.v)
                    k.tt("dve", omixB[:, pr, :], o_ps.v, rl_t.v, ALU.mult)

            def merge2(ra, rb):
                out = []
                na, nb_ = len(ra), len(rb)
                ia = ib = 0
                while ia < na or ib < nb_:
                    if ib >= nb_ or (ia < na and ia * nb_ <= ib * na):
                        out.append(ra[ia]); ia += 1
                    else:
                        out.append(rb[ib]); ib += 1
                return out

            att_rec = None
            for t in range(NT):
                k.rec = []
                for bi in range(4 * t, 4 * t + 4):
                    t0 = bi * 128
                    inproj(x_d.v[t0:t0 + 128, :], 128, w_in, C_QA, GROUPS_B)
                    mla_block(bi, bi % 4, qT_tiles[t % 2])
                blk_rec = k.rec
                k.rec = None
                k.play(blk_rec if att_rec is None else merge2(att_rec, blk_rec))
                k.rec = []
                attention_tile(t)
                k.dma("sp", omix_d.v[:, 4:8, t * 512:(t + 1) * 512], omixB.v)
                att_rec = k.rec
                k.rec = None
            k.play(att_rec)

        k.es_saved = esP1
        es1_cur = [None]
        for name_, fn_ in (("sample", sample_scope), ("a", pass_a), ("b", pass_b)):
            es1 = contextlib.ExitStack()
            with es1:
                k.es = es1
                es1_cur[0] = es1
                fn_()
                k.barrier()
            k.es = k.es_saved
            if stop_after == name_:
                break
        psb.items = banks[:7]
        esP1.__exit__(None, None, None)
        k.es = k.es_outer

        if stop_after is None:
            w_dn = wload("w_ffn_down", D_FF, 1024)
            wg_scr = k.dram("wg_scr", [128, 8, D_FF], BF16, "Internal")
            wu_scr = k.dram("wu_scr", [128, 8, D_FF], BF16, "Internal")
            first_stream = [True]
            g_ffn = bload("g_ffn", 1024); g_ple = bload("g_ple", 1024)
            wg_r = Rot([k.sb(f"wg{i}", [128, 8, 512], BF16) for i in range(2)])
            wu_r = Rot([k.sb(f"wu{i}", [128, 8, 512], BF16) for i in range(2)])
            om_t = k.sb("om_t", [128, 8, 512], BF16)
            h1 = k.sb("h1", [128, 4, 1024], F32)
            un = k.sb("un", [128, 1024], BF16)
            uT = k.sb("uT", [128, 8, 512], BF16)
            hT = k.sb("hT", [128, 22, 512], BF16)
            sg = k.sb("sg", [128, 512], F32)
            sg2 = k.sb("sg2", [128, 512], F32)
            sg_rot = Rot([sg, sg2])
            p_t = k.sb("p_t", [128, 256], F32)
            p_bf = k.sb("p_bf", [128, 256], BF16)
            pT2 = k.sb("pT2", [128, 2, 128], BF16)
            gate_t = sg
            wgs = W["w_ffn_gate"].v.re("(k p) n -> p k n", p=128)
            wus = W["w_ffn_up"].v.re("(k p) n -> p k n", p=128)

            def phase2_tile(nt, om_src, x_src, p_src, y_dst):
                nb = (nt + 127) // 128
                bt = min(nt, 128)
                rows = slice(0, bt)
                k.dma("sp", om_t[:, :, 0:nt], om_src)
                for b in range(nb):
                    k.dma("sp", h1[rows, b, :], x_src(b))
                for b in range(nb):
                    cb = slice(b * 128, b * 128 + bt)
                    for hf in range(2):
                        bank = psb()
                        for kk in range(8):
                            k.mm(bank[rows, :], om_t[:, kk, cb], w_o[:, kk, hf * 512:(hf + 1) * 512], start=(kk == 0), stop=(kk == 7))
                        k.tt("dve", h1[rows, b, hf * 512:(hf + 1) * 512], bank[rows, :], h1[rows, b, hf * 512:(hf + 1) * 512], ALU.add)
                    rmsnorm_rows(un[rows, :], h1[rows, b, :], g_ffn[rows, :], 1024, rows)
                    pb = transpose_bf(None, un[rows, :], bt, 8)
                    evac(uT[:, :, cb], pb[:, 0:1024].re("p (j t) -> p j t", j=8)[:, :, 0:bt])
                for c0 in range(0, D_FF, 512):
                    c1 = min(D_FF, c0 + 512)
                    wg = wg_r(); wu = wu_r()
                    if first_stream[0]:
                        k.dma("pool", wg[:, :, 0:c1 - c0], wgs[:, :, c0:c1])
                        k.dma("pool", wu[:, :, 0:c1 - c0], wus[:, :, c0:c1])
                        k.dma("sp", wg_scr.v[:, :, c0:c1], wg[:, :, 0:c1 - c0])
                        k.dma("sp", wu_scr.v[:, :, c0:c1], wu[:, :, 0:c1 - c0])
                    else:
                        k.dma("sp", wg[:, :, 0:c1 - c0], wg_scr.v[:, :, c0:c1])
                        k.dma("sp", wu[:, :, 0:c1 - c0], wu_scr.v[:, :, c0:c1])
                    for m in range(c0 // 128, c1 // 128):
                        ms_ = slice(m * 128 - c0, (m + 1) * 128 - c0)
                        bg = psb(); bu = psb()
                        for kk in range(8):
                            k.mm(bg[:, 0:nt], wg[:, kk, ms_], uT[:, kk, 0:nt], start=(kk == 0), stop=(kk == 7))
                        for kk in range(8):
                            k.mm(bu[:, 0:nt], wu[:, kk, ms_], uT[:, kk, 0:nt], start=(kk == 0), stop=(kk == 7))
                        sgt = sg_rot()
                        k.act(sgt[:, 0:nt], bg[:, 0:nt], AF.Silu)
                        k.tt("dve", hT[:, m, 0:nt], sgt[:, 0:nt], bu[:, 0:nt], ALU.mult)
                for b in range(nb):
                    cb = slice(b * 128, b * 128 + bt)
                    for hf in range(2):
                        bank = psb()
                        for m in range(22):
                            k.mm(bank[rows, :], hT[:, m, cb], w_dn[:, m, hf * 512:(hf + 1) * 512], start=(m == 0), stop=(m == 21))
                        k.tt("dve", h1[rows, b, hf * 512:(hf + 1) * 512], bank[rows, :], h1[rows, b, hf * 512:(hf + 1) * 512], ALU.add)
                    rmsnorm_rows(un[rows, :], h1[rows, b, :], g_ple[rows, :], 1024, rows)
                    pb = transpose_bf(None, un[rows, :], bt, 8)
                    evac(uT[:, :, cb], pb[:, 0:1024].re("p (j t) -> p j t", j=8)[:, :, 0:bt])
                    k.dma("sp", p_t[rows, :], p_src(b))
                    k.copy("pool", p_bf[rows, :], p_t[rows, :])
                    pb = transpose_bf(None, p_bf[rows, :], bt, 2)
                    evac(pT2[:, :, 0:bt], pb[:, 0:256].re("p (j t) -> p j t", j=2)[:, :, 0:bt])
                    for hf in range(2):
                        hs_ = slice(hf * 512, (hf + 1) * 512)
                        bank = psb()
                        for kk in range(8):
                            k.mm(bank[rows, :], uT[:, kk, cb], w_pg[:, kk, hs_], start=(kk == 0), stop=(kk == 7))
                        k.act(gate_t[rows, :], bank[rows, :], AF.Sigmoid)
                        bank2 = psb()
                        for kk in range(2):
                            k.mm(bank2[rows, :], pT2[:, kk, 0:bt], w_pp[:, kk, hs_], start=(kk == 0), stop=(kk == 1))
                        k.tt("dve", gate_t[rows, :], bank2[rows, :], gate_t[rows, :], ALU.mult)
                        k.tt("pool", h1[rows, b, hs_], gate_t[rows, :], h1[rows, b, hs_], ALU.add)
                    k.dma("sp", y_dst(b), h1[rows, b, :])

            phase2_tile(4, omixs_d.v, lambda b: xs_d.v, lambda b: psm_d.v, lambda b: ys_d.v)
            first_stream[0] = False
            for t in range(NT):
                q0 = t * 512
                phase2_tile(512, omix_d.v[:, :, q0:q0 + 512],
                            lambda b: x_d.v[q0 + b * 128:q0 + (b + 1) * 128, :],
                            lambda b: p_d.v[q0 + b * 128:q0 + (b + 1) * 128, :],
                            lambda b: y_d.v[q0 + b * 128:q0 + (b + 1) * 128, :])
        k.finish()
    return nc, k


_NC_CACHE = {}


def run_cores(inputs, S, NPG, NPHYS, stop_after=None, ncores=8):
    key = (S, NPG, NPHYS, stop_after)
    if key not in _NC_CACHE:
        _NC_CACHE[key] = build(S, NPG, NPHYS, stop_after)
    nc, kb = _NC_CACHE[key]
    f = lambda a: np.ascontiguousarray(np.asarray(a))
    consts = host_consts(S, NPG * 128)
    cache_cat = np.concatenate([np.asarray(inputs["cache_krope"][0]).reshape(NPHYS * 128, 32),
                                np.asarray(inputs["cache_ckv"][0]).reshape(NPHYS * 128, 256)], axis=1)
    cache_cat = np.ascontiguousarray(cache_cat, dtype=np.float32)
    wmap = {n: f(inputs[n][0]).reshape(s) for n, s in W_SHAPES.items()}
    in_maps = []
    for c in range(ncores):
        m = dict(wmap)
        m.update(consts)
        m["x"] = f(inputs["x_prompt"][c])
        m["xs"] = f(inputs["x_sample"][4 * c:4 * c + 4, 0])
        m["p"] = f(inputs["p_prompt"][0, c])
        m["psm"] = f(inputs["p_sample"][0, 4 * c:4 * c + 4, 0])
        m["cache_cat"] = cache_cat
        m["state_gdn"] = f(inputs["state_gdn"][0, 4 * c:4 * c + 4])
        m["state_conv"] = f(inputs["state_conv"][0, 4 * c:4 * c + 4]).reshape(4, 3 * 1536)
        m["pt"] = f(inputs["page_table"][4 * c:4 * c + 4]).astype(np.int32)
        in_maps.append(m)
    res = run_bass_kernel_spmd(nc, in_maps, core_ids=list(range(ncores))).results
    g = lambda n: np.stack([np.asarray(r[n]) for r in res])
    y = g("y"); ys = g("ys").reshape(4 * ncores, 1, 1024)
    return (y, ys, g("o_ckv")[None], g("o_kr")[None], g("o_gdn")[None], g("o_conv")[None],
            g("o_ckv_s").reshape(1, 4 * ncores, 1, 256), g("o_kr_s").reshape(1, 4 * ncores, 1, 32),
            g("o_gdn_s").reshape(1, 4 * ncores, 8, 64, 64), g("o_conv_s").reshape(1, 4 * ncores, 3, 1536))


def kernel(**inputs):
    S = inputs["x_prompt"].shape[1]
    NPG = inputs["page_table"].shape[1]
    NPHYS = inputs["cache_ckv"].shape[1]
    outs = run_cores(inputs, S, NPG, NPHYS)
    return tuple(np.ascontiguousarray(o.astype(np.float32)) for o in outs)
```

```python
import contextlib
import numpy as np
import concourse.bass as bass
import concourse.mybir as mybir

F32 = mybir.dt.float32
F32R = mybir.dt.float32r
BF16 = mybir.dt.bfloat16
I32 = mybir.dt.int32
AF = mybir.ActivationFunctionType
ALU = mybir.AluOpType
AX = mybir.AxisListType
SKIP_SELF_WAIT = False


class V:
    __slots__ = ("t", "ap")

    def __init__(self, t, ap):
        self.t = t
        self.ap = ap

    def __getitem__(self, idx):
        return V(self.t, self.ap[idx])

    def re(self, pat, **kw):
        return V(self.t, self.ap.rearrange(pat, **kw))

    def bc(self, dtype):
        return V(self.t, self.ap.bitcast(dtype))

    def un(self, axis):
        return V(self.t, self.ap.unsqueeze(axis))

    def bto(self, shape):
        return V(self.t, self.ap.broadcast_to(shape))

    def pb(self, n):
        return V(self.t, self.ap.partition_broadcast(n))


class T:
    def __init__(self, name, h):
        self.name = name
        self.h = h
        self.w = None
        self.r = {}
        self.dsem = None
        self.dtot = 0
        self.psum = False

    def __getitem__(self, idx):
        return V(self, self.h[idx])

    @property
    def v(self):
        return V(self, self.h[:])


class KB:
    def __init__(self, nc, es, needed=None):
        self.nc = nc
        self.es = es
        self.es_sem = es
        self.eng = {"pe": nc.tensor, "act": nc.scalar, "dve": nc.vector, "pool": nc.gpsimd, "sp": nc.sync}
        self.sem = {k: es.enter_context(nc.semaphore("s_" + k)) for k in self.eng}
        self.cnt = {k: 0 for k in self.eng}
        self.waited = {k: {} for k in self.eng}
        self.dma_sems = {}
        self.out_events = []
        self.ninst = 0
        self.needed = needed
        self.rec = None
        self.need_rec = {k: set() for k in self.eng}
        self.rank = {k: 0 for k in self.eng}
        self.rankmap = {k: {} for k in self.eng}
        self.engsem = {id(v): k for k, v in self.sem.items()}

    def sb(self, name, shape, dt):
        self.uid = getattr(self, "uid", 0) + 1
        name = f"{name}_u{self.uid}"
        return T(name, self.es.enter_context(self.nc.sbuf_tensor(name, list(shape), dt)))

    def ps(self, name, shape, dt):
        t = T(name, self.es.enter_context(self.nc.psum_tensor(name, list(shape), dt)))
        t.psum = True
        return t

    def dram(self, name, shape, dt, kind):
        return T(name, self.nc.dram_tensor(name, list(shape), dt, kind=kind).ap())

    def _wait(self, e, ev):
        sem, val = ev
        sid = id(sem)
        if e == "pe" and sem is self.sem["pe"]:
            return
        if SKIP_SELF_WAIT and sem is self.sem.get(e):
            return
        if sid in self.dma_sems:
            val = max(val, self.dma_sems[sid][1])
        w = self.waited[e]
        if w.get(sid, 0) >= val:
            return
        w[sid] = val
        f = self.engsem.get(sid)
        if f is not None:
            self.need_rec[f].add(val)
            if self.needed is not None:
                val = self.rankmap[f][val]
        self.eng[e].wait_ge(sem, val)

    def _deps(self, e, reads, writes):
        for v in reads:
            t = v.t
            if t.w is not None:
                self._wait(e, t.w)
            if t.psum:
                for ev in t.r.values():
                    if ev[0] is not self.sem.get(e):
                        self._wait(e, ev)
        for v in writes:
            t = v.t
            if t.w is not None:
                self._wait(e, t.w)
            for ev in t.r.values():
                self._wait(e, ev)

    def _record(self, ev, reads, writes):
        sem, val = ev
        for v in reads:
            v.t.r[id(sem)] = ev
        for v in writes:
            v.t.w = ev
            v.t.r = {}

    def play(self, items):
        for it in items:
            if it[0] == "op":
                self.op(*it[1:])
            else:
                self.dma(it[1], it[2], it[3], **it[4])

    def op(self, e, fn, reads, writes):
        if self.rec is not None:
            self.rec.append(("op", e, fn, reads, writes))
            return None
        reads = [v for v in reads if isinstance(v, V)]
        self._deps(e, reads, writes)
        inst = fn(self.eng[e])
        self.cnt[e] += 1
        if self.needed is None:
            inst.then_inc(self.sem[e], 1)
        elif self.cnt[e] in self.needed[e]:
            inst.then_inc(self.sem[e], 1)
            self.rank[e] += 1
            self.rankmap[e][self.cnt[e]] = self.rank[e]
        self._record((self.sem[e], self.cnt[e]), reads, writes)
        self.ninst += 1
        return inst

    def dma(self, q, out, in_, **kw):
        if self.rec is not None:
            self.rec.append(("dma", q, out, in_, kw))
            return None
        self._deps(q, [in_], [out])
        own = out.t
        if own.dsem is None:
            own.dsem = self.es_sem.enter_context(self.nc.semaphore("d_" + own.name))
            self.dma_sems[id(own.dsem)] = [own.dsem, 0]
        inst = self.eng[q].dma_start(out=out.ap, in_=in_.ap, **kw)
        own.dtot += 16
        self.dma_sems[id(own.dsem)][1] = own.dtot
        inst.then_inc(own.dsem, 16)
        ev = (own.dsem, own.dtot)
        self._record(ev, [in_], [out])
        self.ninst += 1
        return ev

    def gather(self, out, in_, idx, **kw):
        q = "pool"
        self._deps(q, [in_, idx], [out])
        own = out.t
        if own.dsem is None:
            own.dsem = self.es_sem.enter_context(self.nc.semaphore("d_" + own.name))
            self.dma_sems[id(own.dsem)] = [own.dsem, 0]
        inst = self.nc.gpsimd.indirect_dma_start(
            out=out.ap, out_offset=None, in_=in_.ap,
            in_offset=bass.IndirectOffsetOnAxis(ap=idx.ap, axis=0), **kw)
        own.dtot += 16
        self.dma_sems[id(own.dsem)][1] = own.dtot
        inst.then_inc(own.dsem, 16)
        ev = (own.dsem, own.dtot)
        self._record(ev, [in_, idx], [out])
        return ev

    def barrier(self):
        for e in self.eng:
            for sem, tot in self.dma_sems.values():
                if tot:
                    self._wait(e, (sem, tot))
            for f in self.eng:
                if f != e and self.cnt[f]:
                    self._wait(e, (self.sem[f], self.cnt[f]))

    def finish(self):
        for sem, tot in self.dma_sems.values():
            if tot:
                self._wait("sp", (sem, tot))
        for e in self.eng:
            if e != "sp" and self.cnt[e]:
                self._wait("sp", (self.sem[e], self.cnt[e]))

    def mm(self, out, lhsT, rhs, start=True, stop=True):
        return self.op("pe", lambda e: e.matmul(out.ap, lhsT=lhsT.ap, rhs=rhs.ap, start=start, stop=stop),
                       [lhsT, rhs] + ([] if start else [out]), [out])

    def tr(self, out, in_, ident):
        return self.op("pe", lambda e: e.transpose(out.ap, in_.ap, ident.ap), [in_, ident], [out])

    def act(self, out, in_, func, scale=1.0, bias=0.0, accum=None, eng="act"):
        kw = {}
        if accum is not None:
            kw["accum_out"] = accum.ap
        sc = scale.ap if isinstance(scale, V) else scale
        bi = bias.ap if isinstance(bias, V) else bias
        return self.op("act", lambda e: e.activation(out=out.ap, in_=in_.ap, func=func, scale=sc, bias=bi, **kw),
                       [in_, scale, bias], [out] + ([accum] if accum is not None else []))

    def tt(self, e, out, in0, in1, op):
        return self.op(e, lambda g: g.tensor_tensor(out=out.ap, in0=in0.ap, in1=in1.ap, op=op), [in0, in1], [out])

    def ts(self, e, out, in0, s1, op0, s2=None, op1=None, accum=None):
        a1 = s1.ap if isinstance(s1, V) else s1
        a2 = s2.ap if isinstance(s2, V) else s2
        kw = {}
        if op1 is not None:
            kw["op1"] = op1
        if accum is not None:
            kw["accum_out"] = accum.ap
        return self.op(e, lambda g: g.tensor_scalar(out=out.ap, in0=in0.ap, scalar1=a1, scalar2=a2, op0=op0, **kw),
                       [in0, s1, s2], [out] + ([accum] if accum is not None else []))

    def stt(self, out, in0, scalar, in1, op0, op1, e="dve"):
        sc = scalar.ap if isinstance(scalar, V) else scalar
        return self.op(e, lambda g: g.scalar_tensor_tensor(out=out.ap, in0=in0.ap, scalar=sc, in1=in1.ap, op0=op0, op1=op1),
                       [in0, scalar, in1], [out])

    def copy(self, e, out, in_):
        if e == "act":
            return self.act(out, in_, AF.Copy)
        return self.op(e, lambda g: g.tensor_copy(out=out.ap, in_=in_.ap), [in_], [out])

    def memset(self, e, out, val):
        return self.op(e, lambda g: g.memset(out.ap, val), [], [out])

    def reduce(self, out, in_, op=ALU.add, axis=AX.X, e="dve"):
        return self.op(e, lambda g: g.tensor_reduce(out=out.ap, in_=in_.ap, axis=axis, op=op), [in_], [out])

    def recip(self, out, in_):
        return self.op("dve", lambda g: g.reciprocal(out=out.ap, in_=in_.ap), [in_], [out])

from concourse.bass_utils import run_bass_kernel_spmd

D_MODEL = 1024; PLE_DIM = 256
NH = 8; DK = 64; CONV_DIM = 1536
Q_LORA = 384; KV_LORA = 256; ROPE = 32; NOPE = 64; QK = 96
IN_DIM = 2736; D_FF = 2816
EPS = 1e-6
ATTN_SCALE = QK ** -0.5
C_AB = 1536; C_Z = 1552; C_QA = 2064; C_KVA = 2448
BIG = 1.0e4


class Rot:
    def __init__(self, items):
        self.items = items
        self.i = 0

    def __call__(self):
        t = self.items[self.i % len(self.items)]
        self.i += 1
        return t


def host_consts(S, past):
    c = {}
    i = np.arange(128)
    same = (i[:, None] // 64) == (i[None, :] // 64)
    c["ident"] = np.eye(128, dtype=np.float32)
    c["tri"] = (same & (i[:, None] <= i[None, :])).astype(np.float32)
    c["lastsel"] = (i[:, None] == (i[None, :] // 64) * 64 + 63).astype(np.float32)
    vis = same & (i[None, :] < i[:, None])
    c["negs"] = np.where(vis, 0.0, BIG).astype(np.float32)
    c["negt"] = np.where(vis.T, 0.0, -BIG).astype(np.float32)
    c["headblk"] = same.astype(np.float32)
    sel8 = np.zeros((8, 8, 128), np.float32)
    for h in range(8):
        sel8[h, h, :] = 1.0
    c["sel8"] = sel8
    selp = np.zeros((8, 4, 128), np.float32)
    for h in range(8):
        selp[h, h // 2, (h % 2) * 64:(h % 2) * 64 + 64] = 1.0
    c["selpair"] = selp
    c["cmask"] = (i[None, :] >= i[:, None]).astype(np.float32)
    dm = np.zeros((8, 8, 64), np.float32)
    for h in range(8):
        dm[h, h, :] = 1.0
    c["diagmask"] = dm.reshape(8, 512)
    oh = np.zeros((4, 4, 128), np.float32)
    for b in range(4):
        oh[b, b, :] = 1.0
    c["onehot4"] = oh
    half = ROPE // 2
    inv = (10000.0 ** (-np.arange(half, dtype=np.float32) / half)).astype(np.float32)
    pos = np.arange(S, dtype=np.float32)
    ang = pos[:, None] * inv[None, :]
    c["cos_p"] = np.cos(ang).astype(np.float32)
    c["sin_p"] = np.sin(ang).astype(np.float32)
    angs = (np.float32(past) * inv)[None, :].astype(np.float32)
    c["cos_s"] = np.repeat(np.cos(angs), 4, 0).astype(np.float32)
    c["sin_s"] = np.repeat(np.sin(angs), 4, 0).astype(np.float32)
    return c


CONST_SHAPES = dict(ident=[128, 128], tri=[128, 128], lastsel=[128, 128], negs=[128, 128], negt=[128, 128],
                    headblk=[128, 128], sel8=[8, 8, 128], selpair=[8, 4, 128], cmask=[128, 128],
                    diagmask=[8, 512], onehot4=[4, 4, 128], cos_s=[4, 16], sin_s=[4, 16])

W_SHAPES = dict(g_attn=[1, 1024], w_in=[1024, IN_DIM], w_conv=[4, 1536], gdn_a_log=[1, 8], gdn_dt_bias=[1, 8],
                g_gdn_out=[1, 64], g_q_a=[1, 384], w_q_b=[384, 768], g_q_nope=[1, 64], g_q_rope=[1, 32],
                g_kv_a=[1, 256], g_k_rope=[1, 32], w_kv_b=[256, 1024], g_k_nope=[1, 64], w_o=[1024, 1024],
                g_ffn=[1, 1024], w_ffn_gate=[1024, D_FF], w_ffn_up=[1024, D_FF], w_ffn_down=[D_FF, 1024],
                g_ple=[1, 1024], w_ple_gate=[1024, 1024], w_ple_proj=[256, 1024])


def build(S, NPG, NPHYS, stop_after=None):
    _, k1 = build1(S, NPG, NPHYS, stop_after, None)
    return build1(S, NPG, NPHYS, stop_after, k1.need_rec)


INV_DT = BF16
NDUMMY = 0


def build1(S, NPG, NPHYS, stop_after, needed):
    DBG = 0
    NBLK = S // 128
    NT = S // 512
    nc = bass.Bass("TRN2", target_bir_lowering=False)
    es = contextlib.ExitStack()
    with es:
        nc_lp = es.enter_context(nc.allow_low_precision("bf16 matmul operands by design"))
        es.enter_context(nc.allow_non_contiguous_dma("small strided state loads/stores"))
        k = KB(nc, es, needed)
        din = {}

        def DI(name, shape, dt=F32):
            din[name] = k.dram(name, shape, dt, "ExternalInput")
            return din[name]

        def DO(name, shape, dt=F32):
            return k.dram(name, shape, dt, "ExternalOutput")

        x_d = DI("x", [S, 1024]); xs_d = DI("xs", [4, 1024])
        p_d = DI("p", [S, 256]); psm_d = DI("psm", [4, 256])
        ccat_d = DI("cache_cat", [NPHYS * 128, 288])
        sgdn_d = DI("state_gdn", [4, 8, 64, 64]); sconv_d = DI("state_conv", [4, 3 * 1536])
        pt_d = DI("pt", [4, NPG], I32)
        W = {n: DI(n, s) for n, s in W_SHAPES.items()}
        C = {n: DI(n, s) for n, s in CONST_SHAPES.items()}
        C["cos_p"] = DI("cos_p", [S, 16]); C["sin_p"] = DI("sin_p", [S, 16])
        y_d = DO("y", [S, 1024]); ys_d = DO("ys", [4, 1024])
        ockv_d = DO("o_ckv", [S, 256]); okr_d = DO("o_kr", [S, 32])
        ogdn_d = DO("o_gdn", [8, 64, 64]); oconv_d = DO("o_conv", [3, 1536])
        ockvs_d = DO("o_ckv_s", [4, 256]); okrs_d = DO("o_kr_s", [4, 32])
        ogdns_d = DO("o_gdn_s", [4, 8, 64, 64]); oconvs_d = DO("o_conv_s", [4, 3 * 1536])
        omix_d = k.dram("omix_scr", [128, 8, S], BF16, "ExternalOutput" if DBG == 99 else "Internal")
        omixs_d = k.dram("omixs_scr", [128, 8, 4], BF16, "Internal")

        banks = [k.ps(f"ps{i}", [128, 512], F32) for i in range(8)]
        psb = Rot(banks[:7])
        accbank = banks[7]
        evi = [0]

        def evac(out, in_, scale=None):
            evi[0] += 1
            if scale is not None:
                return k.act(out, in_, AF.Copy, scale=scale)
            if evi[0] % 3:
                return k.act(out, in_, AF.Copy)
            return k.copy("dve", out, in_)

        def cload(name, shape, dt=F32, src=None, q="sp"):
            t = k.sb("c_" + name, shape, dt)
            k.dma(q, t.v, (src if src is not None else C[name].v))
            return t

        identf = cload("ident", [128, 128])
        identb = k.sb("identb", [128, 128], BF16); k.copy("dve", identb.v, identf.v)
        identr = k.sb("identr", [128, 128], F32R); k.copy("dve", identr.v, identf.v)
        tri = cload("tri", [128, 128]); lastsel = cload("lastsel", [128, 128])
        negs = cload("negs", [128, 128]); negt = cload("negt", [128, 128])
        headblk = cload("headblk", [128, 128])
        headblk_b = k.sb("headblk_b", [128, 128], BF16); k.copy("dve", headblk_b.v, headblk.v)
        sel8 = cload("sel8", [8, 8, 128]); selpair = cload("selpair", [8, 4, 128])
        cmaskf = cload("cmask", [128, 128])
        cmask = k.sb("cmaskb", [128, 128], BF16); k.copy("dve", cmask.v, cmaskf.v)
        diagmask = cload("diagmask", [8, 512]); onehot4 = cload("onehot4", [4, 4, 128])
        onesb = k.sb("onesb", [128, 64], BF16); k.memset("dve", onesb.v, 1.0)
        onesf = k.sb("onesf", [128, 8], F32); k.memset("dve", onesf.v, 1.0)

        def bload(name, F, q="sp"):
            t = k.sb("b_" + name, [128, F], F32)
            k.dma(q, t.v, W[name].v.bto([128, F]))
            return t

        g_attn = bload("g_attn", 1024); g_q_a = bload("g_q_a", 384); g_kv_a = bload("g_kv_a", 256)
        g_q_nope = bload("g_q_nope", 64); g_q_rope = bload("g_q_rope", 32)
        g_k_rope = bload("g_k_rope", 32); g_k_nope = bload("g_k_nope", 64)
        g_gdn_out = bload("g_gdn_out", 64)
        a_log = bload("gdn_a_log", 8); dtb = bload("gdn_dt_bias", 8)
        eA = k.sb("eA", [128, 8], F32); k.act(eA.v, a_log.v, AF.Exp)
        k.ts("dve", g_q_nope.v, g_q_nope.v, ATTN_SCALE, ALU.mult)
        k.ts("dve", g_q_rope.v, g_q_rope.v, ATTN_SCALE, ALU.mult)
        wcv = k.sb("wcv", [128, 12, 4], F32)
        for j in range(4):
            k.dma("sp", wcv[:, :, j], W["w_conv"].v[j:j + 1, :].re("o (c p) -> p (o c)", p=128))

        def wload(name, K, N, q="pool"):
            kc = K // 128
            t = k.sb("w_" + name, [128, kc, N], BF16)
            src = W[name].v.re("(k p) n -> p k n", p=128)
            for i in range(kc):
                for n0 in range(0, N, 1024):
                    n1 = min(N, n0 + 1024)
                    k.dma(q, t[:, i, n0:n1], src[:, i, n0:n1])
            return t

        sm = Rot([k.sb(f"sm{i}", [128, 8], F32) for i in range(24)])

        def rstd_from_ss(ss, n, F, rows):
            a = sm()
            k.ts("dve", a[rows, 0:n], ss, 1.0 / F, ALU.mult, EPS, ALU.add)
            k.act(a[rows, 0:n], a[rows, 0:n], AF.Ln)
            r = sm()
            k.act(r[rows, 0:n], a[rows, 0:n], AF.Exp, scale=-0.5)
            return r[rows, 0:n]

        junk = k.sb("junk", [128, 1024], BF16)

        def rmsnorm_rows(out, x, g, F, rows):
            ss = sm()
            k.act(junk[rows, 0:F], x, AF.Square, accum=ss[rows, 0:1])
            r = rstd_from_ss(ss[rows, 0:1], 1, F, rows)
            k.stt(out, x, r, g, ALU.mult, ALU.mult)

        def transpose_bf(dst_fn, src, nt, nch, width=128):
            bank = psb()
            pb = bank.v.bc(BF16)
            for j in range(nch):
                k.tr(pb[0:width, j * 128:j * 128 + nt], src[:, j * width:(j + 1) * width], identb[0:nt, 0:nt])
            return pb

        from types import SimpleNamespace as NS
        GROUPS_ALL = [(0, 512), (512, 1024), (1024, 1536), (1536, 1552), (1552, 2064), (2064, 2448), (2448, 2736)]
        GROUPS_A = GROUPS_ALL[:5]
        GROUPS_B = GROUPS_ALL[5:]
        if stop_after is None:
            w_o = wload("w_o", 1024, 1024)
            w_pg = wload("w_ple_gate", 1024, 1024)
            w_pp = wload("w_ple_proj", 256, 1024)
        esP1 = contextlib.ExitStack()
        esP1.__enter__()
        k.es_outer = k.es
        k.es = esP1
        x_blk = k.sb("x_blk", [128, 1024], F32)
        xn_t = k.sb("xn", [128, 1024], BF16)
        xnT = k.sb("xnT", [128, 8, 128], BF16)
        z_tok = k.sb("z_tok", [128, IN_DIM], F32)
        sq_t = k.sb("sq_t", [128, 512], F32)

        def inproj(x_src_v, nt, w_in, c_off, groups):
            rows = slice(0, nt)
            k.dma("sp", x_blk[rows, :], x_src_v)
            rmsnorm_rows(xn_t[rows, :], x_blk[rows, :], g_attn[rows, :], 1024, rows)
            pb = transpose_bf(None, xn_t[rows, :], nt, 8)
            evac(xnT[:, :, 0:nt], pb[:, 0:1024].re("p (j t) -> p j t", j=8)[:, :, 0:nt])
            for (c0, c1) in groups:
                bank = psb()
                for kk in range(8):
                    k.mm(bank[rows, 0:c1 - c0], xnT[:, kk, 0:nt], w_in[:, kk, c0 - c_off:c1 - c_off], start=(kk == 0), stop=(kk == 7))
                evac(z_tok[rows, c0:c1], bank[rows, 0:c1 - c0])

        def wload_cols(name, K, c0, c1, q="pool"):
            kc = K // 128
            t = k.sb("w_" + name + f"_{c0}", [128, kc, c1 - c0], BF16)
            src = W[name].v.re("(k p) n -> p k n", p=128)
            for i in range(kc):
                for n0 in range(c0, c1, 1024):
                    n1 = min(c1, n0 + 1024)
                    k.dma(q, t[:, i, n0 - c0:n1 - c0], src[:, i, n0:n1])
            return t

        def head_rms(out, xin, g, nt, width, scratch=None):
            rows = slice(0, nt)
            sq = (scratch or sq_t)[rows, 0:8 * width].re("p (h d) -> p h d", h=8)
            k.act(sq, xin, AF.Square)
            ss = sm()
            k.reduce(ss[rows, 0:8], sq)
            r = rstd_from_ss(ss[rows, 0:8], 8, width, rows)
            k.tt("dve", out, xin, r.un(2).bto([nt, 8, width]), ALU.mult)
            k.tt("pool", out, out, g.un(1).bto([nt, 8, width]), ALU.mult)

        def alloc_mla(w_qb, w_kvb):
            M = NS()
            M.w_qb = w_qb; M.w_kvb = w_kvb
            M.qa_n = k.sb("qa_n", [128, 384], BF16)
            M.qanT = k.sb("qanT", [128, 3, 128], BF16)
            M.qkv_tok = k.sb("qkv_tok", [128, 1024], F32)
            M.hh = k.sb("hh", [128, 8, 96], F32)
            M.hh_bf = k.sb("hh_bf", [128, 8, 96], BF16)
            M.rtmp = k.sb("rtmp", [128, 8, 32], F32)
            M.rtmp2 = k.sb("rtmp2", [128, 8, 16], F32)
            M.cos_t = k.sb("cos_t", [128, 16], F32); M.sin_t = k.sb("sin_t", [128, 16], F32)
            M.ckv_f = k.sb("ckv_f", [128, 256], F32)
            M.ckv_bf = k.sb("ckv_bf", [128, 260], BF16)
            k.memset("dve", M.ckv_bf[:, 256:257], 1.0)
            M.ckvT = k.sb("ckvT", [128, 2, 128], BF16)
            M.kr_f = k.sb("kr_f", [128, 32], F32)
            M.kr_n = k.sb("kr_n", [128, 32], F32)
            return M

        def rope(M, out, xin, nt, nh):
            rows = slice(0, nt)
            cb = M.cos_t[rows, :].un(1).bto([nt, nh, 16]); sb_ = M.sin_t[rows, :].un(1).bto([nt, nh, 16])
            x1 = xin[:, :, 0:16]; x2 = xin[:, :, 16:32]
            t2 = M.rtmp2[rows, 0:nh, :]
            k.tt("dve", out[:, :, 0:16], x1, cb, ALU.mult)
            k.tt("dve", t2, x2, sb_, ALU.mult)
            k.tt("dve", out[:, :, 0:16], out[:, :, 0:16], t2, ALU.subtract)
            k.tt("dve", out[:, :, 16:32], x1, sb_, ALU.mult)
            k.tt("dve", t2, x2, cb, ALU.mult)
            k.tt("dve", out[:, :, 16:32], out[:, :, 16:32], t2, ALU.add)

        def mla_q(M, nt, cos_src, sin_src):
            rows = slice(0, nt)
            k.dma("sp", M.cos_t[rows, :], cos_src); k.dma("sp", M.sin_t[rows, :], sin_src)
            rmsnorm_rows(M.qa_n[rows, :], z_tok[rows, C_QA:C_QA + 384], g_q_a[rows, :], 384, rows)
            pb = transpose_bf(None, M.qa_n[rows, :], nt, 3)
            evac(M.qanT[:, :, 0:nt], pb[:, 0:384].re("p (j t) -> p j t", j=3)[:, :, 0:nt])
            for (c0, c1) in ((0, 512), (512, 768)):
                bank = psb()
                for kk in range(3):
                    k.mm(bank[rows, 0:c1 - c0], M.qanT[:, kk, 0:nt], M.w_qb[:, kk, c0:c1], start=(kk == 0), stop=(kk == 2))
                evac(M.qkv_tok[rows, c0:c1], bank[rows, 0:c1 - c0])
            q3 = M.qkv_tok[rows, 0:768].re("p (h d) -> p h d", h=8)
            head_rms(M.hh[rows, :, 0:64], q3[:, :, 0:64], g_q_nope[rows, :], nt, 64)
            head_rms(M.rtmp[rows, :, :], q3[:, :, 64:96], g_q_rope[rows, :], nt, 32)
            rope(M, M.hh[rows, :, 64:96], M.rtmp[rows, :, :], nt, 8)

        def mla_kv(M, nt, ockv_v, okr_v):
            rows = slice(0, nt)
            rmsnorm_rows(M.ckv_f[rows, :], z_tok[rows, C_KVA:C_KVA + 256], g_kv_a[rows, :], 256, rows)
            k.dma("sp", ockv_v, M.ckv_f[rows, :])
            k.copy("pool", M.ckv_bf[rows, 0:256], M.ckv_f[rows, :])
            rmsnorm_rows(M.kr_n[rows, :], z_tok[rows, C_KVA + 256:C_KVA + 288], g_k_rope[rows, :], 32, rows)
            rope(M, M.kr_f[rows, :].un(1), M.kr_n[rows, :].un(1), nt, 1)
            k.dma("sp", okr_v, M.kr_f[rows, :])
            pb = transpose_bf(None, M.ckv_bf[rows, 0:256], nt, 2)
            evac(M.ckvT[:, :, 0:nt], pb[:, 0:256].re("p (j t) -> p j t", j=2)[:, :, 0:nt])

        def alloc_gdn(small=False):
            G = NS()
            G.cv = k.sb("cv", [128, 12, 128], F32)
            G.sqt = k.sb("sqt", [128, 4, 128], BF16)
            G.rst = k.sb("rst", [128, 4, 128], F32)
            G.qTg = k.sb("qTg", [128, 4, 128], F32)
            G.kTg = k.sb("kTg", [128, 4, 128], F32)
            G.tmp4 = k.sb("tmp4", [128, 4, 128], F32)
            G.sz_t = k.sb("sz_t", [128, 512], F32)
            if not small:
                G.o_tok = k.sb("o_tok", [128, 512], F32)
                G.on_t = k.sb("on_t", [128, 512], F32)
                G.og_bf = k.sb("og_bf", [128, 512], BF16)
            return G

        def gdn_scalars(nt):
            rows = slice(0, nt)
            ta = sm(); k.tt("dve", ta[rows, :], z_tok[rows, C_AB:C_AB + 8], dtb[rows, :], ALU.add)
            e = sm(); k.act(e[rows, :], ta[rows, :], AF.Exp)
            sp_ = sm(); k.act(sp_[rows, :], e[rows, :], AF.Ln, bias=1.0)
            g_tok = sm(); k.stt(g_tok[rows, :], sp_[rows, :], -1.0, eA[rows, :], ALU.mult, ALU.mult)
            beta = sm(); k.act(beta[rows, :], z_tok[rows, C_AB + 8:C_AB + 16], AF.Sigmoid)
            return g_tok, beta

        def l2norm_fm(G, nt):
            for half in range(2):
                k.act(G.sqt[:, :, 0:nt], G.cv[:, half * 4:half * 4 + 4, 0:nt], AF.Square)
                bank = psb()
                for c in range(4):
                    k.mm(bank[:, c * 128:c * 128 + nt], headblk_b.v, G.sqt[:, c, 0:nt])
                k.ts("dve", G.tmp4[:, :, 0:nt], bank.v.re("p (c t) -> p c t", c=4)[:, :, 0:nt], EPS, ALU.add)
                k.act(G.tmp4[:, :, 0:nt], G.tmp4[:, :, 0:nt], AF.Ln)
                k.act(G.rst[:, :, 0:nt], G.tmp4[:, :, 0:nt], AF.Exp, scale=-0.5)
                if half == 0:
                    k.stt(G.qTg[:, :, 0:nt], G.cv[:, 0:4, 0:nt], DK ** -0.5, G.rst[:, :, 0:nt], ALU.mult, ALU.mult)
                else:
                    k.tt("dve", G.kTg[:, :, 0:nt], G.cv[:, 4:8, 0:nt], G.rst[:, :, 0:nt], ALU.mult)

        def gdn_out_tok(G, nt):
            rows = slice(0, nt)
            o3 = G.o_tok[rows, :].re("p (h d) -> p h d", h=8)
            on3 = G.on_t[rows, :].re("p (h d) -> p h d", h=8)
            head_rms(on3, o3, g_gdn_out[rows, :], nt, 64)
            k.tt("dve", G.og_bf[rows, :], G.on_t[rows, :], G.sz_t[rows, :], ALU.mult)

        def sample_scope():
            w_in = wload_cols("w_in", 1024, 0, IN_DIM)
            w_qb = wload_cols("w_q_b", 384, 0, 768)
            w_kvb = wload_cols("w_kv_b", 256, 0, 1024)
            M = alloc_mla(w_qb, w_kvb)
            G = alloc_gdn(small=True)
            nt = 4; rows = slice(0, 4)
            inproj(xs_d.v, 4, w_in, 0, GROUPS_ALL)
            if DBG == 1:
                return
            oms = k.sb("oms", [128, 8, 4], BF16)
            esg = contextlib.ExitStack()
            with esg:
                k.es = esg
                st_tok = k.sb("st_tok", [12, 1536], F32)
                k.dma("sp", st_tok.v, sconv_d.v.re("b (j c) -> (b j) c", j=3))
                k.dma("sp", oconvs_d.v.re("b (j c) -> b j c", j=3)[:, 0:2, :], sconv_d.v.re("b (j c) -> b j c", j=3)[:, 1:3, :])
                k.dma("sp", oconvs_d.v.re("b (j c) -> b j c", j=3)[:, 2, :], z_tok[rows, 0:1536])
                ext = k.sb("ext_fm", [128, 12, 4, 4], F32)
                bank = psb()
                for c in range(12):
                    k.tr(bank[:, c * 12:(c + 1) * 12], st_tok[:, c * 128:(c + 1) * 128], identf[0:12, 0:12])
                evac(ext[:, :, :, 0:3], bank[:, 0:144].re("p (c b j) -> p c b j", c=12, b=4))
                bank = psb()
                for c in range(12):
                    k.tr(bank[:, c * 4:(c + 1) * 4], z_tok[rows, c * 128:(c + 1) * 128], identf[0:4, 0:4])
                evac(ext[:, :, :, 3], bank[:, 0:48].re("p (c b) -> p c b", c=12))
                k.tt("dve", ext.v, ext.v, wcv.v.un(2).bto([128, 12, 4, 4]), ALU.mult)
                cpre = k.sb("cpre", [128, 12, 4], F32)
                k.reduce(cpre.v, ext.v)
                k.act(G.cv[:, :, 0:4], cpre.v, AF.Silu)
                l2norm_fm(G, 4)
                g_tok, beta = gdn_scalars(4)
                eg = sm(); k.act(eg[rows, :], g_tok[rows, :], AF.Exp)
                sm_b = Rot([k.sb(f"smb{i}", [128, 4, 4], F32) for i in range(3)])

                def bc_bh(src):
                    bank = psb()
                    for b in range(4):
                        k.mm(bank[:, b * 8:(b + 1) * 8], onehot4[:, b, :], src)
                    o = sm_b()
                    for hp in range(2):
                        RH = slice(hp * 64, hp * 64 + 64)
                        evac(o[RH, :, :], bank[RH, 0:32].re("p (b pr hp) -> p b pr hp", b=4, pr=4)[:, :, :, hp])
                    return o
                eg_b = bc_bh(eg[rows, :]); beta_b = bc_bh(beta[rows, :])
                st = k.sb("st_s", [128, 4, 4, 64], F32)
                for b in range(4):
                    for hp in range(2):
                        k.dma("sp", st[hp * 64:(hp + 1) * 64, b, :, :],
                              sgdn_d.v[b].re("(pr hp) k v -> hp k pr v", hp=2)[hp])
                B4 = lambda t: t.v.un(3).bto([128, 4, 4, 64])
                k.tt("dve", st.v, st.v, B4(eg_b), ALU.mult)
                tmp = k.sb("tmp_s", [128, 4, 4, 64], F32)
                kcol = G.kTg[:, :, 0:4].re("p pr b -> p b pr")
                k.tt("dve", tmp.v, st.v, kcol.un(3).bto([128, 4, 4, 64]), ALU.mult)
                kSB = k.sb("kSB_s", [128, 4, 4, 64], F32)
                for hf in range(2):
                    bank = psb()
                    k.mm(bank.v, headblk.v, tmp[:, hf * 2:hf * 2 + 2, :, :].re("p b pr v -> p (b pr v)"))
                    evac(kSB[:, hf * 2:hf * 2 + 2, :, :].re("p b pr v -> p (b pr v)"), bank.v)
                v_tk = k.sb("v_tk", [4, 512], F32)
                bank = psb()
                for c in range(4):
                    k.tr(bank[0:4, c * 128:(c + 1) * 128], G.cv[:, 8 + c, 0:4], identf.v)
                evac(v_tk.v, bank[0:4, :])
                vB = k.sb("vB_s", [128, 4, 4, 64], F32)
                for b in range(4):
                    bank = psb()
                    k.mm(bank.v, onehot4[:, b, :], v_tk.v)
                    for hp in range(2):
                        RH = slice(hp * 64, hp * 64 + 64)
                        evac(vB[RH, b, :, :], bank[RH, :].re("p (pr hp v) -> p pr hp v", pr=4, hp=2)[:, :, hp, :])
                k.tt("dve", vB.v, vB.v, kSB.v, ALU.subtract)
                k.tt("dve", vB.v, vB.v, B4(beta_b), ALU.mult)
                k.tt("dve", tmp.v, vB.v, kcol.un(3).bto([128, 4, 4, 64]), ALU.mult)
                k.tt("dve", st.v, st.v, tmp.v, ALU.add)
                for b in range(4):
                    for hp in range(2):
                        k.dma("sp", ogdns_d.v[b].re("(pr hp) k v -> hp k pr v", hp=2)[hp], st[hp * 64:(hp + 1) * 64, b, :, :])
                oT = k.sb("oT_s", [128, 16], F32)
                for hp in range(2):
                    bank = psb()
                    RH = slice(hp * 64, hp * 64 + 64)
                    for b in range(4):
                        for pr in range(4):
                            k.mm(bank[RH, pr * 4 + b:pr * 4 + b + 1], st[RH, b, pr, :], G.qTg[RH, pr, b:b + 1])
                    evac(oT[RH, :], bank[RH, 0:16])
                osq = k.sb("osq_s", [128, 16], F32)
                k.act(osq.v, oT.v, AF.Square)
                bank = psb()
                k.mm(bank[:, 0:16], headblk.v, osq.v)
                a_ = k.sb("a_s_", [128, 16], F32); r_ = k.sb("r_s_", [128, 16], F32)
                k.ts("dve", a_.v, bank[:, 0:16], 1.0 / 64, ALU.mult, EPS, ALU.add)
                k.act(a_.v, a_.v, AF.Ln)
                k.act(r_.v, a_.v, AF.Exp, scale=-0.5)
                k.tt("dve", oT.v, oT.v, r_.v, ALU.mult)
                ggo_col = k.sb("ggo_col", [128, 1], F32)
                for hp in range(2):
                    k.dma("sp", ggo_col[hp * 64:(hp + 1) * 64, :], W["g_gdn_out"].v.re("o d -> d o"))
                k.ts("dve", oT.v, oT.v, ggo_col[:, 0:1], ALU.mult)
                k.act(G.sz_t[rows, :], z_tok[rows, C_Z:C_Z + 512], AF.Silu)
                bank = psb()
                for c in range(4):
                    k.tr(bank[:, c * 4:c * 4 + 4], G.sz_t[rows, c * 128:(c + 1) * 128], identf[0:4, 0:4])
                k.tt("dve", oms[:, 0:4, :], oT.v.re("p (pr b) -> p pr b", pr=4), bank[:, 0:16].re("p (pr b) -> p pr b", pr=4), ALU.mult)
                k.barrier()
            k.es = es1_cur[0]
            mla_q(M, 4, C["cos_s"].v, C["sin_s"].v)
            mla_kv(M, 4, ockvs_d.v, okrs_d.v)
            krb = k.sb("krb_s", [4, 32], BF16)
            k.copy("dve", krb.v, M.kr_f[0:4, :])
            krT_new = k.sb("krT_new", [32, 4], BF16)
            bank = psb(); pb = bank.v.bc(BF16)
            k.tr(pb[0:32, 0:4], krb.v, identb[0:4, 0:4])
            evac(krT_new.v, pb[0:32, 0:4])
            if DBG == 7:
                return
            WkT = k.sb("WkT", [64, 8, 256], BF16)
            wk4 = w_kvb.v.re("p k (h d) -> p k h d", h=8)
            for kk in range(2):
                bank = psb(); pb = bank.v.bc(BF16)
                for h in range(8):
                    k.tr(pb[0:64, h * 128:(h + 1) * 128], wk4[:, kk, h, 0:64], identb.v)
                evac(WkT[:, :, kk * 128:(kk + 1) * 128], pb[0:64, 0:1024].re("p (h t) -> p h t", h=8))
            qg = k.sb("qg_s", [4, 8, 64], BF16)
            k.tt("dve", qg.v, M.hh[rows, :, 0:64], g_k_nope[rows, :].un(1).bto([4, 8, 64]), ALU.mult)
            qr = k.sb("qr_s", [4, 8, 32], BF16)
            k.copy("dve", qr.v, M.hh[rows, :, 64:96])
            bank = psb(); pb = bank.v.bc(BF16)
            for h in range(8):
                k.tr(pb[0:64, h * 4:h * 4 + 4], qg[:, h, :], identb[0:4, 0:4])
                k.tr(pb[0:32, 64 + h * 4:64 + h * 4 + 4], qr[:, h, :], identb[0:4, 0:4])
            qgT = k.sb("qgT_s", [64, 8, 4], BF16); qrT = k.sb("qrT_s", [32, 8, 4], BF16)
            evac(qgT.v, pb[0:64, 0:32].re("p (h b) -> p h b", h=8))
            evac(qrT.v, pb[0:32, 64:96].re("p (h b) -> p h b", h=8))
            bank = psb()
            for kk in range(2):
                for h in range(8):
                    k.mm(bank[:, kk * 32 + h * 4:kk * 32 + h * 4 + 4], WkT[:, h, kk * 128:(kk + 1) * 128], qgT[:, h, :])
            qpT = k.sb("qpT_s", [128, 2, 4, 8], BF16)
            evac(qpT.v, bank[:, 0:64].re("p (k h b) -> p k b h", k=2, h=8))
            if DBG == 8:
                return
            pti = k.sb("pti", [128, 4 * NPG], I32)
            k.dma("sp", pti.v, pt_d.v.re("(o b) j -> o (b j)", o=1).bto([128, 4 * NPG]))
            ptf = k.sb("ptf", [128, 4 * NPG], F32)
            k.copy("dve", ptf.v, pti.v)
            iot = k.sb("iot", [128, 1], F32)
            k.op("pool", lambda g: g.iota(iot.v.ap, pattern=[[0, 1]], base=0, channel_multiplier=1,
                                          allow_small_or_imprecise_dtypes=True), [], [iot.v])
            k.ts("dve", ptf.v, ptf.v, 128.0, ALU.mult, iot[:, 0:1], ALU.add)
            idx = pti
            k.copy("dve", idx.v, ptf.v)
            G_ = 4
            pg_r = Rot([k.sb(f"pg{i}", [128, 292], BF16) for i in range(2 * G_ + 2)])
            for t_ in pg_r.items:
                k.memset("dve", t_[:, 288:289], 1.0)
            sq_r = Rot([k.sb(f"sqp{i}", [128, G_, 512], BF16) for i in range(2)])
            sq1 = sq_r.items[0][:, 0, :]
            p_r = Rot([k.sb(f"pp{i}", [128, G_ * 8], BF16) for i in range(3)])
            sg_r = Rot([k.sb(f"sgp{i}", [128, G_ * 8], F32) for i in range(6)])
            Wkc = k.sb("Wkc", [128, 2, 512], BF16); Wvc = k.sb("Wvc", [128, 2, 512], BF16)
            for kk in range(2):
                k.copy("dve", Wkc[:, kk, :].re("p (h d) -> p h d", h=8), wk4[:, kk, :, 0:64])
                k.copy("dve", Wvc[:, kk, :].re("p (h d) -> p h d", h=8), wk4[:, kk, :, 64:128])
            wk_rhs = lambda kk: Wkc[:, kk, :]
            wv_rhs = lambda kk: Wvc[:, kk, :]
            acc_sb = k.sb("acc_sb", [8, 257], F32)
            accn = k.sb("accn", [8, 256], BF16)
            accT = k.sb("accT", [128, 2, 8], BF16)
            om_f = k.sb("om_f", [8, 512], F32)
            trb = Rot([banks[0]]); bankA = banks[1:5]; bB = banks[5]
            qr_f = k.sb("qr_f", [4, 256], F32)
            k.copy("dve", qr_f.v.re("p (h d) -> p h d", h=8), M.hh[rows, :, 64:96])
            qrB = k.sb("qrB", [128, 4, 256], BF16)
            for b in range(4):
                bank = trb()
                k.mm(bank[:, 0:256], onehot4[:, b, :], qr_f.v)
                evac(qrB[:, b, :], bank[:, 0:256])
            rp_r = Rot([k.sb(f"rp{i}", [128, G_, 256], BF16) for i in range(2)])

            def newtok(b):
                rws = slice(0, 4)
                bA = bankA[0]
                for kk in range(2):
                    k.mm(bA[rws, 0:512], M.ckvT[:, kk, 0:4], wk_rhs(kk), start=(kk == 0), stop=(kk == 1))
                for kk in range(2):
                    k.mm(bB[rws, 0:8], M.ckvT[:, kk, 0:4], qpT[:, kk, b, :], start=(kk == 0), stop=(kk == 1))
                k.mm(bB[rws, 8:16], krT_new[:, 0:4], qrT[:, :, b])
                k.act(sq1[rws, :], bA[rws, :], AF.Square)
                ss = sm()
                k.reduce(ss[rws, :], sq1[rws, :].re("p (h d) -> p h d", h=8))
                r = rstd_from_ss(ss[rws, :], 8, 64, rws)
                s1 = sm()
                k.tt("dve", s1[rws, :], bB[rws, 0:8], r, ALU.mult)
                k.tt("dve", s1[rws, :], s1[rws, :], bB[rws, 8:16], ALU.add)
                s2 = sm()
                k.act(s2[rws, :], s1[rws, :], AF.Exp)
                pp = p_r()
                k.ts("dve", pp[rws, 0:8], s2[rws, :], identf[0:4, b:b + 1], ALU.mult)
                return pp

            trb2 = Rot([banks[0], banks[6]])
            cT4_r = Rot([k.sb(f"cT4_{i}", [128, 4, 2, 128], BF16) for i in range(2)])

            def frontA(b, j0, g):
                pgs = []
                tb = trb2(); pb = tb.v.bc(BF16)
                for i in range(g):
                    pg = pg_r(); pgs.append(pg)
                    col = b * NPG + j0 + i
                    k.gather(pg[:, 0:288], ccat_d.v, idx[:, col:col + 1])
                for i in range(g):
                    k.tr(pb[:, i * 256:i * 256 + 128], pgs[i][:, 32:160], identb.v)
                    k.tr(pb[:, i * 256 + 128:i * 256 + 256], pgs[i][:, 160:288], identb.v)
                cT4 = cT4_r()
                k.act(cT4[:, 0:g, :, :], pb[:, 0:g * 256].re("p (g j t) -> p g j t", g=g, j=2), AF.Copy)
                return pgs, cT4

            def frontB(b, g, pgs, cT4):
                sq = sq_r(); rp = rp_r()
                for i in range(g):
                    for kk in range(2):
                        k.mm(bankA[i][:, 0:512], cT4[:, i, kk, :], wk_rhs(kk), start=(kk == 0), stop=(kk == 1))
                for i in range(g):
                    for kk in range(2):
                        k.mm(bB[:, i * 8:i * 8 + 8], cT4[:, i, kk, :], qpT[:, kk, b, :], start=(kk == 0), stop=(kk == 1))
                for i in range(g):
                    k.act(sq[:, i, :], bankA[i].v, AF.Square)
                    for _d in range(NDUMMY):
                        k.op("pe", lambda e: e.matmul(accbank[64:128, 0:512].ap, lhsT=Wkc[:, 0, 0:64].ap, rhs=Wkc[:, 1, :].ap,
                                                      start=True, stop=True), [], [])
                    k.tt("dve", rp[:, i, :].re("p (h d) -> p h d", h=8), pgs[i][:, 0:32].un(1).bto([128, 8, 32]),
                         qrB[:, b, :].re("p (h d) -> p h d", h=8), ALU.mult)
                return sq, rp

            def small(g, sq, rp):
                n8 = g * 8
                sr = sg_r()
                k.reduce(sr[:, 0:n8], rp[:, 0:g, :].re("p g (h d) -> p (g h) d", h=8))
                ss = sg_r()
                k.reduce(ss[:, 0:n8], sq[:, 0:g, :].re("p g (h d) -> p (g h) d", h=8))
                a = sg_r()
                k.ts("dve", a[:, 0:n8], ss[:, 0:n8], 1.0 / 64, ALU.mult, EPS, ALU.add)
                k.act(a[:, 0:n8], a[:, 0:n8], AF.Ln)
                r = sg_r()
                k.act(r[:, 0:n8], a[:, 0:n8], AF.Exp, scale=-0.5)
                s1 = sg_r()
                k.tt("dve", s1[:, 0:n8], bB[:, 0:n8], r[:, 0:n8], ALU.mult)
                k.tt("dve", s1[:, 0:n8], s1[:, 0:n8], sr[:, 0:n8], ALU.add)
                pp = p_r()
                k.act(pp[:, 0:n8], s1[:, 0:n8], AF.Exp)
                return pp

            def accm(j0, g, pgs, pp):
                for i in range(g):
                    k.mm(accbank[0:8, 0:257], pp[:, i * 8:(i + 1) * 8], pgs[i][:, 32:289], start=False,
                         stop=(j0 + i == NPG - 1))

            for b in range(4):
                pp = newtok(b)
                k.mm(accbank[0:8, 0:257], pp[0:4, 0:8], M.ckv_bf[0:4, 0:257], start=True, stop=(NPG == 0))
                groups = [(j0, min(G_, NPG - j0)) for j0 in range(0, NPG, G_)]
                if groups:
                    pgs, cT4 = frontA(b, *groups[0])
                    sq, rp = frontB(b, groups[0][1], pgs, cT4)
                    ppg = small(groups[0][1], sq, rp)
                for gi, (j0, g) in enumerate(groups):
                    cur = (pgs, ppg)
                    if gi + 1 < len(groups):
                        pgs, cT4 = frontA(b, *groups[gi + 1])
                    accm(j0, g, *cur)
                    if gi + 1 < len(groups):
                        sq, rp = frontB(b, groups[gi + 1][1], pgs, cT4)
                        ppg = small(groups[gi + 1][1], sq, rp)
                evac(acc_sb.v, accbank[0:8, 0:257])
                rl = sm(); k.recip(rl[0:8, 0:1], acc_sb[:, 256:257])
                k.ts("dve", accn.v, acc_sb[:, 0:256], rl[0:8, 0:1], ALU.mult)
                bank = trb(); pb = bank.v.bc(BF16)
                for kk in range(2):
                    k.tr(pb[:, kk * 8:kk * 8 + 8], accn[:, kk * 128:(kk + 1) * 128], identb[0:8, 0:8])
                evac(accT.v, pb[:, 0:16].re("p (k h) -> p k h", k=2))
                bank = trb()
                for kk in range(2):
                    k.mm(bank[0:8, :], accT[:, kk, :], wv_rhs(kk), start=(kk == 0), stop=(kk == 1))
                k.tt("dve", om_f.v, bank[0:8, :], diagmask.v, ALU.mult)
                bank2 = trb()
                for pr in range(4):
                    k.mm(bank2[:, pr:pr + 1], om_f[:, pr * 128:(pr + 1) * 128], onesf[0:8, 0:1])
                evac(oms[:, 4:8, b], bank2[:, 0:4])
            k.dma("sp", omixs_d.v, oms.v)

        def pass_a():
            w_in = wload_cols("w_in", 1024, 0, 2064)
            G = alloc_gdn()
            S2 = k.sb("S2", [128, 4, 128], F32)
            k.memset("pool", S2.v, 0.0)
            zcT = k.sb("zcT", [128, 12, 131], F32)
            k.memset("pool", zcT.v, 0.0)
            omixA = k.sb("omixA", [128, 4, 512], BF16)
            acc_r = Rot([k.sb(f"acc_c{i}", [128, 128], F32) for i in range(2)])
            accp_r = Rot([k.sb(f"acc_p{i}", [128, 128], F32) for i in range(2)])
            tmp_p = k.sb("tmp_p", [128, 128], F32)
            kbT = k.sb("kbT", [128, 4, 128], BF16); nwT = k.sb("nwT", [128, 4, 128], BF16)
            qgT = k.sb("qgT", [128, 4, 128], BF16)
            kT_b = k.sb("kT_b", [128, 4, 128], BF16); qT_b = k.sb("qT_b", [128, 4, 128], BF16)
            S2b = k.sb("S2b", [128, 4, 128], BF16)
            k.memset("pool", S2b.v, 0.0)
            egc_fm = k.sb("egc_fm", [128, 4, 128], F32)
            k_tok = G.on_t
            v_tok = k.sb("v_tok", [128, 512], F32); u_tok = v_tok
            vb_tok = k.sb("vb_tok", [128, 512], BF16)
            kbg_tok = k.sb("kbg_tok", [128, 512], BF16)
            kdec_tok = k.sb("kdec_tok", [128, 512], BF16)
            vnew_tok = k.sb("vnew_tok", [128, 512], BF16)
            gcT8 = k.sb("gcT8", [8, 128], F32)
            betaT8 = k.sb("betaT8", [8, 128], F32)
            qkT_all = k.sb("qkT_all", [128, 8, 128], BF16)
            U_all = k.sb("U_all", [128, 8, 128], BF16)
            g4 = Rot([k.sb(f"g4_{i}", [128, 4, 128], F32) for i in range(3)])
            r4s = [Rot([k.sb(f"r4_{j}_{i}", [128, 4, 128], INV_DT) for i in range(6)]) for j in range(2)]
            t4 = k.sb("t4", [128, 4, 128], F32)
            v4 = lambda b_: b_.v.re("p (c t) -> p c t", c=4)

            def f1_steps(bi):
                steps = []
                t0 = bi * 128

                def s_in():
                    inproj(x_d.v[t0:t0 + 128, :], 128, w_in, 0, GROUPS_A)
                    if bi == NBLK - 1:
                        k.dma("sp", oconv_d.v, z_tok[125:128, 0:1536])
                steps.append(s_in)

                def s_tr(g3):
                    def f():
                        bank = psb()
                        for c in range(4):
                            k.tr(bank[:, c * 128:(c + 1) * 128], z_tok[:, (g3 * 4 + c) * 128:(g3 * 4 + c + 1) * 128], identf.v)
                        evac(zcT[:, g3 * 4:g3 * 4 + 4, 3:131], v4(bank))
                    return f
                for g3 in range(3):
                    steps.append(s_tr(g3))

                def s_cv(c):
                    def f():
                        ac = acc_r()
                        k.ts("dve", ac.v, zcT[:, c, 0:128], wcv[:, c, 0:1], ALU.mult)
                        for j in (1, 2, 3):
                            k.stt(ac.v, zcT[:, c, j:j + 128], wcv[:, c, j:j + 1], ac.v, ALU.mult, ALU.add)
                        k.act(G.cv[:, c, :], ac.v, AF.Silu)
                    return f
                for c in range(12):
                    steps.append(s_cv(c))
                steps.append(lambda: k.copy("pool", zcT[:, :, 0:3], zcT[:, :, 128:131]))
                return steps

            def gdn_block(tl, inj):
                cv = G.cv; kTg = G.kTg; qTg = G.qTg

                def pump(n):
                    for _ in range(n):
                        if inj:
                            inj.pop(0)()
                l2norm_fm(G, 128)
                k.act(G.sz_t.v, z_tok[:, C_Z:C_Z + 512], AF.Silu)
                k.copy("pool", kT_b.v, kTg.v)
                k.copy("pool", qT_b.v, qTg.v)
                bank = psb()
                for c in range(4):
                    k.tr(bank[:, c * 128:(c + 1) * 128], kTg[:, c, :], identf.v)
                evac(k_tok.v, bank.v)
                bank = psb()
                for c in range(4):
                    k.tr(bank[:, c * 128:(c + 1) * 128], cv[:, 8 + c, :], identf.v)
                evac(v_tok.v, bank.v)
                g_tok, beta = gdn_scalars(128)
                bank = psb()
                k.mm(bank[:, 0:8], tri.v, g_tok.v)
                gc = sm(); evac(gc.v, bank[:, 0:8])
                bank = psb()
                k.mm(bank[:, 0:8], lastsel.v, gc.v)
                dd = sm(); k.tt("dve", dd.v, bank[:, 0:8], gc.v, ALU.subtract)
                edec = sm(); k.act(edec.v, dd.v, AF.Exp)
                egc = sm(); k.act(egc.v, gc.v, AF.Exp)
                bge = sm(); k.tt("dve", bge.v, beta.v, egc.v, ALU.mult)
                b3 = lambda t: t.v.un(2).bto([128, 8, 64])
                r3 = lambda t: t.v.re("p (h d) -> p h d", h=8)
                k.tt("dve", r3(vb_tok), r3(v_tok), b3(beta), ALU.mult)
                k.tt("pool", r3(kdec_tok), r3(k_tok), b3(edec), ALU.mult)
                k.tt("pool", r3(kbg_tok), r3(k_tok), b3(bge), ALU.mult)
                bank = psb()
                k.tr(bank[0:8, 0:128], gc.v, identf.v)
                k.tr(bank[0:8, 128:256], beta.v, identf.v)
                evac(gcT8.v, bank[0:8, 0:128]); evac(betaT8.v, bank[0:8, 128:256])
                bank = psb()
                for pr in range(4):
                    k.mm(bank[:, pr * 128:(pr + 1) * 128], selpair[:, pr, :], gcT8.v)
                k.act(egc_fm.v, v4(bank), AF.Exp)
                bank = psb()
                for pr in range(4):
                    k.mm(bank[:, pr * 128:(pr + 1) * 128], selpair[:, pr, :], betaT8.v)
                k.tt("dve", kbT.v, kTg.v, v4(bank), ALU.mult)
                k.tt("dve", qgT.v, qTg.v, egc_fm.v, ALU.mult)
                R = lambda h: slice((h % 2) * 64, (h % 2) * 64 + 64)
                st8 = []
                for hg in range(2):
                    hs = [hg * 4 + i for i in range(4)]
                    r4 = r4s[hg]
                    bcb = psb()
                    for i, h in enumerate(hs):
                        k.mm(bcb[:, i * 128:(i + 1) * 128], sel8[:, h, :], gcT8.v)
                    d1 = g4()
                    k.tt("dve", d1.v, v4(bcb), gc[:, hg * 4:hg * 4 + 4].un(2).bto([128, 4, 128]), ALU.subtract)
                    e1 = g4()
                    k.tt("dve", e1.v, d1.v, negs.v.un(1).bto([128, 4, 128]), ALU.max)
                    Dm = g4()
                    k.act(Dm.v, e1.v, AF.Exp, scale=-1.0)
                    k.tt("dve", d1.v, d1.v, negt.v.un(1).bto([128, 4, 128]), ALU.min)
                    DTm = e1
                    k.act(DTm.v, d1.v, AF.Exp)
                    Bt = r4(); Ct = r4(); St = r4()
                    k.tt("pool", d1.v, DTm.v, identf.v.un(1).bto([128, 4, 128]), ALU.add)
                    for hp_ in range(2):
                        bKB = psb(); bKBT = psb(); bQKT = psb()
                        for which in range(3):
                            for i, h in enumerate(hs):
                                if h % 2 != hp_:
                                    continue
                                pr = h // 2
                                cs_ = slice((i // 2) * 128, (i // 2 + 1) * 128)
                                if which == 0:
                                    k.mm(bKB[:, cs_], kbT[R(h), pr, :], kT_b[R(h), pr, :])
                                elif which == 1:
                                    k.mm(bKBT[:, cs_], kT_b[R(h), pr, :], kbT[R(h), pr, :])
                                else:
                                    k.mm(bQKT[:, cs_], kT_b[R(h), pr, :], qT_b[R(h), pr, :])
                        v2 = lambda b_: b_[:, 0:256].re("p (c t) -> p c t", c=2)
                        k.stt(Bt[:, hp_::2, :], v2(bKB), -1.0, Dm[:, hp_::2, :], ALU.mult, ALU.mult)
                        k.stt(Ct[:, hp_::2, :], v2(bKBT), -1.0, DTm[:, hp_::2, :], ALU.mult, ALU.mult)
                        k.tt("dve", qkT_all[:, hg * 4 + hp_:hg * 4 + 4:2, :], v2(bQKT), d1[:, hp_::2, :], ALU.mult)
                    k.tt("pool", St.v, Ct.v, identf.v.un(1).bto([128, 4, 128]), ALU.add)
                    st8.append([Bt, Ct, St])
                    if hg == 1:
                        pump(1)
                for lvl in range(1, 6):
                    nBs = []
                    for hg in range(2):
                        Bt, Ct, St = st8[hg]
                        r4 = r4s[hg]
                        bB = psb()
                        for i in range(4):
                            k.mm(bB[:, i * 128:(i + 1) * 128], Ct[:, i, :], Bt[:, i, :])
                        nB = r4()
                        k.act(nB.v, v4(bB), AF.Copy)
                        nC = None
                        if lvl < 5:
                            bC = psb()
                            for i in range(4):
                                k.mm(bC[:, i * 128:(i + 1) * 128], Bt[:, i, :], Ct[:, i, :])
                            nC = r4()
                            k.act(nC.v, v4(bC), AF.Copy)
                        nBs.append((nB, nC))
                        pump(1)
                    for hg in range(2):
                        Bt, Ct, St = st8[hg]
                        nB, nC = nBs[hg]
                        r4 = r4s[hg]
                        bS = psb()
                        for i in range(4):
                            k.mm(bS[:, i * 128:(i + 1) * 128], nB[:, i, :], St[:, i, :])
                        if lvl < 5:
                            nS = r4()
                            k.tt("dve", nS.v, v4(bS), St.v, ALU.add)
                            st8[hg] = [nB, nC, nS]
                        else:
                            k.tt("dve", U_all[:, hg * 4:hg * 4 + 4, :], v4(bS), St.v, ALU.add)
                    pump(1)
                ub = psb(); wb = psb()
                for h in range(8):
                    pr = h // 2; Rh = slice((h % 2) * 64, (h % 2) * 64 + 64)
                    k.mm(ub[:, h * 64:(h + 1) * 64], U_all[:, h, :], vb_tok[:, h * 64:(h + 1) * 64])
                    k.mm(wb[Rh, pr * 128:(pr + 1) * 128], kbg_tok[:, h * 64:(h + 1) * 64], U_all[:, h, :])
                evac(u_tok.v, ub.v)
                k.act(nwT.v, v4(wb), AF.Copy, scale=-1.0)
                for ci in range(2):
                    RR = slice(ci * 64, ci * 64 + 64)
                    vbk = psb()
                    for pr in range(4):
                        k.mm(vbk[RR, pr * 128:(pr + 1) * 128], nwT[:, pr, RR], S2b[:, pr, :])
                    k.tt("dve", vnew_tok[RR, :], vbk[RR, :], u_tok[RR, :], ALU.add)
                    obk = psb()
                    for pr in range(4):
                        k.mm(obk[RR, pr * 128:(pr + 1) * 128], qgT[:, pr, RR], S2b[:, pr, :], start=True, stop=False)
                        for hp in range(2):
                            h = 2 * pr + hp
                            k.mm(obk[RR, pr * 128 + hp * 64:pr * 128 + (hp + 1) * 64], qkT_all[RR, h, RR],
                                 vnew_tok[RR, h * 64:(h + 1) * 64], start=False, stop=(hp == 1))
                    k.act(G.o_tok[RR, :], obk[RR, :], AF.Copy)
                    sbk = psb()
                    for pr in range(4):
                        cs = slice(pr * 128, (pr + 1) * 128)
                        k.mm(sbk[:, cs], kdec_tok[RR, cs], vnew_tok[RR, cs])
                    k.tt("dve", t4.v, v4(sbk), headblk.v.un(1).bto([128, 4, 128]), ALU.mult)
                    k.tt("pool", S2.v, S2.v, egc_fm[:, :, ci * 64 + 63:ci * 64 + 64].bto([128, 4, 128]), ALU.mult)
                    k.tt("pool", S2.v, S2.v, t4.v, ALU.add)
                    k.copy("pool", S2b.v, S2.v)
                    pump(1)
                gdn_out_tok(G, 128)
                pb = transpose_bf(None, G.og_bf.v, 128, 4)
                evac(omixA[:, :, tl * 128:(tl + 1) * 128], pb[:, 0:512].re("p (j t) -> p j t", j=4))
                pump(len(inj))

            for st_ in f1_steps(0):
                st_()
            for bi in range(NBLK):
                tl = bi % 4
                gdn_block(tl, f1_steps(bi + 1) if bi + 1 < NBLK else [])
                if tl == 3:
                    t = bi // 4
                    k.dma("sp", omix_d.v[:, 0:4, t * 512:(t + 1) * 512], omixA.v)
            for pr in range(4):
                for hp in range(2):
                    RH = slice(hp * 64, hp * 64 + 64)
                    k.dma("sp", ogdn_d.v[2 * pr + hp], S2[RH, pr, hp * 64:hp * 64 + 64])

        def pass_b():
            psb.items = banks[:3]
            scb = Rot([banks[3], banks[4]])
            w_in = wload_cols("w_in", 1024, C_QA, IN_DIM)
            w_qb = wload_cols("w_q_b", 384, 0, 768)
            w_kvb = wload_cols("w_kv_b", 256, 0, 1024)
            M = alloc_mla(w_qb, w_kvb)
            Mk = alloc_mla(w_qb, w_kvb)
            Mk.cos_t = M.cos_t; Mk.sin_t = M.sin_t
            sq_t2 = k.sb("sq_t2", [128, 512], F32)
            kT = k.sb("kT", [128, 8, S], BF16)
            v_sb = k.sb("v_sb", [128, NBLK, 8, 64], BF16)
            qT_tiles = [k.sb(f"qT_tile{i}", [128, 8, 512], BF16) for i in range(2)]
            omixB = k.sb("omixB", [128, 4, 512], BF16)
            pT_r = Rot([k.sb(f"pT{i}", [128, 512], BF16) for i in range(4)])
            rl_t = k.sb("rl_t", [128, 512], F32)

            def mla_block(bi, tl, qT_tile):
                t0 = bi * 128

                def chain_q():
                    mla_q(M, 128, C["cos_p"].v[t0:t0 + 128, :], C["sin_p"].v[t0:t0 + 128, :])
                    k.copy("pool", M.hh_bf.v, M.hh.v)
                    bank = psb(); pb = bank.v.bc(BF16)
                    for h in range(8):
                        k.tr(pb[0:96, h * 128:(h + 1) * 128], M.hh_bf[:, h, :], identb.v)
                    evac(qT_tile[0:96, :, tl * 128:(tl + 1) * 128], pb[0:96, 0:1024].re("p (h t) -> p h t", h=8))

                def chain_k():
                    mla_kv(Mk, 128, ockv_d.v[t0:t0 + 128, :], okr_d.v[t0:t0 + 128, :])
                    for (c0, c1) in ((0, 512), (512, 1024)):
                        bank = psb()
                        for kk in range(2):
                            k.mm(bank[:, 0:512], Mk.ckvT[:, kk, :], w_kvb[:, kk, c0:c1], start=(kk == 0), stop=(kk == 1))
                        evac(Mk.qkv_tok[:, c0:c1], bank.v)
                    kv3 = Mk.qkv_tok.v.re("p (h d) -> p h d", h=8)
                    head_rms(Mk.hh[:, :, 0:64], kv3[:, :, 0:64], g_k_nope.v, 128, 64, scratch=sq_t2)
                    k.copy("pool", Mk.hh[:, :, 64:96], Mk.kr_f.v.un(1).bto([128, 8, 32]))
                    k.copy("pool", Mk.hh_bf.v, Mk.hh.v)
                    k.copy("dve", v_sb[:, bi, :, :], kv3[:, :, 64:128])
                    bank = psb(); pb = bank.v.bc(BF16)
                    for h in range(8):
                        k.tr(pb[0:96, h * 128:(h + 1) * 128], Mk.hh_bf[:, h, :], identb.v)
                    evac(kT[0:96, :, t0:t0 + 128], pb[0:96, 0:1024].re("p (h t) -> p h t", h=8))

                outer = k.rec
                k.rec = []
                psb.items = [banks[0]]
                chain_q()
                rq = k.rec
                k.rec = []
                psb.items = [banks[1], banks[2]]
                chain_k()
                rk = k.rec
                psb.items = banks[:3]
                k.rec = outer
                merged = []
                nq, nk = len(rq), len(rk)
                iq = ik = 0
                while iq < nq or ik < nk:
                    if ik >= nk or (iq < nq and iq * nk <= ik * nq):
                        merged.append(rq[iq]); iq += 1
                    else:
                        merged.append(rk[ik]); ik += 1
                if outer is not None:
                    outer.extend(merged)
                else:
                    k.play(merged)

            def attention_tile(t):
                qT_tile = qT_tiles[t % 2]
                for pr in range(4):
                    o_ps = banks[5]; l_ps = banks[6]
                    for hp in range(2):
                        h = 2 * pr + hp
                        RH = slice(hp * 64, hp * 64 + 64)
                        nkb = 4 * t + 4
                        def stepA(j):
                            qlo = max(0, j - 4 * t)
                            ncol = (4 - qlo) * 128
                            qc = slice(qlo * 128, 512)
                            sc = scb()
                            k.mm(sc[:, 0:ncol], kT[0:96, h, j * 128:(j + 1) * 128], qT_tile[0:96, h, qc])
                            pT = pT_r()
                            k.act(pT[:, 0:ncol], sc[:, 0:ncol], AF.Exp)
                            if j >= 4 * t:
                                k.tt("pool", pT[:, 0:128], pT[:, 0:128], cmask.v, ALU.mult)
                            return pT, ncol, qc

                        def stepB(j, pT, ncol, qc):
                            k.mm(o_ps[RH, qc], v_sb[:, j, h, :], pT[:, 0:ncol], start=(j == 0), stop=(j == nkb - 1))
                            k.mm(l_ps[RH, qc], onesb.v, pT[:, 0:ncol], start=(j == 0), stop=(j == nkb - 1))

                        pend = stepA(0)
                        for j in range(nkb):
                            cur = pend
                            if j + 1 < nkb:
                                pend = stepA(j + 1)
                            stepB(j, *cur)
                    k.recip(rl_t.v, l_ps.v)
                    k.tt("dve", omixB[:, pr, :], o_ps.v, rl_t.v, ALU.mult)

            def merge2(ra, rb):
                out = []
                na, nb_ = len(ra), len(rb)
                ia = ib = 0
                while ia < na or ib < nb_:
                    if ib >= nb_ or (ia < na and ia * nb_ <= ib * na):
                        out.append(ra[ia]); ia += 1
                    else:
                        out.append(rb[ib]); ib += 1
                return out

            att_rec = None
            for t in range(NT):
                k.rec = []
                for bi in range(4 * t, 4 * t + 4):
                    t0 = bi * 128
                    inproj(x_d.v[t0:t0 + 128, :], 128, w_in, C_QA, GROUPS_B)
                    mla_block(bi, bi % 4, qT_tiles[t % 2])
                blk_rec = k.rec
                k.rec = None
                k.play(blk_rec if att_rec is None else merge2(att_rec, blk_rec))
                k.rec = []
                attention_tile(t)
                k.dma("sp", omix_d.v[:, 4:8, t * 512:(t + 1) * 512], omixB.v)
                att_rec = k.rec
                k.rec = None
            k.play(att_rec)

        k.es_saved = esP1
        es1_cur = [None]
        for name_, fn_ in (("sample", sample_scope), ("a", pass_a), ("b", pass_b)):
            es1 = contextlib.ExitStack()
            with es1:
                k.es = es1
                es1_cur[0] = es1
                fn_()
                k.barrier()
            k.es = k.es_saved
            if stop_after == name_:
                break
        psb.items = banks[:7]
        esP1.__exit__(None, None, None)
        k.es = k.es_outer

        if stop_after is None:
            w_dn = wload("w_ffn_down", D_FF, 1024)
            wg_scr = k.dram("wg_scr", [128, 8, D_FF], BF16, "Internal")
            wu_scr = k.dram("wu_scr", [128, 8, D_FF], BF16, "Internal")
            first_stream = [True]
            g_ffn = bload("g_ffn", 1024); g_ple = bload("g_ple", 1024)
            wg_r = Rot([k.sb(f"wg{i}", [128, 8, 512], BF16) for i in range(2)])
            wu_r = Rot([k.sb(f"wu{i}", [128, 8, 512], BF16) for i in range(2)])
            om_t = k.sb("om_t", [128, 8, 512], BF16)
            h1 = k.sb("h1", [128, 4, 1024], F32)
            un = k.sb("un", [128, 1024], BF16)
            uT = k.sb("uT", [128, 8, 512], BF16)
            hT = k.sb("hT", [128, 22, 512], BF16)
            sg = k.sb("sg", [128, 512], F32)
            sg2 = k.sb("sg2", [128, 512], F32)
            sg_rot = Rot([sg, sg2])
            p_t = k.sb("p_t", [128, 256], F32)
            p_bf = k.sb("p_bf", [128, 256], BF16)
            pT2 = k.sb("pT2", [128, 2, 128], BF16)
            gate_t = sg
            wgs = W["w_ffn_gate"].v.re("(k p) n -> p k n", p=128)
            wus = W["w_ffn_up"].v.re("(k p) n -> p k n", p=128)

            def phase2_tile(nt, om_src, x_src, p_src, y_dst):
                nb = (nt + 127) // 128
                bt = min(nt, 128)
                rows = slice(0, bt)
                k.dma("sp", om_t[:, :, 0:nt], om_src)
                for b in range(nb):
                    k.dma("sp", h1[rows, b, :], x_src(b))
                for b in range(nb):
                    cb = slice(b * 128, b * 128 + bt)
                    for hf in range(2):
                        bank = psb()
                        for kk in range(8):
                            k.mm(bank[rows, :], om_t[:, kk, cb], w_o[:, kk, hf * 512:(hf + 1) * 512], start=(kk == 0), stop=(kk == 7))
                        k.tt("dve", h1[rows, b, hf * 512:(hf + 1) * 512], bank[rows, :], h1[rows, b, hf * 512:(hf + 1) * 512], ALU.add)
                    rmsnorm_rows(un[rows, :], h1[rows, b, :], g_ffn[rows, :], 1024, rows)
                    pb = transpose_bf(None, un[rows, :], bt, 8)
                    evac(uT[:, :, cb], pb[:, 0:1024].re("p (j t) -> p j t", j=8)[:, :, 0:bt])
                for c0 in range(0, D_FF, 512):
                    c1 = min(D_FF, c0 + 512)
                    wg = wg_r(); wu = wu_r()
                    if first_stream[0]:
                        k.dma("pool", wg[:, :, 0:c1 - c0], wgs[:, :, c0:c1])
                        k.dma("pool", wu[:, :, 0:c1 - c0], wus[:, :, c0:c1])
                        k.dma("sp", wg_scr.v[:, :, c0:c1], wg[:, :, 0:c1 - c0])
                        k.dma("sp", wu_scr.v[:, :, c0:c1], wu[:, :, 0:c1 - c0])
                    else:
                        k.dma("sp", wg[:, :, 0:c1 - c0], wg_scr.v[:, :, c0:c1])
                        k.dma("sp", wu[:, :, 0:c1 - c0], wu_scr.v[:, :, c0:c1])
                    for m in range(c0 // 128, c1 // 128):
                        ms_ = slice(m * 128 - c0, (m + 1) * 128 - c0)
                        bg = psb(); bu = psb()
                        for kk in range(8):
                            k.mm(bg[:, 0:nt], wg[:, kk, ms_], uT[:, kk, 0:nt], start=(kk == 0), stop=(kk == 7))
                        for kk in range(8):
                            k.mm(bu[:, 0:nt], wu[:, kk, ms_], uT[:, kk, 0:nt], start=(kk == 0), stop=(kk == 7))
                        sgt = sg_rot()
                        k.act(sgt[:, 0:nt], bg[:, 0:nt], AF.Silu)
                        k.tt("dve", hT[:, m, 0:nt], sgt[:, 0:nt], bu[:, 0:nt], ALU.mult)
                for b in range(nb):
                    cb = slice(b * 128, b * 128 + bt)
                    for hf in range(2):
                        bank = psb()
                        for m in range(22):
                            k.mm(bank[rows, :], hT[:, m, cb], w_dn[:, m, hf * 512:(hf + 1) * 512], start=(m == 0), stop=(m == 21))
                        k.tt("dve", h1[rows, b, hf * 512:(hf + 1) * 512], bank[rows, :], h1[rows, b, hf * 512:(hf + 1) * 512], ALU.add)
                    rmsnorm_rows(un[rows, :], h1[rows, b, :], g_ple[rows, :], 1024, rows)
                    pb = transpose_bf(None, un[rows, :], bt, 8)
                    evac(uT[:, :, cb], pb[:, 0:1024].re("p (j t) -> p j t", j=8)[:, :, 0:bt])
                    k.dma("sp", p_t[rows, :], p_src(b))
                    k.copy("pool", p_bf[rows, :], p_t[rows, :])
                    pb = transpose_bf(None, p_bf[rows, :], bt, 2)
                    evac(pT2[:, :, 0:bt], pb[:, 0:256].re("p (j t) -> p j t", j=2)[:, :, 0:bt])
                    for hf in range(2):
                        hs_ = slice(hf * 512, (hf + 1) * 512)
                        bank = psb()
                        for kk in range(8):
                            k.mm(bank[rows, :], uT[:, kk, cb], w_pg[:, kk, hs_], start=(kk == 0), stop=(kk == 7))
                        k.act(gate_t[rows, :], bank[rows, :], AF.Sigmoid)
                        bank2 = psb()
                        for kk in range(2):
                            k.mm(bank2[rows, :], pT2[:, kk, 0:bt], w_pp[:, kk, hs_], start=(kk == 0), stop=(kk == 1))
                        k.tt("dve", gate_t[rows, :], bank2[rows, :], gate_t[rows, :], ALU.mult)
                        k.tt("pool", h1[rows, b, hs_], gate_t[rows, :], h1[rows, b, hs_], ALU.add)
                    k.dma("sp", y_dst(b), h1[rows, b, :])

            phase2_tile(4, omixs_d.v, lambda b: xs_d.v, lambda b: psm_d.v, lambda b: ys_d.v)
            first_stream[0] = False
            for t in range(NT):
                q0 = t * 512
                phase2_tile(512, omix_d.v[:, :, q0:q0 + 512],
                            lambda b: x_d.v[q0 + b * 128:q0 + (b + 1) * 128, :],
                            lambda b: p_d.v[q0 + b * 128:q0 + (b + 1) * 128, :],
                            lambda b: y_d.v[q0 + b * 128:q0 + (b + 1) * 128, :])
        k.finish()
    return nc, k


_NC_CACHE = {}


def run_cores(inputs, S, NPG, NPHYS, stop_after=None, ncores=8):
    key = (S, NPG, NPHYS, stop_after)
    if key not in _NC_CACHE:
        _NC_CACHE[key] = build(S, NPG, NPHYS, stop_after)
    nc, kb = _NC_CACHE[key]
    f = lambda a: np.ascontiguousarray(np.asarray(a))
    consts = host_consts(S, NPG * 128)
    cache_cat = np.concatenate([np.asarray(inputs["cache_krope"][0]).reshape(NPHYS * 128, 32),
                                np.asarray(inputs["cache_ckv"][0]).reshape(NPHYS * 128, 256)], axis=1)
    cache_cat = np.ascontiguousarray(cache_cat, dtype=np.float32)
    wmap = {n: f(inputs[n][0]).reshape(s) for n, s in W_SHAPES.items()}
    in_maps = []
    for c in range(ncores):
        m = dict(wmap)
        m.update(consts)
        m["x"] = f(inputs["x_prompt"][c])
        m["xs"] = f(inputs["x_sample"][4 * c:4 * c + 4, 0])
        m["p"] = f(inputs["p_prompt"][0, c])
        m["psm"] = f(inputs["p_sample"][0, 4 * c:4 * c + 4, 0])
        m["cache_cat"] = cache_cat
        m["state_gdn"] = f(inputs["state_gdn"][0, 4 * c:4 * c + 4])
        m["state_conv"] = f(inputs["state_conv"][0, 4 * c:4 * c + 4]).reshape(4, 3 * 1536)
        m["pt"] = f(inputs["page_table"][4 * c:4 * c + 4]).astype(np.int32)
        in_maps.append(m)
    res = run_bass_kernel_spmd(nc, in_maps, core_ids=list(range(ncores))).results
    g = lambda n: np.stack([np.asarray(r[n]) for r in res])
    y = g("y"); ys = g("ys").reshape(4 * ncores, 1, 1024)
    return (y, ys, g("o_ckv")[None], g("o_kr")[None], g("o_gdn")[None], g("o_conv")[None],
            g("o_ckv_s").reshape(1, 4 * ncores, 1, 256), g("o_kr_s").reshape(1, 4 * ncores, 1, 32),
            g("o_gdn_s").reshape(1, 4 * ncores, 8, 64, 64), g("o_conv_s").reshape(1, 4 * ncores, 3, 1536))


def kernel(**inputs):
    S = inputs["x_prompt"].shape[1]
    NPG = inputs["page_table"].shape[1]
    NPHYS = inputs["cache_ckv"].shape[1]
    outs = run_cores(inputs, S, NPG, NPHYS)
    return tuple(np.ascontiguousarray(o.astype(np.float32)) for o in outs)
```
